# Optimizing a Trainium2 kernel written in Bass

```python
import functools
import jax, jax.numpy as jnp
from jax import lax
import numpy as np

D_MODEL = 2048
BATCH = 2
SEQ = 4096
DEPTH = 1
DEC_BATCH = 8
DEC_SEQ = 16
PAST_LEN = 1024

CHUNK = 64
HEAD_DIM = 64
RWKV_WIDTH = D_MODEL // 2
FOX_WIDTH = D_MODEL - RWKV_WIDTH
RWKV_HEADS = RWKV_WIDTH // HEAD_DIM
FOX_HEADS = FOX_WIDTH // HEAD_DIM
DECAY_LORA = 64
AAA_LORA = 64
GATE_LORA = 160
RWKV_PROJ = 3 * RWKV_WIDTH + DECAY_LORA + AAA_LORA + GATE_LORA
FOX_PROJ = 3 * FOX_WIDTH + FOX_HEADS + FOX_WIDTH
P_TOTAL = RWKV_PROJ + FOX_PROJ
D_FF = 4 * D_MODEL
Q_BLOCK = 128
ALPHA = (2 * DEPTH) ** 0.25
BETA = (8 * DEPTH) ** -0.25
LN_EPS = 1e-5
GN_EPS = 64e-5
RMS_EPS = 1e-6
ATTN_SCALE = HEAD_DIM ** -0.5

kernel_name = 'hybrid_rwkv7_fox_stream_encoder'


def layer_norm(x, g, b):
    xf = x.astype(jnp.float32)
    mu = jnp.mean(xf, -1, keepdims=True)
    var = jnp.mean(jnp.square(xf - mu), -1, keepdims=True)
    return ((xf - mu) * lax.rsqrt(var + LN_EPS) * g + b).astype(x.dtype)


def token_shift(p, prev, mu):
    p_prev = jnp.concatenate([prev.astype(p.dtype), p[:, :-1]], axis=1)
    return p + (p_prev - p) * mu


def rwkv7_mixer(ps, s0, w0, w2, a0, a2, g2, k_k, k_a, r_k, lnx_g, lnx_b):
    B, T, _ = ps.shape
    f32 = jnp.float32
    o1, o2, o3 = RWKV_WIDTH, 2 * RWKV_WIDTH, 3 * RWKV_WIDTH
    o4 = o3 + DECAY_LORA
    o5 = o4 + AAA_LORA
    r, k, v = ps[..., :o1], ps[..., o1:o2], ps[..., o2:o3]
    xw, xa, xg = ps[..., o3:o4], ps[..., o4:o5], ps[..., o5:]
    hs = (B, T, RWKV_HEADS, HEAD_DIM)
    wlog = -jax.nn.softplus(-(w0 + jnp.tanh(xw) @ w2).astype(f32)) - 0.5
    decay = jnp.exp(-jnp.exp(wlog)).reshape(hs)
    a_lr = jax.nn.sigmoid((a0 + xa @ a2).astype(f32))
    g = jax.nn.sigmoid(xg) @ g2
    kk = (k * k_k).astype(f32).reshape(hs)
    kk = kk / jnp.maximum(jnp.sqrt(jnp.sum(kk * kk, -1, keepdims=True)), 1e-12)
    k_h = (k.astype(f32) * (1.0 + (a_lr - 1.0) * k_a)).reshape(hs)
    a_h = a_lr.reshape(hs)
    r_h = r.astype(f32).reshape(hs)
    v_h = v.astype(f32).reshape(hs)

    def step(S, inp):
        r_t, w_t, k_t, v_t, a_t, b_t = inp
        sa = jnp.einsum('bhvk,bhk->bhv', S, a_t)
        S = S * w_t[:, :, None, :] + sa[..., None] * b_t[:, :, None, :] + v_t[..., None] * k_t[:, :, None, :]
        return S, jnp.einsum('bhvk,bhk->bhv', S, r_t)

    seqs = tuple(jnp.moveaxis(z, 1, 0) for z in (r_h, decay, k_h, v_h, -kk, kk * a_h))
    s_final, y = lax.scan(step, s0.astype(f32), seqs)
    y = jnp.moveaxis(y, 0, 1)
    mu = jnp.mean(y, -1, keepdims=True)
    var = jnp.mean(jnp.square(y - mu), -1, keepdims=True)
    yn = ((y - mu) * lax.rsqrt(var + GN_EPS)).reshape(B, T, RWKV_WIDTH) * lnx_g + lnx_b
    bonus = jnp.sum(r_h * k_h * r_k, -1, keepdims=True) * v_h
    out = (yn + bonus.reshape(B, T, RWKV_WIDTH)) * g
    return out.astype(ps.dtype), s_final


def fox_prompt_attention(q, k, v, logf):
    B, T, H, Dh = q.shape
    nb = T // Q_BLOCK
    f32 = jnp.float32
    c = jnp.cumsum(logf, axis=1)
    cT = jnp.moveaxis(c, -1, 1)
    kf = k.astype(f32)
    vf = v.astype(f32)
    qb = (q.astype(f32) * ATTN_SCALE).reshape(B, nb, Q_BLOCK, H, Dh).transpose(1, 0, 2, 3, 4)
    cb = c.reshape(B, nb, Q_BLOCK, H).transpose(1, 0, 3, 2)
    kpos = jnp.arange(T)

    def block(args):
        i, q_i, c_i = args
        qpos = i * Q_BLOCK + jnp.arange(Q_BLOCK)
        s = jnp.einsum('bqhd,bkhd->bhqk', q_i, kf) + (c_i[..., :, None] - cT[..., None, :])
        s = jnp.where(kpos[None, :] <= qpos[:, None], s, -jnp.inf)
        p = jax.nn.softmax(s, axis=-1)
        return jnp.einsum('bhqk,bkhd->bqhd', p, vf)

    o = lax.map(block, (jnp.arange(nb), qb, cb))
    return o.transpose(1, 0, 2, 3, 4).reshape(B, T, H, Dh)


def fox_sample_attention(q, k, v, logf, cache_k, cache_v, cache_logf):
    B, T, H, Dh = q.shape
    P = cache_k.shape[1]
    f32 = jnp.float32
    k_all = jnp.concatenate([cache_k.astype(f32), k.astype(f32)], axis=1)
    v_all = jnp.concatenate([cache_v.astype(f32), v.astype(f32)], axis=1)
    c = jnp.cumsum(jnp.concatenate([cache_logf.astype(f32), logf], axis=1), axis=1)
    cT = jnp.moveaxis(c, -1, 1)
    s = jnp.einsum('bqhd,bkhd->bhqk', q.astype(f32) * ATTN_SCALE, k_all) + (cT[..., P:, None] - cT[..., None, :])
    qpos = P + jnp.arange(T)
    kpos = jnp.arange(P + T)
    s = jnp.where(kpos[None, :] <= qpos[:, None], s, -jnp.inf)
    p = jax.nn.softmax(s, axis=-1)
    return jnp.einsum('bhqk,bkhd->bqhd', p, v_all)


def hybrid_layer(x, shift_prev, s0, fox_attn, w_in, rwkv_mu, rwkv_w0, rwkv_w2, rwkv_a0, rwkv_a2, rwkv_g2,
                 rwkv_k_k, rwkv_k_a, rwkv_r_k, rwkv_lnx_g, rwkv_lnx_b, fox_b_f, fox_out_g, w_o,
                 ln1_g, ln1_b, w_up, w_down, ln2_g, ln2_b):
    B, T, _ = x.shape
    f32 = jnp.float32
    proj = x @ w_in
    p_rwkv = proj[..., :RWKV_PROJ]
    p_fox = proj[..., RWKV_PROJ:]
    ps = token_shift(p_rwkv, shift_prev, rwkv_mu)
    y_r, s_new = rwkv7_mixer(ps, s0, rwkv_w0, rwkv_w2, rwkv_a0, rwkv_a2, rwkv_g2,
                             rwkv_k_k, rwkv_k_a, rwkv_r_k, rwkv_lnx_g, rwkv_lnx_b)
    hs = (B, T, FOX_HEADS, HEAD_DIM)
    q = p_fox[..., :FOX_WIDTH].reshape(hs)
    k = p_fox[..., FOX_WIDTH:2 * FOX_WIDTH].reshape(hs)
    v = p_fox[..., 2 * FOX_WIDTH:3 * FOX_WIDTH].reshape(hs)
    f_logit = p_fox[..., 3 * FOX_WIDTH:3 * FOX_WIDTH + FOX_HEADS]
    og = p_fox[..., 3 * FOX_WIDTH + FOX_HEADS:]
    logf = jax.nn.log_sigmoid((f_logit + fox_b_f).astype(f32))
    o = fox_attn(q, k, v, logf)
    o = o * lax.rsqrt(jnp.mean(jnp.square(o), -1, keepdims=True) + RMS_EPS)
    y_f = (o.reshape(B, T, FOX_WIDTH) * fox_out_g * jax.nn.sigmoid(og.astype(f32))).astype(x.dtype)
    mix = jnp.concatenate([y_r, y_f], axis=-1) @ w_o
    h = layer_norm(ALPHA * x + mix, ln1_g, ln1_b)
    ffn = jnp.square(jax.nn.relu(h @ w_up)) @ w_down
    out = layer_norm(ALPHA * h + ffn, ln2_g, ln2_b)
    return out, k, v, logf, s_new, p_rwkv[:, -1:]


def setup_inputs(seed: int = 0) -> dict:
    key = jax.random.key(seed)
    ks = iter(jax.random.split(key, 40))
    f32 = jnp.float32

    def nrm(shape, scale):
        return scale * jax.random.normal(next(ks), shape, f32)

    w0_base = jnp.tile(jnp.linspace(-6.0, -1.0, HEAD_DIM, dtype=f32), RWKV_HEADS)
    return {
        'x_prompt': nrm((BATCH, SEQ, D_MODEL), 1.0),
        'x_sample': nrm((DEC_BATCH, DEC_SEQ, D_MODEL), 1.0),
        'cache_fox_k': nrm((DEPTH, DEC_BATCH, PAST_LEN, FOX_HEADS, HEAD_DIM), 1.0),
        'cache_fox_v': nrm((DEPTH, DEC_BATCH, PAST_LEN, FOX_HEADS, HEAD_DIM), 1.0),
        'cache_fox_logf': jax.nn.log_sigmoid(3.0 + nrm((DEPTH, DEC_BATCH, PAST_LEN, FOX_HEADS), 0.5)),
        'state_rwkv_wkv': nrm((DEPTH, DEC_BATCH, RWKV_HEADS, HEAD_DIM, HEAD_DIM), 0.3),
        'state_rwkv_shift': nrm((DEPTH, DEC_BATCH, 1, RWKV_PROJ), 1.0),
        'w_in': nrm((DEPTH, D_MODEL, P_TOTAL), D_MODEL ** -0.5),
        'rwkv_mu': jax.random.uniform(next(ks), (DEPTH, RWKV_PROJ), f32),
        'rwkv_w0': w0_base + nrm((DEPTH, RWKV_WIDTH), 0.1),
        'rwkv_w2': nrm((DEPTH, DECAY_LORA, RWKV_WIDTH), 0.1 * DECAY_LORA ** -0.5),
        'rwkv_a0': nrm((DEPTH, RWKV_WIDTH), 0.1),
        'rwkv_a2': nrm((DEPTH, AAA_LORA, RWKV_WIDTH), 0.5 * AAA_LORA ** -0.5),
        'rwkv_g2': nrm((DEPTH, GATE_LORA, RWKV_WIDTH), GATE_LORA ** -0.5),
        'rwkv_k_k': 0.85 + nrm((DEPTH, RWKV_WIDTH), 0.02),
        'rwkv_k_a': 1.0 + nrm((DEPTH, RWKV_WIDTH), 0.02),
        'rwkv_r_k': -0.04 + nrm((DEPTH, RWKV_HEADS, HEAD_DIM), 0.01),
        'rwkv_lnx_g': 1.0 + nrm((DEPTH, RWKV_WIDTH), 0.02),
        'rwkv_lnx_b': nrm((DEPTH, RWKV_WIDTH), 0.02),
        'fox_b_f': 3.0 + nrm((DEPTH, FOX_HEADS), 0.5),
        'fox_out_g': 1.0 + nrm((DEPTH, FOX_WIDTH), 0.02),
        'w_o': nrm((DEPTH, D_MODEL, D_MODEL), BETA * D_MODEL ** -0.5),
        'ln1_g': 1.0 + nrm((DEPTH, D_MODEL), 0.02),
        'ln1_b': nrm((DEPTH, D_MODEL), 0.02),
        'w_up': nrm((DEPTH, D_MODEL, D_FF), D_MODEL ** -0.5),
        'w_down': nrm((DEPTH, D_FF, D_MODEL), BETA * D_FF ** -0.5),
        'ln2_g': 1.0 + nrm((DEPTH, D_MODEL), 0.02),
        'ln2_b': nrm((DEPTH, D_MODEL), 0.02),
    }


def reference(x_prompt, x_sample, cache_fox_k, cache_fox_v, cache_fox_logf, state_rwkv_wkv, state_rwkv_shift,
              w_in, rwkv_mu, rwkv_w0, rwkv_w2, rwkv_a0, rwkv_a2, rwkv_g2, rwkv_k_k, rwkv_k_a, rwkv_r_k,
              rwkv_lnx_g, rwkv_lnx_b, fox_b_f, fox_out_g, w_o, ln1_g, ln1_b, w_up, w_down, ln2_g, ln2_b):
    assert x_sample.shape[1] <= CHUNK
    xp, xs = x_prompt, x_sample
    bp = xp.shape[0]
    pk, pv, pf, pS, psh = [], [], [], [], []
    sk, sv, sf, sS, ssh = [], [], [], [], []
    for l in range(DEPTH):
        lw = (w_in[l], rwkv_mu[l], rwkv_w0[l], rwkv_w2[l], rwkv_a0[l], rwkv_a2[l], rwkv_g2[l],
              rwkv_k_k[l], rwkv_k_a[l], rwkv_r_k[l], rwkv_lnx_g[l], rwkv_lnx_b[l], fox_b_f[l], fox_out_g[l],
              w_o[l], ln1_g[l], ln1_b[l], w_up[l], w_down[l], ln2_g[l], ln2_b[l])
        shift0 = jnp.zeros((bp, 1, RWKV_PROJ), xp.dtype)
        s_zero = jnp.zeros((bp, RWKV_HEADS, HEAD_DIM, HEAD_DIM), jnp.float32)
        xp, k_p, v_p, f_p, S_p, sh_p = hybrid_layer(xp, shift0, s_zero, fox_prompt_attention, *lw)
        fox_s = functools.partial(fox_sample_attention, cache_k=cache_fox_k[l], cache_v=cache_fox_v[l],
                                  cache_logf=cache_fox_logf[l])
        xs, k_s, v_s, f_s, S_s, sh_s = hybrid_layer(xs, state_rwkv_shift[l], state_rwkv_wkv[l], fox_s, *lw)
        pk.append(k_p); pv.append(v_p); pf.append(f_p); pS.append(S_p); psh.append(sh_p)
        sk.append(k_s); sv.append(v_s); sf.append(f_s); sS.append(S_s); ssh.append(sh_s)
    return (xp, xs,
            jnp.stack(pk), jnp.stack(pv), jnp.stack(pf), jnp.stack(pS), jnp.stack(psh),
            jnp.stack(sk), jnp.stack(sv), jnp.stack(sf), jnp.stack(sS), jnp.stack(ssh))
```

```python
import numpy as np
import concourse.bass as bass
import concourse.mybir as mybir
from concourse.bass_utils import run_bass_kernel_spmd

F32 = mybir.dt.float32
BF16 = mybir.dt.bfloat16
AF = mybir.ActivationFunctionType
ALU = mybir.AluOpType
AX = mybir.AxisListType

D = 2048
NPR = 4096
NS = 16
NTOK = NPR + NS
OWN0 = 3072
NOWN = 1024 + NS
DFF = 8192
C0 = float(np.exp(-0.5))
ALPHA = 2.0 ** 0.25
BIG = 30000.0

SEG = dict(rk=(1024, 1024), rv=(2048, 1024), xwxa=(3072, 128), fk=(3360 + 1024, 1024), f=(3360 + 3072, 16),
           r=(0, 1024), xg=(3200, 160), q=(3360, 1024), og=(3360 + 3088, 1024), fv=(3360 + 2048, 1024))
ORDER = ["rk", "rv", "xwxa", "fk", "f", "r", "xg", "q", "og", "fv"]
ROW = {}
_o = 0
for _k in ORDER:
    ROW[_k] = _o
    _o += SEG[_k][1]
N_ALL = ROW["r"]
N_OWN = ROW["fv"] - N_ALL
N_FM = ROW["fv"]
PERM = np.concatenate([np.arange(SEG[k][0], SEG[k][0] + SEG[k][1]) for k in ORDER])


class Prog:
    ENG = ("pe", "act", "dve", "pool", "sp")

    def __init__(self):
        self.nc = bass.Bass("TRN2", target_bir_lowering=False)
        nc = self.nc
        self.eng = {"pe": nc.tensor, "act": nc.scalar, "dve": nc.vector, "pool": nc.gpsimd, "sp": nc.sync}
        self.ops = []
        self.sb_base = 16640
        self.sb_top = 229376
        self.sb_off = self.sb_base
        self.nalloc = 0

    def sb(self, name, shape, dtype):
        per = int(np.prod(shape[1:])) * (2 if dtype == BF16 else 4)
        off = (self.sb_off + 63) // 64 * 64
        assert off + per <= self.sb_top, ("sbuf overflow", name, off, per)
        self.sb_off = off + per
        self.nalloc += 1
        return self.nc.alloc_sbuf_tensor_at("%s_%d" % (name, self.nalloc), list(shape), dtype, offset=off)

    def mark(self):
        return self.sb_off

    def release(self, m):
        self.sb_off = m

    def op(self, eng, fn, reads=(), writes=()):
        self.ops.append(dict(eng=eng, fn=fn, reads=tuple(reads), writes=tuple(writes), dma=False, bar=False))

    def dma(self, q, out, in_, reads=(), writes=(), semkey=None, final=False, **kw):
        if semkey is None:
            semkey = writes[0] if writes else reads[0]
        e = self.eng[q]
        self.ops.append(dict(eng=q, fn=(lambda e=e, out=out, in_=in_, kw=kw: e.dma_start(out=out, in_=in_, **kw)),
                             reads=tuple(reads), writes=tuple(writes), dma=True, semkey=semkey, final=final, bar=False))

    def barrier(self):
        for e in self.ENG:
            self.ops.append(dict(eng=e, fn=None, reads=(), writes=(), dma=False, bar=True))

    def emit(self):
        nc = self.nc
        ops = self.ops
        n = len(ops)
        lastw = {}
        readers = {}
        deps = [None] * n
        last_eng = {}
        dma_since = []
        bar_group = None
        for i, o in enumerate(ops):
            if o["bar"]:
                if bar_group is None:
                    bar_group = (set(last_eng.values()) | set(dma_since))
                deps[i] = set(bar_group)
                nxt = ops[i + 1] if i + 1 < n else None
                if nxt is None or not nxt["bar"]:
                    bar_group = None
                    dma_since = []
                    lastw = {}
                    readers = {}
                continue
            d = set()
            for k in o["reads"]:
                if k in lastw:
                    d.add(lastw[k])
            for k in o["writes"]:
                if k in lastw:
                    d.add(lastw[k])
                for r in readers.get(k, ()):
                    d.add(r)
            d.discard(i)
            deps[i] = d
            for k in o["reads"]:
                readers.setdefault(k, []).append(i)
            for k in o["writes"]:
                lastw[k] = i
                readers[k] = []
            if o["dma"]:
                dma_since.append(i)
            else:
                last_eng[o["eng"]] = i

        def need_wait(o, pj):
            if pj["dma"] or o["dma"] or o["bar"]:
                return not (o["bar"] and (not pj["dma"]) and pj["eng"] == o["eng"])
            if pj["eng"] != o["eng"]:
                return True
            if o["eng"] == "pe":
                return False
            return any(k in pj["writes"] for k in o["reads"])

        needed = set()
        for i, o in enumerate(ops):
            for j in deps[i]:
                if need_wait(o, ops[j]):
                    needed.add(j)
            if o["dma"]:
                needed.add(i)
        esem = {e: nc.alloc_semaphore("c_" + e) for e in self.ENG}
        ecnt = {e: 0 for e in self.ENG}
        dpool = []
        dcnt = []
        keymap = {}
        token = [None] * n
        for i, o in enumerate(ops):
            if o["bar"]:
                keymap = {}
                continue
            if o["dma"]:
                sk = o["semkey"]
                if sk not in keymap:
                    idx = len(keymap)
                    if idx >= len(dpool):
                        dpool.append(nc.alloc_semaphore("d%d" % idx))
                        dcnt.append(0)
                    keymap[sk] = idx
                idx = keymap[sk]
                dcnt[idx] += 16
                token[i] = ("d:%d" % idx, dpool[idx], dcnt[idx])
            elif i in needed:
                e = o["eng"]
                ecnt[e] += 1
                token[i] = ("e:" + e, esem[e], ecnt[e])
        dsem = dpool
        seen = {e: {} for e in self.ENG}
        dlatest = {}
        nwaits = 0
        for i, o in enumerate(ops):
            e = o["eng"]
            eng = self.eng[e]
            want = {}
            for j in deps[i]:
                tk = token[j]
                if tk is None or not need_wait(o, ops[j]):
                    continue
                name, sem, val = tk
                if name.startswith("d:"):
                    val = dlatest[name]
                if want.get(name, (None, 0))[1] < val:
                    want[name] = (sem, val)
            for name, (sem, val) in want.items():
                if seen[e].get(name, 0) < val:
                    eng.wait_ge(sem, val)
                    seen[e][name] = val
                    nwaits += 1
            if o["bar"]:
                continue
            ins = o["fn"]()
            tk = token[i]
            if tk is not None:
                name, sem, val = tk
                if o["dma"]:
                    ins.then_inc(sem, 16)
                    dlatest[name] = val
                else:
                    ins.then_inc(sem, 1)
        for idx in range(len(dpool)):
            name = "d:%d" % idx
            val = dlatest.get(name, 0)
            if val and seen["sp"].get(name, 0) < val:
                self.eng["sp"].wait_ge(dpool[idx], val)
                seen["sp"][name] = val
        self.stats = dict(n_ops=n, n_waits=nwaits, n_dsem=len(dsem), cnt=dict(ecnt))
        return nc


TT_ALL = [(i * 512, 512) for i in range(8)] + [(NPR, NS)]
TT_OWN = [(OWN0 - 1, 1), (OWN0, 512), (OWN0 + 512, 512), (NPR, NS)]


def stage_A(P, io):
    nc = P.nc
    m0 = P.mark()
    xs = P.sb("xs", [128, 16, NTOK], BF16)
    wt = [P.sb("wt%d" % i, [128, 16, 512], BF16) for i in range(2)]
    NOT = 8
    ot = [P.sb("ot%d" % i, [128, 512], F32) for i in range(NOT)]
    vt = [P.sb("vt%d" % i, [128, 1024], F32) for i in range(2)]
    ps = io["ps"]
    xTr = io["xT"].rearrange("(kc p) t -> p kc t", p=128)
    wr = io["w_in"].rearrange("(kc p) c -> p kc c", p=128)
    PJ = io["PJ"]
    VTM = io["VTM"]
    for g in range(8):
        P.dma("pool", xs[:, 2 * g:2 * g + 2, :], xTr[:, 2 * g:2 * g + 2, :], writes=["xs%d" % g])
    st = dict(oi=0, pi=0, wi=0, vi=0)

    def evac(dst, src, wkey, pkey):
        st["pi"] += 1
        if st["pi"] % 2 == 0:
            P.op("act", lambda: nc.scalar.copy(dst, src), reads=[pkey], writes=[wkey])
        else:
            P.op("dve", lambda: nc.vector.tensor_copy(dst, src), reads=[pkey], writes=[wkey])

    def fm_chunk(c0, cw, tts):
        wb = st["wi"] % 2
        st["wi"] += 1
        P.dma("pool", wt[wb][:, :, 0:cw], wr[:, :, c0:c0 + cw], writes=["wt%d" % wb])
        for m in range(0, cw, 128):
            mw = min(128, cw - m)
            for (t0, tw) in tts:
                pb = st["pi"] % 6
                for kc in range(16):
                    P.op("pe", (lambda pb=pb, wb=wb, kc=kc, m=m, mw=mw, t0=t0, tw=tw:
                                nc.tensor.matmul(ps[pb][0:mw, 0:tw], wt[wb][:, kc, m:m + mw], xs[:, kc, t0:t0 + tw],
                                                 start=(kc == 0), stop=(kc == 15))),
                         reads=["wt%d" % wb, "xs%d" % (kc // 2)], writes=["ps%d" % pb])
                ob = st["oi"] % NOT
                st["oi"] += 1
                evac(ot[ob][0:mw, 0:tw], ps[pb][0:mw, 0:tw], "ot%d" % ob, "ps%d" % pb)
                P.dma("sp", PJ[c0 + m:c0 + m + mw, t0:t0 + tw], ot[ob][0:mw, 0:tw], reads=["ot%d" % ob],
                      semkey="oto%d" % ob, **({"allow_slow_non_contiguous": True} if tw == 1 else {}))

    for c0 in range(0, N_ALL, 512):
        fm_chunk(c0, min(512, N_ALL - c0), TT_ALL)
    for c0 in range(N_ALL, N_FM, 512):
        fm_chunk(c0, min(512, N_FM - c0), TT_OWN)
    wbs = []
    for h in range(2):
        wb = st["wi"] % 2
        st["wi"] += 1
        P.dma("pool", wt[wb][:, :, :], wr[:, :, N_FM + 512 * h:N_FM + 512 * h + 512], writes=["wt%d" % wb])
        wbs.append(wb)
    tb = [(i * 128, 128) for i in range(32)] + [(NPR, NS)]
    for (t0, tw) in tb:
        vb = st["vi"] % 2
        st["vi"] += 1
        for h in range(2):
            pb = st["pi"] % 6
            wb = wbs[h]
            for kc in range(16):
                P.op("pe", (lambda pb=pb, wb=wb, kc=kc, t0=t0, tw=tw:
                            nc.tensor.matmul(ps[pb][0:tw, 0:512], xs[:, kc, t0:t0 + tw], wt[wb][:, kc, :],
                                             start=(kc == 0), stop=(kc == 15))),
                     reads=["wt%d" % wb, "xs%d" % (kc // 2)], writes=["ps%d" % pb])
            evac(vt[vb][0:tw, 512 * h:512 * h + 512], ps[pb][0:tw, 0:512], "vt%d" % vb, "ps%d" % pb)
        P.dma("sp", VTM[t0:t0 + tw, :], vt[vb][0:tw, :], reads=["vt%d" % vb], semkey="vto%d" % vb)
    P.barrier()
    P.release(m0)


class Ops:
    def __init__(self, P):
        self.P = P
        self.nc = P.nc
        self.rr = 0

    def E(self, which):
        return {"dve": self.nc.vector, "pool": self.nc.gpsimd}[which]

    def pick(self, choices=("dve", "pool")):
        self.rr += 1
        return choices[self.rr % len(choices)]

    def tt(self, eng, out, a, b, op, r, w):
        e = self.E(eng)
        self.P.op(eng, lambda: e.tensor_tensor(out, a, b, op), reads=r, writes=w)

    def ts(self, eng, out, a, s1, s2, op0, op1, r, w):
        e = self.E(eng)
        if s2 is None:
            self.P.op(eng, lambda: e.tensor_scalar(out, a, s1, None, op0), reads=r, writes=w)
        else:
            self.P.op(eng, lambda: e.tensor_scalar(out, a, s1, s2, op0, op1), reads=r, writes=w)

    def stt(self, eng, out, a, sc, b, op0, op1, r, w):
        e = self.E(eng)
        self.P.op(eng, lambda: e.scalar_tensor_tensor(out, a, sc, b, op0, op1), reads=r, writes=w)

    def cp(self, eng, out, a, r, w):
        if eng == "act":
            self.P.op("act", lambda: self.nc.scalar.copy(out, a), reads=r, writes=w)
        else:
            e = self.E(eng)
            self.P.op(eng, lambda: e.tensor_copy(out, a), reads=r, writes=w)

    def act(self, out, a, func, r, w, bias=None, scale=None):
        kw = {}
        if bias is not None:
            kw["bias"] = bias
        if scale is not None:
            kw["scale"] = scale
        self.P.op("act", lambda: self.nc.scalar.activation(out=out, in_=a, func=func, **kw), reads=r, writes=w)

    def mm(self, out, lhsT, rhs, r, w, start=True, stop=True):
        self.P.op("pe", lambda: self.nc.tensor.matmul(out, lhsT, rhs, start=start, stop=stop), reads=r, writes=w)

    def tr(self, out, a, ident, r, w):
        self.P.op("pe", lambda: self.nc.tensor.transpose(out, a, ident), reads=r, writes=w)

    def memset(self, eng, out, val, w):
        e = self.E(eng)
        self.P.op(eng, lambda: e.memset(out, val), writes=w)

    def recip(self, out, a, r, w):
        self.P.op("dve", lambda: self.nc.vector.reciprocal(out, a), reads=r, writes=w)

    def rsqrt(self, out, a, r, w):
        self.act(out, a, AF.Sqrt, r, w)
        self.recip(out, out, w, w)


def load_consts(P, io):
    nc = P.nc
    c = {}
    c["identb"] = P.sb("identb", [128, 128], BF16)
    c["identf"] = P.sb("identf", [128, 128], F32)
    c["b64"] = P.sb("b64", [128, 128], F32)
    c["b64m"] = P.sb("b64m", [128, 128], F32)
    c["mskL"] = P.sb("mskL", [64, 64], F32)
    c["mskU"] = P.sb("mskU", [64, 128], F32)
    c["i2"] = P.sb("i2", [128, 64], F32)
    c["maskR"] = P.sb("maskR", [128, 512], F32)
    c["tri"] = P.sb("tri", [128, 128], BF16)
    c["zcol"] = P.sb("zcol", [128, 1], F32)
    P.dma("pool", c["identb"][:], io["c_ident"], writes=["c_identb"])
    P.dma("sp", c["identf"][:], io["c_ident"], writes=["c_identf"])
    P.dma("sp", c["b64"][:], io["c_b64"], writes=["c_b64"])
    P.dma("sp", c["b64m"][:], io["c_b64m"], writes=["c_b64m"])
    P.dma("sp", c["mskL"][:], io["c_mskL"], writes=["c_mskL"])
    P.dma("sp", c["mskU"][:], io["c_mskU"], writes=["c_mskU"])
    P.dma("sp", c["i2"][:], io["c_i2"], writes=["c_i2"])
    P.dma("sp", c["maskR"][:], io["c_maskR"], writes=["c_maskR"])
    P.dma("pool", c["tri"][:], io["c_tri"], writes=["c_tri"])
    P.op("pool", lambda: nc.gpsimd.memset(c["zcol"][:], 0.0), writes=["c_zcol"])
    c["keys"] = ["c_identb", "c_identf", "c_b64", "c_b64m", "c_mskL", "c_mskU", "c_i2", "c_maskR", "c_tri", "c_zcol"]
    return c


PRM = dict(mu_r=0, mu_k=1, mu_v=2, w0=3, a0=4, k_k=5, k_a=6, r_k=7, lnx_g=8, lnx_b=9, omk=10)


def stage_B1(P, io, cst, yT, pairs=range(8), tiles=range(9), lvl=99):
    nc = P.nc
    O = Ops(P)
    PJ = io["PJ"]
    ps = io["ps"]
    m0 = P.mark()
    CK = cst["keys"]
    prm = P.sb("prm", [128, 8, 11], F32)
    P.dma("sp", prm[:, :, 0:10], io["rw_prm"], writes=["prm"])
    O.ts("dve", prm[:, :, 10:11], prm[:, :, 6:7], -1.0, 1.0, ALU.mult, ALU.add, ["prm"], ["prm"])
    mul = P.sb("mul", [128, 3], F32)
    P.dma("sp", mul[:], io["rw_mul"], writes=["mul"])
    w2a2 = P.sb("w2a2", [128, 1024], BF16)
    g2a = P.sb("g2a", [128, 1024], BF16)
    g2b = P.sb("g2b", [32, 1024], BF16)
    P.dma("pool", w2a2[:], io["rw_w2a2"], writes=["w2a2"])
    P.dma("pool", g2a[:], io["rw_g2"][0:128, :], writes=["g2a"])
    P.dma("pool", g2b[:], io["rw_g2"][128:160, :], writes=["g2b"])
    Hst = P.sb("Hst", [128, 8, 64], F32)
    O.memset("pool", Hst[:], 0.0, ["H%d" % p for p in range(8)])

    def S(name, shape, dt, nb=2):
        return [P.sb(name + str(i), shape, dt) for i in range(nb)]
    xwc = S("xwc", [128, 512], F32); xwp = S("xwp", [128, 512], F32, 1) * 2
    xgc = S("xgc", [128, 512], F32); xgp = S("xgp", [128, 512], F32, 1) * 2
    xhc = S("xhc", [32, 512], F32, 1) * 2; xhp = S("xhp", [32, 512], F32, 1) * 2
    txa = S("txa", [128, 512], BF16); sga = S("sga", [128, 512], BF16); sgb = S("sgb", [32, 512], BF16)
    names32 = ["kc", "kp", "vc", "vp", "rc", "rp", "ks", "vs", "rs", "sw", "al", "gg", "kk", "t1", "kkn", "bb", "kh",
               "cum", "cumx", "e_r", "e_a", "e_n", "e_c", "bon"]
    U = {n: S(n, [128, 512], F32, 2 if n in ("kc", "kp", "vc", "vp") else 1) for n in names32}
    gC = S("gC", [128, 8], F32)
    QQ = S("QQ", [128, 8, 2, 64], BF16); KK = S("KK", [128, 8, 2, 64], BF16)
    BH = S("BH", [128, 8, 64], BF16); KH = S("KH", [128, 8, 64], BF16); VV = S("VV", [128, 8, 64], BF16)
    TMa = S("TMa", [64, 8, 128], BF16, 1) * 2; TMb = S("TMb", [64, 8, 128], BF16, 1) * 2
    TMk = S("TMk", [64, 8, 128], BF16, 1) * 2; TMv = S("TMv", [64, 8, 128], BF16, 1) * 2
    Lb = S("Lb", [64, 8, 64], BF16, 4); LTb = S("LTb", [64, 8, 64], BF16, 4)
    GB = S("GB", [64, 8, 128], BF16); GK = S("GK", [64, 8, 128], BF16)
    Zf = S("Zf", [64, 8, 128], F32); Zb = S("Zb", [64, 8, 128], BF16)
    MT = S("MT", [128, 8, 64], F32, 1) * 2; NN = S("NN", [128, 8, 64], F32, 1) * 2
    DG = S("DG", [128, 8, 64], F32, 1) * 2
    Hb = S("Hb", [128, 8, 64], BF16)
    QeT = S("QeT", [128, 8, 64], BF16)
    ysb = S("ysb", [128, 512], F32, 1) * 2; yc_ = S("yc", [128, 512], F32, 1) * 2; sq_ = S("sq", [128, 512], F32, 1) * 2
    rstd = S("rstd", [128, 512], F32, 1) * 2

    def v3(ap, n):
        return ap.rearrange("p (a b) -> p a b", a=n)

    def psv(bank, rows, n, w):
        return ps[bank][rows, 0:n * w].rearrange("p (a b) -> p a b", a=n)

    def psb(bank, rows, n, w):
        return ps[bank][:, :].bitcast(BF16)[rows, 0:n * w].rearrange("p (a b) -> p a b", a=n)

    unit = [0]
    for ct in tiles:
        samp = (ct == 8)
        t0 = NPR if samp else 512 * ct
        TW = 64 if samp else 512
        RW_ = NS if samp else 512
        NCH = TW // 64
        do_out = ct >= 6
        lb = ct % 2

        def load_pair(cur, prv, r0, rows, kc_, kp_, q="sp"):
            if samp:
                O.memset("pool", cur[0:rows, 0:TW], 0.0, [kc_])
                O.memset("pool", prv[0:rows, 0:TW], 0.0, [kp_])
                P.dma(q, cur[0:rows, 0:NS], PJ[r0:r0 + rows, NPR:NPR + NS], writes=[kc_])
                P.dma(q, prv[0:rows, 1:NS], PJ[r0:r0 + rows, NPR:NPR + NS - 1], writes=[kp_])
                P.dma(q, prv[0:rows, 0:1], io["shiftT"][r0:r0 + rows, :], writes=[kp_])
            else:
                P.dma(q, cur[0:rows, 0:TW], PJ[r0:r0 + rows, t0:t0 + TW], writes=[kc_])
                if t0 == 0:
                    O.memset("pool", prv[0:rows, 0:1], 0.0, [kp_])
                    P.dma(q, prv[0:rows, 1:TW], PJ[r0:r0 + rows, 0:TW - 1], writes=[kp_])
                else:
                    P.dma(q, prv[0:rows, 0:TW], PJ[r0:r0 + rows, t0 - 1:t0 + TW - 1], writes=[kp_])

        def shift(eng, out, cur, prv, mu, r, w, rows=128):
            O.tt(eng, prv[0:rows, 0:TW], prv[0:rows, 0:TW], cur[0:rows, 0:TW], ALU.subtract, r, [r[1]])
            if eng == "dve":
                O.stt(eng, out[0:rows, 0:TW], prv[0:rows, 0:TW], mu, cur[0:rows, 0:TW], ALU.mult, ALU.add, r, w)
            else:
                O.ts(eng, prv[0:rows, 0:TW], prv[0:rows, 0:TW], mu, None, ALU.mult, None, r, [r[1]])
                O.tt(eng, out[0:rows, 0:TW], prv[0:rows, 0:TW], cur[0:rows, 0:TW], ALU.add, r, w)

        kx = "L%d" % lb
        load_pair(xwc[lb], xwp[lb], ROW["xwxa"], 128, kx + "xwc", "Lxwp")
        shift("dve", xwc[lb], xwc[lb], xwp[lb], mul[:, 0:1], [kx + "xwc", "Lxwp", "mul"], [kx + "xwc"])
        O.act(txa[lb][0:64, 0:TW], xwc[lb][0:64, 0:TW], AF.Tanh, [kx + "xwc"], [kx + "txa"])
        O.cp("pool", txa[lb][64:128, 0:TW], xwc[lb][64:128, 0:TW], [kx + "xwc"], [kx + "txa"])
        if do_out:
            load_pair(xgc[lb], xgp[lb], ROW["xg"], 128, kx + "xgc", "Lxgp")
            shift("pool", xgc[lb], xgc[lb], xgp[lb], mul[:, 1:2], [kx + "xgc", "Lxgp", "mul"], [kx + "xgc"])
            O.act(sga[lb][:, 0:TW], xgc[lb][:, 0:TW], AF.Sigmoid, [kx + "xgc"], [kx + "sga"])
            r0 = ROW["xg"] + 128
            if samp:
                O.memset("pool", xhc[lb][:, 0:TW], 0.0, ["Lxhc"])
                O.memset("pool", xhp[lb][:, 0:TW], 0.0, ["Lxhp"])
                P.dma("sp", xhc[lb][:, 0:NS], PJ[r0:r0 + 32, NPR:NPR + NS], writes=["Lxhc"])
                P.dma("sp", xhp[lb][:, 1:NS], PJ[r0:r0 + 32, NPR:NPR + NS - 1], writes=["Lxhp"])
                P.dma("sp", xhp[lb][:, 0:1], io["shiftT"][r0:r0 + 32, :], writes=["Lxhp"])
            else:
                P.dma("sp", xhc[lb][:, 0:TW], PJ[r0:r0 + 32, t0:t0 + TW], writes=["Lxhc"])
                P.dma("sp", xhp[lb][:, 0:TW], PJ[r0:r0 + 32, t0 - 1:t0 + TW - 1], writes=["Lxhp"])
            shift("pool", xhc[lb], xhc[lb], xhp[lb], mul[0:32, 2:3], ["Lxhc", "Lxhp", "mul"], ["Lxhc"], rows=32)
            O.act(sgb[lb][:, 0:TW], xhc[lb][:, 0:TW], AF.Sigmoid, ["Lxhc"], [kx + "sgb"])

        for p in pairs:
            u = unit[0] % 2
            unit[0] += 1
            SINGLE = ("xhc", "xhp", "MT", "NN", "DG", "ysb", "yc", "sq", "rstd", "TMa", "TMb", "TMk", "TMv")
            K = (lambda n, u=u: "u%d_%s" % (0 if (n in SINGLE or (n in U and len(U[n]) == 1)) else u, n))
            T = (lambda n, u=u: U[n][u % len(U[n])])
            pc = lambda j: prm[:, p, j:j + 1]
            cols = slice(128 * p, 128 * p + 128)
            W_ = slice(0, TW)
            load_pair(T("kc"), T("kp"), ROW["rk"] + 128 * p, 128, K("kc"), K("kp"))
            load_pair(T("vc"), T("vp"), ROW["rv"] + 128 * p, 128, K("vc"), K("vp"))
            shift("dve", T("ks"), T("kc"), T("kp"), pc(1), [K("kc"), K("kp"), "prm"], [K("ks")])
            shift("pool", T("vs"), T("vc"), T("vp"), pc(2), [K("vc"), K("vp"), "prm"], [K("vs")])
            if do_out:
                load_pair(T("rc"), T("rp"), ROW["r"] + 128 * p, 128, K("rc"), K("rp"))
                shift("pool", T("rs"), T("rc"), T("rp"), pc(0), [K("rc"), K("rp"), "prm"], [K("rs")])
            O.mm(ps[0][:, W_], w2a2[0:64, cols], txa[lb][0:64, W_], ["w2a2", kx + "txa"], ["ps0"])
            O.mm(ps[1][:, W_], w2a2[64:128, cols], txa[lb][64:128, W_], ["w2a2", kx + "txa"], ["ps1"])
            O.act(T("sw")[:, W_], ps[0][:, W_], AF.Sigmoid, ["ps0", "prm"], [K("sw")], bias=pc(3))
            O.act(T("al")[:, W_], ps[1][:, W_], AF.Sigmoid, ["ps1", "prm"], [K("al")], bias=pc(4))
            if samp:
                O.memset("pool", T("sw")[:, NS:TW], 0.0, [K("sw")])
            if do_out:
                O.mm(ps[2][:, W_], g2a[:, cols], sga[lb][:, W_], ["g2a", kx + "sga"], ["ps2"], start=True, stop=False)
                O.mm(ps[2][:, W_], g2b[0:32, cols], sgb[lb][0:32, W_], ["g2b", kx + "sgb"], ["ps2"], start=False, stop=True)
                O.cp("act", T("gg")[:, W_], ps[2][:, W_], ["ps2"], [K("gg")])
            O.ts("pool", T("kk")[:, W_], T("ks")[:, W_], pc(5), None, ALU.mult, None, [K("ks"), "prm"], [K("kk")])
            O.tt("pool", T("t1")[:, W_], T("kk")[:, W_], T("kk")[:, W_], ALU.mult, [K("kk")], [K("t1")])
            O.mm(ps[2][:, W_], cst["b64"][:, :], T("t1")[:, W_], ["c_b64", K("t1")], ["ps2"])
            O.ts("dve", T("t1")[:, W_], ps[2][:, W_], 1e-24, None, ALU.max, None, ["ps2"], [K("t1")])
            O.rsqrt(T("t1")[:, W_], T("t1")[:, W_], [K("t1")], [K("t1")])
            O.tt("pool", T("kkn")[:, W_], T("kk")[:, W_], T("t1")[:, W_], ALU.mult, [K("kk"), K("t1")], [K("kkn")])
            O.tt("dve", T("bb")[:, W_], T("kkn")[:, W_], T("al")[:, W_], ALU.mult, [K("kkn"), K("al")], [K("bb")])
            O.ts("dve", T("t1")[:, W_], T("al")[:, W_], pc(6), pc(10), ALU.mult, ALU.add, [K("al"), "prm", K("t1")], [K("t1")])
            O.tt("pool", T("kh")[:, W_], T("ks")[:, W_], T("t1")[:, W_], ALU.mult, [K("ks"), K("t1")], [K("kh")])
            sc_o, sc_m, sc_i, sc_z = T("cum")[:, W_], cst["maskR"][:, W_], T("sw")[:, W_], cst["zcol"][:, 0:1]
            P.op("dve", (lambda sc_o=sc_o, sc_m=sc_m, sc_i=sc_i, sc_z=sc_z:
                         nc.vector.tensor_tensor_scan(sc_o, sc_m, sc_i, sc_z, ALU.mult, ALU.add)),
                 reads=[K("sw")], writes=[K("cum")])
            O.tt("pool", T("cumx")[:, W_], T("cum")[:, W_], T("sw")[:, W_], ALU.subtract, [K("cum"), K("sw")], [K("cumx")])
            O.act(T("e_r")[:, W_], T("cum")[:, W_], AF.Exp, [K("cum")], [K("e_r")], scale=-C0)
            O.act(T("e_a")[:, W_], T("cumx")[:, W_], AF.Exp, [K("cumx")], [K("e_a")], scale=-C0)
            O.recip(T("e_n")[:, W_], T("e_r")[:, W_], [K("e_r")], [K("e_n")])
            cv = v3(T("cum")[:, W_], NCH)
            O.tt("dve", v3(T("cumx")[:, W_], NCH), cv[:, :, 63:64].to_broadcast([128, NCH, 64]), cv, ALU.subtract,
                 [K("cum"), K("e_a")], [K("cumx")])
            O.act(T("e_c")[:, W_], T("cumx")[:, W_], AF.Exp, [K("cumx")], [K("e_c")], scale=-C0)
            O.cp("pool", gC[u][:, 0:NCH], v3(T("e_r")[:, W_], NCH)[:, :, 63], [K("e_r")], [K("gC")])
            qq = QQ[u]; kq = KK[u]
            O.stt("dve", qq[:, 0:NCH, 0, :], v3(T("kkn")[:, W_], NCH), -1.0, v3(T("e_a")[:, W_], NCH), ALU.mult, ALU.mult,
                  [K("kkn"), K("e_a")], [K("QQ")])
            if do_out:
                O.tt("pool", qq[:, 0:NCH, 1, :], v3(T("rs")[:, W_], NCH), v3(T("e_r")[:, W_], NCH), ALU.mult,
                     [K("rs"), K("e_r")], [K("QQ")])
            O.tt("dve", kq[:, 0:NCH, 0, :], v3(T("bb")[:, W_], NCH), v3(T("e_n")[:, W_], NCH), ALU.mult, [K("bb"), K("e_n")], [K("KK")])
            O.tt("pool", kq[:, 0:NCH, 1, :], v3(T("kh")[:, W_], NCH), v3(T("e_n")[:, W_], NCH), ALU.mult, [K("kh"), K("e_n")], [K("KK")])
            O.tt("dve", BH[u][:, 0:NCH, :], v3(T("bb")[:, W_], NCH), v3(T("e_c")[:, W_], NCH), ALU.mult, [K("bb"), K("e_c")], [K("BH")])
            O.tt("pool", KH[u][:, 0:NCH, :], v3(T("kh")[:, W_], NCH), v3(T("e_c")[:, W_], NCH), ALU.mult, [K("kh"), K("e_c")], [K("KH")])
            O.cp("act", VV[u][:, 0:NCH, :], v3(T("vs")[:, W_], NCH), [K("vs")], [K("VV")])
            if do_out:
                O.stt("dve", T("bon")[:, W_], T("rs")[:, W_], pc(7), T("kh")[:, W_], ALU.mult, ALU.mult, [K("rs"), K("kh"), "prm"], [K("bon")])
                O.mm(ps[2][:, W_], cst["b64"][:, :], T("bon")[:, W_], ["c_b64", K("bon")], ["ps2"])
                O.tt("dve", T("bon")[:, W_], ps[2][:, W_], T("vs")[:, W_], ALU.mult, ["ps2", K("vs"), K("bon")], [K("bon")])
            if lvl < 1:
                continue
            O.tt("pool", DG[u][:, 0:NCH, :], cst["i2"][:, :].unsqueeze(1).to_broadcast([128, NCH, 64]),
                 gC[u][:, 0:NCH].unsqueeze(2).to_broadcast([128, NCH, 64]), ALU.mult, ["c_i2", K("gC")], [K("DG")])
            for (src, skey, dst, dkey, bank) in ((qq, K("QQ"), TMa[u], K("TMa"), 3), (BH[u], K("BH"), TMb[u], K("TMb"), 4),
                                                 (KH[u], K("KH"), TMk[u], K("TMk"), 3), (VV[u], K("VV"), TMv[u], K("TMv"), 4)):
                pv = psb(bank, slice(0, 64), NCH, 128)
                for i in range(NCH):
                    s_ap = src[:, i, 0, :] if src is qq else src[:, i, :]
                    O.tr(pv[:, i, :], s_ap, cst["identb"][:, :], [skey, "c_identb"], ["ps%d" % bank])
                O.cp(O.pick(("act", "dve")), dst[:, 0:NCH, :], pv, ["ps%d" % bank], [dkey])
            if lvl < 2:
                continue
            for h in range(2):
                R = slice(64 * h, 64 * h + 64)
                hk = lambda n: "u%dh%d_%s" % (u, h, n)
                zi = (2 * u + h) % 2
                Zf_, Zb_ = Zf[h], Zb[h]
                kz = "Z%d" % h
                GB_, GK_ = GB[h], GK[h]
                kg = "G%d" % h
                NQ = 128 if do_out else 64
                gab = psv(3, slice(0, 64), NCH, 64)
                for i in range(NCH):
                    O.mm(gab[:, i, :], qq[R, i, 0, :], kq[R, i, 0, :], [K("QQ"), K("KK")], ["ps3"])
                for i in range(NCH):
                    bk = 4 + i // 4
                    o_ = ps[bk][0:64, (i % 4) * 128:(i % 4) * 128 + NQ]
                    O.mm(o_, kq[R, i, 0, :], qq[R, i, :, :].rearrange("p a b -> p (a b)")[:, 0:NQ], [K("QQ"), K("KK")], ["ps%d" % bk])
                for i in range(NCH):
                    bk = 6 + i // 4
                    o_ = ps[bk][0:64, (i % 4) * 128:(i % 4) * 128 + NQ]
                    O.mm(o_, kq[R, i, 1, :], qq[R, i, :, :].rearrange("p a b -> p (a b)")[:, 0:NQ], [K("QQ"), K("KK")], ["ps%d" % bk])
                li = 2 * h
                O.tt("dve", Lb[li][:, 0:NCH, :], gab, cst["mskL"][:, :].unsqueeze(1).to_broadcast([64, NCH, 64]), ALU.mult,
                     ["ps3", "c_mskL"], ["Lb%d" % li])
                for (bank0, G_, kname) in ((4, GB_, kg + "B"), (6, GK_, kg + "K")):
                    for hb in range((NCH + 3) // 4):
                        n4 = min(4, NCH - 4 * hb)
                        src = ps[bank0 + hb][0:64, 0:n4 * 128].rearrange("p (a b) -> p a b", a=n4)[:, :, 0:NQ]
                        O.tt("dve" if hb == 0 else "dve", G_[:, 4 * hb:4 * hb + n4, 0:NQ], src,
                             cst["mskU"][:, 0:NQ].unsqueeze(1).to_broadcast([64, n4, NQ]), ALU.mult,
                             ["ps%d" % (bank0 + hb), "c_mskU"], [kname])
                O.cp("pool", LTb[li][:, 0:NCH, :], GB_[:, 0:NCH, 0:64], [kg + "B"], ["LTb%d" % li])
                if lvl < 3:
                    continue
                xp = psv(3, slice(0, 64), NCH, 64)
                for i in range(NCH):
                    O.mm(xp[:, i, :], GK_[:, i, 0:64], TMv[u][:, i, R], [kg + "K", K("TMv")], ["ps3"])
                O.cp("act", Zf_[:, 0:NCH, 0:64], TMa[u][:, 0:NCH, R], [K("TMa")], [kz + "f"])
                O.cp("dve", Zf_[:, 0:NCH, 64:128], xp, ["ps3"], [kz + "f"])
                O.cp("pool", Zb_[:, 0:NCH, :], Zf_[:, 0:NCH, :], [kz + "f"], [kz + "b"])
                cur = li
                for j in range(6):
                    for i in range(NCH):
                        bk = 4 + i // 4
                        o_ = ps[bk][0:64, (i % 4) * 128:(i % 4) * 128 + 128]
                        O.mm(o_, LTb[cur][:, i, :], Zb_[:, i, :], ["LTb%d" % cur, kz + "b"], ["ps%d" % bk])
                    if j < 5:
                        nxt = li + 1 if cur == li else li
                        pn = psv(6, slice(0, 64), NCH, 64)
                        ptn = psv(7, slice(0, 64), NCH, 64)
                        for i in range(NCH):
                            O.mm(pn[:, i, :], LTb[cur][:, i, :], Lb[cur][:, i, :], ["LTb%d" % cur, "Lb%d" % cur], ["ps6"])
                        for i in range(NCH):
                            O.mm(ptn[:, i, :], Lb[cur][:, i, :], LTb[cur][:, i, :], ["LTb%d" % cur, "Lb%d" % cur], ["ps7"])
                    for hb in range((NCH + 3) // 4):
                        n4 = min(4, NCH - 4 * hb)
                        src = ps[4 + hb][0:64, 0:n4 * 128].rearrange("p (a b) -> p a b", a=n4)
                        O.tt("dve", Zf_[:, 4 * hb:4 * hb + n4, :], Zf_[:, 4 * hb:4 * hb + n4, :], src, ALU.add,
                             ["ps%d" % (4 + hb), kz + "f"], [kz + "f"])
                    O.cp("pool", Zb_[:, 0:NCH, :], Zf_[:, 0:NCH, :], [kz + "f"], [kz + "b"])
                    if j < 5:
                        O.cp("act", Lb[nxt][:, 0:NCH, :], pn, ["ps6"], ["Lb%d" % nxt])
                        O.cp("act", LTb[nxt][:, 0:NCH, :], ptn, ["ps7"], ["LTb%d" % nxt])
                        cur = nxt
                if lvl < 4:
                    continue
                mtp = psv(3, R, NCH, 64)
                npp = psv(6, R, NCH, 64)
                for i in range(NCH):
                    O.mm(mtp[:, i, :], Zb_[:, i, 0:64], TMb[u][:, i, R], [kz + "b", K("TMb")], ["ps3"])
                for i in range(NCH):
                    O.mm(npp[:, i, :], TMb[u][:, i, R], Zb_[:, i, 64:128], [kz + "b", K("TMb")], ["ps6"], start=True, stop=False)
                    O.mm(npp[:, i, :], TMk[u][:, i, R], TMv[u][:, i, R], [K("TMk"), K("TMv")], ["ps6"], start=False, stop=True)
                if do_out:
                    qep = psv(7, R, NCH, 64)
                    for i in range(NCH):
                        O.mm(qep[:, i, :], Zb_[:, i, 0:64], GB_[:, i, 64:128], [kz + "b", kg + "B"], ["ps7"])
                O.tt("dve", MT[u][R, 0:NCH, :], mtp, DG[u][R, 0:NCH, :], ALU.add, ["ps3", K("DG")], [K("MT")])
                O.cp("act", NN[u][R, 0:NCH, :], npp, ["ps6"], [K("NN")])
                if do_out:
                    O.tt("dve", QeT[u][R, 0:NCH, :], qep, qq[R, 0:NCH, 1, :], ALU.add, ["ps7", K("QQ")], [K("QeT")])
            if lvl < 5:
                continue
            Hcur = Hst[:, p, :]
            if samp:
                P.dma("sp", Hst[:, p, :], io["s0T"][:, p, :], reads=[], writes=["H%d" % p])
            hkey = "H%d" % p
            stp = psv(3, slice(0, 128), NCH, 64)
            for i in range(NCH):
                if do_out:
                    O.cp("act", Hb[u][:, i, :], Hcur, [hkey], [K("Hb")])
                for h in range(2):
                    R = slice(64 * h, 64 * h + 64)
                    O.mm(stp[R, i, :], MT[u][R, i, :], Hst[R, p, :], [K("MT"), hkey], ["ps3"])
                O.tt("dve", Hcur, stp[:, i, :], NN[u][:, i, :], ALU.add, ["ps3", K("NN"), hkey], [hkey])
            if samp:
                P.dma("sp", io["wkv_s"][:, p, :], Hst[:, p, :], reads=[hkey], semkey="wkvs%d" % p, final=True)
            elif ct == 7:
                P.dma("sp", io["wkv_p"][:, p, :], Hst[:, p, :], reads=[hkey], semkey="wkvp%d" % p, final=True)
            if lvl < 6:
                continue
            if do_out:
                ytp = psv(7, slice(0, 128), NCH, 64)
                yt1 = psv(4, slice(0, 128), NCH, 64)
                for h in range(2):
                    R = slice(64 * h, 64 * h + 64)
                    kz = "Z%d" % h
                    kg = "G%d" % h
                    for i in range(NCH):
                        O.mm(yt1[R, i, :], Hb[u][R, i, :], QeT[u][R, i, :], [K("Hb"), K("QeT")], ["ps4"])
                    for i in range(NCH):
                        O.mm(ytp[R, i, :], Zb[h][:, i, 64:128], GB[h][:, i, 64:128], [kz + "b", kg + "B"], ["ps7"], start=True, stop=False)
                        O.mm(ytp[R, i, :], TMv[u][:, i, R], GK[h][:, i, 64:128], [K("TMv"), kg + "K"], ["ps7"], start=False, stop=True)
                y_ = ysb[u]
                O.cp("act", y_[:, W_], ps[4][:, W_], ["ps4"], [K("ysb")])
                O.tt("dve", y_[:, W_], y_[:, W_], ps[7][:, W_], ALU.add, ["ps7", K("ysb")], [K("ysb")])
                O.mm(ps[2][:, W_], cst["b64m"][:, :], y_[:, W_], ["c_b64m", K("ysb")], ["ps2"])
                O.tt("dve", yc_[u][:, W_], y_[:, W_], ps[2][:, W_], ALU.subtract, ["ps2", K("ysb")], [K("yc")])
                O.tt("pool", sq_[u][:, W_], yc_[u][:, W_], yc_[u][:, W_], ALU.mult, [K("yc")], [K("sq")])
                O.mm(ps[2][:, W_], cst["b64m"][:, :], sq_[u][:, W_], ["c_b64m", K("sq")], ["ps2"])
                O.ts("dve", rstd[u][:, W_], ps[2][:, W_], 64e-5, None, ALU.add, None, ["ps2"], [K("rstd")])
                O.rsqrt(rstd[u][:, W_], rstd[u][:, W_], [K("rstd")], [K("rstd")])
                O.tt("pool", yc_[u][:, W_], yc_[u][:, W_], rstd[u][:, W_], ALU.mult, [K("yc"), K("rstd")], [K("yc")])
                O.ts("dve", yc_[u][:, W_], yc_[u][:, W_], pc(8), pc(9), ALU.mult, ALU.add, [K("yc"), "prm"], [K("yc")])
                O.tt("pool", yc_[u][:, W_], yc_[u][:, W_], T("bon")[:, W_], ALU.add, [K("yc"), K("bon")], [K("yc")])
                oc0 = 1024 if samp else (ct - 6) * 512
                O.tt("dve", yT[:, p, oc0:oc0 + RW_], yc_[u][:, 0:RW_], T("gg")[:, 0:RW_], ALU.mult, [K("yc"), K("gg")], ["yT%d" % p])
    P.barrier()
    P.release(m0)


IN_SPECS = dict(
    xT=[D, NTOK], w_in=[D, 7472], shiftT=[N_FM, 1], s0T=[128, 8, 64],
    rw_prm=[128, 8, 10], rw_mul=[128, 3], rw_w2a2=[128, 1024], rw_g2=[160, 1024],
    c_ident=[128, 128], c_b64=[128, 128], c_b64m=[128, 128], c_mskL=[64, 64], c_mskU=[64, 128], c_i2=[128, 64],
    c_maskR=[128, 512], c_tri=[128, 128], c_cvec=[65, 1],
    xo=[NOWN, D], w_o=[D, D], ln1_g=[1, D], ln1_b=[1, D], w_up=[D, DFF], w_down=[DFF, D], ln2_g=[1, D], ln2_b=[1, D],
    fx_g=[64, 16], fx_bf=[16, 1], fx_padb=[128, 32], fx_clfT=[16, 1024], fx_ckT=[16, 64, 1024], fx_cv=[1024, 1024],
)


def make_consts():
    c = {}
    c["c_ident"] = np.eye(128, dtype=np.float32)
    b = np.zeros((128, 128), np.float32)
    b[:64, :64] = 1.0
    b[64:, 64:] = 1.0
    c["c_b64"] = b
    c["c_b64m"] = b / 64.0
    t = np.arange(64)
    c["c_mskL"] = (t[:, None] > t[None, :]).astype(np.float32)
    c["c_mskU"] = np.concatenate([(t[:, None] < t[None, :]), (t[:, None] <= t[None, :])], 1).astype(np.float32)
    c["c_i2"] = np.concatenate([np.eye(64), np.eye(64)], 0).astype(np.float32)
    m = np.ones((128, 512), np.float32)
    m[:, ::64] = 0.0
    c["c_maskR"] = m
    k = np.arange(128)
    c["c_tri"] = (k[:, None] <= k[None, :]).astype(np.float32)
    cv = np.full((65, 1), 1.0 / 64.0, np.float32)
    cv[64, 0] = 1e-6
    c["c_cvec"] = cv
    return c


def build(stages=("A", "B1", "B2", "C"), debug=(), pj_input=False, b1_kw=None, b2_kw=None, c_kw=None, extra_in=()):
    P = Prog()
    nc = P.nc
    io = {}
    need = set()
    if "A" in stages:
        need |= {"xT", "w_in"}
    if "B1" in stages:
        need |= {"shiftT", "s0T", "rw_prm", "rw_mul", "rw_w2a2", "rw_g2"}
    if "B2" in stages:
        need |= {"fx_g", "fx_bf", "fx_padb", "fx_clfT", "fx_ckT", "fx_cv"}
    if "C" in stages:
        need |= {"xo", "w_o", "ln1_g", "ln1_b", "w_up", "w_down", "ln2_g", "ln2_b"}
    need |= {k for k in IN_SPECS if k.startswith("c_")}
    need |= set(extra_in)
    for name, shape in IN_SPECS.items():
        if name in need:
            io[name] = nc.dram_tensor(name, list(shape), F32, kind="ExternalInput").ap()
    kind = "ExternalInput" if pj_input else ("ExternalOutput" if "PJ" in debug else "Internal")
    io["PJ"] = nc.dram_tensor("PJ", [N_FM, NTOK], F32, kind=kind).ap()
    io["VTM"] = nc.dram_tensor("VTM", [NTOK, 1024], F32, kind=kind).ap()
    io["wkv_p"] = nc.dram_tensor("wkv_p", [128, 8, 64], F32, kind="ExternalOutput").ap()
    io["wkv_s"] = nc.dram_tensor("wkv_s", [128, 8, 64], F32, kind="ExternalOutput").ap()
    io["spl_p"] = nc.dram_tensor("spl_p", [16, 3, 1024], BF16, kind="Internal").ap()
    io["spl_s"] = nc.dram_tensor("spl_s", [16, 3, NS], BF16, kind="Internal").ap()
    io["logf_o"] = nc.dram_tensor("logf_o", [16, NOWN], F32, kind="ExternalOutput").ap()
    io["ps"] = [nc.alloc_psum_tensor("psb%d" % i, [128, 512], F32) for i in range(8)]
    cst = load_consts(P, io)
    if "A" in stages:
        stage_A(P, io)
    else:
        P.barrier()
    yT = P.sb("yT", [128, 16, NOWN], BF16)
    if "A" in stages or "outs" in debug:
        io["k_out"] = nc.dram_tensor("k_out", [1024, NOWN], F32, kind="ExternalOutput").ap()
        io["v_out"] = nc.dram_tensor("v_out", [NOWN, 1024], F32, kind="ExternalOutput").ap()
        io["shA"] = nc.dram_tensor("shA", [2176, 2], F32, kind="ExternalOutput").ap()
        io["shB"] = nc.dram_tensor("shB", [1184, 2], F32, kind="ExternalOutput").ap()
        PJ = io["PJ"]
        P.dma("sp", io["k_out"], PJ[ROW["fk"]:ROW["fk"] + 1024, OWN0:NTOK], semkey="o_k", final=True)
        P.dma("sp", io["v_out"], io["VTM"][OWN0:NTOK, :], semkey="o_v", final=True)
        for ci, col in enumerate((NPR - 1, NTOK - 1)):
            P.dma("sp", io["shA"][:, ci:ci + 1], PJ[0:2176, col:col + 1], semkey="o_sa", final=True, allow_slow_non_contiguous=True)
            P.dma("sp", io["shB"][:, ci:ci + 1], PJ[ROW["r"]:ROW["r"] + 1184, col:col + 1], semkey="o_sb", final=True,
                  allow_slow_non_contiguous=True)
    if "B1" in stages:
        stage_B1(P, io, cst, yT, **(b1_kw or {}))
    if "B2" in stages:
        stage_B2(P, io, cst, yT, **(b2_kw or {}))
    if "yTin" in debug:
        io["yT_in"] = nc.dram_tensor("yT_in", [128, 16, NOWN], F32, kind="ExternalInput").ap()
        P.dma("pool", yT[:], io["yT_in"], writes=["yT%d" % i for i in range(16)], semkey="ytin")
    if "C" in stages:
        io["y_out"] = nc.dram_tensor("y_out", [NOWN, D], F32, kind="ExternalOutput").ap()
        stage_C(P, io, cst, yT, **(c_kw or {}))
    if "yT" in debug:
        io["yT_dbg"] = nc.dram_tensor("yT_dbg", [128, 16, NOWN], F32, kind="ExternalOutput").ap()
        P.dma("pool", io["yT_dbg"], yT[:], reads=["yT%d" % i for i in range(16)], semkey="ytdbg", final=True)
    P.emit()
    return P


def stage_B2(P, io, cst, yT, heads=range(16), insts=("p", "s")):
    nc = P.nc
    O = Ops(P)
    PJ = io["PJ"]
    VTM = io["VTM"]
    ps = io["ps"]
    m0 = P.mark()
    ones = P.sb("ones", [128, 512], F32)
    O.memset("pool", ones[:], 1.0, ["ones"])
    fog = P.sb("fog", [64, 16], F32)
    P.dma("sp", fog[:], io["fx_g"], writes=["fog"])
    nbf = P.sb("nbf", [16, 1], F32)
    P.dma("sp", nbf[:], io["fx_bf"], writes=["nbf"])
    O.ts("dve", nbf[:], nbf[:], -1.0, None, ALU.mult, None, ["nbf"], ["nbf"])
    cvec = P.sb("cvec", [65, 1], F32)
    P.dma("sp", cvec[:], io["c_cvec"], writes=["cvec"])
    padb = P.sb("padb", [128, 32], F32)
    P.dma("sp", padb[:], io["fx_padb"], writes=["padb"])
    Crow = P.sb("Crow", [16, NPR], F32)
    Ccum = P.sb("Ccum", [16, NPR], F32)
    Ckb = P.sb("Ckb", [128, 32, 16], F32)
    spl = P.sb("spl", [16, 3, 1024], BF16)
    q8 = P.sb("q8", [16, 1024], F32)
    lfo = P.sb("lfo", [16, NOWN], F32)
    KTa = [P.sb("KTa%d" % i, [67, NPR], BF16) for i in range(2)]
    QTa = [P.sb("QTa%d" % i, [67, 1024], BF16) for i in range(2)]
    Va = [P.sb("Va%d" % i, [128, 32, 65], BF16) for i in range(2)]
    ogT = [P.sb("ogT%d" % i, [64, 1024], F32) for i in range(2)]
    pt = [P.sb("pt%d" % i, [128, 512], BF16) for i in range(3)]
    Oa = P.sb("Oa", [65, 512], F32)
    sq = P.sb("sqf", [65, 512], F32)
    rs = P.sb("rs", [1, 512], F32)
    y1 = P.sb("y1", [64, 512], F32)
    ytmp = [P.sb("ytmp%d" % i, [64, 512], BF16) for i in range(2)]
    for i in range(2):
        O.memset("pool", KTa[i][64:67, :], 1.0, ["KTa%d" % i])
        O.memset("pool", Va[i][:, :, 64:65], 1.0, ["Va%d" % i])
    cnt = dict(h=0, s=0, p=0, y=0)

    for inst in insts:
        if inst == "p":
            TK, NQ, QPOS0 = NPR, 1024, OWN0
            qtiles = [(0, 512), (512, 512)]
            ocol0 = 0
        else:
            TK, NQ, QPOS0 = 1024 + NS, NS, 1024
            qtiles = [(0, NS)]
            ocol0 = 1024
        nblk = (TK + 127) // 128
        if inst == "p":
            P.dma("sp", Crow[:, 0:TK], PJ[ROW["f"]:ROW["f"] + 16, 0:TK], writes=["Crow"])
            fsl = slice(0, TK)
        else:
            P.dma("sp", Crow[:, 0:1024], io["fx_clfT"], writes=["Crow"])
            P.dma("sp", Crow[:, 1024:TK], PJ[ROW["f"]:ROW["f"] + 16, NPR:NPR + NS], writes=["Crow"], semkey="Crow_b")
            O.ts("dve", Crow[:, 0:1024], Crow[:, 0:1024], -1.0, None, ALU.mult, None, ["Crow"], ["Crow"])
            fsl = slice(1024, TK)
        O.act(Crow[:, fsl], Crow[:, fsl], AF.Exp, ["Crow", "nbf"], ["Crow"], bias=nbf[:, 0:1], scale=-1.0)
        O.act(Crow[:, fsl], Crow[:, fsl], AF.Ln, ["Crow"], ["Crow"], bias=1.0)
        if inst == "p":
            P.op("act", lambda: nc.scalar.mul(lfo[:, 0:1024], Crow[:, OWN0:NPR], -1.0), reads=["Crow"], writes=["lfo"])
        else:
            P.op("act", lambda: nc.scalar.mul(lfo[:, 1024:NOWN], Crow[:, 1024:TK], -1.0), reads=["Crow"], writes=["lfo"])
            P.dma("sp", io["logf_o"], lfo[:], reads=["lfo"], semkey="lfo", final=True)
        for c0 in range(0, TK, 512):
            cw = min(512, TK - c0)
            init = cst["zcol"][0:16, 0:1] if c0 == 0 else Ccum[:, c0 - 1:c0]
            o_ap, d0, d1 = Ccum[:, c0:c0 + cw], ones[0:16, 0:cw], Crow[:, c0:c0 + cw]
            P.op("dve", (lambda o_ap=o_ap, d0=d0, d1=d1, init=init: nc.vector.tensor_tensor_scan(o_ap, d0, d1, init, ALU.mult, ALU.add)),
                 reads=["Crow", "ones", "Ccum"], writes=["Ccum"])
        tp = ps[6][:, 0:nblk * 16].rearrange("p (a b) -> p a b", a=nblk)
        for kb in range(nblk):
            nk = min(128, TK - kb * 128)
            O.tr(tp[0:nk, kb, :], Ccum[:, kb * 128:kb * 128 + nk], cst["identf"][0:16, 0:16], ["Ccum", "c_identf"], ["ps6"])
        nfull = TK // 128
        if inst == "p":
            O.tt("dve", Ckb[:, 0:nblk, :], tp[:, 0:nblk, :], padb[:, 0:nblk].unsqueeze(2).to_broadcast([128, nblk, 16]), ALU.add,
                 ["ps6", "padb"], ["Ckb"])
        else:
            O.cp("dve", Ckb[:, 0:nfull, :], tp[:, 0:nfull, :], ["ps6"], ["Ckb"])
            O.cp("dve", Ckb[0:NS, nfull, :], tp[0:NS, nfull, :], ["ps6", "Ckb"], ["Ckb"])
        qs = slice(QPOS0, QPOS0 + NQ)
        O.ts("dve", q8[:, 0:NQ], Ccum[:, qs], -8.0, None, ALU.mult, None, ["Ccum"], ["q8"])
        for k3 in range(3):
            O.cp("dve", spl[:, k3, 0:NQ], q8[:, 0:NQ], ["q8"], ["spl"])
            if k3 < 2:
                O.tt("dve", q8[:, 0:NQ], q8[:, 0:NQ], spl[:, k3, 0:NQ], ALU.subtract, ["q8", "spl"], ["q8"])
        sk = "splD" + inst
        P.dma("sp", io["spl_" + inst], spl[:, :, 0:NQ], reads=["spl"], writes=[sk])
        for h in heads:
            hb = cnt["h"] % 2
            cnt["h"] += 1
            kK, kQ, kV, kG = "KTa%d" % hb, "QTa%d" % hb, "Va%d" % hb, "ogT%d" % hb
            fk0 = ROW["fk"] + 64 * h
            if inst == "p":
                P.dma("pool", KTa[hb][0:64, 0:TK], PJ[fk0:fk0 + 64, 0:TK], writes=[kK])
                P.dma("pool", Va[hb][:, 0:32, 0:64], VTM[0:NPR, 64 * h:64 * h + 64].rearrange("(b p) d -> p b d", p=128), writes=[kV])
                ocs = slice(OWN0, NPR)
            else:
                P.dma("pool", KTa[hb][0:64, 0:1024], io["fx_ckT"][h], writes=[kK])
                P.dma("pool", KTa[hb][0:64, 1024:TK], PJ[fk0:fk0 + 64, NPR:NPR + NS], writes=[kK])
                P.dma("pool", Va[hb][:, 0:8, 0:64], io["fx_cv"][:, 64 * h:64 * h + 64].rearrange("(b p) d -> p b d", p=128), writes=[kV])
                P.dma("pool", Va[hb][0:NS, 8, 0:64], VTM[NPR:NPR + NS, 64 * h:64 * h + 64], writes=[kV])
                ocs = slice(NPR, NPR + NS)
            P.dma("pool", QTa[hb][0:64, 0:NQ], PJ[ROW["q"] + 64 * h:ROW["q"] + 64 * h + 64, ocs], writes=[kQ])
            P.dma("sp", QTa[hb][64:67, 0:NQ], io["spl_" + inst][h], reads=[sk], writes=[kQ])
            P.dma("sp", ogT[hb][:, 0:NQ], PJ[ROW["og"] + 64 * h:ROW["og"] + 64 * h + 64, ocs], writes=[kG])
            for (q0, qw) in qtiles:
                ob = 2 + cnt["y"] % 2
                okey = "ps%d" % ob
                last_q = QPOS0 + q0 + qw - 1
                kbs = [kb for kb in range(nblk) if kb * 128 <= last_q]
                for n_, kb in enumerate(kbs):
                    nk = min(128, TK - kb * 128)
                    off = max(0, kb * 128 - (QPOS0 + q0))
                    diag = kb * 128 + nk - 1 > QPOS0 + q0
                    sb_ = cnt["s"] % 2
                    cnt["s"] += 1
                    pb_ = cnt["p"] % 3
                    cnt["p"] += 1
                    skey, pkey = "ps%d" % sb_, "pt%d" % pb_
                    O.mm(ps[sb_][0:nk, off:qw], KTa[hb][0:67, kb * 128:kb * 128 + nk], QTa[hb][0:67, q0 + off:q0 + qw], [kK, kQ], [skey])
                    O.act(pt[pb_][0:nk, off:qw], ps[sb_][0:nk, off:qw], AF.Exp, [skey, "Ckb"], [pkey], bias=Ckb[0:nk, kb, h:h + 1], scale=0.125)
                    if diag:
                        mw = min(128, qw - off, nk)
                        O.tt(O.pick(("dve", "pool")), pt[pb_][0:nk, off:off + mw], pt[pb_][0:nk, off:off + mw], cst["tri"][0:nk, 0:mw], ALU.mult,
                             [pkey, "c_tri"], [pkey])
                    O.mm(ps[ob][0:65, off:qw], Va[hb][0:nk, kb, 0:65], pt[pb_][0:nk, off:qw], [kV, pkey], [okey],
                         start=(n_ == 0), stop=(n_ == len(kbs) - 1))
                cnt["y"] += 1
                yb = cnt["y"] % 2
                O.cp("act", Oa[:, 0:qw], ps[ob][0:65, 0:qw], [okey], ["Oa"])
                O.act(sq[:, 0:qw], Oa[:, 0:qw], AF.Square, ["Oa"], ["sqf"])
                O.mm(ps[4][0:1, 0:qw], cvec[:, 0:1], sq[:, 0:qw], ["cvec", "sqf"], ["ps4"])
                O.act(rs[:, 0:qw], ps[4][0:1, 0:qw], AF.Sqrt, ["ps4"], ["rs"])
                O.recip(rs[:, 0:qw], rs[:, 0:qw], ["rs"], ["rs"])
                O.mm(ps[5][0:64, 0:qw], ones[0:1, 0:64], rs[:, 0:qw], ["ones", "rs"], ["ps5"])
                O.tt("dve", y1[:, 0:qw], Oa[0:64, 0:qw], ps[5][0:64, 0:qw], ALU.mult, ["Oa", "ps5"], ["y1"])
                O.act(ogT[hb][:, q0:q0 + qw], ogT[hb][:, q0:q0 + qw], AF.Sigmoid, [kG], [kG])
                O.stt("dve", ytmp[yb][:, 0:qw], y1[:, 0:qw], fog[:, h:h + 1], ogT[hb][:, q0:q0 + qw], ALU.mult, ALU.mult,
                      ["y1", "fog", kG], ["ytmp%d" % yb])
                r0 = (h % 2) * 64
                P.dma("sp", yT[r0:r0 + 64, 8 + h // 2, ocol0 + q0:ocol0 + q0 + qw], ytmp[yb][:, 0:qw],
                      reads=["ytmp%d" % yb], writes=["yT%d" % (8 + h // 2)], semkey="yT%d_%d" % (8 + h // 2, h % 2))
    P.barrier()
    P.release(m0)


TOKT = [(i * 128, 128) for i in range(8)] + [(1024, NS)]
TT3 = [(0, 512), (512, 512), (1024, NS)]


def layer_norm_tiles(P, O, Z, gvec, bvec, zkeys, out_dram=None):
    nc = P.nc
    m = P.mark()
    gb = P.sb("ln_g", [128, D], F32)
    bb = P.sb("ln_b", [128, D], F32)
    junk = P.sb("ln_junk", [128, D], F32)
    st = P.sb("ln_st", [128, 4], F32)
    P.dma("sp", gb[:], gvec.partition_broadcast(128), writes=["ln_g"])
    P.dma("sp", bb[:], bvec.partition_broadcast(128), writes=["ln_b"])
    for ti, (t0, tw) in enumerate(TOKT):
        z = Z[0:tw, ti, :]
        zk = zkeys[ti]
        s = st[0:tw, :]
        O.memset("pool", st[:, :], 0.0, ["ln_st"])
        P.op("act", (lambda z=z, tw=tw, s=s: nc.scalar.activation(out=junk[0:tw, :], in_=z, func=AF.Identity, accum_out=s[:, 0:1])),
             reads=[zk], writes=["ln_junk", "ln_st"])
        O.ts("dve", s[:, 1:2], s[:, 0:1], -1.0 / D, None, ALU.mult, None, ["ln_st"], ["ln_st"])
        O.act(z, z, AF.Identity, [zk, "ln_st"], [zk], bias=s[:, 1:2])
        P.op("act", (lambda z=z, tw=tw, s=s: nc.scalar.activation(out=junk[0:tw, :], in_=z, func=AF.Square, accum_out=s[:, 2:3])),
             reads=[zk, "ln_st"], writes=["ln_junk", "ln_st"])
        O.ts("dve", s[:, 3:4], s[:, 2:3], 1.0 / D, 1e-5, ALU.mult, ALU.add, ["ln_st"], ["ln_st"])
        O.rsqrt(s[:, 3:4], s[:, 3:4], ["ln_st"], ["ln_st"])
        O.stt("dve", z, z, s[:, 3:4], gb[0:tw, :], ALU.mult, ALU.mult, [zk, "ln_st", "ln_g"], [zk])
        O.tt("pool", z, z, bb[0:tw, :], ALU.add, [zk, "ln_b"], [zk])
        if out_dram is not None:
            P.dma("sp", out_dram[t0:t0 + tw, :], z, reads=[zk], semkey="yo%d" % ti, final=True)
    P.release(m)


def stage_C(P, io, cst, yT, dbg=None):
    nc = P.nc
    O = Ops(P)
    ps = io["ps"]
    m0 = P.mark()
    Z = P.sb("Z", [128, 9, D], F32)
    zk = ["Z%d" % i for i in range(9)]
    yk = ["yT%d" % i for i in range(16)]
    m1 = P.mark()
    wo = [P.sb("wo%d" % i, [128, 16, 512], BF16) for i in range(2)]
    wor = io["w_o"].rearrange("(kc p) c -> p kc c", p=128)
    for ti, (t0, tw) in enumerate(TOKT):
        P.dma("sp", Z[0:tw, ti, :], io["xo"][t0:t0 + tw, :], writes=[zk[ti]])
    pi = 0
    for cg in range(4):
        wb = cg % 2
        P.dma("pool", wo[wb][:], wor[:, :, cg * 512:cg * 512 + 512], writes=["wo%d" % wb])
        for ti, (t0, tw) in enumerate(TOKT):
            pb = pi % 4
            pi += 1
            for kc in range(16):
                O.mm(ps[pb][0:tw, :], yT[:, kc, t0:t0 + tw], wo[wb][:, kc, :], [yk[kc], "wo%d" % wb], ["ps%d" % pb],
                     start=(kc == 0), stop=(kc == 15))
            zs = Z[0:tw, ti, cg * 512:cg * 512 + 512]
            O.stt("dve", zs, zs, ALPHA, ps[pb][0:tw, :], ALU.mult, ALU.add, [zk[ti], "ps%d" % pb], [zk[ti]])
    def dump():
        for ti, (t0, tw) in enumerate(TOKT):
            P.dma("sp", io["y_out"][t0:t0 + tw, :], Z[0:tw, ti, :], reads=[zk[ti]], semkey="yo%d" % ti, final=True)
    if dbg == "z":
        dump()
        return
    layer_norm_tiles(P, O, Z, io["ln1_g"], io["ln1_b"], zk)
    if dbg == "h":
        dump()
        return
    hT = yT
    ei = 0
    for kc in range(16):
        for (grp, bank) in ((range(0, 4), 4), (range(4, 8), 5), (range(8, 9), 6)):
            for ti in grp:
                t0, tw = TOKT[ti]
                c0 = t0 - TOKT[grp[0]][0]
                O.tr(ps[bank][:, c0:c0 + tw], Z[0:tw, ti, kc * 128:kc * 128 + 128], cst["identf"][0:tw, 0:tw], [zk[ti], "c_identf"], ["ps%d" % bank])
            g0 = TOKT[grp[0]][0]
            gw = sum(TOKT[ti][1] for ti in grp)
            ei += 1
            O.cp("act" if ei % 2 else "dve", hT[:, kc, g0:g0 + gw], ps[bank][:, 0:gw], ["ps%d" % bank], [yk[kc]])
    for ti, (t0, tw) in enumerate(TOKT):
        O.ts("pool", Z[0:tw, ti, :], Z[0:tw, ti, :], ALPHA, None, ALU.mult, None, [zk[ti]], [zk[ti]])
    P.barrier()
    P.release(m1)
    m2 = P.mark()
    wu = [P.sb("wu%d" % i, [128, 16, 512], BF16) for i in range(2)]
    wd = [P.sb("wd%d" % i, [128, 4, D], BF16) for i in range(2)]
    aT = [P.sb("aT%d" % i, [128, 4, NOWN], BF16) for i in range(2)]
    rT = [P.sb("rT%d" % i, [128, 512], BF16) for i in range(3)]
    wur = io["w_up"].rearrange("(kc p) c -> p kc c", p=128)
    wdr = io["w_down"].rearrange("(g kc p) c -> g p kc c", p=128, kc=4)
    NG = DFF // 512

    def load_w(g):
        b = g % 2
        P.dma("pool", wu[b][:], wur[:, :, g * 512:g * 512 + 512], writes=["wu%d" % b])
        P.dma("pool", wd[b][:], wdr[g], writes=["wd%d" % b])
    load_w(0)
    pu = 0
    pd = 0
    ri = 0
    for g in range(NG):
        b = g % 2
        if g + 1 < NG:
            load_w(g + 1)
        for blk in range(4):
            for (t0, tw) in TT3:
                pb = pu % 3
                pu += 1
                for kc in range(16):
                    O.mm(ps[pb][:, 0:tw], wu[b][:, kc, blk * 128:blk * 128 + 128], hT[:, kc, t0:t0 + tw], ["wu%d" % b, yk[kc]], ["ps%d" % pb],
                         start=(kc == 0), stop=(kc == 15))
                rb = ri % 3
                ri += 1
                O.act(rT[rb][:, 0:tw], ps[pb][:, 0:tw], AF.Relu, ["ps%d" % pb], ["rT%d" % rb])
                O.tt("pool", aT[b][:, blk, t0:t0 + tw], rT[rb][:, 0:tw], rT[rb][:, 0:tw], ALU.mult, ["rT%d" % rb], ["aT%d" % b])
        for ti, (t0, tw) in enumerate(TOKT):
            for cg in range(4):
                pb = 3 + pd % 5
                pd += 1
                for blk in range(4):
                    O.mm(ps[pb][0:tw, :], aT[b][:, blk, t0:t0 + tw], wd[b][:, blk, cg * 512:cg * 512 + 512], ["aT%d" % b, "wd%d" % b], ["ps%d" % pb],
                         start=(blk == 0), stop=(blk == 3))
                zs = Z[0:tw, ti, cg * 512:cg * 512 + 512]
                O.tt("dve", zs, zs, ps[pb][0:tw, :], ALU.add, [zk[ti], "ps%d" % pb], [zk[ti]])
    P.barrier()
    P.release(m2)
    if dbg == "acc":
        dump()
        return
    layer_norm_tiles(P, O, Z, io["ln2_g"], io["ln2_b"], zk, out_dram=io["y_out"])
    P.release(m0)


_PROG = {}


def host_maps(inp):
    g = lambda k: np.asarray(inp[k], np.float32)[0]
    xp = np.asarray(inp["x_prompt"], np.float32)
    xs = np.asarray(inp["x_sample"], np.float32)
    w_in = g("w_in")
    wperm = np.ascontiguousarray(w_in[:, PERM])
    consts = make_consts()
    mu = g("rwkv_mu")
    cols = [mu[0:1024], mu[1024:2048], mu[2048:3072], g("rwkv_w0"), g("rwkv_a0"), g("rwkv_k_k"), g("rwkv_k_a"),
            g("rwkv_r_k").reshape(-1), g("rwkv_lnx_g"), g("rwkv_lnx_b")]
    rw_prm = np.ascontiguousarray(np.stack([v.reshape(8, 128) for v in cols], -1).transpose(1, 0, 2))
    rw_mul = np.zeros((128, 3), np.float32)
    rw_mul[:, 0] = mu[3072:3200]
    rw_mul[:, 1] = mu[3200:3328]
    rw_mul[:32, 2] = mu[3328:3360]
    shared = dict(consts)
    shared.update(
        w_in=wperm, rw_prm=rw_prm, rw_mul=rw_mul,
        rw_w2a2=np.ascontiguousarray(np.concatenate([g("rwkv_w2"), g("rwkv_a2")], 0)), rw_g2=g("rwkv_g2"),
        fx_g=np.ascontiguousarray(g("fox_out_g").reshape(16, 64).T), fx_bf=g("fox_b_f").reshape(16, 1),
        w_o=g("w_o"), ln1_g=g("ln1_g").reshape(1, D), ln1_b=g("ln1_b").reshape(1, D), w_up=g("w_up"), w_down=g("w_down"),
        ln2_g=g("ln2_g").reshape(1, D), ln2_b=g("ln2_b").reshape(1, D))
    maps = []
    pos = np.arange(NPR).reshape(32, 128).T
    for c in range(8):
        b, j = divmod(c, 4)
        n = 1024 * (j + 1)
        xpad = np.zeros((NTOK, D), np.float32)
        xpad[NPR - n:NPR] = xp[b, :n]
        xpad[NPR:] = xs[c]
        m = dict(shared)
        m["xT"] = np.ascontiguousarray(xpad.T)
        m["xo"] = np.ascontiguousarray(xpad[OWN0:NTOK])
        sh = np.zeros((N_FM, 1), np.float32)
        st = np.asarray(inp["state_rwkv_shift"], np.float32)[0, c, 0]
        for seg in ("rk", "rv", "xwxa", "r", "xg"):
            o, n_ = SEG[seg]
            sh[ROW[seg]:ROW[seg] + n_, 0] = st[o:o + n_]
        m["shiftT"] = sh
        s0 = np.asarray(inp["state_rwkv_wkv"], np.float32)[0, c]
        m["s0T"] = np.ascontiguousarray(s0.transpose(0, 2, 1).reshape(8, 128, 64).transpose(1, 0, 2))
        m["fx_padb"] = np.where(pos < NPR - n, -BIG, 0.0).astype(np.float32)
        m["fx_clfT"] = np.ascontiguousarray(np.asarray(inp["cache_fox_logf"], np.float32)[0, c].T)
        m["fx_ckT"] = np.ascontiguousarray(np.asarray(inp["cache_fox_k"], np.float32)[0, c].transpose(1, 2, 0))
        m["fx_cv"] = np.ascontiguousarray(np.asarray(inp["cache_fox_v"], np.float32)[0, c].reshape(1024, 1024))
        maps.append(m)
    return maps


def assemble(results):
    f32 = np.float32
    y_p = np.zeros((2, 4096, D), f32)
    y_s = np.zeros((8, NS, D), f32)
    pk = np.zeros((1, 2, 4096, 16, 64), f32)
    pv = np.zeros((1, 2, 4096, 16, 64), f32)
    pf = np.zeros((1, 2, 4096, 16), f32)
    pS = np.zeros((1, 2, 16, 64, 64), f32)
    psh = np.zeros((1, 2, 1, 3360), f32)
    sk = np.zeros((1, 8, NS, 16, 64), f32)
    sv = np.zeros((1, 8, NS, 16, 64), f32)
    sf = np.zeros((1, 8, NS, 16), f32)
    sS = np.zeros((1, 8, 16, 64, 64), f32)
    ssh = np.zeros((1, 8, 1, 3360), f32)

    def wkv(a):
        return a.reshape(2, 64, 8, 64).transpose(2, 0, 3, 1).reshape(16, 64, 64)

    def shift(r, ci):
        out = np.zeros(3360, f32)
        a, bq = r["shA"][:, ci], r["shB"][:, ci]
        out[1024:2048] = a[0:1024]
        out[2048:3072] = a[1024:2048]
        out[3072:3200] = a[2048:2176]
        out[0:1024] = bq[0:1024]
        out[3200:3360] = bq[1024:1184]
        return out
    for c in range(8):
        r = results[c]
        b, j = divmod(c, 4)
        sl = slice(1024 * j, 1024 * (j + 1))
        y_p[b, sl] = r["y_out"][0:1024]
        y_s[c] = r["y_out"][1024:NOWN]
        kT = r["k_out"]
        pk[0, b, sl] = kT[:, 0:1024].T.reshape(1024, 16, 64)
        sk[0, c] = kT[:, 1024:NOWN].T.reshape(NS, 16, 64)
        pv[0, b, sl] = r["v_out"][0:1024].reshape(1024, 16, 64)
        sv[0, c] = r["v_out"][1024:NOWN].reshape(NS, 16, 64)
        pf[0, b, sl] = r["logf_o"][:, 0:1024].T
        sf[0, c] = r["logf_o"][:, 1024:NOWN].T
        sS[0, c] = wkv(r["wkv_s"])
        ssh[0, c, 0] = shift(r, 1)
        if j == 3:
            pS[0, b] = wkv(r["wkv_p"])
            psh[0, b, 0] = shift(r, 0)
    return (y_p, y_s, pk, pv, pf, pS, psh, sk, sv, sf, sS, ssh)


def kernel(**inputs):
    if "full" not in _PROG:
        _PROG["full"] = build()
    P = _PROG["full"]
    maps = host_maps(inputs)
    res = run_bass_kernel_spmd(P.nc, maps, core_ids=list(range(8)))
    return assemble(res.results)
```

```python
import numpy as np
import concourse.bass as bass
import concourse.mybir as mybir
from concourse.bass_utils import run_bass_kernel_spmd

F32 = mybir.dt.float32
BF16 = mybir.dt.bfloat16
AF = mybir.ActivationFunctionType
ALU = mybir.AluOpType
AX = mybir.AxisListType

D = 2048
NPR = 4096
NS = 16
NTOK = NPR + NS
OWN0 = 3072
NOWN = 1024 + NS
DFF = 8192
C0 = float(np.exp(-0.5))
ALPHA = 2.0 ** 0.25
BIG = 30000.0

SEG = dict(rk=(1024, 1024), rv=(2048, 1024), xwxa=(3072, 128), fk=(3360 + 1024, 1024), f=(3360 + 3072, 16),
           r=(0, 1024), xg=(3200, 160), q=(3360, 1024), og=(3360 + 3088, 1024), fv=(3360 + 2048, 1024))
ORDER = ["rk", "rv", "xwxa", "fk", "f", "r", "xg", "q", "og", "fv"]
ROW = {}
_o = 0
for _k in ORDER:
    ROW[_k] = _o
    _o += SEG[_k][1]
N_ALL = ROW["r"]
N_OWN = ROW["fv"] - N_ALL
N_FM = ROW["fv"]
PERM = np.concatenate([np.arange(SEG[k][0], SEG[k][0] + SEG[k][1]) for k in ORDER])


EMBED_WAIT = True
SAME_ENG_NOWAIT = ()


class Prog:
    ENG = ("pe", "act", "dve", "pool", "sp")

    def __init__(self):
        self.nc = bass.Bass("TRN2", target_bir_lowering=False)
        nc = self.nc
        self.eng = {"pe": nc.tensor, "act": nc.scalar, "dve": nc.vector, "pool": nc.gpsimd, "sp": nc.sync}
        self.ops = []
        self.sb_base = 16640
        self.sb_top = 229376
        self.sb_off = self.sb_base
        self.nalloc = 0

    def sb(self, name, shape, dtype):
        per = int(np.prod(shape[1:])) * (2 if dtype == BF16 else 4)
        off = (self.sb_off + 63) // 64 * 64
        assert off + per <= self.sb_top, ("sbuf overflow", name, off, per)
        self.sb_off = off + per
        self.nalloc += 1
        return self.nc.alloc_sbuf_tensor_at("%s_%d" % (name, self.nalloc), list(shape), dtype, offset=off)

    def mark(self):
        return self.sb_off

    def release(self, m):
        self.sb_off = m

    def op(self, eng, fn, reads=(), writes=()):
        self.ops.append(dict(eng=eng, fn=fn, reads=tuple(reads), writes=tuple(writes), dma=False, bar=False))

    def dma(self, q, out, in_, reads=(), writes=(), semkey=None, final=False, **kw):
        if semkey is None:
            semkey = writes[0] if writes else reads[0]
        e = self.eng[q]
        self.ops.append(dict(eng=q, fn=(lambda e=e, out=out, in_=in_, kw=kw: e.dma_start(out=out, in_=in_, **kw)),
                             reads=tuple(reads), writes=tuple(writes), dma=True, semkey=semkey, final=final, bar=False))

    def barrier(self):
        for e in self.ENG:
            self.ops.append(dict(eng=e, fn=None, reads=(), writes=(), dma=False, bar=True))

    def emit(self):
        nc = self.nc
        ops = self.ops
        n = len(ops)
        lastw = {}
        readers = {}
        deps = [None] * n
        last_eng = {}
        dma_since = []
        bar_group = None
        for i, o in enumerate(ops):
            if o["bar"]:
                if bar_group is None:
                    bar_group = (set(last_eng.values()) | set(dma_since))
                deps[i] = set(bar_group)
                nxt = ops[i + 1] if i + 1 < n else None
                if nxt is None or not nxt["bar"]:
                    bar_group = None
                    dma_since = []
                    lastw = {}
                    readers = {}
                continue
            d = set()
            for k in o["reads"]:
                if k in lastw:
                    d.add(lastw[k])
            for k in o["writes"]:
                if k in lastw:
                    d.add(lastw[k])
                for r in readers.get(k, ()):
                    d.add(r)
            d.discard(i)
            deps[i] = d
            for k in o["reads"]:
                readers.setdefault(k, []).append(i)
            for k in o["writes"]:
                lastw[k] = i
                readers[k] = []
            if o["dma"]:
                dma_since.append(i)
            else:
                last_eng[o["eng"]] = i

        def need_wait(o, pj):
            if pj["dma"] or o["dma"] or o["bar"]:
                return not (o["bar"] and (not pj["dma"]) and pj["eng"] == o["eng"])
            if pj["eng"] != o["eng"]:
                return True
            if o["eng"] == "pe" or o["eng"] in SAME_ENG_NOWAIT:
                return False
            return any(k in pj["writes"] for k in o["reads"])

        needed = set()
        for i, o in enumerate(ops):
            for j in deps[i]:
                if need_wait(o, ops[j]):
                    needed.add(j)
            if o["dma"]:
                needed.add(i)
        esem = {e: nc.alloc_semaphore("c_" + e) for e in self.ENG}
        ecnt = {e: 0 for e in self.ENG}
        dpool = []
        dcnt = []
        keymap = {}
        token = [None] * n
        for i, o in enumerate(ops):
            if o["bar"]:
                keymap = {}
                continue
            if o["dma"]:
                sk = o["semkey"]
                if sk not in keymap:
                    idx = len(keymap)
                    if idx >= len(dpool):
                        dpool.append(nc.alloc_semaphore("d%d" % idx))
                        dcnt.append(0)
                    keymap[sk] = idx
                idx = keymap[sk]
                dcnt[idx] += 16
                token[i] = ("d:%d" % idx, dpool[idx], dcnt[idx])
            elif i in needed:
                e = o["eng"]
                ecnt[e] += 1
                token[i] = ("e:" + e, esem[e], ecnt[e])
        dsem = dpool
        seen = {e: {} for e in self.ENG}
        dlatest = {}
        nwaits = 0
        for i, o in enumerate(ops):
            e = o["eng"]
            eng = self.eng[e]
            want = {}
            for j in deps[i]:
                tk = token[j]
                if tk is None or not need_wait(o, ops[j]):
                    continue
                name, sem, val = tk
                if name.startswith("d:"):
                    val = dlatest[name]
                if want.get(name, (None, 0))[1] < val:
                    want[name] = (sem, val)
            todo = [(name, sem, val) for name, (sem, val) in want.items() if seen[e].get(name, 0) < val]
            attach = None
            if EMBED_WAIT and todo and not o["bar"] and not o["dma"]:
                attach = todo.pop()
            for name, sem, val in todo:
                eng.wait_ge(sem, val)
                seen[e][name] = val
                nwaits += 1
            if o["bar"]:
                continue
            ins = o["fn"]()
            if attach is not None:
                name, sem, val = attach
                ins._wait_ge(sem, val)
                seen[e][name] = val
            tk = token[i]
            if tk is not None:
                name, sem, val = tk
                if o["dma"]:
                    ins.then_inc(sem, 16)
                    dlatest[name] = val
                else:
                    ins.then_inc(sem, 1)
        for idx in range(len(dpool)):
            name = "d:%d" % idx
            val = dlatest.get(name, 0)
            if val and seen["sp"].get(name, 0) < val:
                self.eng["sp"].wait_ge(dpool[idx], val)
                seen["sp"][name] = val
        self.stats = dict(n_ops=n, n_waits=nwaits, n_dsem=len(dsem), cnt=dict(ecnt))
        return nc


TT_ALL = [(i * 512, 512) for i in range(8)] + [(NPR, NS)]
TT_OWN = [(OWN0 - 1, 1), (OWN0, 512), (OWN0 + 512, 512), (NPR, NS)]


def stage_A(P, io):
    nc = P.nc
    m0 = P.mark()
    xs = P.sb("xs", [128, 16, NTOK], BF16)
    wt = [P.sb("wt%d" % i, [128, 16, 512], BF16) for i in range(2)]
    NOT = 8
    ot = [P.sb("ot%d" % i, [128, 512], F32) for i in range(NOT)]
    vt = [P.sb("vt%d" % i, [128, 1024], F32) for i in range(2)]
    ps = io["ps"]
    xTr = io["xT"].rearrange("(kc p) t -> p kc t", p=128)
    wr = io["w_in"].rearrange("(kc p) c -> p kc c", p=128)
    PJ = io["PJ"]
    VTM = io["VTM"]
    for g in range(8):
        P.dma("pool", xs[:, 2 * g:2 * g + 2, :], xTr[:, 2 * g:2 * g + 2, :], writes=["xs%d" % g])
    st = dict(oi=0, pi=0, wi=0, vi=0)

    def evac(dst, src, wkey, pkey):
        st["pi"] += 1
        if st["pi"] % 2 == 0:
            P.op("act", lambda: nc.scalar.copy(dst, src), reads=[pkey], writes=[wkey])
        else:
            P.op("dve", lambda: nc.vector.tensor_copy(dst, src), reads=[pkey], writes=[wkey])

    def fm_chunk(c0, cw, tts):
        wb = st["wi"] % 2
        st["wi"] += 1
        P.dma("pool", wt[wb][:, :, 0:cw], wr[:, :, c0:c0 + cw], writes=["wt%d" % wb])
        for m in range(0, cw, 128):
            mw = min(128, cw - m)
            for (t0, tw) in tts:
                pb = st["pi"] % 6
                for kc in range(16):
                    P.op("pe", (lambda pb=pb, wb=wb, kc=kc, m=m, mw=mw, t0=t0, tw=tw:
                                nc.tensor.matmul(ps[pb][0:mw, 0:tw], wt[wb][:, kc, m:m + mw], xs[:, kc, t0:t0 + tw],
                                                 start=(kc == 0), stop=(kc == 15))),
                         reads=["wt%d" % wb, "xs%d" % (kc // 2)], writes=["ps%d" % pb])
                ob = st["oi"] % NOT
                st["oi"] += 1
                evac(ot[ob][0:mw, 0:tw], ps[pb][0:mw, 0:tw], "ot%d" % ob, "ps%d" % pb)
                P.dma("sp", PJ[c0 + m:c0 + m + mw, t0:t0 + tw], ot[ob][0:mw, 0:tw], reads=["ot%d" % ob],
                      semkey="oto%d" % ob, **({"allow_slow_non_contiguous": True} if tw == 1 else {}))

    for c0 in range(0, N_ALL, 512):
        fm_chunk(c0, min(512, N_ALL - c0), TT_ALL)
    for c0 in range(N_ALL, N_FM, 512):
        fm_chunk(c0, min(512, N_FM - c0), TT_OWN)
    wbs = []
    for h in range(2):
        wb = st["wi"] % 2
        st["wi"] += 1
        P.dma("pool", wt[wb][:, :, :], wr[:, :, N_FM + 512 * h:N_FM + 512 * h + 512], writes=["wt%d" % wb])
        wbs.append(wb)
    tb = [(i * 128, 128) for i in range(32)] + [(NPR, NS)]
    for (t0, tw) in tb:
        vb = st["vi"] % 2
        st["vi"] += 1
        for h in range(2):
            pb = st["pi"] % 6
            wb = wbs[h]
            for kc in range(16):
                P.op("pe", (lambda pb=pb, wb=wb, kc=kc, t0=t0, tw=tw:
                            nc.tensor.matmul(ps[pb][0:tw, 0:512], xs[:, kc, t0:t0 + tw], wt[wb][:, kc, :],
                                             start=(kc == 0), stop=(kc == 15))),
                     reads=["wt%d" % wb, "xs%d" % (kc // 2)], writes=["ps%d" % pb])
            evac(vt[vb][0:tw, 512 * h:512 * h + 512], ps[pb][0:tw, 0:512], "vt%d" % vb, "ps%d" % pb)
        P.dma("sp", VTM[t0:t0 + tw, :], vt[vb][0:tw, :], reads=["vt%d" % vb], semkey="vto%d" % vb)
    P.barrier()
    P.release(m0)


class Ops:
    def __init__(self, P):
        self.P = P
        self.nc = P.nc
        self.rr = 0

    def E(self, which):
        return {"dve": self.nc.vector, "pool": self.nc.gpsimd}[which]

    def pick(self, choices=("dve", "pool")):
        self.rr += 1
        return choices[self.rr % len(choices)]

    def tt(self, eng, out, a, b, op, r, w):
        e = self.E(eng)
        self.P.op(eng, lambda: e.tensor_tensor(out, a, b, op), reads=r, writes=w)

    def ts(self, eng, out, a, s1, s2, op0, op1, r, w):
        e = self.E(eng)
        if s2 is None:
            self.P.op(eng, lambda: e.tensor_scalar(out, a, s1, None, op0), reads=r, writes=w)
        else:
            self.P.op(eng, lambda: e.tensor_scalar(out, a, s1, s2, op0, op1), reads=r, writes=w)

    def stt(self, eng, out, a, sc, b, op0, op1, r, w):
        e = self.E(eng)
        self.P.op(eng, lambda: e.scalar_tensor_tensor(out, a, sc, b, op0, op1), reads=r, writes=w)

    def cp(self, eng, out, a, r, w):
        if eng == "act":
            self.P.op("act", lambda: self.nc.scalar.copy(out, a), reads=r, writes=w)
        else:
            e = self.E(eng)
            self.P.op(eng, lambda: e.tensor_copy(out, a), reads=r, writes=w)

    def act(self, out, a, func, r, w, bias=None, scale=None):
        kw = {}
        if bias is not None:
            kw["bias"] = bias
        if scale is not None:
            kw["scale"] = scale
        self.P.op("act", lambda: self.nc.scalar.activation(out=out, in_=a, func=func, **kw), reads=r, writes=w)

    def mm(self, out, lhsT, rhs, r, w, start=True, stop=True):
        self.P.op("pe", lambda: self.nc.tensor.matmul(out, lhsT, rhs, start=start, stop=stop), reads=r, writes=w)

    def tr(self, out, a, ident, r, w):
        self.P.op("pe", lambda: self.nc.tensor.transpose(out, a, ident), reads=r, writes=w)

    def memset(self, eng, out, val, w):
        e = self.E(eng)
        self.P.op(eng, lambda: e.memset(out, val), writes=w)

    def recip(self, out, a, r, w):
        self.P.op("dve", lambda: self.nc.vector.reciprocal(out, a), reads=r, writes=w)

    def rsqrt(self, out, a, r, w):
        self.act(out, a, AF.Sqrt, r, w)
        self.recip(out, out, w, w)


def load_consts(P, io):
    nc = P.nc
    c = {}
    c["identb"] = P.sb("identb", [128, 128], BF16)
    c["identf"] = P.sb("identf", [128, 128], F32)
    c["b64"] = P.sb("b64", [128, 128], F32)
    c["b64m"] = P.sb("b64m", [128, 128], F32)
    c["mskL"] = P.sb("mskL", [64, 64], F32)
    c["mskU"] = P.sb("mskU", [64, 128], F32)
    c["i2"] = P.sb("i2", [128, 64], F32)
    c["maskR"] = P.sb("maskR", [128, 512], F32)
    c["tri"] = P.sb("tri", [128, 128], BF16)
    c["zcol"] = P.sb("zcol", [128, 1], F32)
    P.dma("pool", c["identb"][:], io["c_ident"], writes=["c_identb"])
    P.dma("sp", c["identf"][:], io["c_ident"], writes=["c_identf"])
    P.dma("sp", c["b64"][:], io["c_b64"], writes=["c_b64"])
    P.dma("sp", c["b64m"][:], io["c_b64m"], writes=["c_b64m"])
    P.dma("sp", c["mskL"][:], io["c_mskL"], writes=["c_mskL"])
    P.dma("sp", c["mskU"][:], io["c_mskU"], writes=["c_mskU"])
    P.dma("sp", c["i2"][:], io["c_i2"], writes=["c_i2"])
    P.dma("sp", c["maskR"][:], io["c_maskR"], writes=["c_maskR"])
    P.dma("pool", c["tri"][:], io["c_tri"], writes=["c_tri"])
    P.op("pool", lambda: nc.gpsimd.memset(c["zcol"][:], 0.0), writes=["c_zcol"])
    c["keys"] = ["c_identb", "c_identf", "c_b64", "c_b64m", "c_mskL", "c_mskU", "c_i2", "c_maskR", "c_tri", "c_zcol"]
    return c


B1_RATIO = 6
PRM = dict(mu_r=0, mu_k=1, mu_v=2, w0=3, a0=4, k_k=5, k_a=6, r_k=7, lnx_g=8, lnx_b=9, omk=10)


def stage_B1(P, io, cst, yT, pairs=range(8), tiles=range(9), lvl=99):
    nc = P.nc
    O = Ops(P)
    PJ = io["PJ"]
    ps = io["ps"]
    m0 = P.mark()
    CK = cst["keys"]
    prm = P.sb("prm", [128, 8, 11], F32)
    P.dma("sp", prm[:, :, 0:10], io["rw_prm"], writes=["prm"])
    O.ts("dve", prm[:, :, 10:11], prm[:, :, 6:7], -1.0, 1.0, ALU.mult, ALU.add, ["prm"], ["prm"])
    mul = P.sb("mul", [128, 3], F32)
    P.dma("sp", mul[:], io["rw_mul"], writes=["mul"])
    w2a2 = P.sb("w2a2", [128, 1024], BF16)
    g2a = P.sb("g2a", [128, 1024], BF16)
    g2b = P.sb("g2b", [32, 1024], BF16)
    P.dma("pool", w2a2[:], io["rw_w2a2"], writes=["w2a2"])
    P.dma("pool", g2a[:], io["rw_g2"][0:128, :], writes=["g2a"])
    P.dma("pool", g2b[:], io["rw_g2"][128:160, :], writes=["g2b"])
    Hst = P.sb("Hst", [128, 8, 64], F32)
    O.memset("pool", Hst[:], 0.0, ["H%d" % p for p in range(8)])

    def S(name, shape, dt, nb=2):
        return [P.sb(name + str(i), shape, dt) for i in range(nb)]
    xwc = S("xwc", [128, 512], F32); xwp = S("xwp", [128, 512], F32, 1) * 2
    xgc = S("xgc", [128, 512], F32); xgp = S("xgp", [128, 512], F32, 1) * 2
    xhc = S("xhc", [32, 512], F32, 1) * 2; xhp = S("xhp", [32, 512], F32, 1) * 2
    txa = S("txa", [128, 512], BF16); sga = S("sga", [128, 512], BF16); sgb = S("sgb", [32, 512], BF16)
    names32 = ["kc", "kp", "vc", "vp", "rc", "rp", "ks", "vs", "rs", "sw", "al", "gg", "kk", "t1", "kkn", "bb", "kh",
               "cum", "cumx", "e_r", "e_a", "e_n", "e_c", "bon"]
    U = {n: S(n, [128, 512], F32, 2 if n in ("kc", "kp", "vc", "vp", "bon", "gg") else 1) for n in names32}
    gC = S("gC", [128, 8], F32)
    QQ = S("QQ", [128, 8, 2, 64], BF16); KK = S("KK", [128, 8, 2, 64], BF16)
    BH = S("BH", [128, 8, 64], BF16); KH = S("KH", [128, 8, 64], BF16); VV = S("VV", [128, 8, 64], BF16)
    TMa = S("TMa", [64, 8, 128], BF16, 1) * 2; TMb = S("TMb", [64, 8, 128], BF16, 1) * 2
    TMk = S("TMk", [64, 8, 128], BF16, 1) * 2; TMv = S("TMv", [64, 8, 128], BF16, 1) * 2
    Lb = S("Lb", [64, 8, 64], BF16, 4); LTb = S("LTb", [64, 8, 64], BF16, 4)
    GB = S("GB", [64, 8, 128], BF16); GK = S("GK", [64, 8, 128], BF16)
    Zf = S("Zf", [64, 8, 128], F32); Zb = S("Zb", [64, 8, 128], BF16)
    MT = S("MT", [128, 8, 64], F32, 1) * 2; NN = S("NN", [128, 8, 64], F32, 1) * 2
    DG = S("DG", [128, 8, 64], F32, 1) * 2
    Hb = S("Hb", [128, 8, 64], BF16)
    QeT = S("QeT", [128, 8, 64], BF16)
    ysb = S("ysb", [128, 512], F32, 1) * 2; yc_ = S("yc", [128, 512], F32, 1) * 2; sq_ = S("sq", [128, 512], F32, 1) * 2
    rstd = S("rstd", [128, 512], F32, 1) * 2

    def v3(ap, n):
        return ap.rearrange("p (a b) -> p a b", a=n)

    def psv(bank, rows, n, w):
        return ps[bank][rows, 0:n * w].rearrange("p (a b) -> p a b", a=n)

    def psb(bank, rows, n, w):
        return ps[bank][:, :].bitcast(BF16)[rows, 0:n * w].rearrange("p (a b) -> p a b", a=n)

    def unit_gen(ct, p, u, first):
        samp = (ct == 8)
        t0 = NPR if samp else 512 * ct
        TW = 64 if samp else 512
        RW_ = NS if samp else 512
        NCH = TW // 64
        do_out = ct >= 6
        lb = ct % 2

        def load_pair(cur, prv, r0, rows, kc_, kp_, q="sp"):
            if samp:
                O.memset("pool", cur[0:rows, 0:TW], 0.0, [kc_])
                O.memset("pool", prv[0:rows, 0:TW], 0.0, [kp_])
                P.dma(q, cur[0:rows, 0:NS], PJ[r0:r0 + rows, NPR:NPR + NS], writes=[kc_])
                P.dma(q, prv[0:rows, 1:NS], PJ[r0:r0 + rows, NPR:NPR + NS - 1], writes=[kp_])
                P.dma(q, prv[0:rows, 0:1], io["shiftT"][r0:r0 + rows, :], writes=[kp_])
            else:
                P.dma(q, cur[0:rows, 0:TW], PJ[r0:r0 + rows, t0:t0 + TW], writes=[kc_])
                if t0 == 0:
                    O.memset("pool", prv[0:rows, 0:1], 0.0, [kp_])
                    P.dma(q, prv[0:rows, 1:TW], PJ[r0:r0 + rows, 0:TW - 1], writes=[kp_])
                else:
                    P.dma(q, prv[0:rows, 0:TW], PJ[r0:r0 + rows, t0 - 1:t0 + TW - 1], writes=[kp_])

        def shift(eng, out, cur, prv, mu, r, w, rows=128):
            O.tt(eng, prv[0:rows, 0:TW], prv[0:rows, 0:TW], cur[0:rows, 0:TW], ALU.subtract, r, [r[1]])
            if eng == "dve":
                O.stt(eng, out[0:rows, 0:TW], prv[0:rows, 0:TW], mu, cur[0:rows, 0:TW], ALU.mult, ALU.add, r, w)
            else:
                O.ts(eng, prv[0:rows, 0:TW], prv[0:rows, 0:TW], mu, None, ALU.mult, None, r, [r[1]])
                O.tt(eng, out[0:rows, 0:TW], prv[0:rows, 0:TW], cur[0:rows, 0:TW], ALU.add, r, w)

        kx = "L%d" % lb
        if first:
            load_pair(xwc[lb], xwp[lb], ROW["xwxa"], 128, kx + "xwc", "Lxwp")
            shift("dve", xwc[lb], xwc[lb], xwp[lb], mul[:, 0:1], [kx + "xwc", "Lxwp", "mul"], [kx + "xwc"])
            O.act(txa[lb][0:64, 0:TW], xwc[lb][0:64, 0:TW], AF.Tanh, [kx + "xwc"], [kx + "txa"])
            O.cp("pool", txa[lb][64:128, 0:TW], xwc[lb][64:128, 0:TW], [kx + "xwc"], [kx + "txa"])
            if do_out:
                load_pair(xgc[lb], xgp[lb], ROW["xg"], 128, kx + "xgc", "Lxgp")
                shift("pool", xgc[lb], xgc[lb], xgp[lb], mul[:, 1:2], [kx + "xgc", "Lxgp", "mul"], [kx + "xgc"])
                O.act(sga[lb][:, 0:TW], xgc[lb][:, 0:TW], AF.Sigmoid, [kx + "xgc"], [kx + "sga"])
                r0 = ROW["xg"] + 128
                if samp:
                    O.memset("pool", xhc[lb][:, 0:TW], 0.0, ["Lxhc"])
                    O.memset("pool", xhp[lb][:, 0:TW], 0.0, ["Lxhp"])
                    P.dma("sp", xhc[lb][:, 0:NS], PJ[r0:r0 + 32, NPR:NPR + NS], writes=["Lxhc"])
                    P.dma("sp", xhp[lb][:, 1:NS], PJ[r0:r0 + 32, NPR:NPR + NS - 1], writes=["Lxhp"])
                    P.dma("sp", xhp[lb][:, 0:1], io["shiftT"][r0:r0 + 32, :], writes=["Lxhp"])
                else:
                    P.dma("sp", xhc[lb][:, 0:TW], PJ[r0:r0 + 32, t0:t0 + TW], writes=["Lxhc"])
                    P.dma("sp", xhp[lb][:, 0:TW], PJ[r0:r0 + 32, t0 - 1:t0 + TW - 1], writes=["Lxhp"])
                shift("pool", xhc[lb], xhc[lb], xhp[lb], mul[0:32, 2:3], ["Lxhc", "Lxhp", "mul"], ["Lxhc"], rows=32)
                O.act(sgb[lb][:, 0:TW], xhc[lb][:, 0:TW], AF.Sigmoid, ["Lxhc"], [kx + "sgb"])

            yield "d"
        SINGLE = ("xhc", "xhp", "MT", "NN", "DG", "ysb", "yc", "sq", "rstd", "TMa", "TMb", "TMk", "TMv")
        K = (lambda n, u=u: "u%d_%s" % (0 if (n in SINGLE or (n in U and len(U[n]) == 1)) else u, n))
        T = (lambda n, u=u: U[n][u % len(U[n])])
        pc = lambda j: prm[:, p, j:j + 1]
        cols = slice(128 * p, 128 * p + 128)
        W_ = slice(0, TW)
        load_pair(T("kc"), T("kp"), ROW["rk"] + 128 * p, 128, K("kc"), K("kp"))
        load_pair(T("vc"), T("vp"), ROW["rv"] + 128 * p, 128, K("vc"), K("vp"))
        shift("dve", T("ks"), T("kc"), T("kp"), pc(1), [K("kc"), K("kp"), "prm"], [K("ks")])
        shift("pool", T("vs"), T("vc"), T("vp"), pc(2), [K("vc"), K("vp"), "prm"], [K("vs")])
        yield "d"
        if do_out:
            load_pair(T("rc"), T("rp"), ROW["r"] + 128 * p, 128, K("rc"), K("rp"))
            shift("pool", T("rs"), T("rc"), T("rp"), pc(0), [K("rc"), K("rp"), "prm"], [K("rs")])
        O.mm(ps[0][:, W_], w2a2[0:64, cols], txa[lb][0:64, W_], ["w2a2", kx + "txa"], ["ps0"])
        O.mm(ps[1][:, W_], w2a2[64:128, cols], txa[lb][64:128, W_], ["w2a2", kx + "txa"], ["ps1"])
        O.act(T("sw")[:, W_], ps[0][:, W_], AF.Sigmoid, ["ps0", "prm"], [K("sw")], bias=pc(3))
        O.act(T("al")[:, W_], ps[1][:, W_], AF.Sigmoid, ["ps1", "prm"], [K("al")], bias=pc(4))
        yield "d"
        if samp:
            O.memset("pool", T("sw")[:, NS:TW], 0.0, [K("sw")])
        if do_out:
            O.mm(ps[2][:, W_], g2a[:, cols], sga[lb][:, W_], ["g2a", kx + "sga"], ["ps2"], start=True, stop=False)
            O.mm(ps[2][:, W_], g2b[0:32, cols], sgb[lb][0:32, W_], ["g2b", kx + "sgb"], ["ps2"], start=False, stop=True)
            O.cp("act", T("gg")[:, W_], ps[2][:, W_], ["ps2"], [K("gg")])
        O.ts("pool", T("kk")[:, W_], T("ks")[:, W_], pc(5), None, ALU.mult, None, [K("ks"), "prm"], [K("kk")])
        O.tt("pool", T("t1")[:, W_], T("kk")[:, W_], T("kk")[:, W_], ALU.mult, [K("kk")], [K("t1")])
        O.mm(ps[2][:, W_], cst["b64"][:, :], T("t1")[:, W_], ["c_b64", K("t1")], ["ps2"])
        O.ts("dve", T("t1")[:, W_], ps[2][:, W_], 1e-24, None, ALU.max, None, ["ps2"], [K("t1")])
        O.rsqrt(T("t1")[:, W_], T("t1")[:, W_], [K("t1")], [K("t1")])
        yield "d"
        O.tt("pool", T("kkn")[:, W_], T("kk")[:, W_], T("t1")[:, W_], ALU.mult, [K("kk"), K("t1")], [K("kkn")])
        O.tt("dve", T("bb")[:, W_], T("kkn")[:, W_], T("al")[:, W_], ALU.mult, [K("kkn"), K("al")], [K("bb")])
        O.ts("dve", T("t1")[:, W_], T("al")[:, W_], pc(6), pc(10), ALU.mult, ALU.add, [K("al"), "prm", K("t1")], [K("t1")])
        O.tt("pool", T("kh")[:, W_], T("ks")[:, W_], T("t1")[:, W_], ALU.mult, [K("ks"), K("t1")], [K("kh")])
        yield "d"
        sc_o, sc_m, sc_i, sc_z = T("cum")[:, W_], cst["maskR"][:, W_], T("sw")[:, W_], cst["zcol"][:, 0:1]
        P.op("dve", (lambda sc_o=sc_o, sc_m=sc_m, sc_i=sc_i, sc_z=sc_z:
                     nc.vector.tensor_tensor_scan(sc_o, sc_m, sc_i, sc_z, ALU.mult, ALU.add)),
             reads=[K("sw")], writes=[K("cum")])
        O.tt("pool", T("cumx")[:, W_], T("cum")[:, W_], T("sw")[:, W_], ALU.subtract, [K("cum"), K("sw")], [K("cumx")])
        O.act(T("e_r")[:, W_], T("cum")[:, W_], AF.Exp, [K("cum")], [K("e_r")], scale=-C0)
        O.act(T("e_a")[:, W_], T("cumx")[:, W_], AF.Exp, [K("cumx")], [K("e_a")], scale=-C0)
        O.recip(T("e_n")[:, W_], T("e_r")[:, W_], [K("e_r")], [K("e_n")])
        yield "d"
        cv = v3(T("cum")[:, W_], NCH)
        O.tt("dve", v3(T("cumx")[:, W_], NCH), cv[:, :, 63:64].to_broadcast([128, NCH, 64]), cv, ALU.subtract,
             [K("cum"), K("e_a")], [K("cumx")])
        O.act(T("e_c")[:, W_], T("cumx")[:, W_], AF.Exp, [K("cumx")], [K("e_c")], scale=-C0)
        O.cp("pool", gC[u][:, 0:NCH], v3(T("e_r")[:, W_], NCH)[:, :, 63], [K("e_r")], [K("gC")])
        yield "d"
        qq = QQ[u]; kq = KK[u]
        O.stt("dve", qq[:, 0:NCH, 0, :], v3(T("kkn")[:, W_], NCH), -1.0, v3(T("e_a")[:, W_], NCH), ALU.mult, ALU.mult,
              [K("kkn"), K("e_a")], [K("QQ")])
        if do_out:
            O.tt("pool", qq[:, 0:NCH, 1, :], v3(T("rs")[:, W_], NCH), v3(T("e_r")[:, W_], NCH), ALU.mult,
                 [K("rs"), K("e_r")], [K("QQ")])
        O.tt("dve", kq[:, 0:NCH, 0, :], v3(T("bb")[:, W_], NCH), v3(T("e_n")[:, W_], NCH), ALU.mult, [K("bb"), K("e_n")], [K("KK")])
        O.tt("pool", kq[:, 0:NCH, 1, :], v3(T("kh")[:, W_], NCH), v3(T("e_n")[:, W_], NCH), ALU.mult, [K("kh"), K("e_n")], [K("KK")])
        O.tt("dve", BH[u][:, 0:NCH, :], v3(T("bb")[:, W_], NCH), v3(T("e_c")[:, W_], NCH), ALU.mult, [K("bb"), K("e_c")], [K("BH")])
        O.tt("pool", KH[u][:, 0:NCH, :], v3(T("kh")[:, W_], NCH), v3(T("e_c")[:, W_], NCH), ALU.mult, [K("kh"), K("e_c")], [K("KH")])
        yield "d"
        O.cp("act", VV[u][:, 0:NCH, :], v3(T("vs")[:, W_], NCH), [K("vs")], [K("VV")])
        if do_out:
            O.stt("dve", T("bon")[:, W_], T("rs")[:, W_], pc(7), T("kh")[:, W_], ALU.mult, ALU.mult, [K("rs"), K("kh"), "prm"], [K("bon")])
            O.mm(ps[2][:, W_], cst["b64"][:, :], T("bon")[:, W_], ["c_b64", K("bon")], ["ps2"])
            O.tt("dve", T("bon")[:, W_], ps[2][:, W_], T("vs")[:, W_], ALU.mult, ["ps2", K("vs"), K("bon")], [K("bon")])
        yield "DONE_D"
        if lvl < 1:
            return
        O.tt("pool", DG[u][:, 0:NCH, :], cst["i2"][:, :].unsqueeze(1).to_broadcast([128, NCH, 64]),
             gC[u][:, 0:NCH].unsqueeze(2).to_broadcast([128, NCH, 64]), ALU.mult, ["c_i2", K("gC")], [K("DG")])
        for (src, skey, dst, dkey, bank) in ((qq, K("QQ"), TMa[u], K("TMa"), 3), (BH[u], K("BH"), TMb[u], K("TMb"), 4),
                                             (KH[u], K("KH"), TMk[u], K("TMk"), 3), (VV[u], K("VV"), TMv[u], K("TMv"), 4)):
            pv = psb(bank, slice(0, 64), NCH, 128)
            for i in range(NCH):
                s_ap = src[:, i, 0, :] if src is qq else src[:, i, :]
                O.tr(pv[:, i, :], s_ap, cst["identb"][:, :], [skey, "c_identb"], ["ps%d" % bank])
            O.cp(O.pick(("act", "dve")), dst[:, 0:NCH, :], pv, ["ps%d" % bank], [dkey])
            yield "c"
        if lvl < 2:
            return
        for h in range(2):
            R = slice(64 * h, 64 * h + 64)
            hk = lambda n: "u%dh%d_%s" % (u, h, n)
            zi = (2 * u + h) % 2
            Zf_, Zb_ = Zf[h], Zb[h]
            kz = "Z%d" % h
            GB_, GK_ = GB[h], GK[h]
            kg = "G%d" % h
            NQ = 128 if do_out else 64
            gab = psv(3, slice(0, 64), NCH, 64)
            for i in range(NCH):
                O.mm(gab[:, i, :], qq[R, i, 0, :], kq[R, i, 0, :], [K("QQ"), K("KK")], ["ps3"])
            for i in range(NCH):
                bk = 4 + i // 4
                o_ = ps[bk][0:64, (i % 4) * 128:(i % 4) * 128 + NQ]
                O.mm(o_, kq[R, i, 0, :], qq[R, i, :, :].rearrange("p a b -> p (a b)")[:, 0:NQ], [K("QQ"), K("KK")], ["ps%d" % bk])
            for i in range(NCH):
                bk = 6 + i // 4
                o_ = ps[bk][0:64, (i % 4) * 128:(i % 4) * 128 + NQ]
                O.mm(o_, kq[R, i, 1, :], qq[R, i, :, :].rearrange("p a b -> p (a b)")[:, 0:NQ], [K("QQ"), K("KK")], ["ps%d" % bk])
            li = 2 * h
            O.tt("dve", Lb[li][:, 0:NCH, :], gab, cst["mskL"][:, :].unsqueeze(1).to_broadcast([64, NCH, 64]), ALU.mult,
                 ["ps3", "c_mskL"], ["Lb%d" % li])
            yield "c"
            for (bank0, G_, kname) in ((4, GB_, kg + "B"), (6, GK_, kg + "K")):
                for hb in range((NCH + 3) // 4):
                    n4 = min(4, NCH - 4 * hb)
                    src = ps[bank0 + hb][0:64, 0:n4 * 128].rearrange("p (a b) -> p a b", a=n4)[:, :, 0:NQ]
                    O.tt("dve" if hb == 0 else "dve", G_[:, 4 * hb:4 * hb + n4, 0:NQ], src,
                         cst["mskU"][:, 0:NQ].unsqueeze(1).to_broadcast([64, n4, NQ]), ALU.mult,
                         ["ps%d" % (bank0 + hb), "c_mskU"], [kname])
            O.cp("pool", LTb[li][:, 0:NCH, :], GB_[:, 0:NCH, 0:64], [kg + "B"], ["LTb%d" % li])
            yield "c"
            if lvl < 3:
                continue
            xp = psv(3, slice(0, 64), NCH, 64)
            for i in range(NCH):
                O.mm(xp[:, i, :], GK_[:, i, 0:64], TMv[u][:, i, R], [kg + "K", K("TMv")], ["ps3"])
            O.cp("act", Zf_[:, 0:NCH, 0:64], TMa[u][:, 0:NCH, R], [K("TMa")], [kz + "f"])
            O.cp("dve", Zf_[:, 0:NCH, 64:128], xp, ["ps3"], [kz + "f"])
            yield "c"
            O.cp("pool", Zb_[:, 0:NCH, :], Zf_[:, 0:NCH, :], [kz + "f"], [kz + "b"])
            cur = li
            for j in range(6):
                for i in range(NCH):
                    bk = 4 + i // 4
                    o_ = ps[bk][0:64, (i % 4) * 128:(i % 4) * 128 + 128]
                    O.mm(o_, LTb[cur][:, i, :], Zb_[:, i, :], ["LTb%d" % cur, kz + "b"], ["ps%d" % bk])
                    yield "c"
                if j < 5:
                    nxt = li + 1 if cur == li else li
                    pn = psv(6, slice(0, 64), NCH, 64)
                    ptn = psv(7, slice(0, 64), NCH, 64)
                    for i in range(NCH):
                        O.mm(pn[:, i, :], LTb[cur][:, i, :], Lb[cur][:, i, :], ["LTb%d" % cur, "Lb%d" % cur], ["ps6"])
                    for i in range(NCH):
                        O.mm(ptn[:, i, :], Lb[cur][:, i, :], LTb[cur][:, i, :], ["LTb%d" % cur, "Lb%d" % cur], ["ps7"])
                        yield "c"
                for hb in range((NCH + 3) // 4):
                    n4 = min(4, NCH - 4 * hb)
                    src = ps[4 + hb][0:64, 0:n4 * 128].rearrange("p (a b) -> p a b", a=n4)
                    O.tt("dve", Zb_[:, 4 * hb:4 * hb + n4, :], Zf_[:, 4 * hb:4 * hb + n4, :], src, ALU.add,
                         ["ps%d" % (4 + hb), kz + "f"], [kz + "b"])
                if j < 5:
                    for hb in range((NCH + 3) // 4):
                        n4 = min(4, NCH - 4 * hb)
                        src = ps[4 + hb][0:64, 0:n4 * 128].rearrange("p (a b) -> p a b", a=n4)
                        O.tt("dve", Zf_[:, 4 * hb:4 * hb + n4, :], Zf_[:, 4 * hb:4 * hb + n4, :], src, ALU.add,
                             ["ps%d" % (4 + hb), kz + "f"], [kz + "f"])
                if j < 5:
                    O.cp("act", Lb[nxt][:, 0:NCH, :], pn, ["ps6"], ["Lb%d" % nxt])
                    O.cp("act", LTb[nxt][:, 0:NCH, :], ptn, ["ps7"], ["LTb%d" % nxt])
                    yield "c"
                    cur = nxt
            if lvl < 4:
                continue
            mtp = psv(3, R, NCH, 64)
            npp = psv(6, R, NCH, 64)
            for i in range(NCH):
                O.mm(mtp[:, i, :], Zb_[:, i, 0:64], TMb[u][:, i, R], [kz + "b", K("TMb")], ["ps3"])
            for i in range(NCH):
                O.mm(npp[:, i, :], TMb[u][:, i, R], Zb_[:, i, 64:128], [kz + "b", K("TMb")], ["ps6"], start=True, stop=False)
                O.mm(npp[:, i, :], TMk[u][:, i, R], TMv[u][:, i, R], [K("TMk"), K("TMv")], ["ps6"], start=False, stop=True)
            if do_out:
                qep = psv(7, R, NCH, 64)
                for i in range(NCH):
                    O.mm(qep[:, i, :], Zb_[:, i, 0:64], GB_[:, i, 64:128], [kz + "b", kg + "B"], ["ps7"])
            O.tt("dve", MT[u][R, 0:NCH, :], mtp, DG[u][R, 0:NCH, :], ALU.add, ["ps3", K("DG")], [K("MT")])
            O.cp("act", NN[u][R, 0:NCH, :], npp, ["ps6"], [K("NN")])
            yield "c"
            if do_out:
                O.tt("dve", QeT[u][R, 0:NCH, :], qep, qq[R, 0:NCH, 1, :], ALU.add, ["ps7", K("QQ")], [K("QeT")])
        if lvl < 5:
            return
        Hcur = Hst[:, p, :]
        if samp:
            P.dma("sp", Hst[:, p, :], io["s0T"][:, p, :], reads=[], writes=["H%d" % p])
        hkey = "H%d" % p
        stp = psv(3, slice(0, 128), NCH, 64)
        for i in range(NCH):
            if do_out:
                O.cp("act", Hb[u][:, i, :], Hcur, [hkey], [K("Hb")])
            for h in range(2):
                R = slice(64 * h, 64 * h + 64)
                O.mm(stp[R, i, :], MT[u][R, i, :], Hst[R, p, :], [K("MT"), hkey], ["ps3"])
            O.tt("dve", Hcur, stp[:, i, :], NN[u][:, i, :], ALU.add, ["ps3", K("NN"), hkey], [hkey])
            yield "c"
        if samp:
            P.dma("sp", io["wkv_s"][:, p, :], Hst[:, p, :], reads=[hkey], semkey="wkvs%d" % p, final=True)
        elif ct == 7:
            P.dma("sp", io["wkv_p"][:, p, :], Hst[:, p, :], reads=[hkey], semkey="wkvp%d" % p, final=True)
        if lvl < 6:
            return
        if do_out:
            ytp = psv(7, slice(0, 128), NCH, 64)
            yt1 = psv(4, slice(0, 128), NCH, 64)
            for h in range(2):
                R = slice(64 * h, 64 * h + 64)
                kz = "Z%d" % h
                kg = "G%d" % h
                for i in range(NCH):
                    O.mm(yt1[R, i, :], Hb[u][R, i, :], QeT[u][R, i, :], [K("Hb"), K("QeT")], ["ps4"])
                for i in range(NCH):
                    O.mm(ytp[R, i, :], Zb[h][:, i, 64:128], GB[h][:, i, 64:128], [kz + "b", kg + "B"], ["ps7"], start=True, stop=False)
                    O.mm(ytp[R, i, :], TMv[u][:, i, R], GK[h][:, i, 64:128], [K("TMv"), kg + "K"], ["ps7"], start=False, stop=True)
                    yield "c"
            y_ = ysb[u]
            O.cp("act", y_[:, W_], ps[4][:, W_], ["ps4"], [K("ysb")])
            O.tt("dve", y_[:, W_], y_[:, W_], ps[7][:, W_], ALU.add, ["ps7", K("ysb")], [K("ysb")])
            O.mm(ps[3][:, W_], cst["b64m"][:, :], y_[:, W_], ["c_b64m", K("ysb")], ["ps3"])
            O.tt("dve", yc_[u][:, W_], y_[:, W_], ps[3][:, W_], ALU.subtract, ["ps3", K("ysb")], [K("yc")])
            O.tt("pool", sq_[u][:, W_], yc_[u][:, W_], yc_[u][:, W_], ALU.mult, [K("yc")], [K("sq")])
            O.mm(ps[3][:, W_], cst["b64m"][:, :], sq_[u][:, W_], ["c_b64m", K("sq")], ["ps3"])
            O.ts("dve", rstd[u][:, W_], ps[3][:, W_], 64e-5, None, ALU.add, None, ["ps3"], [K("rstd")])
            O.rsqrt(rstd[u][:, W_], rstd[u][:, W_], [K("rstd")], [K("rstd")])
            yield "c"
            O.tt("pool", yc_[u][:, W_], yc_[u][:, W_], rstd[u][:, W_], ALU.mult, [K("yc"), K("rstd")], [K("yc")])
            O.ts("dve", yc_[u][:, W_], yc_[u][:, W_], pc(8), pc(9), ALU.mult, ALU.add, [K("yc"), "prm"], [K("yc")])
            O.tt("pool", yc_[u][:, W_], yc_[u][:, W_], T("bon")[:, W_], ALU.add, [K("yc"), K("bon")], [K("yc")])
            oc0 = 1024 if samp else (ct - 6) * 512
            O.tt("dve", yT[:, p, oc0:oc0 + RW_], yc_[u][:, 0:RW_], T("gg")[:, 0:RW_], ALU.mult, [K("yc"), K("gg")], ["yT%d" % p])

    units = []
    cnt_u = 0
    for ct in tiles:
        for pi_, p in enumerate(pairs):
            units.append(unit_gen(ct, p, cnt_u % 2, pi_ == 0))
            cnt_u += 1
    prev = None
    for g in units:
        d_done = False
        c_done = prev is None
        while not (d_done and c_done):
            if not d_done:
                try:
                    if next(g) == "DONE_D":
                        d_done = True
                except StopIteration:
                    d_done = True
                    g = None
            for _rep in range(B1_RATIO):
                if not c_done:
                    try:
                        next(prev)
                    except StopIteration:
                        c_done = True
        prev = g
    if prev is not None:
        for _ in prev:
            pass
    P.barrier()
    P.release(m0)


IN_SPECS = dict(
    xT=[D, NTOK], w_in=[D, 7472], shiftT=[N_FM, 1], s0T=[128, 8, 64],
    rw_prm=[128, 8, 10], rw_mul=[128, 3], rw_w2a2=[128, 1024], rw_g2=[160, 1024],
    c_ident=[128, 128], c_b64=[128, 128], c_b64m=[128, 128], c_mskL=[64, 64], c_mskU=[64, 128], c_i2=[128, 64],
    c_maskR=[128, 512], c_tri=[128, 128], c_cvec=[65, 1],
    xo=[NOWN, D], w_o=[D, D], ln1_g=[1, D], ln1_b=[1, D], w_up=[D, DFF], w_down=[DFF, D], ln2_g=[1, D], ln2_b=[1, D],
    fx_g=[64, 16], fx_bf=[16, 1], fx_padb=[128, 32], fx_clfT=[16, 1024], fx_ckT=[16, 64, 1024], fx_cv=[1024, 1024],
)


def make_consts():
    c = {}
    c["c_ident"] = np.eye(128, dtype=np.float32)
    b = np.zeros((128, 128), np.float32)
    b[:64, :64] = 1.0
    b[64:, 64:] = 1.0
    c["c_b64"] = b
    c["c_b64m"] = b / 64.0
    t = np.arange(64)
    c["c_mskL"] = (t[:, None] > t[None, :]).astype(np.float32)
    c["c_mskU"] = np.concatenate([(t[:, None] < t[None, :]), (t[:, None] <= t[None, :])], 1).astype(np.float32)
    c["c_i2"] = np.concatenate([np.eye(64), np.eye(64)], 0).astype(np.float32)
    m = np.ones((128, 512), np.float32)
    m[:, ::64] = 0.0
    c["c_maskR"] = m
    k = np.arange(128)
    c["c_tri"] = (k[:, None] <= k[None, :]).astype(np.float32)
    cv = np.full((65, 1), 1.0 / 64.0, np.float32)
    cv[64, 0] = 1e-6
    c["c_cvec"] = cv
    return c


def build(stages=("A", "B1", "B2", "C"), debug=(), pj_input=False, b1_kw=None, b2_kw=None, c_kw=None, extra_in=()):
    P = Prog()
    nc = P.nc
    io = {}
    need = set()
    if "A" in stages:
        need |= {"xT", "w_in"}
    if "B1" in stages:
        need |= {"shiftT", "s0T", "rw_prm", "rw_mul", "rw_w2a2", "rw_g2"}
    if "B2" in stages:
        need |= {"fx_g", "fx_bf", "fx_padb", "fx_clfT", "fx_ckT", "fx_cv"}
    if "C" in stages:
        need |= {"xo", "w_o", "ln1_g", "ln1_b", "w_up", "w_down", "ln2_g", "ln2_b"}
    need |= {k for k in IN_SPECS if k.startswith("c_")}
    need |= set(extra_in)
    for name, shape in IN_SPECS.items():
        if name in need:
            io[name] = nc.dram_tensor(name, list(shape), F32, kind="ExternalInput").ap()
    kind = "ExternalInput" if pj_input else ("ExternalOutput" if "PJ" in debug else "Internal")
    io["PJ"] = nc.dram_tensor("PJ", [N_FM, NTOK], F32, kind=kind).ap()
    io["VTM"] = nc.dram_tensor("VTM", [NTOK, 1024], F32, kind=kind).ap()
    io["wkv_p"] = nc.dram_tensor("wkv_p", [128, 8, 64], F32, kind="ExternalOutput").ap()
    io["wkv_s"] = nc.dram_tensor("wkv_s", [128, 8, 64], F32, kind="ExternalOutput").ap()
    io["spl_p"] = nc.dram_tensor("spl_p", [16, 3, 1024], BF16, kind="Internal").ap()
    io["spl_s"] = nc.dram_tensor("spl_s", [16, 3, NS], BF16, kind="Internal").ap()
    io["logf_o"] = nc.dram_tensor("logf_o", [16, NOWN], F32, kind="ExternalOutput").ap()
    io["ps"] = [nc.alloc_psum_tensor("psb%d" % i, [128, 512], F32) for i in range(8)]
    cst = load_consts(P, io)
    if "A" in stages:
        stage_A(P, io)
    else:
        P.barrier()
    yT = P.sb("yT", [128, 16, NOWN], BF16)
    if "A" in stages or "outs" in debug:
        io["k_out"] = nc.dram_tensor("k_out", [1024, NOWN], F32, kind="ExternalOutput").ap()
        io["v_out"] = nc.dram_tensor("v_out", [NOWN, 1024], F32, kind="ExternalOutput").ap()
        io["shA"] = nc.dram_tensor("shA", [2176, 2], F32, kind="ExternalOutput").ap()
        io["shB"] = nc.dram_tensor("shB", [1184, 2], F32, kind="ExternalOutput").ap()
        PJ = io["PJ"]
        P.dma("sp", io["k_out"], PJ[ROW["fk"]:ROW["fk"] + 1024, OWN0:NTOK], semkey="o_k", final=True)
        P.dma("sp", io["v_out"], io["VTM"][OWN0:NTOK, :], semkey="o_v", final=True)
        for ci, col in enumerate((NPR - 1, NTOK - 1)):
            P.dma("sp", io["shA"][:, ci:ci + 1], PJ[0:2176, col:col + 1], semkey="o_sa", final=True, allow_slow_non_contiguous=True)
            P.dma("sp", io["shB"][:, ci:ci + 1], PJ[ROW["r"]:ROW["r"] + 1184, col:col + 1], semkey="o_sb", final=True,
                  allow_slow_non_contiguous=True)
    if "B1" in stages:
        stage_B1(P, io, cst, yT, **(b1_kw or {}))
    if "B2" in stages:
        stage_B2(P, io, cst, yT, **(b2_kw or {}))
    if "yTin" in debug:
        io["yT_in"] = nc.dram_tensor("yT_in", [128, 16, NOWN], F32, kind="ExternalInput").ap()
        P.dma("pool", yT[:], io["yT_in"], writes=["yT%d" % i for i in range(16)], semkey="ytin")
    if "C" in stages:
        io["y_out"] = nc.dram_tensor("y_out", [NOWN, D], F32, kind="ExternalOutput").ap()
        stage_C(P, io, cst, yT, **(c_kw or {}))
    if "yT" in debug:
        io["yT_dbg"] = nc.dram_tensor("yT_dbg", [128, 16, NOWN], F32, kind="ExternalOutput").ap()
        P.dma("pool", io["yT_dbg"], yT[:], reads=["yT%d" % i for i in range(16)], semkey="ytdbg", final=True)
    P.emit()
    return P


def stage_B2(P, io, cst, yT, heads=range(16), insts=("p", "s")):
    nc = P.nc
    O = Ops(P)
    PJ = io["PJ"]
    VTM = io["VTM"]
    ps = io["ps"]
    m0 = P.mark()
    ones = P.sb("ones", [128, 512], F32)
    O.memset("pool", ones[:], 1.0, ["ones"])
    fog = P.sb("fog", [64, 16], F32)
    P.dma("sp", fog[:], io["fx_g"], writes=["fog"])
    nbf = P.sb("nbf", [16, 1], F32)
    P.dma("sp", nbf[:], io["fx_bf"], writes=["nbf"])
    O.ts("dve", nbf[:], nbf[:], -1.0, None, ALU.mult, None, ["nbf"], ["nbf"])
    cvec = P.sb("cvec", [65, 1], F32)
    P.dma("sp", cvec[:], io["c_cvec"], writes=["cvec"])
    padb = P.sb("padb", [128, 32], F32)
    P.dma("sp", padb[:], io["fx_padb"], writes=["padb"])
    Crow = P.sb("Crow", [16, NPR], F32)
    Ccum = P.sb("Ccum", [16, NPR], F32)
    Ckb = P.sb("Ckb", [128, 32, 16], F32)
    spl = P.sb("spl", [16, 3, 1024], BF16)
    q8 = P.sb("q8", [16, 1024], F32)
    lfo = P.sb("lfo", [16, NOWN], F32)
    KTa = [P.sb("KTa%d" % i, [67, NPR], BF16) for i in range(2)]
    QTa = [P.sb("QTa%d" % i, [67, 1024], BF16) for i in range(2)]
    Va = [P.sb("Va%d" % i, [128, 32, 65], BF16) for i in range(2)]
    ogT = [P.sb("ogT%d" % i, [64, 1024], F32) for i in range(2)]
    pt = [P.sb("pt%d" % i, [128, 512], BF16) for i in range(4)]
    Oa = P.sb("Oa", [65, 512], F32)
    sq = P.sb("sqf", [65, 512], F32)
    rs = P.sb("rs", [1, 512], F32)
    y1 = P.sb("y1", [64, 512], F32)
    ytmp = [P.sb("ytmp%d" % i, [64, 512], BF16) for i in range(2)]
    for i in range(2):
        O.memset("pool", KTa[i][64:67, :], 1.0, ["KTa%d" % i])
        O.memset("pool", Va[i][:, :, 64:65], 1.0, ["Va%d" % i])
    cnt = dict(h=0, s=0, p=0, y=0)

    for inst in insts:
        if inst == "p":
            TK, NQ, QPOS0 = NPR, 1024, OWN0
            qtiles = [(0, 512), (512, 512)]
            ocol0 = 0
        else:
            TK, NQ, QPOS0 = 1024 + NS, NS, 1024
            qtiles = [(0, NS)]
            ocol0 = 1024
        nblk = (TK + 127) // 128
        if inst == "p":
            P.dma("sp", Crow[:, 0:TK], PJ[ROW["f"]:ROW["f"] + 16, 0:TK], writes=["Crow"])
            fsl = slice(0, TK)
        else:
            P.dma("sp", Crow[:, 0:1024], io["fx_clfT"], writes=["Crow"])
            P.dma("sp", Crow[:, 1024:TK], PJ[ROW["f"]:ROW["f"] + 16, NPR:NPR + NS], writes=["Crow"], semkey="Crow_b")
            O.ts("dve", Crow[:, 0:1024], Crow[:, 0:1024], -1.0, None, ALU.mult, None, ["Crow"], ["Crow"])
            fsl = slice(1024, TK)
        O.act(Crow[:, fsl], Crow[:, fsl], AF.Exp, ["Crow", "nbf"], ["Crow"], bias=nbf[:, 0:1], scale=-1.0)
        O.act(Crow[:, fsl], Crow[:, fsl], AF.Ln, ["Crow"], ["Crow"], bias=1.0)
        if inst == "p":
            P.op("act", lambda: nc.scalar.mul(lfo[:, 0:1024], Crow[:, OWN0:NPR], -1.0), reads=["Crow"], writes=["lfo"])
        else:
            P.op("act", lambda: nc.scalar.mul(lfo[:, 1024:NOWN], Crow[:, 1024:TK], -1.0), reads=["Crow"], writes=["lfo"])
            P.dma("sp", io["logf_o"], lfo[:], reads=["lfo"], semkey="lfo", final=True)
        for c0 in range(0, TK, 512):
            cw = min(512, TK - c0)
            init = cst["zcol"][0:16, 0:1] if c0 == 0 else Ccum[:, c0 - 1:c0]
            o_ap, d0, d1 = Ccum[:, c0:c0 + cw], ones[0:16, 0:cw], Crow[:, c0:c0 + cw]
            P.op("dve", (lambda o_ap=o_ap, d0=d0, d1=d1, init=init: nc.vector.tensor_tensor_scan(o_ap, d0, d1, init, ALU.mult, ALU.add)),
                 reads=["Crow", "ones", "Ccum"], writes=["Ccum"])
        tp = ps[6][:, 0:nblk * 16].rearrange("p (a b) -> p a b", a=nblk)
        for kb in range(nblk):
            nk = min(128, TK - kb * 128)
            O.tr(tp[0:nk, kb, :], Ccum[:, kb * 128:kb * 128 + nk], cst["identf"][0:16, 0:16], ["Ccum", "c_identf"], ["ps6"])
        nfull = TK // 128
        if inst == "p":
            O.tt("dve", Ckb[:, 0:nblk, :], tp[:, 0:nblk, :], padb[:, 0:nblk].unsqueeze(2).to_broadcast([128, nblk, 16]), ALU.add,
                 ["ps6", "padb"], ["Ckb"])
        else:
            O.cp("dve", Ckb[:, 0:nfull, :], tp[:, 0:nfull, :], ["ps6"], ["Ckb"])
            O.cp("dve", Ckb[0:NS, nfull, :], tp[0:NS, nfull, :], ["ps6", "Ckb"], ["Ckb"])
        qs = slice(QPOS0, QPOS0 + NQ)
        O.ts("dve", q8[:, 0:NQ], Ccum[:, qs], -8.0, None, ALU.mult, None, ["Ccum"], ["q8"])
        for k3 in range(3):
            O.cp("dve", spl[:, k3, 0:NQ], q8[:, 0:NQ], ["q8"], ["spl"])
            if k3 < 2:
                O.tt("dve", q8[:, 0:NQ], q8[:, 0:NQ], spl[:, k3, 0:NQ], ALU.subtract, ["q8", "spl"], ["q8"])
        sk = "splD" + inst
        P.dma("sp", io["spl_" + inst], spl[:, :, 0:NQ], reads=["spl"], writes=[sk])
        for h in heads:
            hb = cnt["h"] % 2
            cnt["h"] += 1
            kK, kQ, kV, kG = "KTa%d" % hb, "QTa%d" % hb, "Va%d" % hb, "ogT%d" % hb
            fk0 = ROW["fk"] + 64 * h
            if inst == "p":
                P.dma("pool", KTa[hb][0:64, 0:TK], PJ[fk0:fk0 + 64, 0:TK], writes=[kK])
                P.dma("pool", Va[hb][:, 0:32, 0:64], VTM[0:NPR, 64 * h:64 * h + 64].rearrange("(b p) d -> p b d", p=128), writes=[kV])
                ocs = slice(OWN0, NPR)
            else:
                P.dma("pool", KTa[hb][0:64, 0:1024], io["fx_ckT"][h], writes=[kK])
                P.dma("pool", KTa[hb][0:64, 1024:TK], PJ[fk0:fk0 + 64, NPR:NPR + NS], writes=[kK])
                P.dma("pool", Va[hb][:, 0:8, 0:64], io["fx_cv"][:, 64 * h:64 * h + 64].rearrange("(b p) d -> p b d", p=128), writes=[kV])
                P.dma("pool", Va[hb][0:NS, 8, 0:64], VTM[NPR:NPR + NS, 64 * h:64 * h + 64], writes=[kV])
                ocs = slice(NPR, NPR + NS)
            P.dma("pool", QTa[hb][0:64, 0:NQ], PJ[ROW["q"] + 64 * h:ROW["q"] + 64 * h + 64, ocs], writes=[kQ])
            P.dma("sp", QTa[hb][64:67, 0:NQ], io["spl_" + inst][h], reads=[sk], writes=[kQ])
            P.dma("sp", ogT[hb][:, 0:NQ], PJ[ROW["og"] + 64 * h:ROW["og"] + 64 * h + 64, ocs], writes=[kG])
            for (q0, qw) in qtiles:
                ob = 2 + cnt["y"] % 2
                okey = "ps%d" % ob
                last_q = QPOS0 + q0 + qw - 1
                kbs = [kb for kb in range(nblk) if kb * 128 <= last_q]
                pend = []
                for n_, kb in enumerate(kbs):
                    nk = min(128, TK - kb * 128)
                    off = max(0, kb * 128 - (QPOS0 + q0))
                    diag = kb * 128 + nk - 1 > QPOS0 + q0
                    sb_ = (0, 1, 7)[cnt["s"] % 3]
                    cnt["s"] += 1
                    pb_ = cnt["p"] % 4
                    cnt["p"] += 1
                    skey, pkey = "ps%d" % sb_, "pt%d" % pb_
                    O.mm(ps[sb_][0:nk, off:qw], KTa[hb][0:67, kb * 128:kb * 128 + nk], QTa[hb][0:67, q0 + off:q0 + qw], [kK, kQ], [skey])
                    if len(pend) >= 2:
                        pend.pop(0)()
                    O.act(pt[pb_][0:nk, off:qw], ps[sb_][0:nk, off:qw], AF.Exp, [skey, "Ckb"], [pkey], bias=Ckb[0:nk, kb, h:h + 1], scale=0.125)
                    if diag:
                        mw = min(128, qw - off, nk)
                        O.tt(O.pick(("dve", "pool")), pt[pb_][0:nk, off:off + mw], pt[pb_][0:nk, off:off + mw], cst["tri"][0:nk, 0:mw], ALU.mult,
                             [pkey, "c_tri"], [pkey])
                    pend.append(lambda nk=nk, kb=kb, pb_=pb_, off=off, pkey=pkey, n_=n_:
                                O.mm(ps[ob][0:65, off:qw], Va[hb][0:nk, kb, 0:65], pt[pb_][0:nk, off:qw], [kV, pkey], [okey],
                                     start=(n_ == 0), stop=(n_ == len(kbs) - 1)))
                while pend:
                    pend.pop(0)()
                cnt["y"] += 1
                yb = cnt["y"] % 2
                O.cp("act", Oa[:, 0:qw], ps[ob][0:65, 0:qw], [okey], ["Oa"])
                O.act(sq[:, 0:qw], Oa[:, 0:qw], AF.Square, ["Oa"], ["sqf"])
                O.mm(ps[4][0:1, 0:qw], cvec[:, 0:1], sq[:, 0:qw], ["cvec", "sqf"], ["ps4"])
                O.act(rs[:, 0:qw], ps[4][0:1, 0:qw], AF.Sqrt, ["ps4"], ["rs"])
                O.recip(rs[:, 0:qw], rs[:, 0:qw], ["rs"], ["rs"])
                O.mm(ps[5][0:64, 0:qw], ones[0:1, 0:64], rs[:, 0:qw], ["ones", "rs"], ["ps5"])
                O.tt("dve", y1[:, 0:qw], Oa[0:64, 0:qw], ps[5][0:64, 0:qw], ALU.mult, ["Oa", "ps5"], ["y1"])
                O.act(ogT[hb][:, q0:q0 + qw], ogT[hb][:, q0:q0 + qw], AF.Sigmoid, [kG], [kG])
                O.stt("dve", ytmp[yb][:, 0:qw], y1[:, 0:qw], fog[:, h:h + 1], ogT[hb][:, q0:q0 + qw], ALU.mult, ALU.mult,
                      ["y1", "fog", kG], ["ytmp%d" % yb])
                r0 = (h % 2) * 64
                P.dma("sp", yT[r0:r0 + 64, 8 + h // 2, ocol0 + q0:ocol0 + q0 + qw], ytmp[yb][:, 0:qw],
                      reads=["ytmp%d" % yb], writes=["yT%d" % (8 + h // 2)], semkey="yT%d_%d" % (8 + h // 2, h % 2))
    P.barrier()
    P.release(m0)


TOKT = [(i * 128, 128) for i in range(8)] + [(1024, NS)]
TT3 = [(0, 512), (512, 512), (1024, NS)]


def layer_norm_tiles(P, O, Z, gvec, bvec, zkeys, out_dram=None):
    nc = P.nc
    m = P.mark()
    gb = P.sb("ln_g", [128, D], F32)
    bb = P.sb("ln_b", [128, D], F32)
    junk = P.sb("ln_junk", [128, D], F32)
    st = P.sb("ln_st", [128, 4], F32)
    P.dma("sp", gb[:], gvec.partition_broadcast(128), writes=["ln_g"])
    P.dma("sp", bb[:], bvec.partition_broadcast(128), writes=["ln_b"])
    for ti, (t0, tw) in enumerate(TOKT):
        z = Z[0:tw, ti, :]
        zk = zkeys[ti]
        s = st[0:tw, :]
        O.memset("pool", st[:, :], 0.0, ["ln_st"])
        P.op("act", (lambda z=z, tw=tw, s=s: nc.scalar.activation(out=junk[0:tw, :], in_=z, func=AF.Identity, accum_out=s[:, 0:1])),
             reads=[zk], writes=["ln_junk", "ln_st"])
        O.ts("dve", s[:, 1:2], s[:, 0:1], -1.0 / D, None, ALU.mult, None, ["ln_st"], ["ln_st"])
        O.act(z, z, AF.Identity, [zk, "ln_st"], [zk], bias=s[:, 1:2])
        P.op("act", (lambda z=z, tw=tw, s=s: nc.scalar.activation(out=junk[0:tw, :], in_=z, func=AF.Square, accum_out=s[:, 2:3])),
             reads=[zk, "ln_st"], writes=["ln_junk", "ln_st"])
        O.ts("dve", s[:, 3:4], s[:, 2:3], 1.0 / D, 1e-5, ALU.mult, ALU.add, ["ln_st"], ["ln_st"])
        O.rsqrt(s[:, 3:4], s[:, 3:4], ["ln_st"], ["ln_st"])
        O.stt("dve", z, z, s[:, 3:4], gb[0:tw, :], ALU.mult, ALU.mult, [zk, "ln_st", "ln_g"], [zk])
        O.tt("pool", z, z, bb[0:tw, :], ALU.add, [zk, "ln_b"], [zk])
        if out_dram is not None:
            P.dma("sp", out_dram[t0:t0 + tw, :], z, reads=[zk], semkey="yo%d" % ti, final=True)
    P.release(m)


def stage_C(P, io, cst, yT, dbg=None):
    nc = P.nc
    O = Ops(P)
    ps = io["ps"]
    m0 = P.mark()
    Z = P.sb("Z", [128, 9, D], F32)
    zk = ["Z%d" % i for i in range(9)]
    yk = ["yT%d" % i for i in range(16)]
    m1 = P.mark()
    wo = [P.sb("wo%d" % i, [128, 16, 512], BF16) for i in range(2)]
    wor = io["w_o"].rearrange("(kc p) c -> p kc c", p=128)
    for ti, (t0, tw) in enumerate(TOKT):
        P.dma("sp", Z[0:tw, ti, :], io["xo"][t0:t0 + tw, :], writes=[zk[ti]])
    pi = 0
    for cg in range(4):
        wb = cg % 2
        P.dma("pool", wo[wb][:], wor[:, :, cg * 512:cg * 512 + 512], writes=["wo%d" % wb])
        for ti, (t0, tw) in enumerate(TOKT):
            pb = pi % 4
            pi += 1
            for kc in range(16):
                O.mm(ps[pb][0:tw, :], yT[:, kc, t0:t0 + tw], wo[wb][:, kc, :], [yk[kc], "wo%d" % wb], ["ps%d" % pb],
                     start=(kc == 0), stop=(kc == 15))
            zs = Z[0:tw, ti, cg * 512:cg * 512 + 512]
            O.stt("dve", zs, zs, ALPHA, ps[pb][0:tw, :], ALU.mult, ALU.add, [zk[ti], "ps%d" % pb], [zk[ti]])
    def dump():
        for ti, (t0, tw) in enumerate(TOKT):
            P.dma("sp", io["y_out"][t0:t0 + tw, :], Z[0:tw, ti, :], reads=[zk[ti]], semkey="yo%d" % ti, final=True)
    if dbg == "z":
        dump()
        return
    layer_norm_tiles(P, O, Z, io["ln1_g"], io["ln1_b"], zk)
    if dbg == "h":
        dump()
        return
    hT = yT
    ei = 0
    for kc in range(16):
        for (grp, bank) in ((range(0, 4), 4), (range(4, 8), 5), (range(8, 9), 6)):
            for ti in grp:
                t0, tw = TOKT[ti]
                c0 = t0 - TOKT[grp[0]][0]
                O.tr(ps[bank][:, c0:c0 + tw], Z[0:tw, ti, kc * 128:kc * 128 + 128], cst["identf"][0:tw, 0:tw], [zk[ti], "c_identf"], ["ps%d" % bank])
            g0 = TOKT[grp[0]][0]
            gw = sum(TOKT[ti][1] for ti in grp)
            ei += 1
            O.cp("act" if ei % 2 else "dve", hT[:, kc, g0:g0 + gw], ps[bank][:, 0:gw], ["ps%d" % bank], [yk[kc]])
    for ti, (t0, tw) in enumerate(TOKT):
        O.ts("pool", Z[0:tw, ti, :], Z[0:tw, ti, :], ALPHA, None, ALU.mult, None, [zk[ti]], [zk[ti]])
    P.barrier()
    P.release(m1)
    m2 = P.mark()
    wu = [P.sb("wu%d" % i, [128, 16, 512], BF16) for i in range(2)]
    wd = [P.sb("wd%d" % i, [128, 4, D], BF16) for i in range(2)]
    aT = [P.sb("aT%d" % i, [128, 4, NOWN], BF16) for i in range(2)]
    rT = [P.sb("rT%d" % i, [128, 512], BF16) for i in range(3)]
    wur = io["w_up"].rearrange("(kc p) c -> p kc c", p=128)
    wdr = io["w_down"].rearrange("(g kc p) c -> g p kc c", p=128, kc=4)
    NG = DFF // 512

    def load_w(g):
        b = g % 2
        P.dma("pool", wu[b][:], wur[:, :, g * 512:g * 512 + 512], writes=["wu%d" % b])
        P.dma("pool", wd[b][:], wdr[g], writes=["wd%d" % b])
    load_w(0)
    pu = 0
    pd = 0
    ri = 0
    for g in range(NG):
        b = g % 2
        if g + 1 < NG:
            load_w(g + 1)
        for blk in range(4):
            for (t0, tw) in TT3:
                pb = pu % 3
                pu += 1
                for kc in range(16):
                    O.mm(ps[pb][:, 0:tw], wu[b][:, kc, blk * 128:blk * 128 + 128], hT[:, kc, t0:t0 + tw], ["wu%d" % b, yk[kc]], ["ps%d" % pb],
                         start=(kc == 0), stop=(kc == 15))
                rb = ri % 3
                ri += 1
                O.act(rT[rb][:, 0:tw], ps[pb][:, 0:tw], AF.Relu, ["ps%d" % pb], ["rT%d" % rb])
                O.tt("pool", aT[b][:, blk, t0:t0 + tw], rT[rb][:, 0:tw], rT[rb][:, 0:tw], ALU.mult, ["rT%d" % rb], ["aT%d" % b])
        for ti, (t0, tw) in enumerate(TOKT):
            for cg in range(4):
                pb = 3 + pd % 5
                pd += 1
                for blk in range(4):
                    O.mm(ps[pb][0:tw, :], aT[b][:, blk, t0:t0 + tw], wd[b][:, blk, cg * 512:cg * 512 + 512], ["aT%d" % b, "wd%d" % b], ["ps%d" % pb],
                         start=(blk == 0), stop=(blk == 3))
                zs = Z[0:tw, ti, cg * 512:cg * 512 + 512]
                O.tt("dve", zs, zs, ps[pb][0:tw, :], ALU.add, [zk[ti], "ps%d" % pb], [zk[ti]])
    P.barrier()
    P.release(m2)
    if dbg == "acc":
        dump()
        return
    layer_norm_tiles(P, O, Z, io["ln2_g"], io["ln2_b"], zk, out_dram=io["y_out"])
    P.release(m0)


_PROG = {}


def host_maps(inp):
    g = lambda k: np.asarray(inp[k], np.float32)[0]
    xp = np.asarray(inp["x_prompt"], np.float32)
    xs = np.asarray(inp["x_sample"], np.float32)
    w_in = g("w_in")
    wperm = np.ascontiguousarray(w_in[:, PERM])
    consts = make_consts()
    mu = g("rwkv_mu")
    cols = [mu[0:1024], mu[1024:2048], mu[2048:3072], g("rwkv_w0"), g("rwkv_a0"), g("rwkv_k_k"), g("rwkv_k_a"),
            g("rwkv_r_k").reshape(-1), g("rwkv_lnx_g"), g("rwkv_lnx_b")]
    rw_prm = np.ascontiguousarray(np.stack([v.reshape(8, 128) for v in cols], -1).transpose(1, 0, 2))
    rw_mul = np.zeros((128, 3), np.float32)
    rw_mul[:, 0] = mu[3072:3200]
    rw_mul[:, 1] = mu[3200:3328]
    rw_mul[:32, 2] = mu[3328:3360]
    shared = dict(consts)
    shared.update(
        w_in=wperm, rw_prm=rw_prm, rw_mul=rw_mul,
        rw_w2a2=np.ascontiguousarray(np.concatenate([g("rwkv_w2"), g("rwkv_a2")], 0)), rw_g2=g("rwkv_g2"),
        fx_g=np.ascontiguousarray(g("fox_out_g").reshape(16, 64).T), fx_bf=g("fox_b_f").reshape(16, 1),
        w_o=g("w_o"), ln1_g=g("ln1_g").reshape(1, D), ln1_b=g("ln1_b").reshape(1, D), w_up=g("w_up"), w_down=g("w_down"),
        ln2_g=g("ln2_g").reshape(1, D), ln2_b=g("ln2_b").reshape(1, D))
    maps = []
    pos = np.arange(NPR).reshape(32, 128).T
    for c in range(8):
        b, j = divmod(c, 4)
        n = 1024 * (j + 1)
        xpad = np.zeros((NTOK, D), np.float32)
        xpad[NPR - n:NPR] = xp[b, :n]
        xpad[NPR:] = xs[c]
        m = dict(shared)
        m["xT"] = np.ascontiguousarray(xpad.T)
        m["xo"] = np.ascontiguousarray(xpad[OWN0:NTOK])
        sh = np.zeros((N_FM, 1), np.float32)
        st = np.asarray(inp["state_rwkv_shift"], np.float32)[0, c, 0]
        for seg in ("rk", "rv", "xwxa", "r", "xg"):
            o, n_ = SEG[seg]
            sh[ROW[seg]:ROW[seg] + n_, 0] = st[o:o + n_]
        m["shiftT"] = sh
        s0 = np.asarray(inp["state_rwkv_wkv"], np.float32)[0, c]
        m["s0T"] = np.ascontiguousarray(s0.transpose(0, 2, 1).reshape(8, 128, 64).transpose(1, 0, 2))
        m["fx_padb"] = np.where(pos < NPR - n, -BIG, 0.0).astype(np.float32)
        m["fx_clfT"] = np.ascontiguousarray(np.asarray(inp["cache_fox_logf"], np.float32)[0, c].T)
        m["fx_ckT"] = np.ascontiguousarray(np.asarray(inp["cache_fox_k"], np.float32)[0, c].transpose(1, 2, 0))
        m["fx_cv"] = np.ascontiguousarray(np.asarray(inp["cache_fox_v"], np.float32)[0, c].reshape(1024, 1024))
        maps.append(m)
    return maps


def assemble(results):
    f32 = np.float32
    y_p = np.zeros((2, 4096, D), f32)
    y_s = np.zeros((8, NS, D), f32)
    pk = np.zeros((1, 2, 4096, 16, 64), f32)
    pv = np.zeros((1, 2, 4096, 16, 64), f32)
    pf = np.zeros((1, 2, 4096, 16), f32)
    pS = np.zeros((1, 2, 16, 64, 64), f32)
    psh = np.zeros((1, 2, 1, 3360), f32)
    sk = np.zeros((1, 8, NS, 16, 64), f32)
    sv = np.zeros((1, 8, NS, 16, 64), f32)
    sf = np.zeros((1, 8, NS, 16), f32)
    sS = np.zeros((1, 8, 16, 64, 64), f32)
    ssh = np.zeros((1, 8, 1, 3360), f32)

    def wkv(a):
        return a.reshape(2, 64, 8, 64).transpose(2, 0, 3, 1).reshape(16, 64, 64)

    def shift(r, ci):
        out = np.zeros(3360, f32)
        a, bq = r["shA"][:, ci], r["shB"][:, ci]
        out[1024:2048] = a[0:1024]
        out[2048:3072] = a[1024:2048]
        out[3072:3200] = a[2048:2176]
        out[0:1024] = bq[0:1024]
        out[3200:3360] = bq[1024:1184]
        return out
    for c in range(8):
        r = results[c]
        b, j = divmod(c, 4)
        sl = slice(1024 * j, 1024 * (j + 1))
        y_p[b, sl] = r["y_out"][0:1024]
        y_s[c] = r["y_out"][1024:NOWN]
        kT = r["k_out"]
        pk[0, b, sl] = kT[:, 0:1024].T.reshape(1024, 16, 64)
        sk[0, c] = kT[:, 1024:NOWN].T.reshape(NS, 16, 64)
        pv[0, b, sl] = r["v_out"][0:1024].reshape(1024, 16, 64)
        sv[0, c] = r["v_out"][1024:NOWN].reshape(NS, 16, 64)
        pf[0, b, sl] = r["logf_o"][:, 0:1024].T
        sf[0, c] = r["logf_o"][:, 1024:NOWN].T
        sS[0, c] = wkv(r["wkv_s"])
        ssh[0, c, 0] = shift(r, 1)
        if j == 3:
            pS[0, b] = wkv(r["wkv_p"])
            psh[0, b, 0] = shift(r, 0)
    return (y_p, y_s, pk, pv, pf, pS, psh, sk, sv, sf, sS, ssh)


def kernel(**inputs):
    if "full" not in _PROG:
        _PROG["full"] = build()
    P = _PROG["full"]
    maps = host_maps(inputs)
    res = run_bass_kernel_spmd(P.nc, maps, core_ids=list(range(8)))
    return assemble(res.results)
```

```python
import numpy as np
import concourse.bass as bass
import concourse.mybir as mybir
from concourse.bass_utils import run_bass_kernel_spmd

F32 = mybir.dt.float32
BF16 = mybir.dt.bfloat16
AF = mybir.ActivationFunctionType
ALU = mybir.AluOpType
AX = mybir.AxisListType

D = 2048
NPR = 4096
NS = 16
NTOK = NPR + NS
OWN0 = 3072
NOWN = 1024 + NS
DFF = 8192
C0 = float(np.exp(-0.5))
ALPHA = 2.0 ** 0.25
BIG = 30000.0

SEG = dict(rk=(1024, 1024), rv=(2048, 1024), xwxa=(3072, 128), fk=(3360 + 1024, 1024), f=(3360 + 3072, 16),
           r=(0, 1024), xg=(3200, 160), q=(3360, 1024), og=(3360 + 3088, 1024), fv=(3360 + 2048, 1024))
ORDER = ["rk", "rv", "xwxa", "fk", "f", "r", "xg", "q", "og", "fv"]
ROW = {}
_o = 0
for _k in ORDER:
    ROW[_k] = _o
    _o += SEG[_k][1]
N_ALL = ROW["r"]
N_OWN = ROW["fv"] - N_ALL
N_FM = ROW["fv"]
PERM = np.concatenate([np.arange(SEG[k][0], SEG[k][0] + SEG[k][1]) for k in ORDER])


EMBED_WAIT = True
SAME_ENG_NOWAIT = ()


class Prog:
    ENG = ("pe", "act", "dve", "pool", "sp")

    def __init__(self):
        self.nc = bass.Bass("TRN2", target_bir_lowering=False)
        nc = self.nc
        self.eng = {"pe": nc.tensor, "act": nc.scalar, "dve": nc.vector, "pool": nc.gpsimd, "sp": nc.sync}
        self.ops = []
        self.sb_base = 16640
        self.sb_top = 229376
        self.sb_off = self.sb_base
        self.nalloc = 0

    def sb(self, name, shape, dtype):
        per = int(np.prod(shape[1:])) * (2 if dtype == BF16 else 4)
        off = (self.sb_off + 63) // 64 * 64
        assert off + per <= self.sb_top, ("sbuf overflow", name, off, per)
        self.sb_off = off + per
        self.nalloc += 1
        return self.nc.alloc_sbuf_tensor_at("%s_%d" % (name, self.nalloc), list(shape), dtype, offset=off)

    def mark(self):
        return self.sb_off

    def release(self, m):
        self.sb_off = m

    def op(self, eng, fn, reads=(), writes=()):
        self.ops.append(dict(eng=eng, fn=fn, reads=tuple(reads), writes=tuple(writes), dma=False, bar=False))

    def dma(self, q, out, in_, reads=(), writes=(), semkey=None, final=False, **kw):
        if semkey is None:
            semkey = writes[0] if writes else reads[0]
        e = self.eng[q]
        self.ops.append(dict(eng=q, fn=(lambda e=e, out=out, in_=in_, kw=kw: e.dma_start(out=out, in_=in_, **kw)),
                             reads=tuple(reads), writes=tuple(writes), dma=True, semkey=semkey, final=final, bar=False))

    def barrier(self):
        for e in self.ENG:
            self.ops.append(dict(eng=e, fn=None, reads=(), writes=(), dma=False, bar=True))

    def emit(self):
        nc = self.nc
        ops = self.ops
        n = len(ops)
        lastw = {}
        readers = {}
        deps = [None] * n
        last_eng = {}
        dma_since = []
        bar_group = None
        for i, o in enumerate(ops):
            if o["bar"]:
                if bar_group is None:
                    bar_group = (set(last_eng.values()) | set(dma_since))
                deps[i] = set(bar_group)
                nxt = ops[i + 1] if i + 1 < n else None
                if nxt is None or not nxt["bar"]:
                    bar_group = None
                    dma_since = []
                    lastw = {}
                    readers = {}
                continue
            d = set()
            for k in o["reads"]:
                if k in lastw:
                    d.add(lastw[k])
            for k in o["writes"]:
                if k in lastw:
                    d.add(lastw[k])
                for r in readers.get(k, ()):
                    d.add(r)
            d.discard(i)
            deps[i] = d
            for k in o["reads"]:
                readers.setdefault(k, []).append(i)
            for k in o["writes"]:
                lastw[k] = i
                readers[k] = []
            if o["dma"]:
                dma_since.append(i)
            else:
                last_eng[o["eng"]] = i

        def need_wait(o, pj):
            if pj["dma"] or o["dma"] or o["bar"]:
                return not (o["bar"] and (not pj["dma"]) and pj["eng"] == o["eng"])
            if pj["eng"] != o["eng"]:
                return True
            if o["eng"] == "pe" or o["eng"] in SAME_ENG_NOWAIT:
                return False
            return any(k in pj["writes"] for k in o["reads"])

        needed = set()
        for i, o in enumerate(ops):
            for j in deps[i]:
                if need_wait(o, ops[j]):
                    needed.add(j)
            if o["dma"]:
                needed.add(i)
        esem = {e: nc.alloc_semaphore("c_" + e) for e in self.ENG}
        ecnt = {e: 0 for e in self.ENG}
        dpool = []
        dcnt = []
        keymap = {}
        token = [None] * n
        for i, o in enumerate(ops):
            if o["bar"]:
                keymap = {}
                continue
            if o["dma"]:
                sk = o["semkey"]
                if sk not in keymap:
                    idx = len(keymap)
                    if idx >= len(dpool):
                        dpool.append(nc.alloc_semaphore("d%d" % idx))
                        dcnt.append(0)
                    keymap[sk] = idx
                idx = keymap[sk]
                dcnt[idx] += 16
                token[i] = ("d:%d" % idx, dpool[idx], dcnt[idx])
            elif i in needed:
                e = o["eng"]
                ecnt[e] += 1
                token[i] = ("e:" + e, esem[e], ecnt[e])
        dsem = dpool
        seen = {e: {} for e in self.ENG}
        dlatest = {}
        nwaits = 0
        for i, o in enumerate(ops):
            e = o["eng"]
            eng = self.eng[e]
            want = {}
            for j in deps[i]:
                tk = token[j]
                if tk is None or not need_wait(o, ops[j]):
                    continue
                name, sem, val = tk
                if name.startswith("d:"):
                    val = dlatest[name]
                if want.get(name, (None, 0))[1] < val:
                    want[name] = (sem, val)
            todo = [(name, sem, val) for name, (sem, val) in want.items() if seen[e].get(name, 0) < val]
            attach = None
            if EMBED_WAIT and todo and not o["bar"] and not o["dma"]:
                attach = todo.pop()
            for name, sem, val in todo:
                eng.wait_ge(sem, val)
                seen[e][name] = val
                nwaits += 1
            if o["bar"]:
                continue
            ins = o["fn"]()
            if attach is not None:
                name, sem, val = attach
                ins._wait_ge(sem, val)
                seen[e][name] = val
            tk = token[i]
            if tk is not None:
                name, sem, val = tk
                if o["dma"]:
                    ins.then_inc(sem, 16)
                    dlatest[name] = val
                else:
                    ins.then_inc(sem, 1)
        for idx in range(len(dpool)):
            name = "d:%d" % idx
            val = dlatest.get(name, 0)
            if val and seen["sp"].get(name, 0) < val:
                self.eng["sp"].wait_ge(dpool[idx], val)
                seen["sp"][name] = val
        self.stats = dict(n_ops=n, n_waits=nwaits, n_dsem=len(dsem), cnt=dict(ecnt))
        return nc


TT_ALL = [(i * 512, 512) for i in range(8)] + [(NPR, NS)]
TT_OWN = [(OWN0 - 1, 1), (OWN0, 512), (OWN0 + 512, 512), (NPR, NS)]


def stage_A(P, io):
    nc = P.nc
    m0 = P.mark()
    xs = P.sb("xs", [128, 16, NTOK], BF16)
    wt = [P.sb("wt%d" % i, [128, 16, 512], BF16) for i in range(2)]
    NOT = 8
    ot = [P.sb("ot%d" % i, [128, 512], F32) for i in range(NOT)]
    vt = [P.sb("vt%d" % i, [128, 1024], F32) for i in range(2)]
    ps = io["ps"]
    xTr = io["xT"].rearrange("(kc p) t -> p kc t", p=128)
    wr = io["w_in"].rearrange("(kc p) c -> p kc c", p=128)
    PJ = io["PJ"]
    VTM = io["VTM"]
    for g in range(8):
        P.dma("pool", xs[:, 2 * g:2 * g + 2, :], xTr[:, 2 * g:2 * g + 2, :], writes=["xs%d" % g])
    st = dict(oi=0, pi=0, wi=0, vi=0)

    def evac(dst, src, wkey, pkey):
        st["pi"] += 1
        if st["pi"] % 2 == 0:
            P.op("act", lambda: nc.scalar.copy(dst, src), reads=[pkey], writes=[wkey])
        else:
            P.op("dve", lambda: nc.vector.tensor_copy(dst, src), reads=[pkey], writes=[wkey])

    def fm_chunk(c0, cw, tts):
        wb = st["wi"] % 2
        st["wi"] += 1
        P.dma("pool", wt[wb][:, :, 0:cw], wr[:, :, c0:c0 + cw], writes=["wt%d" % wb])
        for m in range(0, cw, 128):
            mw = min(128, cw - m)
            for (t0, tw) in tts:
                pb = st["pi"] % 6
                for kc in range(16):
                    P.op("pe", (lambda pb=pb, wb=wb, kc=kc, m=m, mw=mw, t0=t0, tw=tw:
                                nc.tensor.matmul(ps[pb][0:mw, 0:tw], wt[wb][:, kc, m:m + mw], xs[:, kc, t0:t0 + tw],
                                                 start=(kc == 0), stop=(kc == 15))),
                         reads=["wt%d" % wb, "xs%d" % (kc // 2)], writes=["ps%d" % pb])
                ob = st["oi"] % NOT
                st["oi"] += 1
                evac(ot[ob][0:mw, 0:tw], ps[pb][0:mw, 0:tw], "ot%d" % ob, "ps%d" % pb)
                P.dma("sp", PJ[c0 + m:c0 + m + mw, t0:t0 + tw], ot[ob][0:mw, 0:tw], reads=["ot%d" % ob],
                      semkey="oto%d" % ob, **({"allow_slow_non_contiguous": True} if tw == 1 else {}))

    for c0 in range(0, N_ALL, 512):
        fm_chunk(c0, min(512, N_ALL - c0), TT_ALL)
    for c0 in range(N_ALL, N_FM, 512):
        fm_chunk(c0, min(512, N_FM - c0), TT_OWN)
    wbs = []
    for h in range(2):
        wb = st["wi"] % 2
        st["wi"] += 1
        P.dma("pool", wt[wb][:, :, :], wr[:, :, N_FM + 512 * h:N_FM + 512 * h + 512], writes=["wt%d" % wb])
        wbs.append(wb)
    tb = [(i * 128, 128) for i in range(32)] + [(NPR, NS)]
    for (t0, tw) in tb:
        vb = st["vi"] % 2
        st["vi"] += 1
        for h in range(2):
            pb = st["pi"] % 6
            wb = wbs[h]
            for kc in range(16):
                P.op("pe", (lambda pb=pb, wb=wb, kc=kc, t0=t0, tw=tw:
                            nc.tensor.matmul(ps[pb][0:tw, 0:512], xs[:, kc, t0:t0 + tw], wt[wb][:, kc, :],
                                             start=(kc == 0), stop=(kc == 15))),
                     reads=["wt%d" % wb, "xs%d" % (kc // 2)], writes=["ps%d" % pb])
            evac(vt[vb][0:tw, 512 * h:512 * h + 512], ps[pb][0:tw, 0:512], "vt%d" % vb, "ps%d" % pb)
        P.dma("sp", VTM[t0:t0 + tw, :], vt[vb][0:tw, :], reads=["vt%d" % vb], semkey="vto%d" % vb)
    P.barrier()
    P.release(m0)


class Ops:
    def __init__(self, P):
        self.P = P
        self.nc = P.nc
        self.rr = 0

    def E(self, which):
        return {"dve": self.nc.vector, "pool": self.nc.gpsimd}[which]

    def pick(self, choices=("dve", "pool")):
        self.rr += 1
        return choices[self.rr % len(choices)]

    def tt(self, eng, out, a, b, op, r, w):
        e = self.E(eng)
        self.P.op(eng, lambda: e.tensor_tensor(out, a, b, op), reads=r, writes=w)

    def ts(self, eng, out, a, s1, s2, op0, op1, r, w):
        e = self.E(eng)
        if s2 is None:
            self.P.op(eng, lambda: e.tensor_scalar(out, a, s1, None, op0), reads=r, writes=w)
        else:
            self.P.op(eng, lambda: e.tensor_scalar(out, a, s1, s2, op0, op1), reads=r, writes=w)

    def stt(self, eng, out, a, sc, b, op0, op1, r, w):
        e = self.E(eng)
        self.P.op(eng, lambda: e.scalar_tensor_tensor(out, a, sc, b, op0, op1), reads=r, writes=w)

    def cp(self, eng, out, a, r, w):
        if eng == "act":
            self.P.op("act", lambda: self.nc.scalar.copy(out, a), reads=r, writes=w)
        else:
            e = self.E(eng)
            self.P.op(eng, lambda: e.tensor_copy(out, a), reads=r, writes=w)

    def act(self, out, a, func, r, w, bias=None, scale=None):
        kw = {}
        if bias is not None:
            kw["bias"] = bias
        if scale is not None:
            kw["scale"] = scale
        self.P.op("act", lambda: self.nc.scalar.activation(out=out, in_=a, func=func, **kw), reads=r, writes=w)

    def mm(self, out, lhsT, rhs, r, w, start=True, stop=True):
        self.P.op("pe", lambda: self.nc.tensor.matmul(out, lhsT, rhs, start=start, stop=stop), reads=r, writes=w)

    def tr(self, out, a, ident, r, w):
        self.P.op("pe", lambda: self.nc.tensor.transpose(out, a, ident), reads=r, writes=w)

    def memset(self, eng, out, val, w):
        e = self.E(eng)
        self.P.op(eng, lambda: e.memset(out, val), writes=w)

    def recip(self, out, a, r, w):
        self.P.op("dve", lambda: self.nc.vector.reciprocal(out, a), reads=r, writes=w)

    def rsqrt(self, out, a, r, w):
        self.act(out, a, AF.Sqrt, r, w)
        self.recip(out, out, w, w)


def load_consts(P, io):
    nc = P.nc
    c = {}
    c["identb"] = P.sb("identb", [128, 128], BF16)
    c["identf"] = P.sb("identf", [128, 128], F32)
    c["b64"] = P.sb("b64", [128, 128], F32)
    c["b64m"] = P.sb("b64m", [128, 128], F32)
    c["mskL"] = P.sb("mskL", [64, 64], F32)
    c["mskU"] = P.sb("mskU", [64, 128], F32)
    c["i2"] = P.sb("i2", [128, 64], F32)
    c["maskR"] = P.sb("maskR", [128, 512], F32)
    c["tri"] = P.sb("tri", [128, 128], BF16)
    c["zcol"] = P.sb("zcol", [128, 1], F32)
    P.dma("pool", c["identb"][:], io["c_ident"], writes=["c_identb"])
    P.dma("sp", c["identf"][:], io["c_ident"], writes=["c_identf"])
    P.dma("sp", c["b64"][:], io["c_b64"], writes=["c_b64"])
    P.dma("sp", c["b64m"][:], io["c_b64m"], writes=["c_b64m"])
    P.dma("sp", c["mskL"][:], io["c_mskL"], writes=["c_mskL"])
    P.dma("sp", c["mskU"][:], io["c_mskU"], writes=["c_mskU"])
    P.dma("sp", c["i2"][:], io["c_i2"], writes=["c_i2"])
    P.dma("sp", c["maskR"][:], io["c_maskR"], writes=["c_maskR"])
    P.dma("pool", c["tri"][:], io["c_tri"], writes=["c_tri"])
    P.op("pool", lambda: nc.gpsimd.memset(c["zcol"][:], 0.0), writes=["c_zcol"])
    c["keys"] = ["c_identb", "c_identf", "c_b64", "c_b64m", "c_mskL", "c_mskU", "c_i2", "c_maskR", "c_tri", "c_zcol"]
    return c


B1_RATIO = 6
PRM = dict(mu_r=0, mu_k=1, mu_v=2, w0=3, a0=4, k_k=5, k_a=6, r_k=7, lnx_g=8, lnx_b=9, omk=10)


def stage_B1(P, io, cst, yT, pairs=range(8), tiles=range(9), lvl=99):
    nc = P.nc
    O = Ops(P)
    PJ = io["PJ"]
    ps = io["ps"]
    m0 = P.mark()
    CK = cst["keys"]
    prm = P.sb("prm", [128, 8, 11], F32)
    P.dma("sp", prm[:, :, 0:10], io["rw_prm"], writes=["prm"])
    O.ts("dve", prm[:, :, 10:11], prm[:, :, 6:7], -1.0, 1.0, ALU.mult, ALU.add, ["prm"], ["prm"])
    mul = P.sb("mul", [128, 3], F32)
    P.dma("sp", mul[:], io["rw_mul"], writes=["mul"])
    w2a2 = P.sb("w2a2", [128, 1024], BF16)
    g2a = P.sb("g2a", [128, 1024], BF16)
    g2b = P.sb("g2b", [32, 1024], BF16)
    P.dma("pool", w2a2[:], io["rw_w2a2"], writes=["w2a2"])
    P.dma("pool", g2a[:], io["rw_g2"][0:128, :], writes=["g2a"])
    P.dma("pool", g2b[:], io["rw_g2"][128:160, :], writes=["g2b"])
    Hst = P.sb("Hst", [128, 8, 64], F32)
    O.memset("pool", Hst[:], 0.0, ["H%d" % p for p in range(8)])

    def S(name, shape, dt, nb=2):
        return [P.sb(name + str(i), shape, dt) for i in range(nb)]
    xwc = S("xwc", [128, 512], F32); xwp = S("xwp", [128, 512], F32, 1) * 2
    xgc = S("xgc", [128, 512], F32); xgp = S("xgp", [128, 512], F32, 1) * 2
    xhc = S("xhc", [32, 512], F32, 1) * 2; xhp = S("xhp", [32, 512], F32, 1) * 2
    txa = S("txa", [128, 512], BF16); sga = S("sga", [128, 512], BF16); sgb = S("sgb", [32, 512], BF16)
    names32 = ["kc", "kp", "vc", "vp", "rc", "rp", "ks", "vs", "rs", "sw", "al", "gg", "kk", "t1", "kkn", "bb", "kh",
               "cum", "cumx", "e_r", "e_a", "e_n", "e_c", "bon"]
    U = {n: S(n, [128, 512], F32, 2 if n in ("kc", "kp", "vc", "vp", "bon", "gg") else 1) for n in names32}
    gC = S("gC", [128, 8], F32)
    QQ = S("QQ", [128, 8, 2, 64], BF16); KK = S("KK", [128, 8, 2, 64], BF16)
    BH = S("BH", [128, 8, 64], BF16); KH = S("KH", [128, 8, 64], BF16); VV = S("VV", [128, 8, 64], BF16)
    TMa = S("TMa", [64, 8, 128], BF16, 1) * 2; TMb = S("TMb", [64, 8, 128], BF16, 1) * 2
    TMk = S("TMk", [64, 8, 128], BF16, 1) * 2; TMv = S("TMv", [64, 8, 128], BF16, 1) * 2
    Lb = S("Lb", [64, 8, 64], BF16, 4); LTb = S("LTb", [64, 8, 64], BF16, 4)
    GB = S("GB", [64, 8, 128], BF16); GK = S("GK", [64, 8, 128], BF16)
    Zf = S("Zf", [64, 8, 128], F32); Zb = S("Zb", [64, 8, 128], BF16)
    MT = S("MT", [128, 8, 64], F32, 1) * 2; NN = S("NN", [128, 8, 64], F32, 1) * 2
    DG = S("DG", [128, 8, 64], F32, 1) * 2
    Hb = S("Hb", [128, 8, 64], BF16)
    QeT = S("QeT", [128, 8, 64], BF16)
    ysb = S("ysb", [128, 512], F32, 1) * 2; yc_ = S("yc", [128, 512], F32, 1) * 2; sq_ = S("sq", [128, 512], F32, 1) * 2
    rstd = S("rstd", [128, 512], F32, 1) * 2

    def v3(ap, n):
        return ap.rearrange("p (a b) -> p a b", a=n)

    def psv(bank, rows, n, w):
        return ps[bank][rows, 0:n * w].rearrange("p (a b) -> p a b", a=n)

    def psb(bank, rows, n, w):
        return ps[bank][:, :].bitcast(BF16)[rows, 0:n * w].rearrange("p (a b) -> p a b", a=n)

    def unit_gen(ct, p, u, first):
        samp = (ct == 8)
        t0 = NPR if samp else 512 * ct
        TW = 64 if samp else 512
        RW_ = NS if samp else 512
        NCH = TW // 64
        do_out = ct >= 6
        lb = ct % 2

        def load_pair(cur, prv, r0, rows, kc_, kp_, q="sp"):
            if samp:
                O.memset("pool", cur[0:rows, 0:TW], 0.0, [kc_])
                O.memset("pool", prv[0:rows, 0:TW], 0.0, [kp_])
                P.dma(q, cur[0:rows, 0:NS], PJ[r0:r0 + rows, NPR:NPR + NS], writes=[kc_])
                P.dma(q, prv[0:rows, 1:NS], PJ[r0:r0 + rows, NPR:NPR + NS - 1], writes=[kp_])
                P.dma(q, prv[0:rows, 0:1], io["shiftT"][r0:r0 + rows, :], writes=[kp_])
            else:
                P.dma(q, cur[0:rows, 0:TW], PJ[r0:r0 + rows, t0:t0 + TW], writes=[kc_])
                if t0 == 0:
                    O.memset("pool", prv[0:rows, 0:1], 0.0, [kp_])
                    P.dma(q, prv[0:rows, 1:TW], PJ[r0:r0 + rows, 0:TW - 1], writes=[kp_])
                else:
                    P.dma(q, prv[0:rows, 0:TW], PJ[r0:r0 + rows, t0 - 1:t0 + TW - 1], writes=[kp_])

        def shift(eng, out, cur, prv, mu, r, w, rows=128):
            O.tt(eng, prv[0:rows, 0:TW], prv[0:rows, 0:TW], cur[0:rows, 0:TW], ALU.subtract, r, [r[1]])
            if eng == "dve":
                O.stt(eng, out[0:rows, 0:TW], prv[0:rows, 0:TW], mu, cur[0:rows, 0:TW], ALU.mult, ALU.add, r, w)
            else:
                O.ts(eng, prv[0:rows, 0:TW], prv[0:rows, 0:TW], mu, None, ALU.mult, None, r, [r[1]])
                O.tt(eng, out[0:rows, 0:TW], prv[0:rows, 0:TW], cur[0:rows, 0:TW], ALU.add, r, w)

        kx = "L%d" % lb
        if first:
            load_pair(xwc[lb], xwp[lb], ROW["xwxa"], 128, kx + "xwc", "Lxwp")
            shift("dve", xwc[lb], xwc[lb], xwp[lb], mul[:, 0:1], [kx + "xwc", "Lxwp", "mul"], [kx + "xwc"])
            O.act(txa[lb][0:64, 0:TW], xwc[lb][0:64, 0:TW], AF.Tanh, [kx + "xwc"], [kx + "txa"])
            O.cp("pool", txa[lb][64:128, 0:TW], xwc[lb][64:128, 0:TW], [kx + "xwc"], [kx + "txa"])
            if do_out:
                load_pair(xgc[lb], xgp[lb], ROW["xg"], 128, kx + "xgc", "Lxgp")
                shift("pool", xgc[lb], xgc[lb], xgp[lb], mul[:, 1:2], [kx + "xgc", "Lxgp", "mul"], [kx + "xgc"])
                O.act(sga[lb][:, 0:TW], xgc[lb][:, 0:TW], AF.Sigmoid, [kx + "xgc"], [kx + "sga"])
                r0 = ROW["xg"] + 128
                if samp:
                    O.memset("pool", xhc[lb][:, 0:TW], 0.0, ["Lxhc"])
                    O.memset("pool", xhp[lb][:, 0:TW], 0.0, ["Lxhp"])
                    P.dma("sp", xhc[lb][:, 0:NS], PJ[r0:r0 + 32, NPR:NPR + NS], writes=["Lxhc"])
                    P.dma("sp", xhp[lb][:, 1:NS], PJ[r0:r0 + 32, NPR:NPR + NS - 1], writes=["Lxhp"])
                    P.dma("sp", xhp[lb][:, 0:1], io["shiftT"][r0:r0 + 32, :], writes=["Lxhp"])
                else:
                    P.dma("sp", xhc[lb][:, 0:TW], PJ[r0:r0 + 32, t0:t0 + TW], writes=["Lxhc"])
                    P.dma("sp", xhp[lb][:, 0:TW], PJ[r0:r0 + 32, t0 - 1:t0 + TW - 1], writes=["Lxhp"])
                shift("pool", xhc[lb], xhc[lb], xhp[lb], mul[0:32, 2:3], ["Lxhc", "Lxhp", "mul"], ["Lxhc"], rows=32)
                O.act(sgb[lb][:, 0:TW], xhc[lb][:, 0:TW], AF.Sigmoid, ["Lxhc"], [kx + "sgb"])

            yield "d"
        SINGLE = ("xhc", "xhp", "MT", "NN", "DG", "ysb", "yc", "sq", "rstd", "TMa", "TMb", "TMk", "TMv")
        K = (lambda n, u=u: "u%d_%s" % (0 if (n in SINGLE or (n in U and len(U[n]) == 1)) else u, n))
        T = (lambda n, u=u: U[n][u % len(U[n])])
        pc = lambda j: prm[:, p, j:j + 1]
        cols = slice(128 * p, 128 * p + 128)
        W_ = slice(0, TW)
        load_pair(T("kc"), T("kp"), ROW["rk"] + 128 * p, 128, K("kc"), K("kp"))
        load_pair(T("vc"), T("vp"), ROW["rv"] + 128 * p, 128, K("vc"), K("vp"))
        shift("dve", T("ks"), T("kc"), T("kp"), pc(1), [K("kc"), K("kp"), "prm"], [K("ks")])
        shift("pool", T("vs"), T("vc"), T("vp"), pc(2), [K("vc"), K("vp"), "prm"], [K("vs")])
        yield "d"
        if do_out:
            load_pair(T("rc"), T("rp"), ROW["r"] + 128 * p, 128, K("rc"), K("rp"))
            shift("pool", T("rs"), T("rc"), T("rp"), pc(0), [K("rc"), K("rp"), "prm"], [K("rs")])
        O.mm(ps[0][:, W_], w2a2[0:64, cols], txa[lb][0:64, W_], ["w2a2", kx + "txa"], ["ps0"])
        O.mm(ps[1][:, W_], w2a2[64:128, cols], txa[lb][64:128, W_], ["w2a2", kx + "txa"], ["ps1"])
        O.act(T("sw")[:, W_], ps[0][:, W_], AF.Sigmoid, ["ps0", "prm"], [K("sw")], bias=pc(3))
        O.act(T("al")[:, W_], ps[1][:, W_], AF.Sigmoid, ["ps1", "prm"], [K("al")], bias=pc(4))
        yield "d"
        if samp:
            O.memset("pool", T("sw")[:, NS:TW], 0.0, [K("sw")])
        if do_out:
            O.mm(ps[0][:, W_], g2a[:, cols], sga[lb][:, W_], ["g2a", kx + "sga"], ["ps0"], start=True, stop=False)
            O.mm(ps[0][:, W_], g2b[0:32, cols], sgb[lb][0:32, W_], ["g2b", kx + "sgb"], ["ps0"], start=False, stop=True)
            O.cp("act", T("gg")[:, W_], ps[0][:, W_], ["ps0"], [K("gg")])
        O.ts("pool", T("kk")[:, W_], T("ks")[:, W_], pc(5), None, ALU.mult, None, [K("ks"), "prm"], [K("kk")])
        O.tt("pool", T("t1")[:, W_], T("kk")[:, W_], T("kk")[:, W_], ALU.mult, [K("kk")], [K("t1")])
        O.mm(ps[1][:, W_], cst["b64"][:, :], T("t1")[:, W_], ["c_b64", K("t1")], ["ps1"])
        O.ts("dve", T("t1")[:, W_], ps[1][:, W_], 1e-24, None, ALU.max, None, ["ps1"], [K("t1")])
        O.rsqrt(T("t1")[:, W_], T("t1")[:, W_], [K("t1")], [K("t1")])
        yield "d"
        O.tt("pool", T("kkn")[:, W_], T("kk")[:, W_], T("t1")[:, W_], ALU.mult, [K("kk"), K("t1")], [K("kkn")])
        O.tt("dve", T("bb")[:, W_], T("kkn")[:, W_], T("al")[:, W_], ALU.mult, [K("kkn"), K("al")], [K("bb")])
        O.ts("dve", T("t1")[:, W_], T("al")[:, W_], pc(6), pc(10), ALU.mult, ALU.add, [K("al"), "prm", K("t1")], [K("t1")])
        O.tt("pool", T("kh")[:, W_], T("ks")[:, W_], T("t1")[:, W_], ALU.mult, [K("ks"), K("t1")], [K("kh")])
        yield "d"
        sc_o, sc_m, sc_i, sc_z = T("cum")[:, W_], cst["maskR"][:, W_], T("sw")[:, W_], cst["zcol"][:, 0:1]
        P.op("dve", (lambda sc_o=sc_o, sc_m=sc_m, sc_i=sc_i, sc_z=sc_z:
                     nc.vector.tensor_tensor_scan(sc_o, sc_m, sc_i, sc_z, ALU.mult, ALU.add)),
             reads=[K("sw")], writes=[K("cum")])
        O.tt("pool", T("cumx")[:, W_], T("cum")[:, W_], T("sw")[:, W_], ALU.subtract, [K("cum"), K("sw")], [K("cumx")])
        O.act(T("e_r")[:, W_], T("cum")[:, W_], AF.Exp, [K("cum")], [K("e_r")], scale=-C0)
        O.act(T("e_a")[:, W_], T("cumx")[:, W_], AF.Exp, [K("cumx")], [K("e_a")], scale=-C0)
        O.recip(T("e_n")[:, W_], T("e_r")[:, W_], [K("e_r")], [K("e_n")])
        yield "d"
        cv = v3(T("cum")[:, W_], NCH)
        O.tt("dve", v3(T("cumx")[:, W_], NCH), cv[:, :, 63:64].to_broadcast([128, NCH, 64]), cv, ALU.subtract,
             [K("cum"), K("e_a")], [K("cumx")])
        O.act(T("e_c")[:, W_], T("cumx")[:, W_], AF.Exp, [K("cumx")], [K("e_c")], scale=-C0)
        O.cp("pool", gC[u][:, 0:NCH], v3(T("e_r")[:, W_], NCH)[:, :, 63], [K("e_r")], [K("gC")])
        yield "d"
        qq = QQ[u]; kq = KK[u]
        O.stt("dve", qq[:, 0:NCH, 0, :], v3(T("kkn")[:, W_], NCH), -1.0, v3(T("e_a")[:, W_], NCH), ALU.mult, ALU.mult,
              [K("kkn"), K("e_a")], [K("QQ")])
        if do_out:
            O.tt("pool", qq[:, 0:NCH, 1, :], v3(T("rs")[:, W_], NCH), v3(T("e_r")[:, W_], NCH), ALU.mult,
                 [K("rs"), K("e_r")], [K("QQ")])
        O.tt("dve", kq[:, 0:NCH, 0, :], v3(T("bb")[:, W_], NCH), v3(T("e_n")[:, W_], NCH), ALU.mult, [K("bb"), K("e_n")], [K("KK")])
        O.tt("pool", kq[:, 0:NCH, 1, :], v3(T("kh")[:, W_], NCH), v3(T("e_n")[:, W_], NCH), ALU.mult, [K("kh"), K("e_n")], [K("KK")])
        O.tt("dve", BH[u][:, 0:NCH, :], v3(T("bb")[:, W_], NCH), v3(T("e_c")[:, W_], NCH), ALU.mult, [K("bb"), K("e_c")], [K("BH")])
        O.tt("pool", KH[u][:, 0:NCH, :], v3(T("kh")[:, W_], NCH), v3(T("e_c")[:, W_], NCH), ALU.mult, [K("kh"), K("e_c")], [K("KH")])
        yield "d"
        O.cp("act", VV[u][:, 0:NCH, :], v3(T("vs")[:, W_], NCH), [K("vs")], [K("VV")])
        if do_out:
            O.stt("dve", T("bon")[:, W_], T("rs")[:, W_], pc(7), T("kh")[:, W_], ALU.mult, ALU.mult, [K("rs"), K("kh"), "prm"], [K("bon")])
            O.mm(ps[0][:, W_], cst["b64"][:, :], T("bon")[:, W_], ["c_b64", K("bon")], ["ps0"])
            O.tt("dve", T("bon")[:, W_], ps[0][:, W_], T("vs")[:, W_], ALU.mult, ["ps0", K("vs"), K("bon")], [K("bon")])
        yield "DONE_D"
        if lvl < 1:
            return
        O.tt("pool", DG[u][:, 0:NCH, :], cst["i2"][:, :].unsqueeze(1).to_broadcast([128, NCH, 64]),
             gC[u][:, 0:NCH].unsqueeze(2).to_broadcast([128, NCH, 64]), ALU.mult, ["c_i2", K("gC")], [K("DG")])
        for (src, skey, dst, dkey, bank) in ((qq, K("QQ"), TMa[u], K("TMa"), 3), (BH[u], K("BH"), TMb[u], K("TMb"), 4),
                                             (KH[u], K("KH"), TMk[u], K("TMk"), 3), (VV[u], K("VV"), TMv[u], K("TMv"), 4)):
            pv = psb(bank, slice(0, 64), NCH, 128)
            for i in range(NCH):
                s_ap = src[:, i, 0, :] if src is qq else src[:, i, :]
                O.tr(pv[:, i, :], s_ap, cst["identb"][:, :], [skey, "c_identb"], ["ps%d" % bank])
            O.cp(O.pick(("act", "dve")), dst[:, 0:NCH, :], pv, ["ps%d" % bank], [dkey])
            yield "c"
        if lvl < 2:
            return
        HS = []
        NQ = 128 if do_out else 64
        for h in range(2):
            R = slice(64 * h, 64 * h + 64)
            Zf_, Zb_ = Zf[h], Zb[h]
            kz = "Z%d" % h
            GB_, GK_ = GB[h], GK[h]
            kg = "G%d" % h
            gab = psv(3, slice(0, 64), NCH, 64)
            for i in range(NCH):
                O.mm(gab[:, i, :], qq[R, i, 0, :], kq[R, i, 0, :], [K("QQ"), K("KK")], ["ps3"])
            for i in range(NCH):
                bk = 4 + i // 4
                o_ = ps[bk][0:64, (i % 4) * 128:(i % 4) * 128 + NQ]
                O.mm(o_, kq[R, i, 0, :], qq[R, i, :, :].rearrange("p a b -> p (a b)")[:, 0:NQ], [K("QQ"), K("KK")], ["ps%d" % bk])
            for i in range(NCH):
                bk = 6 + i // 4
                o_ = ps[bk][0:64, (i % 4) * 128:(i % 4) * 128 + NQ]
                O.mm(o_, kq[R, i, 1, :], qq[R, i, :, :].rearrange("p a b -> p (a b)")[:, 0:NQ], [K("QQ"), K("KK")], ["ps%d" % bk])
            li = 2 * h
            O.tt("dve", Lb[li][:, 0:NCH, :], gab, cst["mskL"][:, :].unsqueeze(1).to_broadcast([64, NCH, 64]), ALU.mult,
                 ["ps3", "c_mskL"], ["Lb%d" % li])
            yield "c"
            for (bank0, G_, kname) in ((4, GB_, kg + "B"), (6, GK_, kg + "K")):
                for hb in range((NCH + 3) // 4):
                    n4 = min(4, NCH - 4 * hb)
                    src = ps[bank0 + hb][0:64, 0:n4 * 128].rearrange("p (a b) -> p a b", a=n4)[:, :, 0:NQ]
                    O.tt("dve", G_[:, 4 * hb:4 * hb + n4, 0:NQ], src,
                         cst["mskU"][:, 0:NQ].unsqueeze(1).to_broadcast([64, n4, NQ]), ALU.mult,
                         ["ps%d" % (bank0 + hb), "c_mskU"], [kname])
            O.cp("pool", LTb[li][:, 0:NCH, :], GB_[:, 0:NCH, 0:64], [kg + "B"], ["LTb%d" % li])
            yield "c"
            if lvl < 3:
                continue
            xp = psv(3, slice(0, 64), NCH, 64)
            for i in range(NCH):
                O.mm(xp[:, i, :], GK_[:, i, 0:64], TMv[u][:, i, R], [kg + "K", K("TMv")], ["ps3"])
            O.cp("act", Zf_[:, 0:NCH, 0:64], TMa[u][:, 0:NCH, R], [K("TMa")], [kz + "f"])
            O.cp("dve", Zf_[:, 0:NCH, 64:128], xp, ["ps3"], [kz + "f"])
            yield "c"
            O.cp("pool", Zb_[:, 0:NCH, :], Zf_[:, 0:NCH, :], [kz + "f"], [kz + "b"])
            HS.append(dict(h=h, R=R, Zf=Zf_, Zb=Zb_, kz=kz, GB=GB_, GK=GK_, kg=kg, li=li, cur=li, zb0=2 + 2 * h))
        for j in range(6):
            for hd in HS:
                cur, li, Zb_, Zf_, kz, zb0 = hd["cur"], hd["li"], hd["Zb"], hd["Zf"], hd["kz"], hd["zb0"]
                for i in range(NCH):
                    bk = zb0 + i // 4
                    o_ = ps[bk][0:64, (i % 4) * 128:(i % 4) * 128 + 128]
                    O.mm(o_, LTb[cur][:, i, :], Zb_[:, i, :], ["LTb%d" % cur, kz + "b"], ["ps%d" % bk])
                yield "c"
                if j < 5:
                    nxt = li + 1 if cur == li else li
                    pn = psv(6, slice(0, 64), NCH, 64)
                    ptn = psv(7, slice(0, 64), NCH, 64)
                    for i in range(NCH):
                        O.mm(pn[:, i, :], LTb[cur][:, i, :], Lb[cur][:, i, :], ["LTb%d" % cur, "Lb%d" % cur], ["ps6"])
                    for i in range(NCH):
                        O.mm(ptn[:, i, :], Lb[cur][:, i, :], LTb[cur][:, i, :], ["LTb%d" % cur, "Lb%d" % cur], ["ps7"])
                    yield "c"
                for hb in range((NCH + 3) // 4):
                    n4 = min(4, NCH - 4 * hb)
                    src = ps[zb0 + hb][0:64, 0:n4 * 128].rearrange("p (a b) -> p a b", a=n4)
                    O.tt("dve", Zb_[:, 4 * hb:4 * hb + n4, :], Zf_[:, 4 * hb:4 * hb + n4, :], src, ALU.add,
                         ["ps%d" % (zb0 + hb), kz + "f"], [kz + "b"])
                if j < 5:
                    for hb in range((NCH + 3) // 4):
                        n4 = min(4, NCH - 4 * hb)
                        src = ps[zb0 + hb][0:64, 0:n4 * 128].rearrange("p (a b) -> p a b", a=n4)
                        O.tt("dve", Zf_[:, 4 * hb:4 * hb + n4, :], Zf_[:, 4 * hb:4 * hb + n4, :], src, ALU.add,
                             ["ps%d" % (zb0 + hb), kz + "f"], [kz + "f"])
                    O.cp("act", Lb[nxt][:, 0:NCH, :], pn, ["ps6"], ["Lb%d" % nxt])
                    O.cp("act", LTb[nxt][:, 0:NCH, :], ptn, ["ps7"], ["LTb%d" % nxt])
                    hd["cur"] = nxt
                yield "c"
        if lvl < 4:
            return
        for hd in HS:
            h, R, Zb_, kz, GB_, kg = hd["h"], hd["R"], hd["Zb"], hd["kz"], hd["GB"], hd["kg"]
            mtp = psv(3, R, NCH, 64)
            npp = psv(6, R, NCH, 64)
            for i in range(NCH):
                O.mm(mtp[:, i, :], Zb_[:, i, 0:64], TMb[u][:, i, R], [kz + "b", K("TMb")], ["ps3"])
            for i in range(NCH):
                O.mm(npp[:, i, :], TMb[u][:, i, R], Zb_[:, i, 64:128], [kz + "b", K("TMb")], ["ps6"], start=True, stop=False)
                O.mm(npp[:, i, :], TMk[u][:, i, R], TMv[u][:, i, R], [K("TMk"), K("TMv")], ["ps6"], start=False, stop=True)
            if do_out:
                qep = psv(7, R, NCH, 64)
                for i in range(NCH):
                    O.mm(qep[:, i, :], Zb_[:, i, 0:64], GB_[:, i, 64:128], [kz + "b", kg + "B"], ["ps7"])
            O.tt("dve", MT[u][R, 0:NCH, :], mtp, DG[u][R, 0:NCH, :], ALU.add, ["ps3", K("DG")], [K("MT")])
            O.cp("act", NN[u][R, 0:NCH, :], npp, ["ps6"], [K("NN")])
            yield "c"
            if do_out:
                O.tt("dve", QeT[u][R, 0:NCH, :], qep, qq[R, 0:NCH, 1, :], ALU.add, ["ps7", K("QQ")], [K("QeT")])
        if lvl < 5:
            return
        Hcur = Hst[:, p, :]
        if samp:
            P.dma("sp", Hst[:, p, :], io["s0T"][:, p, :], reads=[], writes=["H%d" % p])
        hkey = "H%d" % p
        stp = psv(3, slice(0, 128), NCH, 64)
        for i in range(NCH):
            if do_out:
                O.cp("act", Hb[u][:, i, :], Hcur, [hkey], [K("Hb")])
            for h in range(2):
                R = slice(64 * h, 64 * h + 64)
                O.mm(stp[R, i, :], MT[u][R, i, :], Hst[R, p, :], [K("MT"), hkey], ["ps3"])
            O.tt("dve", Hcur, stp[:, i, :], NN[u][:, i, :], ALU.add, ["ps3", K("NN"), hkey], [hkey])
            yield "c"
        if samp:
            P.dma("sp", io["wkv_s"][:, p, :], Hst[:, p, :], reads=[hkey], semkey="wkvs%d" % p, final=True)
        elif ct == 7:
            P.dma("sp", io["wkv_p"][:, p, :], Hst[:, p, :], reads=[hkey], semkey="wkvp%d" % p, final=True)
        if lvl < 6:
            return
        if do_out:
            ytp = psv(7, slice(0, 128), NCH, 64)
            yt1 = psv(4, slice(0, 128), NCH, 64)
            for h in range(2):
                R = slice(64 * h, 64 * h + 64)
                kz = "Z%d" % h
                kg = "G%d" % h
                for i in range(NCH):
                    O.mm(yt1[R, i, :], Hb[u][R, i, :], QeT[u][R, i, :], [K("Hb"), K("QeT")], ["ps4"])
                for i in range(NCH):
                    O.mm(ytp[R, i, :], Zb[h][:, i, 64:128], GB[h][:, i, 64:128], [kz + "b", kg + "B"], ["ps7"], start=True, stop=False)
                    O.mm(ytp[R, i, :], TMv[u][:, i, R], GK[h][:, i, 64:128], [K("TMv"), kg + "K"], ["ps7"], start=False, stop=True)
                    yield "c"
            y_ = ysb[u]
            O.cp("act", y_[:, W_], ps[4][:, W_], ["ps4"], [K("ysb")])
            O.tt("dve", y_[:, W_], y_[:, W_], ps[7][:, W_], ALU.add, ["ps7", K("ysb")], [K("ysb")])
            O.mm(ps[3][:, W_], cst["b64m"][:, :], y_[:, W_], ["c_b64m", K("ysb")], ["ps3"])
            O.tt("dve", yc_[u][:, W_], y_[:, W_], ps[3][:, W_], ALU.subtract, ["ps3", K("ysb")], [K("yc")])
            O.tt("pool", sq_[u][:, W_], yc_[u][:, W_], yc_[u][:, W_], ALU.mult, [K("yc")], [K("sq")])
            O.mm(ps[3][:, W_], cst["b64m"][:, :], sq_[u][:, W_], ["c_b64m", K("sq")], ["ps3"])
            O.ts("dve", rstd[u][:, W_], ps[3][:, W_], 64e-5, None, ALU.add, None, ["ps3"], [K("rstd")])
            O.rsqrt(rstd[u][:, W_], rstd[u][:, W_], [K("rstd")], [K("rstd")])
            yield "c"
            O.tt("pool", yc_[u][:, W_], yc_[u][:, W_], rstd[u][:, W_], ALU.mult, [K("yc"), K("rstd")], [K("yc")])
            O.ts("dve", yc_[u][:, W_], yc_[u][:, W_], pc(8), pc(9), ALU.mult, ALU.add, [K("yc"), "prm"], [K("yc")])
            O.tt("pool", yc_[u][:, W_], yc_[u][:, W_], T("bon")[:, W_], ALU.add, [K("yc"), K("bon")], [K("yc")])
            oc0 = 1024 if samp else (ct - 6) * 512
            O.tt("dve", yT[:, p, oc0:oc0 + RW_], yc_[u][:, 0:RW_], T("gg")[:, 0:RW_], ALU.mult, [K("yc"), K("gg")], ["yT%d" % p])

    units = []
    cnt_u = 0
    for ct in tiles:
        for pi_, p in enumerate(pairs):
            units.append(unit_gen(ct, p, cnt_u % 2, pi_ == 0))
            cnt_u += 1
    prev = None
    for g in units:
        d_done = False
        c_done = prev is None
        while not (d_done and c_done):
            if not d_done:
                try:
                    if next(g) == "DONE_D":
                        d_done = True
                except StopIteration:
                    d_done = True
                    g = None
            for _rep in range(B1_RATIO):
                if not c_done:
                    try:
                        next(prev)
                    except StopIteration:
                        c_done = True
        prev = g
    if prev is not None:
        for _ in prev:
            pass
    P.barrier()
    P.release(m0)


IN_SPECS = dict(
    xT=[D, NTOK], w_in=[D, 7472], shiftT=[N_FM, 1], s0T=[128, 8, 64],
    rw_prm=[128, 8, 10], rw_mul=[128, 3], rw_w2a2=[128, 1024], rw_g2=[160, 1024],
    c_ident=[128, 128], c_b64=[128, 128], c_b64m=[128, 128], c_mskL=[64, 64], c_mskU=[64, 128], c_i2=[128, 64],
    c_maskR=[128, 512], c_tri=[128, 128], c_cvec=[65, 1],
    xo=[NOWN, D], w_o=[D, D], ln1_g=[1, D], ln1_b=[1, D], w_up=[D, DFF], w_down=[DFF, D], ln2_g=[1, D], ln2_b=[1, D],
    fx_g=[64, 16], fx_bf=[16, 1], fx_padb=[128, 32], fx_clfT=[16, 1024], fx_ckT=[16, 64, 1024], fx_cv=[1024, 1024],
)


def make_consts():
    c = {}
    c["c_ident"] = np.eye(128, dtype=np.float32)
    b = np.zeros((128, 128), np.float32)
    b[:64, :64] = 1.0
    b[64:, 64:] = 1.0
    c["c_b64"] = b
    c["c_b64m"] = b / 64.0
    t = np.arange(64)
    c["c_mskL"] = (t[:, None] > t[None, :]).astype(np.float32)
    c["c_mskU"] = np.concatenate([(t[:, None] < t[None, :]), (t[:, None] <= t[None, :])], 1).astype(np.float32)
    c["c_i2"] = np.concatenate([np.eye(64), np.eye(64)], 0).astype(np.float32)
    m = np.ones((128, 512), np.float32)
    m[:, ::64] = 0.0
    c["c_maskR"] = m
    k = np.arange(128)
    c["c_tri"] = (k[:, None] <= k[None, :]).astype(np.float32)
    cv = np.full((65, 1), 1.0 / 64.0, np.float32)
    cv[64, 0] = 1e-6
    c["c_cvec"] = cv
    return c


def build(stages=("A", "B1", "B2", "C"), debug=(), pj_input=False, b1_kw=None, b2_kw=None, c_kw=None, extra_in=()):
    P = Prog()
    nc = P.nc
    io = {}
    need = set()
    if "A" in stages:
        need |= {"xT", "w_in"}
    if "B1" in stages:
        need |= {"shiftT", "s0T", "rw_prm", "rw_mul", "rw_w2a2", "rw_g2"}
    if "B2" in stages:
        need |= {"fx_g", "fx_bf", "fx_padb", "fx_clfT", "fx_ckT", "fx_cv"}
    if "C" in stages:
        need |= {"xo", "w_o", "ln1_g", "ln1_b", "w_up", "w_down", "ln2_g", "ln2_b"}
    need |= {k for k in IN_SPECS if k.startswith("c_")}
    need |= set(extra_in)
    for name, shape in IN_SPECS.items():
        if name in need:
            io[name] = nc.dram_tensor(name, list(shape), F32, kind="ExternalInput").ap()
    kind = "ExternalInput" if pj_input else ("ExternalOutput" if "PJ" in debug else "Internal")
    io["PJ"] = nc.dram_tensor("PJ", [N_FM, NTOK], F32, kind=kind).ap()
    io["VTM"] = nc.dram_tensor("VTM", [NTOK, 1024], F32, kind=kind).ap()
    io["wkv_p"] = nc.dram_tensor("wkv_p", [128, 8, 64], F32, kind="ExternalOutput").ap()
    io["wkv_s"] = nc.dram_tensor("wkv_s", [128, 8, 64], F32, kind="ExternalOutput").ap()
    io["spl_p"] = nc.dram_tensor("spl_p", [16, 3, 1024], BF16, kind="Internal").ap()
    io["spl_s"] = nc.dram_tensor("spl_s", [16, 3, NS], BF16, kind="Internal").ap()
    io["logf_o"] = nc.dram_tensor("logf_o", [16, NOWN], F32, kind="ExternalOutput").ap()
    io["ps"] = [nc.alloc_psum_tensor("psb%d" % i, [128, 512], F32) for i in range(8)]
    cst = load_consts(P, io)
    if "A" in stages:
        stage_A(P, io)
    else:
        P.barrier()
    yT = P.sb("yT", [128, 16, NOWN], BF16)
    if "A" in stages or "outs" in debug:
        io["k_out"] = nc.dram_tensor("k_out", [1024, NOWN], F32, kind="ExternalOutput").ap()
        io["v_out"] = nc.dram_tensor("v_out", [NOWN, 1024], F32, kind="ExternalOutput").ap()
        io["shA"] = nc.dram_tensor("shA", [2176, 2], F32, kind="ExternalOutput").ap()
        io["shB"] = nc.dram_tensor("shB", [1184, 2], F32, kind="ExternalOutput").ap()
        PJ = io["PJ"]
        P.dma("sp", io["k_out"], PJ[ROW["fk"]:ROW["fk"] + 1024, OWN0:NTOK], semkey="o_k", final=True)
        P.dma("sp", io["v_out"], io["VTM"][OWN0:NTOK, :], semkey="o_v", final=True)
        for ci, col in enumerate((NPR - 1, NTOK - 1)):
            P.dma("sp", io["shA"][:, ci:ci + 1], PJ[0:2176, col:col + 1], semkey="o_sa", final=True, allow_slow_non_contiguous=True)
            P.dma("sp", io["shB"][:, ci:ci + 1], PJ[ROW["r"]:ROW["r"] + 1184, col:col + 1], semkey="o_sb", final=True,
                  allow_slow_non_contiguous=True)
    if "B1" in stages:
        stage_B1(P, io, cst, yT, **(b1_kw or {}))
    if "B2" in stages:
        stage_B2(P, io, cst, yT, **(b2_kw or {}))
    if "yTin" in debug:
        io["yT_in"] = nc.dram_tensor("yT_in", [128, 16, NOWN], F32, kind="ExternalInput").ap()
        P.dma("pool", yT[:], io["yT_in"], writes=["yT%d" % i for i in range(16)], semkey="ytin")
    if "C" in stages:
        io["y_out"] = nc.dram_tensor("y_out", [NOWN, D], F32, kind="ExternalOutput").ap()
        stage_C(P, io, cst, yT, **(c_kw or {}))
    if "yT" in debug:
        io["yT_dbg"] = nc.dram_tensor("yT_dbg", [128, 16, NOWN], F32, kind="ExternalOutput").ap()
        P.dma("pool", io["yT_dbg"], yT[:], reads=["yT%d" % i for i in range(16)], semkey="ytdbg", final=True)
    P.emit()
    return P


def stage_B2(P, io, cst, yT, heads=range(16), insts=("p", "s")):
    nc = P.nc
    O = Ops(P)
    PJ = io["PJ"]
    VTM = io["VTM"]
    ps = io["ps"]
    m0 = P.mark()
    ones = P.sb("ones", [128, 512], F32)
    O.memset("pool", ones[:], 1.0, ["ones"])
    fog = P.sb("fog", [64, 16], F32)
    P.dma("sp", fog[:], io["fx_g"], writes=["fog"])
    nbf = P.sb("nbf", [16, 1], F32)
    P.dma("sp", nbf[:], io["fx_bf"], writes=["nbf"])
    O.ts("dve", nbf[:], nbf[:], -1.0, None, ALU.mult, None, ["nbf"], ["nbf"])
    cvec = P.sb("cvec", [65, 1], F32)
    P.dma("sp", cvec[:], io["c_cvec"], writes=["cvec"])
    padb = P.sb("padb", [128, 32], F32)
    P.dma("sp", padb[:], io["fx_padb"], writes=["padb"])
    Crow = P.sb("Crow", [16, NPR], F32)
    Ccum = P.sb("Ccum", [16, NPR], F32)
    Ckb = P.sb("Ckb", [128, 32, 16], F32)
    spl = P.sb("spl", [16, 3, 1024], BF16)
    q8 = P.sb("q8", [16, 1024], F32)
    lfo = P.sb("lfo", [16, NOWN], F32)
    KTa = [P.sb("KTa%d" % i, [67, NPR], BF16) for i in range(2)]
    QTa = [P.sb("QTa%d" % i, [67, 1024], BF16) for i in range(2)]
    Va = [P.sb("Va%d" % i, [128, 32, 65], BF16) for i in range(2)]
    ogT = [P.sb("ogT%d" % i, [64, 1024], F32) for i in range(2)]
    pt = [P.sb("pt%d" % i, [128, 512], BF16) for i in range(4)]
    Oa = P.sb("Oa", [65, 512], F32)
    sq = P.sb("sqf", [65, 512], F32)
    rs = P.sb("rs", [1, 512], F32)
    y1 = P.sb("y1", [64, 512], F32)
    ytmp = [P.sb("ytmp%d" % i, [64, 512], BF16) for i in range(2)]
    for i in range(2):
        O.memset("pool", KTa[i][64:67, :], 1.0, ["KTa%d" % i])
        O.memset("pool", Va[i][:, :, 64:65], 1.0, ["Va%d" % i])
    cnt = dict(h=0, s=0, p=0, y=0)

    for inst in insts:
        if inst == "p":
            TK, NQ, QPOS0 = NPR, 1024, OWN0
            qtiles = [(0, 512), (512, 512)]
            ocol0 = 0
        else:
            TK, NQ, QPOS0 = 1024 + NS, NS, 1024
            qtiles = [(0, NS)]
            ocol0 = 1024
        nblk = (TK + 127) // 128
        if inst == "p":
            P.dma("sp", Crow[:, 0:TK], PJ[ROW["f"]:ROW["f"] + 16, 0:TK], writes=["Crow"])
            fsl = slice(0, TK)
        else:
            P.dma("sp", Crow[:, 0:1024], io["fx_clfT"], writes=["Crow"])
            P.dma("sp", Crow[:, 1024:TK], PJ[ROW["f"]:ROW["f"] + 16, NPR:NPR + NS], writes=["Crow"], semkey="Crow_b")
            O.ts("dve", Crow[:, 0:1024], Crow[:, 0:1024], -1.0, None, ALU.mult, None, ["Crow"], ["Crow"])
            fsl = slice(1024, TK)
        O.act(Crow[:, fsl], Crow[:, fsl], AF.Exp, ["Crow", "nbf"], ["Crow"], bias=nbf[:, 0:1], scale=-1.0)
        O.act(Crow[:, fsl], Crow[:, fsl], AF.Ln, ["Crow"], ["Crow"], bias=1.0)
        if inst == "p":
            P.op("act", lambda: nc.scalar.mul(lfo[:, 0:1024], Crow[:, OWN0:NPR], -1.0), reads=["Crow"], writes=["lfo"])
        else:
            P.op("act", lambda: nc.scalar.mul(lfo[:, 1024:NOWN], Crow[:, 1024:TK], -1.0), reads=["Crow"], writes=["lfo"])
            P.dma("sp", io["logf_o"], lfo[:], reads=["lfo"], semkey="lfo", final=True)
        for c0 in range(0, TK, 512):
            cw = min(512, TK - c0)
            init = cst["zcol"][0:16, 0:1] if c0 == 0 else Ccum[:, c0 - 1:c0]
            o_ap, d0, d1 = Ccum[:, c0:c0 + cw], ones[0:16, 0:cw], Crow[:, c0:c0 + cw]
            P.op("dve", (lambda o_ap=o_ap, d0=d0, d1=d1, init=init: nc.vector.tensor_tensor_scan(o_ap, d0, d1, init, ALU.mult, ALU.add)),
                 reads=["Crow", "ones", "Ccum"], writes=["Ccum"])
        tp = ps[6][:, 0:nblk * 16].rearrange("p (a b) -> p a b", a=nblk)
        for kb in range(nblk):
            nk = min(128, TK - kb * 128)
            O.tr(tp[0:nk, kb, :], Ccum[:, kb * 128:kb * 128 + nk], cst["identf"][0:16, 0:16], ["Ccum", "c_identf"], ["ps6"])
        nfull = TK // 128
        if inst == "p":
            O.tt("dve", Ckb[:, 0:nblk, :], tp[:, 0:nblk, :], padb[:, 0:nblk].unsqueeze(2).to_broadcast([128, nblk, 16]), ALU.add,
                 ["ps6", "padb"], ["Ckb"])
        else:
            O.cp("dve", Ckb[:, 0:nfull, :], tp[:, 0:nfull, :], ["ps6"], ["Ckb"])
            O.cp("dve", Ckb[0:NS, nfull, :], tp[0:NS, nfull, :], ["ps6", "Ckb"], ["Ckb"])
        qs = slice(QPOS0, QPOS0 + NQ)
        O.ts("dve", q8[:, 0:NQ], Ccum[:, qs], -8.0, None, ALU.mult, None, ["Ccum"], ["q8"])
        for k3 in range(3):
            O.cp("dve", spl[:, k3, 0:NQ], q8[:, 0:NQ], ["q8"], ["spl"])
            if k3 < 2:
                O.tt("dve", q8[:, 0:NQ], q8[:, 0:NQ], spl[:, k3, 0:NQ], ALU.subtract, ["q8", "spl"], ["q8"])
        sk = "splD" + inst
        P.dma("sp", io["spl_" + inst], spl[:, :, 0:NQ], reads=["spl"], writes=[sk])
        for h in heads:
            hb = cnt["h"] % 2
            cnt["h"] += 1
            kK, kQ, kV, kG = "KTa%d" % hb, "QTa%d" % hb, "Va%d" % hb, "ogT%d" % hb
            fk0 = ROW["fk"] + 64 * h
            if inst == "p":
                P.dma("pool", KTa[hb][0:64, 0:TK], PJ[fk0:fk0 + 64, 0:TK], writes=[kK])
                P.dma("pool", Va[hb][:, 0:32, 0:64], VTM[0:NPR, 64 * h:64 * h + 64].rearrange("(b p) d -> p b d", p=128), writes=[kV])
                ocs = slice(OWN0, NPR)
            else:
                P.dma("pool", KTa[hb][0:64, 0:1024], io["fx_ckT"][h], writes=[kK])
                P.dma("pool", KTa[hb][0:64, 1024:TK], PJ[fk0:fk0 + 64, NPR:NPR + NS], writes=[kK])
                P.dma("pool", Va[hb][:, 0:8, 0:64], io["fx_cv"][:, 64 * h:64 * h + 64].rearrange("(b p) d -> p b d", p=128), writes=[kV])
                P.dma("pool", Va[hb][0:NS, 8, 0:64], VTM[NPR:NPR + NS, 64 * h:64 * h + 64], writes=[kV])
                ocs = slice(NPR, NPR + NS)
            P.dma("pool", QTa[hb][0:64, 0:NQ], PJ[ROW["q"] + 64 * h:ROW["q"] + 64 * h + 64, ocs], writes=[kQ])
            P.dma("sp", QTa[hb][64:67, 0:NQ], io["spl_" + inst][h], reads=[sk], writes=[kQ])
            P.dma("sp", ogT[hb][:, 0:NQ], PJ[ROW["og"] + 64 * h:ROW["og"] + 64 * h + 64, ocs], writes=[kG])
            for (q0, qw) in qtiles:
                ob = 2 + cnt["y"] % 2
                okey = "ps%d" % ob
                last_q = QPOS0 + q0 + qw - 1
                kbs = [kb for kb in range(nblk) if kb * 128 <= last_q]
                pend = []
                for n_, kb in enumerate(kbs):
                    nk = min(128, TK - kb * 128)
                    off = max(0, kb * 128 - (QPOS0 + q0))
                    diag = kb * 128 + nk - 1 > QPOS0 + q0
                    sb_ = (0, 1, 7)[cnt["s"] % 3]
                    cnt["s"] += 1
                    pb_ = cnt["p"] % 4
                    cnt["p"] += 1
                    skey, pkey = "ps%d" % sb_, "pt%d" % pb_
                    O.mm(ps[sb_][0:nk, off:qw], KTa[hb][0:67, kb * 128:kb * 128 + nk], QTa[hb][0:67, q0 + off:q0 + qw], [kK, kQ], [skey])
                    if len(pend) >= 2:
                        pend.pop(0)()
                    O.act(pt[pb_][0:nk, off:qw], ps[sb_][0:nk, off:qw], AF.Exp, [skey, "Ckb"], [pkey], bias=Ckb[0:nk, kb, h:h + 1], scale=0.125)
                    if diag:
                        mw = min(128, qw - off, nk)
                        O.tt(O.pick(("dve", "pool")), pt[pb_][0:nk, off:off + mw], pt[pb_][0:nk, off:off + mw], cst["tri"][0:nk, 0:mw], ALU.mult,
                             [pkey, "c_tri"], [pkey])
                    pend.append(lambda nk=nk, kb=kb, pb_=pb_, off=off, pkey=pkey, n_=n_:
                                O.mm(ps[ob][0:65, off:qw], Va[hb][0:nk, kb, 0:65], pt[pb_][0:nk, off:qw], [kV, pkey], [okey],
                                     start=(n_ == 0), stop=(n_ == len(kbs) - 1)))
                while pend:
                    pend.pop(0)()
                cnt["y"] += 1
                yb = cnt["y"] % 2
                O.cp("act", Oa[:, 0:qw], ps[ob][0:65, 0:qw], [okey], ["Oa"])
                O.act(sq[:, 0:qw], Oa[:, 0:qw], AF.Square, ["Oa"], ["sqf"])
                O.mm(ps[4][0:1, 0:qw], cvec[:, 0:1], sq[:, 0:qw], ["cvec", "sqf"], ["ps4"])
                O.act(rs[:, 0:qw], ps[4][0:1, 0:qw], AF.Sqrt, ["ps4"], ["rs"])
                O.recip(rs[:, 0:qw], rs[:, 0:qw], ["rs"], ["rs"])
                O.mm(ps[5][0:64, 0:qw], ones[0:1, 0:64], rs[:, 0:qw], ["ones", "rs"], ["ps5"])
                O.tt("dve", y1[:, 0:qw], Oa[0:64, 0:qw], ps[5][0:64, 0:qw], ALU.mult, ["Oa", "ps5"], ["y1"])
                O.act(ogT[hb][:, q0:q0 + qw], ogT[hb][:, q0:q0 + qw], AF.Sigmoid, [kG], [kG])
                O.stt("dve", ytmp[yb][:, 0:qw], y1[:, 0:qw], fog[:, h:h + 1], ogT[hb][:, q0:q0 + qw], ALU.mult, ALU.mult,
                      ["y1", "fog", kG], ["ytmp%d" % yb])
                r0 = (h % 2) * 64
                P.dma("sp", yT[r0:r0 + 64, 8 + h // 2, ocol0 + q0:ocol0 + q0 + qw], ytmp[yb][:, 0:qw],
                      reads=["ytmp%d" % yb], writes=["yT%d" % (8 + h // 2)], semkey="yT%d_%d" % (8 + h // 2, h % 2))
    P.barrier()
    P.release(m0)


TOKT = [(i * 128, 128) for i in range(8)] + [(1024, NS)]
TT3 = [(0, 512), (512, 512), (1024, NS)]


def layer_norm_tiles(P, O, Z, gvec, bvec, zkeys, out_dram=None):
    nc = P.nc
    m = P.mark()
    gb = P.sb("ln_g", [128, D], F32)
    bb = P.sb("ln_b", [128, D], F32)
    junk = P.sb("ln_junk", [128, D], F32)
    st = P.sb("ln_st", [128, 4], F32)
    P.dma("sp", gb[:], gvec.partition_broadcast(128), writes=["ln_g"])
    P.dma("sp", bb[:], bvec.partition_broadcast(128), writes=["ln_b"])
    for ti, (t0, tw) in enumerate(TOKT):
        z = Z[0:tw, ti, :]
        zk = zkeys[ti]
        s = st[0:tw, :]
        O.memset("pool", st[:, :], 0.0, ["ln_st"])
        P.op("act", (lambda z=z, tw=tw, s=s: nc.scalar.activation(out=junk[0:tw, :], in_=z, func=AF.Identity, accum_out=s[:, 0:1])),
             reads=[zk], writes=["ln_junk", "ln_st"])
        O.ts("dve", s[:, 1:2], s[:, 0:1], -1.0 / D, None, ALU.mult, None, ["ln_st"], ["ln_st"])
        O.act(z, z, AF.Identity, [zk, "ln_st"], [zk], bias=s[:, 1:2])
        P.op("act", (lambda z=z, tw=tw, s=s: nc.scalar.activation(out=junk[0:tw, :], in_=z, func=AF.Square, accum_out=s[:, 2:3])),
             reads=[zk, "ln_st"], writes=["ln_junk", "ln_st"])
        O.ts("dve", s[:, 3:4], s[:, 2:3], 1.0 / D, 1e-5, ALU.mult, ALU.add, ["ln_st"], ["ln_st"])
        O.rsqrt(s[:, 3:4], s[:, 3:4], ["ln_st"], ["ln_st"])
        O.stt("dve", z, z, s[:, 3:4], gb[0:tw, :], ALU.mult, ALU.mult, [zk, "ln_st", "ln_g"], [zk])
        O.tt("pool", z, z, bb[0:tw, :], ALU.add, [zk, "ln_b"], [zk])
        if out_dram is not None:
            P.dma("sp", out_dram[t0:t0 + tw, :], z, reads=[zk], semkey="yo%d" % ti, final=True)
    P.release(m)


def stage_C(P, io, cst, yT, dbg=None):
    nc = P.nc
    O = Ops(P)
    ps = io["ps"]
    m0 = P.mark()
    Z = P.sb("Z", [128, 9, D], F32)
    zk = ["Z%d" % i for i in range(9)]
    yk = ["yT%d" % i for i in range(16)]
    m1 = P.mark()
    wo = [P.sb("wo%d" % i, [128, 16, 512], BF16) for i in range(2)]
    wor = io["w_o"].rearrange("(kc p) c -> p kc c", p=128)
    for ti, (t0, tw) in enumerate(TOKT):
        P.dma("sp", Z[0:tw, ti, :], io["xo"][t0:t0 + tw, :], writes=[zk[ti]])
    pi = 0
    for cg in range(4):
        wb = cg % 2
        P.dma("pool", wo[wb][:], wor[:, :, cg * 512:cg * 512 + 512], writes=["wo%d" % wb])
        for ti, (t0, tw) in enumerate(TOKT):
            pb = pi % 4
            pi += 1
            for kc in range(16):
                O.mm(ps[pb][0:tw, :], yT[:, kc, t0:t0 + tw], wo[wb][:, kc, :], [yk[kc], "wo%d" % wb], ["ps%d" % pb],
                     start=(kc == 0), stop=(kc == 15))
            zs = Z[0:tw, ti, cg * 512:cg * 512 + 512]
            O.stt("dve", zs, zs, ALPHA, ps[pb][0:tw, :], ALU.mult, ALU.add, [zk[ti], "ps%d" % pb], [zk[ti]])
    def dump():
        for ti, (t0, tw) in enumerate(TOKT):
            P.dma("sp", io["y_out"][t0:t0 + tw, :], Z[0:tw, ti, :], reads=[zk[ti]], semkey="yo%d" % ti, final=True)
    if dbg == "z":
        dump()
        return
    layer_norm_tiles(P, O, Z, io["ln1_g"], io["ln1_b"], zk)
    if dbg == "h":
        dump()
        return
    hT = yT
    ei = 0
    for kc in range(16):
        for (grp, bank) in ((range(0, 4), 4), (range(4, 8), 5), (range(8, 9), 6)):
            for ti in grp:
                t0, tw = TOKT[ti]
                c0 = t0 - TOKT[grp[0]][0]
                O.tr(ps[bank][:, c0:c0 + tw], Z[0:tw, ti, kc * 128:kc * 128 + 128], cst["identf"][0:tw, 0:tw], [zk[ti], "c_identf"], ["ps%d" % bank])
            g0 = TOKT[grp[0]][0]
            gw = sum(TOKT[ti][1] for ti in grp)
            ei += 1
            O.cp("act" if ei % 2 else "dve", hT[:, kc, g0:g0 + gw], ps[bank][:, 0:gw], ["ps%d" % bank], [yk[kc]])
    for ti, (t0, tw) in enumerate(TOKT):
        O.ts("pool", Z[0:tw, ti, :], Z[0:tw, ti, :], ALPHA, None, ALU.mult, None, [zk[ti]], [zk[ti]])
    P.barrier()
    P.release(m1)
    m2 = P.mark()
    wu = [P.sb("wu%d" % i, [128, 16, 512], BF16) for i in range(2)]
    wd = [P.sb("wd%d" % i, [128, 4, D], BF16) for i in range(2)]
    aT = [P.sb("aT%d" % i, [128, 4, NOWN], BF16) for i in range(2)]
    rT = [P.sb("rT%d" % i, [128, 512], BF16) for i in range(3)]
    wur = io["w_up"].rearrange("(kc p) c -> p kc c", p=128)
    wdr = io["w_down"].rearrange("(g kc p) c -> g p kc c", p=128, kc=4)
    NG = DFF // 512

    def load_w(g):
        b = g % 2
        P.dma("pool", wu[b][:], wur[:, :, g * 512:g * 512 + 512], writes=["wu%d" % b])
        P.dma("pool", wd[b][:], wdr[g], writes=["wd%d" % b])
    load_w(0)
    pu = 0
    pd = 0
    ri = 0
    for g in range(NG):
        b = g % 2
        if g + 1 < NG:
            load_w(g + 1)
        for blk in range(4):
            for (t0, tw) in TT3:
                pb = pu % 3
                pu += 1
                for kc in range(16):
                    O.mm(ps[pb][:, 0:tw], wu[b][:, kc, blk * 128:blk * 128 + 128], hT[:, kc, t0:t0 + tw], ["wu%d" % b, yk[kc]], ["ps%d" % pb],
                         start=(kc == 0), stop=(kc == 15))
                rb = ri % 3
                ri += 1
                O.act(rT[rb][:, 0:tw], ps[pb][:, 0:tw], AF.Relu, ["ps%d" % pb], ["rT%d" % rb])
                O.tt("pool", aT[b][:, blk, t0:t0 + tw], rT[rb][:, 0:tw], rT[rb][:, 0:tw], ALU.mult, ["rT%d" % rb], ["aT%d" % b])
        for ti, (t0, tw) in enumerate(TOKT):
            for cg in range(4):
                pb = 3 + pd % 5
                pd += 1
                for blk in range(4):
                    O.mm(ps[pb][0:tw, :], aT[b][:, blk, t0:t0 + tw], wd[b][:, blk, cg * 512:cg * 512 + 512], ["aT%d" % b, "wd%d" % b], ["ps%d" % pb],
                         start=(blk == 0), stop=(blk == 3))
                zs = Z[0:tw, ti, cg * 512:cg * 512 + 512]
                O.tt("dve", zs, zs, ps[pb][0:tw, :], ALU.add, [zk[ti], "ps%d" % pb], [zk[ti]])
    P.barrier()
    P.release(m2)
    if dbg == "acc":
        dump()
        return
    layer_norm_tiles(P, O, Z, io["ln2_g"], io["ln2_b"], zk, out_dram=io["y_out"])
    P.release(m0)


_PROG = {}


def host_maps(inp):
    g = lambda k: np.asarray(inp[k], np.float32)[0]
    xp = np.asarray(inp["x_prompt"], np.float32)
    xs = np.asarray(inp["x_sample"], np.float32)
    w_in = g("w_in")
    wperm = np.ascontiguousarray(w_in[:, PERM])
    consts = make_consts()
    mu = g("rwkv_mu")
    cols = [mu[0:1024], mu[1024:2048], mu[2048:3072], g("rwkv_w0"), g("rwkv_a0"), g("rwkv_k_k"), g("rwkv_k_a"),
            g("rwkv_r_k").reshape(-1), g("rwkv_lnx_g"), g("rwkv_lnx_b")]
    rw_prm = np.ascontiguousarray(np.stack([v.reshape(8, 128) for v in cols], -1).transpose(1, 0, 2))
    rw_mul = np.zeros((128, 3), np.float32)
    rw_mul[:, 0] = mu[3072:3200]
    rw_mul[:, 1] = mu[3200:3328]
    rw_mul[:32, 2] = mu[3328:3360]
    shared = dict(consts)
    shared.update(
        w_in=wperm, rw_prm=rw_prm, rw_mul=rw_mul,
        rw_w2a2=np.ascontiguousarray(np.concatenate([g("rwkv_w2"), g("rwkv_a2")], 0)), rw_g2=g("rwkv_g2"),
        fx_g=np.ascontiguousarray(g("fox_out_g").reshape(16, 64).T), fx_bf=g("fox_b_f").reshape(16, 1),
        w_o=g("w_o"), ln1_g=g("ln1_g").reshape(1, D), ln1_b=g("ln1_b").reshape(1, D), w_up=g("w_up"), w_down=g("w_down"),
        ln2_g=g("ln2_g").reshape(1, D), ln2_b=g("ln2_b").reshape(1, D))
    maps = []
    pos = np.arange(NPR).reshape(32, 128).T
    for c in range(8):
        b, j = divmod(c, 4)
        n = 1024 * (j + 1)
        xpad = np.zeros((NTOK, D), np.float32)
        xpad[NPR - n:NPR] = xp[b, :n]
        xpad[NPR:] = xs[c]
        m = dict(shared)
        m["xT"] = np.ascontiguousarray(xpad.T)
        m["xo"] = np.ascontiguousarray(xpad[OWN0:NTOK])
        sh = np.zeros((N_FM, 1), np.float32)
        st = np.asarray(inp["state_rwkv_shift"], np.float32)[0, c, 0]
        for seg in ("rk", "rv", "xwxa", "r", "xg"):
            o, n_ = SEG[seg]
            sh[ROW[seg]:ROW[seg] + n_, 0] = st[o:o + n_]
        m["shiftT"] = sh
        s0 = np.asarray(inp["state_rwkv_wkv"], np.float32)[0, c]
        m["s0T"] = np.ascontiguousarray(s0.transpose(0, 2, 1).reshape(8, 128, 64).transpose(1, 0, 2))
        m["fx_padb"] = np.where(pos < NPR - n, -BIG, 0.0).astype(np.float32)
        m["fx_clfT"] = np.ascontiguousarray(np.asarray(inp["cache_fox_logf"], np.float32)[0, c].T)
        m["fx_ckT"] = np.ascontiguousarray(np.asarray(inp["cache_fox_k"], np.float32)[0, c].transpose(1, 2, 0))
        m["fx_cv"] = np.ascontiguousarray(np.asarray(inp["cache_fox_v"], np.float32)[0, c].reshape(1024, 1024))
        maps.append(m)
    return maps


def assemble(results):
    f32 = np.float32
    y_p = np.zeros((2, 4096, D), f32)
    y_s = np.zeros((8, NS, D), f32)
    pk = np.zeros((1, 2, 4096, 16, 64), f32)
    pv = np.zeros((1, 2, 4096, 16, 64), f32)
    pf = np.zeros((1, 2, 4096, 16), f32)
    pS = np.zeros((1, 2, 16, 64, 64), f32)
    psh = np.zeros((1, 2, 1, 3360), f32)
    sk = np.zeros((1, 8, NS, 16, 64), f32)
    sv = np.zeros((1, 8, NS, 16, 64), f32)
    sf = np.zeros((1, 8, NS, 16), f32)
    sS = np.zeros((1, 8, 16, 64, 64), f32)
    ssh = np.zeros((1, 8, 1, 3360), f32)

    def wkv(a):
        return a.reshape(2, 64, 8, 64).transpose(2, 0, 3, 1).reshape(16, 64, 64)

    def shift(r, ci):
        out = np.zeros(3360, f32)
        a, bq = r["shA"][:, ci], r["shB"][:, ci]
        out[1024:2048] = a[0:1024]
        out[2048:3072] = a[1024:2048]
        out[3072:3200] = a[2048:2176]
        out[0:1024] = bq[0:1024]
        out[3200:3360] = bq[1024:1184]
        return out
    for c in range(8):
        r = results[c]
        b, j = divmod(c, 4)
        sl = slice(1024 * j, 1024 * (j + 1))
        y_p[b, sl] = r["y_out"][0:1024]
        y_s[c] = r["y_out"][1024:NOWN]
        kT = r["k_out"]
        pk[0, b, sl] = kT[:, 0:1024].T.reshape(1024, 16, 64)
        sk[0, c] = kT[:, 1024:NOWN].T.reshape(NS, 16, 64)
        pv[0, b, sl] = r["v_out"][0:1024].reshape(1024, 16, 64)
        sv[0, c] = r["v_out"][1024:NOWN].reshape(NS, 16, 64)
        pf[0, b, sl] = r["logf_o"][:, 0:1024].T
        sf[0, c] = r["logf_o"][:, 1024:NOWN].T
        sS[0, c] = wkv(r["wkv_s"])
        ssh[0, c, 0] = shift(r, 1)
        if j == 3:
            pS[0, b] = wkv(r["wkv_p"])
            psh[0, b, 0] = shift(r, 0)
    return (y_p, y_s, pk, pv, pf, pS, psh, sk, sv, sf, sS, ssh)


def kernel(**inputs):
    if "full" not in _PROG:
        _PROG["full"] = build()
    P = _PROG["full"]
    maps = host_maps(inputs)
    res = run_bass_kernel_spmd(P.nc, maps, core_ids=list(range(8)))
    return assemble(res.results)
```

```python
import numpy as np
import concourse.bass as bass
import concourse.mybir as mybir
from concourse.bass_utils import run_bass_kernel_spmd

F32 = mybir.dt.float32
BF16 = mybir.dt.bfloat16
AF = mybir.ActivationFunctionType
ALU = mybir.AluOpType
AX = mybir.AxisListType

D = 2048
NPR = 4096
NS = 16
NTOK = NPR + NS
OWN0 = 3072
NOWN = 1024 + NS
DFF = 8192
C0 = float(np.exp(-0.5))
ALPHA = 2.0 ** 0.25
BIG = 30000.0

SEG = dict(rk=(1024, 1024), rv=(2048, 1024), xwxa=(3072, 128), fk=(3360 + 1024, 1024), f=(3360 + 3072, 16),
           r=(0, 1024), xg=(3200, 160), q=(3360, 1024), og=(3360 + 3088, 1024), fv=(3360 + 2048, 1024))
ORDER = ["rk", "rv", "xwxa", "fk", "f", "r", "xg", "q", "og", "fv"]
ROW = {}
_o = 0
for _k in ORDER:
    ROW[_k] = _o
    _o += SEG[_k][1]
N_ALL = ROW["r"]
N_OWN = ROW["fv"] - N_ALL
N_FM = ROW["fv"]
PERM = np.concatenate([np.arange(SEG[k][0], SEG[k][0] + SEG[k][1]) for k in ORDER])


EMBED_WAIT = True
SAME_ENG_NOWAIT = ()


class Prog:
    ENG = ("pe", "act", "dve", "pool", "sp")

    def __init__(self):
        self.nc = bass.Bass("TRN2", target_bir_lowering=False)
        nc = self.nc
        self.eng = {"pe": nc.tensor, "act": nc.scalar, "dve": nc.vector, "pool": nc.gpsimd, "sp": nc.sync}
        self.ops = []
        self.sb_base = 16640
        self.sb_top = 229376
        self.sb_off = self.sb_base
        self.nalloc = 0

    def sb(self, name, shape, dtype):
        per = int(np.prod(shape[1:])) * (2 if dtype == BF16 else 4)
        off = (self.sb_off + 63) // 64 * 64
        assert off + per <= self.sb_top, ("sbuf overflow", name, off, per)
        self.sb_off = off + per
        self.nalloc += 1
        return self.nc.alloc_sbuf_tensor_at("%s_%d" % (name, self.nalloc), list(shape), dtype, offset=off)

    def mark(self):
        return self.sb_off

    def release(self, m):
        self.sb_off = m

    def op(self, eng, fn, reads=(), writes=()):
        self.ops.append(dict(eng=eng, fn=fn, reads=tuple(reads), writes=tuple(writes), dma=False, bar=False))

    def dma(self, q, out, in_, reads=(), writes=(), semkey=None, final=False, **kw):
        if semkey is None:
            semkey = writes[0] if writes else reads[0]
        e = self.eng[q]
        self.ops.append(dict(eng=q, fn=(lambda e=e, out=out, in_=in_, kw=kw: e.dma_start(out=out, in_=in_, **kw)),
                             reads=tuple(reads), writes=tuple(writes), dma=True, semkey=semkey, final=final, bar=False))

    def barrier(self):
        for e in self.ENG:
            self.ops.append(dict(eng=e, fn=None, reads=(), writes=(), dma=False, bar=True))

    def emit(self):
        nc = self.nc
        ops = self.ops
        n = len(ops)
        lastw = {}
        readers = {}
        deps = [None] * n
        last_eng = {}
        dma_since = []
        bar_group = None
        for i, o in enumerate(ops):
            if o["bar"]:
                if bar_group is None:
                    bar_group = (set(last_eng.values()) | set(dma_since))
                deps[i] = set(bar_group)
                nxt = ops[i + 1] if i + 1 < n else None
                if nxt is None or not nxt["bar"]:
                    bar_group = None
                    dma_since = []
                    lastw = {}
                    readers = {}
                continue
            d = set()
            for k in o["reads"]:
                if k in lastw:
                    d.add(lastw[k])
            for k in o["writes"]:
                if k in lastw:
                    d.add(lastw[k])
                for r in readers.get(k, ()):
                    d.add(r)
            d.discard(i)
            deps[i] = d
            for k in o["reads"]:
                readers.setdefault(k, []).append(i)
            for k in o["writes"]:
                lastw[k] = i
                readers[k] = []
            if o["dma"]:
                dma_since.append(i)
            else:
                last_eng[o["eng"]] = i

        def need_wait(o, pj):
            if pj["dma"] or o["dma"] or o["bar"]:
                return not (o["bar"] and (not pj["dma"]) and pj["eng"] == o["eng"])
            if pj["eng"] != o["eng"]:
                return True
            if o["eng"] == "pe" or o["eng"] in SAME_ENG_NOWAIT:
                return False
            return any(k in pj["writes"] for k in o["reads"])

        needed = set()
        for i, o in enumerate(ops):
            for j in deps[i]:
                if need_wait(o, ops[j]):
                    needed.add(j)
            if o["dma"]:
                needed.add(i)
        esem = {e: nc.alloc_semaphore("c_" + e) for e in self.ENG}
        ecnt = {e: 0 for e in self.ENG}
        dpool = []
        dcnt = []
        keymap = {}
        token = [None] * n
        for i, o in enumerate(ops):
            if o["bar"]:
                keymap = {}
                continue
            if o["dma"]:
                sk = o["semkey"]
                if sk not in keymap:
                    idx = len(keymap)
                    if idx >= len(dpool):
                        dpool.append(nc.alloc_semaphore("d%d" % idx))
                        dcnt.append(0)
                    keymap[sk] = idx
                idx = keymap[sk]
                dcnt[idx] += 16
                token[i] = ("d:%d" % idx, dpool[idx], dcnt[idx])
            elif i in needed:
                e = o["eng"]
                ecnt[e] += 1
                token[i] = ("e:" + e, esem[e], ecnt[e])
        dsem = dpool
        seen = {e: {} for e in self.ENG}
        dlatest = {}
        nwaits = 0
        for i, o in enumerate(ops):
            e = o["eng"]
            eng = self.eng[e]
            want = {}
            for j in deps[i]:
                tk = token[j]
                if tk is None or not need_wait(o, ops[j]):
                    continue
                name, sem, val = tk
                if name.startswith("d:"):
                    val = dlatest[name]
                if want.get(name, (None, 0))[1] < val:
                    want[name] = (sem, val)
            todo = [(name, sem, val) for name, (sem, val) in want.items() if seen[e].get(name, 0) < val]
            attach = None
            if EMBED_WAIT and todo and not o["bar"] and not o["dma"]:
                attach = todo.pop()
            for name, sem, val in todo:
                eng.wait_ge(sem, val)
                seen[e][name] = val
                nwaits += 1
            if o["bar"]:
                continue
            ins = o["fn"]()
            if attach is not None:
                name, sem, val = attach
                ins._wait_ge(sem, val)
                seen[e][name] = val
            tk = token[i]
            if tk is not None:
                name, sem, val = tk
                if o["dma"]:
                    ins.then_inc(sem, 16)
                    dlatest[name] = val
                else:
                    ins.then_inc(sem, 1)
        for idx in range(len(dpool)):
            name = "d:%d" % idx
            val = dlatest.get(name, 0)
            if val and seen["sp"].get(name, 0) < val:
                self.eng["sp"].wait_ge(dpool[idx], val)
                seen["sp"][name] = val
        self.stats = dict(n_ops=n, n_waits=nwaits, n_dsem=len(dsem), cnt=dict(ecnt))
        return nc


TT_ALL = [(i * 512, 512) for i in range(8)] + [(NPR, NS)]
TT_OWN = [(OWN0 - 1, 1), (OWN0, 512), (OWN0 + 512, 512), (NPR, NS)]


def stage_A(P, io):
    nc = P.nc
    m0 = P.mark()
    xs = P.sb("xs", [128, 16, NTOK], BF16)
    wt = [P.sb("wt%d" % i, [128, 16, 512], BF16) for i in range(2)]
    NOT = 8
    ot = [P.sb("ot%d" % i, [128, 512], F32) for i in range(NOT)]
    vt = [P.sb("vt%d" % i, [128, 1024], F32) for i in range(2)]
    ps = io["ps"]
    xTr = io["xT"].rearrange("(kc p) t -> p kc t", p=128)
    wr = io["w_in"].rearrange("(kc p) c -> p kc c", p=128)
    PJ = io["PJ"]
    VTM = io["VTM"]
    for g in range(8):
        P.dma("pool", xs[:, 2 * g:2 * g + 2, :], xTr[:, 2 * g:2 * g + 2, :], writes=["xs%d" % g])
    st = dict(oi=0, pi=0, wi=0, vi=0)

    def evac(dst, src, wkey, pkey):
        st["pi"] += 1
        if st["pi"] % 2 == 0:
            P.op("act", lambda: nc.scalar.copy(dst, src), reads=[pkey], writes=[wkey])
        else:
            P.op("dve", lambda: nc.vector.tensor_copy(dst, src), reads=[pkey], writes=[wkey])

    def fm_chunk(c0, cw, tts):
        wb = st["wi"] % 2
        st["wi"] += 1
        P.dma("pool", wt[wb][:, :, 0:cw], wr[:, :, c0:c0 + cw], writes=["wt%d" % wb])
        for m in range(0, cw, 128):
            mw = min(128, cw - m)
            for (t0, tw) in tts:
                pb = st["pi"] % 6
                for kc in range(16):
                    P.op("pe", (lambda pb=pb, wb=wb, kc=kc, m=m, mw=mw, t0=t0, tw=tw:
                                nc.tensor.matmul(ps[pb][0:mw, 0:tw], wt[wb][:, kc, m:m + mw], xs[:, kc, t0:t0 + tw],
                                                 start=(kc == 0), stop=(kc == 15))),
                         reads=["wt%d" % wb, "xs%d" % (kc // 2)], writes=["ps%d" % pb])
                ob = st["oi"] % NOT
                st["oi"] += 1
                evac(ot[ob][0:mw, 0:tw], ps[pb][0:mw, 0:tw], "ot%d" % ob, "ps%d" % pb)
                P.dma("sp", PJ[c0 + m:c0 + m + mw, t0:t0 + tw], ot[ob][0:mw, 0:tw], reads=["ot%d" % ob],
                      semkey="oto%d" % ob, **({"allow_slow_non_contiguous": True} if tw == 1 else {}))

    for c0 in range(0, N_ALL, 512):
        fm_chunk(c0, min(512, N_ALL - c0), TT_ALL)
    for c0 in range(N_ALL, N_FM, 512):
        fm_chunk(c0, min(512, N_FM - c0), TT_OWN)
    wbs = []
    for h in range(2):
        wb = st["wi"] % 2
        st["wi"] += 1
        P.dma("pool", wt[wb][:, :, :], wr[:, :, N_FM + 512 * h:N_FM + 512 * h + 512], writes=["wt%d" % wb])
        wbs.append(wb)
    tb = [(i * 128, 128) for i in range(32)] + [(NPR, NS)]
    for (t0, tw) in tb:
        vb = st["vi"] % 2
        st["vi"] += 1
        for h in range(2):
            pb = st["pi"] % 6
            wb = wbs[h]
            for kc in range(16):
                P.op("pe", (lambda pb=pb, wb=wb, kc=kc, t0=t0, tw=tw:
                            nc.tensor.matmul(ps[pb][0:tw, 0:512], xs[:, kc, t0:t0 + tw], wt[wb][:, kc, :],
                                             start=(kc == 0), stop=(kc == 15))),
                     reads=["wt%d" % wb, "xs%d" % (kc // 2)], writes=["ps%d" % pb])
            evac(vt[vb][0:tw, 512 * h:512 * h + 512], ps[pb][0:tw, 0:512], "vt%d" % vb, "ps%d" % pb)
        P.dma("sp", VTM[t0:t0 + tw, :], vt[vb][0:tw, :], reads=["vt%d" % vb], semkey="vto%d" % vb)
    P.barrier()
    P.release(m0)


class Ops:
    def __init__(self, P):
        self.P = P
        self.nc = P.nc
        self.rr = 0

    def E(self, which):
        return {"dve": self.nc.vector, "pool": self.nc.gpsimd}[which]

    def pick(self, choices=("dve", "pool")):
        self.rr += 1
        return choices[self.rr % len(choices)]

    def tt(self, eng, out, a, b, op, r, w):
        e = self.E(eng)
        self.P.op(eng, lambda: e.tensor_tensor(out, a, b, op), reads=r, writes=w)

    def ts(self, eng, out, a, s1, s2, op0, op1, r, w):
        e = self.E(eng)
        if s2 is None:
            self.P.op(eng, lambda: e.tensor_scalar(out, a, s1, None, op0), reads=r, writes=w)
        else:
            self.P.op(eng, lambda: e.tensor_scalar(out, a, s1, s2, op0, op1), reads=r, writes=w)

    def stt(self, eng, out, a, sc, b, op0, op1, r, w):
        e = self.E(eng)
        self.P.op(eng, lambda: e.scalar_tensor_tensor(out, a, sc, b, op0, op1), reads=r, writes=w)

    def cp(self, eng, out, a, r, w):
        if eng == "act":
            self.P.op("act", lambda: self.nc.scalar.copy(out, a), reads=r, writes=w)
        else:
            e = self.E(eng)
            self.P.op(eng, lambda: e.tensor_copy(out, a), reads=r, writes=w)

    def act(self, out, a, func, r, w, bias=None, scale=None):
        kw = {}
        if bias is not None:
            kw["bias"] = bias
        if scale is not None:
            kw["scale"] = scale
        self.P.op("act", lambda: self.nc.scalar.activation(out=out, in_=a, func=func, **kw), reads=r, writes=w)

    def mm(self, out, lhsT, rhs, r, w, start=True, stop=True):
        self.P.op("pe", lambda: self.nc.tensor.matmul(out, lhsT, rhs, start=start, stop=stop), reads=r, writes=w)

    def tr(self, out, a, ident, r, w):
        self.P.op("pe", lambda: self.nc.tensor.transpose(out, a, ident), reads=r, writes=w)

    def memset(self, eng, out, val, w):
        e = self.E(eng)
        self.P.op(eng, lambda: e.memset(out, val), writes=w)

    def recip(self, out, a, r, w):
        self.P.op("dve", lambda: self.nc.vector.reciprocal(out, a), reads=r, writes=w)

    def rsqrt(self, out, a, r, w):
        self.act(out, a, AF.Sqrt, r, w)
        self.recip(out, out, w, w)


def load_consts(P, io):
    nc = P.nc
    c = {}
    c["identb"] = P.sb("identb", [128, 128], BF16)
    c["identf"] = P.sb("identf", [128, 128], F32)
    c["b64"] = P.sb("b64", [128, 128], F32)
    c["b64m"] = P.sb("b64m", [128, 128], F32)
    c["mskL"] = P.sb("mskL", [64, 64], F32)
    c["mskU"] = P.sb("mskU", [64, 128], F32)
    c["i2"] = P.sb("i2", [128, 64], F32)
    c["maskR"] = P.sb("maskR", [128, 512], F32)
    c["tri"] = P.sb("tri", [128, 128], BF16)
    c["zcol"] = P.sb("zcol", [128, 1], F32)
    P.dma("pool", c["identb"][:], io["c_ident"], writes=["c_identb"])
    P.dma("sp", c["identf"][:], io["c_ident"], writes=["c_identf"])
    P.dma("sp", c["b64"][:], io["c_b64"], writes=["c_b64"])
    P.dma("sp", c["b64m"][:], io["c_b64m"], writes=["c_b64m"])
    P.dma("sp", c["mskL"][:], io["c_mskL"], writes=["c_mskL"])
    P.dma("sp", c["mskU"][:], io["c_mskU"], writes=["c_mskU"])
    P.dma("sp", c["i2"][:], io["c_i2"], writes=["c_i2"])
    P.dma("sp", c["maskR"][:], io["c_maskR"], writes=["c_maskR"])
    P.dma("pool", c["tri"][:], io["c_tri"], writes=["c_tri"])
    P.op("pool", lambda: nc.gpsimd.memset(c["zcol"][:], 0.0), writes=["c_zcol"])
    c["keys"] = ["c_identb", "c_identf", "c_b64", "c_b64m", "c_mskL", "c_mskU", "c_i2", "c_maskR", "c_tri", "c_zcol"]
    return c


B1_RATIO = 6
PRM = dict(mu_r=0, mu_k=1, mu_v=2, w0=3, a0=4, k_k=5, k_a=6, r_k=7, lnx_g=8, lnx_b=9, omk=10)


def stage_B1(P, io, cst, yT, pairs=range(8), tiles=range(9), lvl=99):
    nc = P.nc
    O = Ops(P)
    PJ = io["PJ"]
    ps = io["ps"]
    m0 = P.mark()
    CK = cst["keys"]
    prm = P.sb("prm", [128, 8, 11], F32)
    P.dma("sp", prm[:, :, 0:10], io["rw_prm"], writes=["prm"])
    O.ts("dve", prm[:, :, 10:11], prm[:, :, 6:7], -1.0, 1.0, ALU.mult, ALU.add, ["prm"], ["prm"])
    mul = P.sb("mul", [128, 3], F32)
    P.dma("sp", mul[:], io["rw_mul"], writes=["mul"])
    w2a2 = P.sb("w2a2", [128, 1024], BF16)
    g2a = P.sb("g2a", [128, 1024], BF16)
    g2b = P.sb("g2b", [32, 1024], BF16)
    P.dma("pool", w2a2[:], io["rw_w2a2"], writes=["w2a2"])
    P.dma("pool", g2a[:], io["rw_g2"][0:128, :], writes=["g2a"])
    P.dma("pool", g2b[:], io["rw_g2"][128:160, :], writes=["g2b"])
    Hst = P.sb("Hst", [128, 8, 64], F32)
    O.memset("pool", Hst[:], 0.0, ["H%d" % p for p in range(8)])

    def S(name, shape, dt, nb=2):
        return [P.sb(name + str(i), shape, dt) for i in range(nb)]
    xwc = S("xwc", [128, 512], F32); xwp = S("xwp", [128, 512], F32, 1) * 2
    xgc = S("xgc", [128, 512], F32); xgp = S("xgp", [128, 512], F32, 1) * 2
    xhc = S("xhc", [32, 512], F32, 1) * 2; xhp = S("xhp", [32, 512], F32, 1) * 2
    txa = S("txa", [128, 512], BF16); sga = S("sga", [128, 512], BF16); sgb = S("sgb", [32, 512], BF16)
    names32 = ["kc", "kp", "vc", "vp", "rc", "rp", "ks", "vs", "rs", "sw", "al", "gg", "kk", "t1", "kkn", "bb", "kh",
               "cum", "cumx", "e_r", "e_a", "e_n", "e_c", "bon"]
    U = {n: S(n, [128, 512], F32, 2 if n in ("kc", "kp", "vc", "vp", "bon", "gg") else 1) for n in names32}
    gC = S("gC", [128, 8], F32)
    QQ = S("QQ", [128, 8, 2, 64], BF16); KK = S("KK", [128, 8, 2, 64], BF16)
    BH = S("BH", [128, 8, 64], BF16); KH = S("KH", [128, 8, 64], BF16); VV = S("VV", [128, 8, 64], BF16)
    TMa = S("TMa", [64, 8, 128], BF16, 1) * 2; TMb = S("TMb", [64, 8, 128], BF16, 1) * 2
    TMk = S("TMk", [64, 8, 128], BF16, 1) * 2; TMv = S("TMv", [64, 8, 128], BF16, 1) * 2
    Lb = S("Lb", [64, 8, 64], BF16, 4); LTb = S("LTb", [64, 8, 64], BF16, 4)
    GB = S("GB", [64, 8, 128], BF16); GK = S("GK", [64, 8, 128], BF16)
    Zf = S("Zf", [64, 8, 128], F32); Zb = S("Zb", [64, 8, 128], BF16)
    MT = S("MT", [128, 8, 64], F32, 1) * 2; NN = S("NN", [128, 8, 64], F32, 1) * 2
    DG = S("DG", [128, 8, 64], F32, 1) * 2
    Hb = S("Hb", [128, 8, 64], BF16)
    QeT = S("QeT", [128, 8, 64], BF16)
    ysb = S("ysb", [128, 512], F32, 1) * 2; yc_ = S("yc", [128, 512], F32, 1) * 2; sq_ = S("sq", [128, 512], F32, 1) * 2
    rstd = S("rstd", [128, 512], F32, 1) * 2

    def v3(ap, n):
        return ap.rearrange("p (a b) -> p a b", a=n)

    def psv(bank, rows, n, w):
        return ps[bank][rows, 0:n * w].rearrange("p (a b) -> p a b", a=n)

    def psb(bank, rows, n, w):
        return ps[bank][:, :].bitcast(BF16)[rows, 0:n * w].rearrange("p (a b) -> p a b", a=n)

    def unit_gen(ct, p, u, first):
        samp = (ct == 8)
        t0 = NPR if samp else 512 * ct
        TW = 64 if samp else 512
        RW_ = NS if samp else 512
        NCH = TW // 64
        do_out = ct >= 6
        lb = ct % 2

        def load_pair(cur, prv, r0, rows, kc_, kp_, q="sp"):
            if samp:
                O.memset("pool", cur[0:rows, 0:TW], 0.0, [kc_])
                O.memset("pool", prv[0:rows, 0:TW], 0.0, [kp_])
                P.dma(q, cur[0:rows, 0:NS], PJ[r0:r0 + rows, NPR:NPR + NS], writes=[kc_])
                P.dma(q, prv[0:rows, 1:NS], PJ[r0:r0 + rows, NPR:NPR + NS - 1], writes=[kp_])
                P.dma(q, prv[0:rows, 0:1], io["shiftT"][r0:r0 + rows, :], writes=[kp_])
            else:
                P.dma(q, cur[0:rows, 0:TW], PJ[r0:r0 + rows, t0:t0 + TW], writes=[kc_])
                if t0 == 0:
                    O.memset("pool", prv[0:rows, 0:1], 0.0, [kp_])
                    P.dma(q, prv[0:rows, 1:TW], PJ[r0:r0 + rows, 0:TW - 1], writes=[kp_])
                else:
                    P.dma(q, prv[0:rows, 0:TW], PJ[r0:r0 + rows, t0 - 1:t0 + TW - 1], writes=[kp_])

        def shift(eng, out, cur, prv, mu, r, w, rows=128):
            O.tt(eng, prv[0:rows, 0:TW], prv[0:rows, 0:TW], cur[0:rows, 0:TW], ALU.subtract, r, [r[1]])
            if eng == "dve":
                O.stt(eng, out[0:rows, 0:TW], prv[0:rows, 0:TW], mu, cur[0:rows, 0:TW], ALU.mult, ALU.add, r, w)
            else:
                O.ts(eng, prv[0:rows, 0:TW], prv[0:rows, 0:TW], mu, None, ALU.mult, None, r, [r[1]])
                O.tt(eng, out[0:rows, 0:TW], prv[0:rows, 0:TW], cur[0:rows, 0:TW], ALU.add, r, w)

        kx = "L%d" % lb
        if first:
            load_pair(xwc[lb], xwp[lb], ROW["xwxa"], 128, kx + "xwc", "Lxwp")
            shift("dve", xwc[lb], xwc[lb], xwp[lb], mul[:, 0:1], [kx + "xwc", "Lxwp", "mul"], [kx + "xwc"])
            O.act(txa[lb][0:64, 0:TW], xwc[lb][0:64, 0:TW], AF.Tanh, [kx + "xwc"], [kx + "txa"])
            O.cp("pool", txa[lb][64:128, 0:TW], xwc[lb][64:128, 0:TW], [kx + "xwc"], [kx + "txa"])
            if do_out:
                load_pair(xgc[lb], xgp[lb], ROW["xg"], 128, kx + "xgc", "Lxgp")
                shift("pool", xgc[lb], xgc[lb], xgp[lb], mul[:, 1:2], [kx + "xgc", "Lxgp", "mul"], [kx + "xgc"])
                O.act(sga[lb][:, 0:TW], xgc[lb][:, 0:TW], AF.Sigmoid, [kx + "xgc"], [kx + "sga"])
                r0 = ROW["xg"] + 128
                if samp:
                    O.memset("pool", xhc[lb][:, 0:TW], 0.0, ["Lxhc"])
                    O.memset("pool", xhp[lb][:, 0:TW], 0.0, ["Lxhp"])
                    P.dma("sp", xhc[lb][:, 0:NS], PJ[r0:r0 + 32, NPR:NPR + NS], writes=["Lxhc"])
                    P.dma("sp", xhp[lb][:, 1:NS], PJ[r0:r0 + 32, NPR:NPR + NS - 1], writes=["Lxhp"])
                    P.dma("sp", xhp[lb][:, 0:1], io["shiftT"][r0:r0 + 32, :], writes=["Lxhp"])
                else:
                    P.dma("sp", xhc[lb][:, 0:TW], PJ[r0:r0 + 32, t0:t0 + TW], writes=["Lxhc"])
                    P.dma("sp", xhp[lb][:, 0:TW], PJ[r0:r0 + 32, t0 - 1:t0 + TW - 1], writes=["Lxhp"])
                shift("pool", xhc[lb], xhc[lb], xhp[lb], mul[0:32, 2:3], ["Lxhc", "Lxhp", "mul"], ["Lxhc"], rows=32)
                O.act(sgb[lb][:, 0:TW], xhc[lb][:, 0:TW], AF.Sigmoid, ["Lxhc"], [kx + "sgb"])

            yield "d"
        SINGLE = ("xhc", "xhp", "MT", "NN", "DG", "ysb", "yc", "sq", "rstd", "TMa", "TMb", "TMk", "TMv")
        K = (lambda n, u=u: "u%d_%s" % (0 if (n in SINGLE or (n in U and len(U[n]) == 1)) else u, n))
        T = (lambda n, u=u: U[n][u % len(U[n])])
        pc = lambda j: prm[:, p, j:j + 1]
        cols = slice(128 * p, 128 * p + 128)
        W_ = slice(0, TW)
        load_pair(T("kc"), T("kp"), ROW["rk"] + 128 * p, 128, K("kc"), K("kp"))
        load_pair(T("vc"), T("vp"), ROW["rv"] + 128 * p, 128, K("vc"), K("vp"))
        shift("dve", T("ks"), T("kc"), T("kp"), pc(1), [K("kc"), K("kp"), "prm"], [K("ks")])
        shift("pool", T("vs"), T("vc"), T("vp"), pc(2), [K("vc"), K("vp"), "prm"], [K("vs")])
        yield "d"
        if do_out:
            load_pair(T("rc"), T("rp"), ROW["r"] + 128 * p, 128, K("rc"), K("rp"))
            shift("pool", T("rs"), T("rc"), T("rp"), pc(0), [K("rc"), K("rp"), "prm"], [K("rs")])
        O.mm(ps[0][:, W_], w2a2[0:64, cols], txa[lb][0:64, W_], ["w2a2", kx + "txa"], ["ps0"])
        O.mm(ps[1][:, W_], w2a2[64:128, cols], txa[lb][64:128, W_], ["w2a2", kx + "txa"], ["ps1"])
        O.act(T("sw")[:, W_], ps[0][:, W_], AF.Sigmoid, ["ps0", "prm"], [K("sw")], bias=pc(3))
        O.act(T("al")[:, W_], ps[1][:, W_], AF.Sigmoid, ["ps1", "prm"], [K("al")], bias=pc(4))
        yield "d"
        if samp:
            O.memset("pool", T("sw")[:, NS:TW], 0.0, [K("sw")])
        if do_out:
            O.mm(ps[0][:, W_], g2a[:, cols], sga[lb][:, W_], ["g2a", kx + "sga"], ["ps0"], start=True, stop=False)
            O.mm(ps[0][:, W_], g2b[0:32, cols], sgb[lb][0:32, W_], ["g2b", kx + "sgb"], ["ps0"], start=False, stop=True)
            O.cp("act", T("gg")[:, W_], ps[0][:, W_], ["ps0"], [K("gg")])
        O.ts("pool", T("kk")[:, W_], T("ks")[:, W_], pc(5), None, ALU.mult, None, [K("ks"), "prm"], [K("kk")])
        O.tt("pool", T("t1")[:, W_], T("kk")[:, W_], T("kk")[:, W_], ALU.mult, [K("kk")], [K("t1")])
        O.mm(ps[1][:, W_], cst["b64"][:, :], T("t1")[:, W_], ["c_b64", K("t1")], ["ps1"])
        O.ts("dve", T("t1")[:, W_], ps[1][:, W_], 1e-24, None, ALU.max, None, ["ps1"], [K("t1")])
        O.rsqrt(T("t1")[:, W_], T("t1")[:, W_], [K("t1")], [K("t1")])
        yield "d"
        O.tt("pool", T("kkn")[:, W_], T("kk")[:, W_], T("t1")[:, W_], ALU.mult, [K("kk"), K("t1")], [K("kkn")])
        O.tt("dve", T("bb")[:, W_], T("kkn")[:, W_], T("al")[:, W_], ALU.mult, [K("kkn"), K("al")], [K("bb")])
        O.ts("dve", T("t1")[:, W_], T("al")[:, W_], pc(6), pc(10), ALU.mult, ALU.add, [K("al"), "prm", K("t1")], [K("t1")])
        O.tt("pool", T("kh")[:, W_], T("ks")[:, W_], T("t1")[:, W_], ALU.mult, [K("ks"), K("t1")], [K("kh")])
        yield "d"
        sc_o, sc_m, sc_i, sc_z = T("cum")[:, W_], cst["maskR"][:, W_], T("sw")[:, W_], cst["zcol"][:, 0:1]
        P.op("dve", (lambda sc_o=sc_o, sc_m=sc_m, sc_i=sc_i, sc_z=sc_z:
                     nc.vector.tensor_tensor_scan(sc_o, sc_m, sc_i, sc_z, ALU.mult, ALU.add)),
             reads=[K("sw")], writes=[K("cum")])
        O.tt("pool", T("cumx")[:, W_], T("cum")[:, W_], T("sw")[:, W_], ALU.subtract, [K("cum"), K("sw")], [K("cumx")])
        O.act(T("e_r")[:, W_], T("cum")[:, W_], AF.Exp, [K("cum")], [K("e_r")], scale=-C0)
        O.act(T("e_a")[:, W_], T("cumx")[:, W_], AF.Exp, [K("cumx")], [K("e_a")], scale=-C0)
        O.recip(T("e_n")[:, W_], T("e_r")[:, W_], [K("e_r")], [K("e_n")])
        yield "d"
        cv = v3(T("cum")[:, W_], NCH)
        O.tt("dve", v3(T("cumx")[:, W_], NCH), cv[:, :, 63:64].to_broadcast([128, NCH, 64]), cv, ALU.subtract,
             [K("cum"), K("e_a")], [K("cumx")])
        O.act(T("e_c")[:, W_], T("cumx")[:, W_], AF.Exp, [K("cumx")], [K("e_c")], scale=-C0)
        O.cp("pool", gC[u][:, 0:NCH], v3(T("e_r")[:, W_], NCH)[:, :, 63], [K("e_r")], [K("gC")])
        yield "d"
        qq = QQ[u]; kq = KK[u]
        O.stt("dve", qq[:, 0:NCH, 0, :], v3(T("kkn")[:, W_], NCH), -1.0, v3(T("e_a")[:, W_], NCH), ALU.mult, ALU.mult,
              [K("kkn"), K("e_a")], [K("QQ")])
        if do_out:
            O.tt("pool", qq[:, 0:NCH, 1, :], v3(T("rs")[:, W_], NCH), v3(T("e_r")[:, W_], NCH), ALU.mult,
                 [K("rs"), K("e_r")], [K("QQ")])
        O.tt("dve", kq[:, 0:NCH, 0, :], v3(T("bb")[:, W_], NCH), v3(T("e_n")[:, W_], NCH), ALU.mult, [K("bb"), K("e_n")], [K("KK")])
        O.tt("pool", kq[:, 0:NCH, 1, :], v3(T("kh")[:, W_], NCH), v3(T("e_n")[:, W_], NCH), ALU.mult, [K("kh"), K("e_n")], [K("KK")])
        O.tt("dve", BH[u][:, 0:NCH, :], v3(T("bb")[:, W_], NCH), v3(T("e_c")[:, W_], NCH), ALU.mult, [K("bb"), K("e_c")], [K("BH")])
        O.tt("pool", KH[u][:, 0:NCH, :], v3(T("kh")[:, W_], NCH), v3(T("e_c")[:, W_], NCH), ALU.mult, [K("kh"), K("e_c")], [K("KH")])
        yield "d"
        O.cp("act", VV[u][:, 0:NCH, :], v3(T("vs")[:, W_], NCH), [K("vs")], [K("VV")])
        if do_out:
            O.stt("dve", T("bon")[:, W_], T("rs")[:, W_], pc(7), T("kh")[:, W_], ALU.mult, ALU.mult, [K("rs"), K("kh"), "prm"], [K("bon")])
            O.mm(ps[0][:, W_], cst["b64"][:, :], T("bon")[:, W_], ["c_b64", K("bon")], ["ps0"])
            O.tt("dve", T("bon")[:, W_], ps[0][:, W_], T("vs")[:, W_], ALU.mult, ["ps0", K("vs"), K("bon")], [K("bon")])
        yield "DONE_D"
        if lvl < 1:
            return
        O.tt("pool", DG[u][:, 0:NCH, :], cst["i2"][:, :].unsqueeze(1).to_broadcast([128, NCH, 64]),
             gC[u][:, 0:NCH].unsqueeze(2).to_broadcast([128, NCH, 64]), ALU.mult, ["c_i2", K("gC")], [K("DG")])
        for (src, skey, dst, dkey, bank) in ((qq, K("QQ"), TMa[u], K("TMa"), 3), (BH[u], K("BH"), TMb[u], K("TMb"), 4),
                                             (KH[u], K("KH"), TMk[u], K("TMk"), 3), (VV[u], K("VV"), TMv[u], K("TMv"), 4)):
            pv = psb(bank, slice(0, 64), NCH, 128)
            for i in range(NCH):
                s_ap = src[:, i, 0, :] if src is qq else src[:, i, :]
                O.tr(pv[:, i, :], s_ap, cst["identb"][:, :], [skey, "c_identb"], ["ps%d" % bank])
            O.cp(O.pick(("act", "dve")), dst[:, 0:NCH, :], pv, ["ps%d" % bank], [dkey])
            yield "c"
        if lvl < 2:
            return
        HS = []
        NQ = 128 if do_out else 64
        for h in range(2):
            R = slice(64 * h, 64 * h + 64)
            Zf_, Zb_ = Zf[h], Zb[h]
            kz = "Z%d" % h
            GB_, GK_ = GB[h], GK[h]
            kg = "G%d" % h
            gab = psv(3, slice(0, 64), NCH, 64)
            for i in range(NCH):
                O.mm(gab[:, i, :], qq[R, i, 0, :], kq[R, i, 0, :], [K("QQ"), K("KK")], ["ps3"])
            for i in range(NCH):
                bk = 4 + i // 4
                o_ = ps[bk][0:64, (i % 4) * 128:(i % 4) * 128 + NQ]
                O.mm(o_, kq[R, i, 0, :], qq[R, i, :, :].rearrange("p a b -> p (a b)")[:, 0:NQ], [K("QQ"), K("KK")], ["ps%d" % bk])
            for i in range(NCH):
                bk = 6 + i // 4
                o_ = ps[bk][0:64, (i % 4) * 128:(i % 4) * 128 + NQ]
                O.mm(o_, kq[R, i, 1, :], qq[R, i, :, :].rearrange("p a b -> p (a b)")[:, 0:NQ], [K("QQ"), K("KK")], ["ps%d" % bk])
            li = 2 * h
            O.tt("dve", Lb[li][:, 0:NCH, :], gab, cst["mskL"][:, :].unsqueeze(1).to_broadcast([64, NCH, 64]), ALU.mult,
                 ["ps3", "c_mskL"], ["Lb%d" % li])
            yield "c"
            for (bank0, G_, kname) in ((4, GB_, kg + "B"), (6, GK_, kg + "K")):
                for hb in range((NCH + 3) // 4):
                    n4 = min(4, NCH - 4 * hb)
                    src = ps[bank0 + hb][0:64, 0:n4 * 128].rearrange("p (a b) -> p a b", a=n4)[:, :, 0:NQ]
                    O.tt("dve", G_[:, 4 * hb:4 * hb + n4, 0:NQ], src,
                         cst["mskU"][:, 0:NQ].unsqueeze(1).to_broadcast([64, n4, NQ]), ALU.mult,
                         ["ps%d" % (bank0 + hb), "c_mskU"], [kname])
            O.cp("pool", LTb[li][:, 0:NCH, :], GB_[:, 0:NCH, 0:64], [kg + "B"], ["LTb%d" % li])
            yield "c"
            if lvl < 3:
                continue
            xp = psv(3, slice(0, 64), NCH, 64)
            for i in range(NCH):
                O.mm(xp[:, i, :], GK_[:, i, 0:64], TMv[u][:, i, R], [kg + "K", K("TMv")], ["ps3"])
            O.cp("act", Zf_[:, 0:NCH, 0:64], TMa[u][:, 0:NCH, R], [K("TMa")], [kz + "f"])
            O.cp("dve", Zf_[:, 0:NCH, 64:128], xp, ["ps3"], [kz + "f"])
            yield "c"
            O.cp("pool", Zb_[:, 0:NCH, :], Zf_[:, 0:NCH, :], [kz + "f"], [kz + "b"])
            HS.append(dict(h=h, R=R, Zf=Zf_, Zb=Zb_, kz=kz, GB=GB_, GK=GK_, kg=kg, li=li, cur=li, zb0=2 + 2 * h))
        for j in range(6):
            for hd in HS:
                cur, li, Zb_, Zf_, kz, zb0 = hd["cur"], hd["li"], hd["Zb"], hd["Zf"], hd["kz"], hd["zb0"]
                for i in range(NCH):
                    bk = zb0 + i // 4
                    o_ = ps[bk][0:64, (i % 4) * 128:(i % 4) * 128 + 128]
                    O.mm(o_, LTb[cur][:, i, :], Zb_[:, i, :], ["LTb%d" % cur, kz + "b"], ["ps%d" % bk])
                yield "c"
                if j < 5:
                    nxt = li + 1 if cur == li else li
                    pn = psv(6, slice(0, 64), NCH, 64)
                    ptn = psv(7, slice(0, 64), NCH, 64)
                    for i in range(NCH):
                        O.mm(pn[:, i, :], LTb[cur][:, i, :], Lb[cur][:, i, :], ["LTb%d" % cur, "Lb%d" % cur], ["ps6"])
                    for i in range(NCH):
                        O.mm(ptn[:, i, :], Lb[cur][:, i, :], LTb[cur][:, i, :], ["LTb%d" % cur, "Lb%d" % cur], ["ps7"])
                    yield "c"
                for hb in range((NCH + 3) // 4):
                    n4 = min(4, NCH - 4 * hb)
                    src = ps[zb0 + hb][0:64, 0:n4 * 128].rearrange("p (a b) -> p a b", a=n4)
                    O.tt("dve", Zb_[:, 4 * hb:4 * hb + n4, :], Zf_[:, 4 * hb:4 * hb + n4, :], src, ALU.add,
                         ["ps%d" % (zb0 + hb), kz + "f"], [kz + "b"])
                if j < 5:
                    for hb in range((NCH + 3) // 4):
                        n4 = min(4, NCH - 4 * hb)
                        src = ps[zb0 + hb][0:64, 0:n4 * 128].rearrange("p (a b) -> p a b", a=n4)
                        O.tt("dve", Zf_[:, 4 * hb:4 * hb + n4, :], Zf_[:, 4 * hb:4 * hb + n4, :], src, ALU.add,
                             ["ps%d" % (zb0 + hb), kz + "f"], [kz + "f"])
                    O.cp("act", Lb[nxt][:, 0:NCH, :], pn, ["ps6"], ["Lb%d" % nxt])
                    O.cp("act", LTb[nxt][:, 0:NCH, :], ptn, ["ps7"], ["LTb%d" % nxt])
                    hd["cur"] = nxt
                yield "c"
        if lvl < 4:
            return
        for hd in HS:
            h, R, Zb_, kz, GB_, kg = hd["h"], hd["R"], hd["Zb"], hd["kz"], hd["GB"], hd["kg"]
            mtp = psv(3, R, NCH, 64)
            npp = psv(6, R, NCH, 64)
            for i in range(NCH):
                O.mm(mtp[:, i, :], Zb_[:, i, 0:64], TMb[u][:, i, R], [kz + "b", K("TMb")], ["ps3"])
            for i in range(NCH):
                O.mm(npp[:, i, :], TMb[u][:, i, R], Zb_[:, i, 64:128], [kz + "b", K("TMb")], ["ps6"], start=True, stop=False)
                O.mm(npp[:, i, :], TMk[u][:, i, R], TMv[u][:, i, R], [K("TMk"), K("TMv")], ["ps6"], start=False, stop=True)
            if do_out:
                qep = psv(7, R, NCH, 64)
                for i in range(NCH):
                    O.mm(qep[:, i, :], Zb_[:, i, 0:64], GB_[:, i, 64:128], [kz + "b", kg + "B"], ["ps7"])
            O.tt("dve", MT[u][R, 0:NCH, :], mtp, DG[u][R, 0:NCH, :], ALU.add, ["ps3", K("DG")], [K("MT")])
            O.cp("act", NN[u][R, 0:NCH, :], npp, ["ps6"], [K("NN")])
            yield "c"
            if do_out:
                O.tt("dve", QeT[u][R, 0:NCH, :], qep, qq[R, 0:NCH, 1, :], ALU.add, ["ps7", K("QQ")], [K("QeT")])
        if lvl < 5:
            return
        Hcur = Hst[:, p, :]
        if samp:
            P.dma("sp", Hst[:, p, :], io["s0T"][:, p, :], reads=[], writes=["H%d" % p])
        hkey = "H%d" % p
        stp = psv(3, slice(0, 128), NCH, 64)
        for i in range(NCH):
            if do_out:
                O.cp("act", Hb[u][:, i, :], Hcur, [hkey], [K("Hb")])
            for h in range(2):
                R = slice(64 * h, 64 * h + 64)
                O.mm(stp[R, i, :], MT[u][R, i, :], Hst[R, p, :], [K("MT"), hkey], ["ps3"])
            O.tt("dve", Hcur, stp[:, i, :], NN[u][:, i, :], ALU.add, ["ps3", K("NN"), hkey], [hkey])
            yield "c"
        if samp:
            P.dma("sp", io["wkv_s"][:, p, :], Hst[:, p, :], reads=[hkey], semkey="wkvs%d" % p, final=True)
        elif ct == 7:
            P.dma("sp", io["wkv_p"][:, p, :], Hst[:, p, :], reads=[hkey], semkey="wkvp%d" % p, final=True)
        if lvl < 6:
            return
        if do_out:
            ytp = psv(7, slice(0, 128), NCH, 64)
            yt1 = psv(4, slice(0, 128), NCH, 64)
            for h in range(2):
                R = slice(64 * h, 64 * h + 64)
                kz = "Z%d" % h
                kg = "G%d" % h
                for i in range(NCH):
                    O.mm(yt1[R, i, :], Hb[u][R, i, :], QeT[u][R, i, :], [K("Hb"), K("QeT")], ["ps4"])
                for i in range(NCH):
                    O.mm(ytp[R, i, :], Zb[h][:, i, 64:128], GB[h][:, i, 64:128], [kz + "b", kg + "B"], ["ps7"], start=True, stop=False)
                    O.mm(ytp[R, i, :], TMv[u][:, i, R], GK[h][:, i, 64:128], [K("TMv"), kg + "K"], ["ps7"], start=False, stop=True)
                    yield "c"
            y_ = ysb[u]
            O.cp("act", y_[:, W_], ps[4][:, W_], ["ps4"], [K("ysb")])
            O.tt("dve", y_[:, W_], y_[:, W_], ps[7][:, W_], ALU.add, ["ps7", K("ysb")], [K("ysb")])
            O.mm(ps[3][:, W_], cst["b64m"][:, :], y_[:, W_], ["c_b64m", K("ysb")], ["ps3"])
            O.tt("dve", yc_[u][:, W_], y_[:, W_], ps[3][:, W_], ALU.subtract, ["ps3", K("ysb")], [K("yc")])
            O.tt("pool", sq_[u][:, W_], yc_[u][:, W_], yc_[u][:, W_], ALU.mult, [K("yc")], [K("sq")])
            O.mm(ps[3][:, W_], cst["b64m"][:, :], sq_[u][:, W_], ["c_b64m", K("sq")], ["ps3"])
            O.ts("dve", rstd[u][:, W_], ps[3][:, W_], 64e-5, None, ALU.add, None, ["ps3"], [K("rstd")])
            O.rsqrt(rstd[u][:, W_], rstd[u][:, W_], [K("rstd")], [K("rstd")])
            yield "c"
            O.tt("pool", yc_[u][:, W_], yc_[u][:, W_], rstd[u][:, W_], ALU.mult, [K("yc"), K("rstd")], [K("yc")])
            O.ts("dve", yc_[u][:, W_], yc_[u][:, W_], pc(8), pc(9), ALU.mult, ALU.add, [K("yc"), "prm"], [K("yc")])
            O.tt("pool", yc_[u][:, W_], yc_[u][:, W_], T("bon")[:, W_], ALU.add, [K("yc"), K("bon")], [K("yc")])
            oc0 = 1024 if samp else (ct - 6) * 512
            O.tt("dve", yT[:, p, oc0:oc0 + RW_], yc_[u][:, 0:RW_], T("gg")[:, 0:RW_], ALU.mult, [K("yc"), K("gg")], ["yT%d" % p])

    units = []
    cnt_u = 0
    for ct in tiles:
        for pi_, p in enumerate(pairs):
            units.append(unit_gen(ct, p, cnt_u % 2, pi_ == 0))
            cnt_u += 1
    prev = None
    for g in units:
        d_done = False
        c_done = prev is None
        while not (d_done and c_done):
            if not d_done:
                try:
                    if next(g) == "DONE_D":
                        d_done = True
                except StopIteration:
                    d_done = True
                    g = None
            for _rep in range(B1_RATIO):
                if not c_done:
                    try:
                        next(prev)
                    except StopIteration:
                        c_done = True
        prev = g
    if prev is not None:
        for _ in prev:
            pass
    P.barrier()
    P.release(m0)


IN_SPECS = dict(
    xT=[D, NTOK], w_in=[D, 7472], shiftT=[N_FM, 1], s0T=[128, 8, 64],
    rw_prm=[128, 8, 10], rw_mul=[128, 3], rw_w2a2=[128, 1024], rw_g2=[160, 1024],
    c_ident=[128, 128], c_b64=[128, 128], c_b64m=[128, 128], c_mskL=[64, 64], c_mskU=[64, 128], c_i2=[128, 64],
    c_maskR=[128, 512], c_tri=[128, 128], c_cvec=[65, 1],
    xo=[NOWN, D], w_o=[D, D], ln1_g=[1, D], ln1_b=[1, D], w_up=[D, DFF], w_down=[DFF, D], ln2_g=[1, D], ln2_b=[1, D],
    fx_g=[64, 16], fx_bf=[16, 1], fx_padb=[128, 32], fx_clfT=[16, 1024], fx_ckT=[16, 64, 1024], fx_cv=[1024, 1024],
)


def make_consts():
    c = {}
    c["c_ident"] = np.eye(128, dtype=np.float32)
    b = np.zeros((128, 128), np.float32)
    b[:64, :64] = 1.0
    b[64:, 64:] = 1.0
    c["c_b64"] = b
    c["c_b64m"] = b / 64.0
    t = np.arange(64)
    c["c_mskL"] = (t[:, None] > t[None, :]).astype(np.float32)
    c["c_mskU"] = np.concatenate([(t[:, None] < t[None, :]), (t[:, None] <= t[None, :])], 1).astype(np.float32)
    c["c_i2"] = np.concatenate([np.eye(64), np.eye(64)], 0).astype(np.float32)
    m = np.ones((128, 512), np.float32)
    m[:, ::64] = 0.0
    c["c_maskR"] = m
    k = np.arange(128)
    c["c_tri"] = (k[:, None] <= k[None, :]).astype(np.float32)
    cv = np.full((65, 1), 1.0 / 64.0, np.float32)
    cv[64, 0] = 1e-6
    c["c_cvec"] = cv
    return c


def build(stages=("A", "B1", "B2", "C"), debug=(), pj_input=False, b1_kw=None, b2_kw=None, c_kw=None, extra_in=()):
    P = Prog()
    nc = P.nc
    io = {}
    need = set()
    if "A" in stages:
        need |= {"xT", "w_in"}
    if "B1" in stages:
        need |= {"shiftT", "s0T", "rw_prm", "rw_mul", "rw_w2a2", "rw_g2"}
    if "B2" in stages:
        need |= {"fx_g", "fx_bf", "fx_padb", "fx_clfT", "fx_ckT", "fx_cv"}
    if "C" in stages:
        need |= {"xo", "w_o", "ln1_g", "ln1_b", "w_up", "w_down", "ln2_g", "ln2_b"}
    need |= {k for k in IN_SPECS if k.startswith("c_")}
    need |= set(extra_in)
    for name, shape in IN_SPECS.items():
        if name in need:
            io[name] = nc.dram_tensor(name, list(shape), F32, kind="ExternalInput").ap()
    kind = "ExternalInput" if pj_input else ("ExternalOutput" if "PJ" in debug else "Internal")
    io["PJ"] = nc.dram_tensor("PJ", [N_FM, NTOK], F32, kind=kind).ap()
    io["VTM"] = nc.dram_tensor("VTM", [NTOK, 1024], F32, kind=kind).ap()
    io["wkv_p"] = nc.dram_tensor("wkv_p", [128, 8, 64], F32, kind="ExternalOutput").ap()
    io["wkv_s"] = nc.dram_tensor("wkv_s", [128, 8, 64], F32, kind="ExternalOutput").ap()
    io["spl_p"] = nc.dram_tensor("spl_p", [16, 3, 1024], BF16, kind="Internal").ap()
    io["spl_s"] = nc.dram_tensor("spl_s", [16, 3, NS], BF16, kind="Internal").ap()
    io["logf_o"] = nc.dram_tensor("logf_o", [16, NOWN], F32, kind="ExternalOutput").ap()
    io["ps"] = [nc.alloc_psum_tensor("psb%d" % i, [128, 512], F32) for i in range(8)]
    cst = load_consts(P, io)
    if "A" in stages:
        stage_A(P, io)
    else:
        P.barrier()
    yT = P.sb("yT", [128, 16, NOWN], BF16)
    if "A" in stages or "outs" in debug:
        io["k_out"] = nc.dram_tensor("k_out", [1024, NOWN], F32, kind="ExternalOutput").ap()
        io["v_out"] = nc.dram_tensor("v_out", [NOWN, 1024], F32, kind="ExternalOutput").ap()
        io["shA"] = nc.dram_tensor("shA", [2176, 2], F32, kind="ExternalOutput").ap()
        io["shB"] = nc.dram_tensor("shB", [1184, 2], F32, kind="ExternalOutput").ap()
        PJ = io["PJ"]
        P.dma("sp", io["k_out"], PJ[ROW["fk"]:ROW["fk"] + 1024, OWN0:NTOK], semkey="o_k", final=True)
        P.dma("sp", io["v_out"], io["VTM"][OWN0:NTOK, :], semkey="o_v", final=True)
        for ci, col in enumerate((NPR - 1, NTOK - 1)):
            P.dma("sp", io["shA"][:, ci:ci + 1], PJ[0:2176, col:col + 1], semkey="o_sa", final=True, allow_slow_non_contiguous=True)
            P.dma("sp", io["shB"][:, ci:ci + 1], PJ[ROW["r"]:ROW["r"] + 1184, col:col + 1], semkey="o_sb", final=True,
                  allow_slow_non_contiguous=True)
    if "B1" in stages:
        stage_B1(P, io, cst, yT, **(b1_kw or {}))
    if "B2" in stages:
        stage_B2(P, io, cst, yT, **(b2_kw or {}))
    if "yTin" in debug:
        io["yT_in"] = nc.dram_tensor("yT_in", [128, 16, NOWN], F32, kind="ExternalInput").ap()
        P.dma("pool", yT[:], io["yT_in"], writes=["yT%d" % i for i in range(16)], semkey="ytin")
    if "C" in stages:
        io["y_out"] = nc.dram_tensor("y_out", [NOWN, D], F32, kind="ExternalOutput").ap()
        stage_C(P, io, cst, yT, **(c_kw or {}))
    if "yT" in debug:
        io["yT_dbg"] = nc.dram_tensor("yT_dbg", [128, 16, NOWN], F32, kind="ExternalOutput").ap()
        P.dma("pool", io["yT_dbg"], yT[:], reads=["yT%d" % i for i in range(16)], semkey="ytdbg", final=True)
    P.emit()
    return P


def stage_B2(P, io, cst, yT, heads=range(16), insts=("p", "s")):
    nc = P.nc
    O = Ops(P)
    PJ = io["PJ"]
    VTM = io["VTM"]
    ps = io["ps"]
    m0 = P.mark()
    ones = P.sb("ones", [128, 512], F32)
    O.memset("pool", ones[:], 1.0, ["ones"])
    fog = P.sb("fog", [64, 16], F32)
    P.dma("sp", fog[:], io["fx_g"], writes=["fog"])
    nbf = P.sb("nbf", [16, 1], F32)
    P.dma("sp", nbf[:], io["fx_bf"], writes=["nbf"])
    O.ts("dve", nbf[:], nbf[:], -1.0, None, ALU.mult, None, ["nbf"], ["nbf"])
    cvec = P.sb("cvec", [65, 1], F32)
    P.dma("sp", cvec[:], io["c_cvec"], writes=["cvec"])
    padb = P.sb("padb", [128, 32], F32)
    P.dma("sp", padb[:], io["fx_padb"], writes=["padb"])
    Crow = P.sb("Crow", [16, NPR], F32)
    Ccum = P.sb("Ccum", [16, NPR], F32)
    Ckb = P.sb("Ckb", [128, 32, 16], F32)
    spl = P.sb("spl", [16, 3, 1024], BF16)
    q8 = P.sb("q8", [16, 1024], F32)
    lfo = P.sb("lfo", [16, NOWN], F32)
    KTa = [P.sb("KTa%d" % i, [67, NPR], BF16) for i in range(2)]
    QTa = [P.sb("QTa%d" % i, [67, 1024], BF16) for i in range(2)]
    Va = [P.sb("Va%d" % i, [128, 32, 65], BF16) for i in range(2)]
    ogT = [P.sb("ogT%d" % i, [64, 1024], F32) for i in range(2)]
    pt = [P.sb("pt%d" % i, [128, 512], BF16) for i in range(4)]
    Oa = P.sb("Oa", [65, 512], F32)
    sq = P.sb("sqf", [65, 512], F32)
    rs = P.sb("rs", [1, 512], F32)
    y1 = P.sb("y1", [64, 512], F32)
    ytmp = [P.sb("ytmp%d" % i, [64, 512], BF16) for i in range(2)]
    for i in range(2):
        O.memset("pool", KTa[i][64:67, :], 1.0, ["KTa%d" % i])
        O.memset("pool", Va[i][:, :, 64:65], 1.0, ["Va%d" % i])
    cnt = dict(h=0, s=0, p=0, y=0)

    for inst in insts:
        if inst == "p":
            TK, NQ, QPOS0 = NPR, 1024, OWN0
            qtiles = [(0, 512), (512, 512)]
            ocol0 = 0
        else:
            TK, NQ, QPOS0 = 1024 + NS, NS, 1024
            qtiles = [(0, NS)]
            ocol0 = 1024
        nblk = (TK + 127) // 128
        if inst == "p":
            P.dma("sp", Crow[:, 0:TK], PJ[ROW["f"]:ROW["f"] + 16, 0:TK], writes=["Crow"])
            fsl = slice(0, TK)
        else:
            P.dma("sp", Crow[:, 0:1024], io["fx_clfT"], writes=["Crow"])
            P.dma("sp", Crow[:, 1024:TK], PJ[ROW["f"]:ROW["f"] + 16, NPR:NPR + NS], writes=["Crow"], semkey="Crow_b")
            O.ts("dve", Crow[:, 0:1024], Crow[:, 0:1024], -1.0, None, ALU.mult, None, ["Crow"], ["Crow"])
            fsl = slice(1024, TK)
        O.act(Crow[:, fsl], Crow[:, fsl], AF.Exp, ["Crow", "nbf"], ["Crow"], bias=nbf[:, 0:1], scale=-1.0)
        O.act(Crow[:, fsl], Crow[:, fsl], AF.Ln, ["Crow"], ["Crow"], bias=1.0)
        if inst == "p":
            P.op("act", lambda: nc.scalar.mul(lfo[:, 0:1024], Crow[:, OWN0:NPR], -1.0), reads=["Crow"], writes=["lfo"])
        else:
            P.op("act", lambda: nc.scalar.mul(lfo[:, 1024:NOWN], Crow[:, 1024:TK], -1.0), reads=["Crow"], writes=["lfo"])
            P.dma("sp", io["logf_o"], lfo[:], reads=["lfo"], semkey="lfo", final=True)
        for c0 in range(0, TK, 512):
            cw = min(512, TK - c0)
            init = cst["zcol"][0:16, 0:1] if c0 == 0 else Ccum[:, c0 - 1:c0]
            o_ap, d0, d1 = Ccum[:, c0:c0 + cw], ones[0:16, 0:cw], Crow[:, c0:c0 + cw]
            P.op("dve", (lambda o_ap=o_ap, d0=d0, d1=d1, init=init: nc.vector.tensor_tensor_scan(o_ap, d0, d1, init, ALU.mult, ALU.add)),
                 reads=["Crow", "ones", "Ccum"], writes=["Ccum"])
        tp = ps[6][:, 0:nblk * 16].rearrange("p (a b) -> p a b", a=nblk)
        for kb in range(nblk):
            nk = min(128, TK - kb * 128)
            O.tr(tp[0:nk, kb, :], Ccum[:, kb * 128:kb * 128 + nk], cst["identf"][0:16, 0:16], ["Ccum", "c_identf"], ["ps6"])
        nfull = TK // 128
        if inst == "p":
            O.tt("dve", Ckb[:, 0:nblk, :], tp[:, 0:nblk, :], padb[:, 0:nblk].unsqueeze(2).to_broadcast([128, nblk, 16]), ALU.add,
                 ["ps6", "padb"], ["Ckb"])
        else:
            O.cp("dve", Ckb[:, 0:nfull, :], tp[:, 0:nfull, :], ["ps6"], ["Ckb"])
            O.cp("dve", Ckb[0:NS, nfull, :], tp[0:NS, nfull, :], ["ps6", "Ckb"], ["Ckb"])
        qs = slice(QPOS0, QPOS0 + NQ)
        O.ts("dve", q8[:, 0:NQ], Ccum[:, qs], -8.0, None, ALU.mult, None, ["Ccum"], ["q8"])
        for k3 in range(3):
            O.cp("dve", spl[:, k3, 0:NQ], q8[:, 0:NQ], ["q8"], ["spl"])
            if k3 < 2:
                O.tt("dve", q8[:, 0:NQ], q8[:, 0:NQ], spl[:, k3, 0:NQ], ALU.subtract, ["q8", "spl"], ["q8"])
        sk = "splD" + inst
        P.dma("sp", io["spl_" + inst], spl[:, :, 0:NQ], reads=["spl"], writes=[sk])
        for h in heads:
            hb = cnt["h"] % 2
            cnt["h"] += 1
            kK, kQ, kV, kG = "KTa%d" % hb, "QTa%d" % hb, "Va%d" % hb, "ogT%d" % hb
            fk0 = ROW["fk"] + 64 * h
            if inst == "p":
                P.dma("pool", KTa[hb][0:64, 0:TK], PJ[fk0:fk0 + 64, 0:TK], writes=[kK])
                P.dma("pool", Va[hb][:, 0:32, 0:64], VTM[0:NPR, 64 * h:64 * h + 64].rearrange("(b p) d -> p b d", p=128), writes=[kV])
                ocs = slice(OWN0, NPR)
            else:
                P.dma("pool", KTa[hb][0:64, 0:1024], io["fx_ckT"][h], writes=[kK])
                P.dma("pool", KTa[hb][0:64, 1024:TK], PJ[fk0:fk0 + 64, NPR:NPR + NS], writes=[kK])
                P.dma("pool", Va[hb][:, 0:8, 0:64], io["fx_cv"][:, 64 * h:64 * h + 64].rearrange("(b p) d -> p b d", p=128), writes=[kV])
                P.dma("pool", Va[hb][0:NS, 8, 0:64], VTM[NPR:NPR + NS, 64 * h:64 * h + 64], writes=[kV])
                ocs = slice(NPR, NPR + NS)
            P.dma("pool", QTa[hb][0:64, 0:NQ], PJ[ROW["q"] + 64 * h:ROW["q"] + 64 * h + 64, ocs], writes=[kQ])
            P.dma("sp", QTa[hb][64:67, 0:NQ], io["spl_" + inst][h], reads=[sk], writes=[kQ])
            P.dma("sp", ogT[hb][:, 0:NQ], PJ[ROW["og"] + 64 * h:ROW["og"] + 64 * h + 64, ocs], writes=[kG])
            for (q0, qw) in qtiles:
                ob = 2 + cnt["y"] % 2
                okey = "ps%d" % ob
                last_q = QPOS0 + q0 + qw - 1
                kbs = [kb for kb in range(nblk) if kb * 128 <= last_q]
                pend = []
                for n_, kb in enumerate(kbs):
                    nk = min(128, TK - kb * 128)
                    off = max(0, kb * 128 - (QPOS0 + q0))
                    diag = kb * 128 + nk - 1 > QPOS0 + q0
                    sb_ = (0, 1, 7)[cnt["s"] % 3]
                    cnt["s"] += 1
                    pb_ = cnt["p"] % 4
                    cnt["p"] += 1
                    skey, pkey = "ps%d" % sb_, "pt%d" % pb_
                    O.mm(ps[sb_][0:nk, off:qw], KTa[hb][0:67, kb * 128:kb * 128 + nk], QTa[hb][0:67, q0 + off:q0 + qw], [kK, kQ], [skey])
                    if len(pend) >= 2:
                        pend.pop(0)()
                    O.act(pt[pb_][0:nk, off:qw], ps[sb_][0:nk, off:qw], AF.Exp, [skey, "Ckb"], [pkey], bias=Ckb[0:nk, kb, h:h + 1], scale=0.125)
                    if diag:
                        mw = min(128, qw - off, nk)
                        O.tt(O.pick(("dve", "pool")), pt[pb_][0:nk, off:off + mw], pt[pb_][0:nk, off:off + mw], cst["tri"][0:nk, 0:mw], ALU.mult,
                             [pkey, "c_tri"], [pkey])
                    pend.append(lambda nk=nk, kb=kb, pb_=pb_, off=off, pkey=pkey, n_=n_:
                                O.mm(ps[ob][0:65, off:qw], Va[hb][0:nk, kb, 0:65], pt[pb_][0:nk, off:qw], [kV, pkey], [okey],
                                     start=(n_ == 0), stop=(n_ == len(kbs) - 1)))
                while pend:
                    pend.pop(0)()
                cnt["y"] += 1
                yb = cnt["y"] % 2
                O.cp("act", Oa[:, 0:qw], ps[ob][0:65, 0:qw], [okey], ["Oa"])
                O.act(sq[:, 0:qw], Oa[:, 0:qw], AF.Square, ["Oa"], ["sqf"])
                O.mm(ps[4][0:1, 0:qw], cvec[:, 0:1], sq[:, 0:qw], ["cvec", "sqf"], ["ps4"])
                O.act(rs[:, 0:qw], ps[4][0:1, 0:qw], AF.Sqrt, ["ps4"], ["rs"])
                O.recip(rs[:, 0:qw], rs[:, 0:qw], ["rs"], ["rs"])
                O.mm(ps[5][0:64, 0:qw], ones[0:1, 0:64], rs[:, 0:qw], ["ones", "rs"], ["ps5"])
                O.tt("dve", y1[:, 0:qw], Oa[0:64, 0:qw], ps[5][0:64, 0:qw], ALU.mult, ["Oa", "ps5"], ["y1"])
                O.act(ogT[hb][:, q0:q0 + qw], ogT[hb][:, q0:q0 + qw], AF.Sigmoid, [kG], [kG])
                O.stt("dve", ytmp[yb][:, 0:qw], y1[:, 0:qw], fog[:, h:h + 1], ogT[hb][:, q0:q0 + qw], ALU.mult, ALU.mult,
                      ["y1", "fog", kG], ["ytmp%d" % yb])
                r0 = (h % 2) * 64
                P.dma("sp", yT[r0:r0 + 64, 8 + h // 2, ocol0 + q0:ocol0 + q0 + qw], ytmp[yb][:, 0:qw],
                      reads=["ytmp%d" % yb], writes=["yT%d" % (8 + h // 2)], semkey="yT%d_%d" % (8 + h // 2, h % 2))
    P.barrier()
    P.release(m0)


TOKT = [(i * 128, 128) for i in range(8)] + [(1024, NS)]
TT3 = [(0, 512), (512, 512), (1024, NS)]


def layer_norm_tiles(P, O, Z, gvec, bvec, zkeys, out_dram=None):
    nc = P.nc
    m = P.mark()
    gb = P.sb("ln_g", [128, D], F32)
    bb = P.sb("ln_b", [128, D], F32)
    junk = P.sb("ln_junk", [128, D], F32)
    st = P.sb("ln_st", [128, 4], F32)
    P.dma("sp", gb[:], gvec.partition_broadcast(128), writes=["ln_g"])
    P.dma("sp", bb[:], bvec.partition_broadcast(128), writes=["ln_b"])
    for ti, (t0, tw) in enumerate(TOKT):
        z = Z[0:tw, ti, :]
        zk = zkeys[ti]
        s = st[0:tw, :]
        O.memset("pool", st[:, :], 0.0, ["ln_st"])
        P.op("act", (lambda z=z, tw=tw, s=s: nc.scalar.activation(out=junk[0:tw, :], in_=z, func=AF.Identity, accum_out=s[:, 0:1])),
             reads=[zk], writes=["ln_junk", "ln_st"])
        O.ts("dve", s[:, 1:2], s[:, 0:1], -1.0 / D, None, ALU.mult, None, ["ln_st"], ["ln_st"])
        O.act(z, z, AF.Identity, [zk, "ln_st"], [zk], bias=s[:, 1:2])
        P.op("act", (lambda z=z, tw=tw, s=s: nc.scalar.activation(out=junk[0:tw, :], in_=z, func=AF.Square, accum_out=s[:, 2:3])),
             reads=[zk, "ln_st"], writes=["ln_junk", "ln_st"])
        O.ts("dve", s[:, 3:4], s[:, 2:3], 1.0 / D, 1e-5, ALU.mult, ALU.add, ["ln_st"], ["ln_st"])
        O.rsqrt(s[:, 3:4], s[:, 3:4], ["ln_st"], ["ln_st"])
        O.stt("dve", z, z, s[:, 3:4], gb[0:tw, :], ALU.mult, ALU.mult, [zk, "ln_st", "ln_g"], [zk])
        O.tt("pool", z, z, bb[0:tw, :], ALU.add, [zk, "ln_b"], [zk])
        if out_dram is not None:
            P.dma("sp", out_dram[t0:t0 + tw, :], z, reads=[zk], semkey="yo%d" % ti, final=True)
    P.release(m)


def stage_C(P, io, cst, yT, dbg=None):
    nc = P.nc
    O = Ops(P)
    ps = io["ps"]
    m0 = P.mark()
    Z = P.sb("Z", [128, 9, D], F32)
    zk = ["Z%d" % i for i in range(9)]
    yk = ["yT%d" % i for i in range(16)]
    m1 = P.mark()
    wo = [P.sb("wo%d" % i, [128, 16, 512], BF16) for i in range(2)]
    wor = io["w_o"].rearrange("(kc p) c -> p kc c", p=128)
    for ti, (t0, tw) in enumerate(TOKT):
        P.dma("sp", Z[0:tw, ti, :], io["xo"][t0:t0 + tw, :], writes=[zk[ti]])
    pi = 0
    for cg in range(4):
        wb = cg % 2
        P.dma("pool", wo[wb][:], wor[:, :, cg * 512:cg * 512 + 512], writes=["wo%d" % wb])
        for ti, (t0, tw) in enumerate(TOKT):
            pb = pi % 4
            pi += 1
            for kc in range(16):
                O.mm(ps[pb][0:tw, :], yT[:, kc, t0:t0 + tw], wo[wb][:, kc, :], [yk[kc], "wo%d" % wb], ["ps%d" % pb],
                     start=(kc == 0), stop=(kc == 15))
            zs = Z[0:tw, ti, cg * 512:cg * 512 + 512]
            O.stt("dve", zs, zs, ALPHA, ps[pb][0:tw, :], ALU.mult, ALU.add, [zk[ti], "ps%d" % pb], [zk[ti]])
    def dump():
        for ti, (t0, tw) in enumerate(TOKT):
            P.dma("sp", io["y_out"][t0:t0 + tw, :], Z[0:tw, ti, :], reads=[zk[ti]], semkey="yo%d" % ti, final=True)
    if dbg == "z":
        dump()
        return
    layer_norm_tiles(P, O, Z, io["ln1_g"], io["ln1_b"], zk)
    if dbg == "h":
        dump()
        return
    hT = yT
    ei = 0
    for kc in range(16):
        for (grp, bank) in ((range(0, 4), 4), (range(4, 8), 5), (range(8, 9), 6)):
            for ti in grp:
                t0, tw = TOKT[ti]
                c0 = t0 - TOKT[grp[0]][0]
                O.tr(ps[bank][:, c0:c0 + tw], Z[0:tw, ti, kc * 128:kc * 128 + 128], cst["identf"][0:tw, 0:tw], [zk[ti], "c_identf"], ["ps%d" % bank])
            g0 = TOKT[grp[0]][0]
            gw = sum(TOKT[ti][1] for ti in grp)
            ei += 1
            O.cp("act" if ei % 2 else "dve", hT[:, kc, g0:g0 + gw], ps[bank][:, 0:gw], ["ps%d" % bank], [yk[kc]])
    for ti, (t0, tw) in enumerate(TOKT):
        O.ts("pool", Z[0:tw, ti, :], Z[0:tw, ti, :], ALPHA, None, ALU.mult, None, [zk[ti]], [zk[ti]])
    P.barrier()
    P.release(m1)
    m2 = P.mark()
    wu = [P.sb("wu%d" % i, [128, 16, 512], BF16) for i in range(2)]
    wd = [P.sb("wd%d" % i, [128, 4, D], BF16) for i in range(2)]
    aT = [P.sb("aT%d" % i, [128, 4, NOWN], BF16) for i in range(2)]
    rT = [P.sb("rT%d" % i, [128, 512], BF16) for i in range(3)]
    wur = io["w_up"].rearrange("(kc p) c -> p kc c", p=128)
    wdr = io["w_down"].rearrange("(g kc p) c -> g p kc c", p=128, kc=4)
    NG = DFF // 512

    def load_w(g):
        b = g % 2
        P.dma("pool", wu[b][:], wur[:, :, g * 512:g * 512 + 512], writes=["wu%d" % b])
        P.dma("pool", wd[b][:], wdr[g], writes=["wd%d" % b])
    load_w(0)
    st = dict(pu=0, pd=0, ri=0)

    def up(g):
        b = g % 2
        for blk in range(4):
            for (t0, tw) in TT3:
                pb = st["pu"] % 3
                st["pu"] += 1
                for kc in range(16):
                    O.mm(ps[pb][:, 0:tw], wu[b][:, kc, blk * 128:blk * 128 + 128], hT[:, kc, t0:t0 + tw], ["wu%d" % b, yk[kc]], ["ps%d" % pb],
                         start=(kc == 0), stop=(kc == 15))
                rb = st["ri"] % 3
                st["ri"] += 1
                O.act(rT[rb][:, 0:tw], ps[pb][:, 0:tw], AF.Relu, ["ps%d" % pb], ["rT%d" % rb])
                O.tt("pool", aT[b][:, blk, t0:t0 + tw], rT[rb][:, 0:tw], rT[rb][:, 0:tw], ALU.mult, ["rT%d" % rb], ["aT%d" % b])

    def down(g):
        b = g % 2
        for ti, (t0, tw) in enumerate(TOKT):
            for cg in range(4):
                pb = 3 + st["pd"] % 5
                st["pd"] += 1
                for blk in range(4):
                    O.mm(ps[pb][0:tw, :], aT[b][:, blk, t0:t0 + tw], wd[b][:, blk, cg * 512:cg * 512 + 512], ["aT%d" % b, "wd%d" % b], ["ps%d" % pb],
                         start=(blk == 0), stop=(blk == 3))
                zs = Z[0:tw, ti, cg * 512:cg * 512 + 512]
                O.tt("dve", zs, zs, ps[pb][0:tw, :], ALU.add, [zk[ti], "ps%d" % pb], [zk[ti]])
    up(0)
    for g in range(NG):
        if g + 1 < NG:
            load_w(g + 1)
            up(g + 1)
        down(g)
    P.barrier()
    P.release(m2)
    if dbg == "acc":
        dump()
        return
    layer_norm_tiles(P, O, Z, io["ln2_g"], io["ln2_b"], zk, out_dram=io["y_out"])
    P.release(m0)


_PROG = {}


def host_maps(inp):
    g = lambda k: np.asarray(inp[k], np.float32)[0]
    xp = np.asarray(inp["x_prompt"], np.float32)
    xs = np.asarray(inp["x_sample"], np.float32)
    w_in = g("w_in")
    wperm = np.ascontiguousarray(w_in[:, PERM])
    consts = make_consts()
    mu = g("rwkv_mu")
    cols = [mu[0:1024], mu[1024:2048], mu[2048:3072], g("rwkv_w0"), g("rwkv_a0"), g("rwkv_k_k"), g("rwkv_k_a"),
            g("rwkv_r_k").reshape(-1), g("rwkv_lnx_g"), g("rwkv_lnx_b")]
    rw_prm = np.ascontiguousarray(np.stack([v.reshape(8, 128) for v in cols], -1).transpose(1, 0, 2))
    rw_mul = np.zeros((128, 3), np.float32)
    rw_mul[:, 0] = mu[3072:3200]
    rw_mul[:, 1] = mu[3200:3328]
    rw_mul[:32, 2] = mu[3328:3360]
    shared = dict(consts)
    shared.update(
        w_in=wperm, rw_prm=rw_prm, rw_mul=rw_mul,
        rw_w2a2=np.ascontiguousarray(np.concatenate([g("rwkv_w2"), g("rwkv_a2")], 0)), rw_g2=g("rwkv_g2"),
        fx_g=np.ascontiguousarray(g("fox_out_g").reshape(16, 64).T), fx_bf=g("fox_b_f").reshape(16, 1),
        w_o=g("w_o"), ln1_g=g("ln1_g").reshape(1, D), ln1_b=g("ln1_b").reshape(1, D), w_up=g("w_up"), w_down=g("w_down"),
        ln2_g=g("ln2_g").reshape(1, D), ln2_b=g("ln2_b").reshape(1, D))
    maps = []
    pos = np.arange(NPR).reshape(32, 128).T
    for c in range(8):
        b, j = divmod(c, 4)
        n = 1024 * (j + 1)
        xpad = np.zeros((NTOK, D), np.float32)
        xpad[NPR - n:NPR] = xp[b, :n]
        xpad[NPR:] = xs[c]
        m = dict(shared)
        m["xT"] = np.ascontiguousarray(xpad.T)
        m["xo"] = np.ascontiguousarray(xpad[OWN0:NTOK])
        sh = np.zeros((N_FM, 1), np.float32)
        st = np.asarray(inp["state_rwkv_shift"], np.float32)[0, c, 0]
        for seg in ("rk", "rv", "xwxa", "r", "xg"):
            o, n_ = SEG[seg]
            sh[ROW[seg]:ROW[seg] + n_, 0] = st[o:o + n_]
        m["shiftT"] = sh
        s0 = np.asarray(inp["state_rwkv_wkv"], np.float32)[0, c]
        m["s0T"] = np.ascontiguousarray(s0.transpose(0, 2, 1).reshape(8, 128, 64).transpose(1, 0, 2))
        m["fx_padb"] = np.where(pos < NPR - n, -BIG, 0.0).astype(np.float32)
        m["fx_clfT"] = np.ascontiguousarray(np.asarray(inp["cache_fox_logf"], np.float32)[0, c].T)
        m["fx_ckT"] = np.ascontiguousarray(np.asarray(inp["cache_fox_k"], np.float32)[0, c].transpose(1, 2, 0))
        m["fx_cv"] = np.ascontiguousarray(np.asarray(inp["cache_fox_v"], np.float32)[0, c].reshape(1024, 1024))
        maps.append(m)
    return maps


def assemble(results):
    f32 = np.float32
    y_p = np.zeros((2, 4096, D), f32)
    y_s = np.zeros((8, NS, D), f32)
    pk = np.zeros((1, 2, 4096, 16, 64), f32)
    pv = np.zeros((1, 2, 4096, 16, 64), f32)
    pf = np.zeros((1, 2, 4096, 16), f32)
    pS = np.zeros((1, 2, 16, 64, 64), f32)
    psh = np.zeros((1, 2, 1, 3360), f32)
    sk = np.zeros((1, 8, NS, 16, 64), f32)
    sv = np.zeros((1, 8, NS, 16, 64), f32)
    sf = np.zeros((1, 8, NS, 16), f32)
    sS = np.zeros((1, 8, 16, 64, 64), f32)
    ssh = np.zeros((1, 8, 1, 3360), f32)

    def wkv(a):
        return a.reshape(2, 64, 8, 64).transpose(2, 0, 3, 1).reshape(16, 64, 64)

    def shift(r, ci):
        out = np.zeros(3360, f32)
        a, bq = r["shA"][:, ci], r["shB"][:, ci]
        out[1024:2048] = a[0:1024]
        out[2048:3072] = a[1024:2048]
        out[3072:3200] = a[2048:2176]
        out[0:1024] = bq[0:1024]
        out[3200:3360] = bq[1024:1184]
        return out
    for c in range(8):
        r = results[c]
        b, j = divmod(c, 4)
        sl = slice(1024 * j, 1024 * (j + 1))
        y_p[b, sl] = r["y_out"][0:1024]
        y_s[c] = r["y_out"][1024:NOWN]
        kT = r["k_out"]
        pk[0, b, sl] = kT[:, 0:1024].T.reshape(1024, 16, 64)
        sk[0, c] = kT[:, 1024:NOWN].T.reshape(NS, 16, 64)
        pv[0, b, sl] = r["v_out"][0:1024].reshape(1024, 16, 64)
        sv[0, c] = r["v_out"][1024:NOWN].reshape(NS, 16, 64)
        pf[0, b, sl] = r["logf_o"][:, 0:1024].T
        sf[0, c] = r["logf_o"][:, 1024:NOWN].T
        sS[0, c] = wkv(r["wkv_s"])
        ssh[0, c, 0] = shift(r, 1)
        if j == 3:
            pS[0, b] = wkv(r["wkv_p"])
            psh[0, b, 0] = shift(r, 0)
    return (y_p, y_s, pk, pv, pf, pS, psh, sk, sv, sf, sS, ssh)


def kernel(**inputs):
    if "full" not in _PROG:
        _PROG["full"] = build()
    P = _PROG["full"]
    maps = host_maps(inputs)
    res = run_bass_kernel_spmd(P.nc, maps, core_ids=list(range(8)))
    return assemble(res.results)
```

```python
import numpy as np
import concourse.bass as bass
import concourse.mybir as mybir
from concourse.bass_utils import run_bass_kernel_spmd

F32 = mybir.dt.float32
BF16 = mybir.dt.bfloat16
AF = mybir.ActivationFunctionType
ALU = mybir.AluOpType
AX = mybir.AxisListType

D = 2048
NPR = 4096
NS = 16
NTOK = NPR + NS
OWN0 = 3072
NOWN = 1024 + NS
DFF = 8192
C0 = float(np.exp(-0.5))
ALPHA = 2.0 ** 0.25
BIG = 30000.0

SEG = dict(rk=(1024, 1024), rv=(2048, 1024), xwxa=(3072, 128), fk=(3360 + 1024, 1024), f=(3360 + 3072, 16),
           r=(0, 1024), xg=(3200, 160), q=(3360, 1024), og=(3360 + 3088, 1024), fv=(3360 + 2048, 1024))
ORDER = ["rk", "rv", "xwxa", "fk", "f", "r", "xg", "q", "og", "fv"]
ROW = {}
_o = 0
for _k in ORDER:
    ROW[_k] = _o
    _o += SEG[_k][1]
N_ALL = ROW["r"]
N_OWN = ROW["fv"] - N_ALL
N_FM = ROW["fv"]
PERM = np.concatenate([np.arange(SEG[k][0], SEG[k][0] + SEG[k][1]) for k in ORDER])


EMBED_WAIT = True
SAME_ENG_NOWAIT = ()


class Prog:
    ENG = ("pe", "act", "dve", "pool", "sp")

    def __init__(self):
        self.nc = bass.Bass("TRN2", target_bir_lowering=False)
        nc = self.nc
        self.eng = {"pe": nc.tensor, "act": nc.scalar, "dve": nc.vector, "pool": nc.gpsimd, "sp": nc.sync}
        self.ops = []
        self.sb_base = 16640
        self.sb_top = 229376
        self.sb_off = self.sb_base
        self.nalloc = 0

    def sb(self, name, shape, dtype):
        per = int(np.prod(shape[1:])) * (2 if dtype == BF16 else 4)
        off = (self.sb_off + 63) // 64 * 64
        assert off + per <= self.sb_top, ("sbuf overflow", name, off, per)
        self.sb_off = off + per
        self.nalloc += 1
        return self.nc.alloc_sbuf_tensor_at("%s_%d" % (name, self.nalloc), list(shape), dtype, offset=off)

    def mark(self):
        return self.sb_off

    def release(self, m):
        self.sb_off = m

    def op(self, eng, fn, reads=(), writes=()):
        self.ops.append(dict(eng=eng, fn=fn, reads=tuple(reads), writes=tuple(writes), dma=False, bar=False))

    def dma(self, q, out, in_, reads=(), writes=(), semkey=None, final=False, **kw):
        if semkey is None:
            semkey = writes[0] if writes else reads[0]
        e = self.eng[q]
        self.ops.append(dict(eng=q, fn=(lambda e=e, out=out, in_=in_, kw=kw: e.dma_start(out=out, in_=in_, **kw)),
                             reads=tuple(reads), writes=tuple(writes), dma=True, semkey=semkey, final=final, bar=False))

    def barrier(self):
        for e in self.ENG:
            self.ops.append(dict(eng=e, fn=None, reads=(), writes=(), dma=False, bar=True))

    def emit(self):
        nc = self.nc
        ops = self.ops
        n = len(ops)
        lastw = {}
        readers = {}
        deps = [None] * n
        last_eng = {}
        dma_since = []
        bar_group = None
        for i, o in enumerate(ops):
            if o["bar"]:
                if bar_group is None:
                    bar_group = (set(last_eng.values()) | set(dma_since))
                deps[i] = set(bar_group)
                nxt = ops[i + 1] if i + 1 < n else None
                if nxt is None or not nxt["bar"]:
                    bar_group = None
                    dma_since = []
                    lastw = {}
                    readers = {}
                continue
            d = set()
            for k in o["reads"]:
                if k in lastw:
                    d.add(lastw[k])
            for k in o["writes"]:
                if k in lastw:
                    d.add(lastw[k])
                for r in readers.get(k, ()):
                    d.add(r)
            d.discard(i)
            deps[i] = d
            for k in o["reads"]:
                readers.setdefault(k, []).append(i)
            for k in o["writes"]:
                lastw[k] = i
                readers[k] = []
            if o["dma"]:
                dma_since.append(i)
            else:
                last_eng[o["eng"]] = i

        def need_wait(o, pj):
            if pj["dma"] or o["dma"] or o["bar"]:
                return not (o["bar"] and (not pj["dma"]) and pj["eng"] == o["eng"])
            if pj["eng"] != o["eng"]:
                return True
            if o["eng"] == "pe" or o["eng"] in SAME_ENG_NOWAIT:
                return False
            return any(k in pj["writes"] for k in o["reads"])

        needed = set()
        for i, o in enumerate(ops):
            for j in deps[i]:
                if need_wait(o, ops[j]):
                    needed.add(j)
            if o["dma"]:
                needed.add(i)
        esem = {e: nc.alloc_semaphore("c_" + e) for e in self.ENG}
        ecnt = {e: 0 for e in self.ENG}
        dpool = []
        dcnt = []
        keymap = {}
        token = [None] * n
        for i, o in enumerate(ops):
            if o["bar"]:
                keymap = {}
                continue
            if o["dma"]:
                sk = o["semkey"]
                if sk not in keymap:
                    idx = len(keymap)
                    if idx >= len(dpool):
                        dpool.append(nc.alloc_semaphore("d%d" % idx))
                        dcnt.append(0)
                    keymap[sk] = idx
                idx = keymap[sk]
                dcnt[idx] += 16
                token[i] = ("d:%d" % idx, dpool[idx], dcnt[idx])
            elif i in needed:
                e = o["eng"]
                ecnt[e] += 1
                token[i] = ("e:" + e, esem[e], ecnt[e])
        dsem = dpool
        seen = {e: {} for e in self.ENG}
        dlatest = {}
        nwaits = 0
        for i, o in enumerate(ops):
            e = o["eng"]
            eng = self.eng[e]
            want = {}
            for j in deps[i]:
                tk = token[j]
                if tk is None or not need_wait(o, ops[j]):
                    continue
                name, sem, val = tk
                if name.startswith("d:"):
                    val = dlatest[name]
                if want.get(name, (None, 0))[1] < val:
                    want[name] = (sem, val)
            todo = [(name, sem, val) for name, (sem, val) in want.items() if seen[e].get(name, 0) < val]
            attach = None
            if EMBED_WAIT and todo and not o["bar"] and not o["dma"]:
                attach = todo.pop()
            for name, sem, val in todo:
                eng.wait_ge(sem, val)
                seen[e][name] = val
                nwaits += 1
            if o["bar"]:
                continue
            ins = o["fn"]()
            if attach is not None:
                name, sem, val = attach
                ins._wait_ge(sem, val)
                seen[e][name] = val
            tk = token[i]
            if tk is not None:
                name, sem, val = tk
                if o["dma"]:
                    ins.then_inc(sem, 16)
                    dlatest[name] = val
                else:
                    ins.then_inc(sem, 1)
        for idx in range(len(dpool)):
            name = "d:%d" % idx
            val = dlatest.get(name, 0)
            if val and seen["sp"].get(name, 0) < val:
                self.eng["sp"].wait_ge(dpool[idx], val)
                seen["sp"][name] = val
        self.stats = dict(n_ops=n, n_waits=nwaits, n_dsem=len(dsem), cnt=dict(ecnt))
        return nc


TT_ALL = [(i * 512, 512) for i in range(8)] + [(NPR, NS)]
TT_OWN = [(OWN0 - 1, 1), (OWN0, 512), (OWN0 + 512, 512), (NPR, NS)]


def stage_A(P, io):
    nc = P.nc
    m0 = P.mark()
    xs = P.sb("xs", [128, 16, NTOK], BF16)
    wt = [P.sb("wt%d" % i, [128, 16, 512], BF16) for i in range(2)]
    NOT = 8
    ot = [P.sb("ot%d" % i, [128, 512], F32) for i in range(NOT)]
    vt = [P.sb("vt%d" % i, [128, 1024], F32) for i in range(2)]
    ps = io["ps"]
    xTr = io["xT"].rearrange("(kc p) t -> p kc t", p=128)
    wr = io["w_in"].rearrange("(kc p) c -> p kc c", p=128)
    PJ = io["PJ"]
    VTM = io["VTM"]
    for g in range(8):
        P.dma("pool", xs[:, 2 * g:2 * g + 2, :], xTr[:, 2 * g:2 * g + 2, :], writes=["xs%d" % g])
    st = dict(oi=0, pi=0, wi=0, vi=0)

    def evac(dst, src, wkey, pkey):
        st["pi"] += 1
        if st["pi"] % 2 == 0:
            P.op("act", lambda: nc.scalar.copy(dst, src), reads=[pkey], writes=[wkey])
        else:
            P.op("dve", lambda: nc.vector.tensor_copy(dst, src), reads=[pkey], writes=[wkey])

    def fm_chunk(c0, cw, tts):
        wb = st["wi"] % 2
        st["wi"] += 1
        P.dma("pool", wt[wb][:, :, 0:cw], wr[:, :, c0:c0 + cw], writes=["wt%d" % wb])
        for m in range(0, cw, 128):
            mw = min(128, cw - m)
            for (t0, tw) in tts:
                pb = st["pi"] % 6
                for kc in range(16):
                    P.op("pe", (lambda pb=pb, wb=wb, kc=kc, m=m, mw=mw, t0=t0, tw=tw:
                                nc.tensor.matmul(ps[pb][0:mw, 0:tw], wt[wb][:, kc, m:m + mw], xs[:, kc, t0:t0 + tw],
                                                 start=(kc == 0), stop=(kc == 15))),
                         reads=["wt%d" % wb, "xs%d" % (kc // 2)], writes=["ps%d" % pb])
                ob = st["oi"] % NOT
                st["oi"] += 1
                evac(ot[ob][0:mw, 0:tw], ps[pb][0:mw, 0:tw], "ot%d" % ob, "ps%d" % pb)
                P.dma("sp", PJ[c0 + m:c0 + m + mw, t0:t0 + tw], ot[ob][0:mw, 0:tw], reads=["ot%d" % ob],
                      semkey="oto%d" % ob, **({"allow_slow_non_contiguous": True} if tw == 1 else {}))

    for c0 in range(0, N_ALL, 512):
        fm_chunk(c0, min(512, N_ALL - c0), TT_ALL)
    for c0 in range(N_ALL, N_FM, 512):
        fm_chunk(c0, min(512, N_FM - c0), TT_OWN)
    wbs = []
    for h in range(2):
        wb = st["wi"] % 2
        st["wi"] += 1
        P.dma("pool", wt[wb][:, :, :], wr[:, :, N_FM + 512 * h:N_FM + 512 * h + 512], writes=["wt%d" % wb])
        wbs.append(wb)
    tb = [(i * 128, 128) for i in range(32)] + [(NPR, NS)]
    for (t0, tw) in tb:
        vb = st["vi"] % 2
        st["vi"] += 1
        for h in range(2):
            pb = st["pi"] % 6
            wb = wbs[h]
            for kc in range(16):
                P.op("pe", (lambda pb=pb, wb=wb, kc=kc, t0=t0, tw=tw:
                            nc.tensor.matmul(ps[pb][0:tw, 0:512], xs[:, kc, t0:t0 + tw], wt[wb][:, kc, :],
                                             start=(kc == 0), stop=(kc == 15))),
                     reads=["wt%d" % wb, "xs%d" % (kc // 2)], writes=["ps%d" % pb])
            evac(vt[vb][0:tw, 512 * h:512 * h + 512], ps[pb][0:tw, 0:512], "vt%d" % vb, "ps%d" % pb)
        P.dma("sp", VTM[t0:t0 + tw, :], vt[vb][0:tw, :], reads=["vt%d" % vb], semkey="vto%d" % vb)
    P.barrier()
    P.release(m0)


class Ops:
    def __init__(self, P):
        self.P = P
        self.nc = P.nc
        self.rr = 0

    def E(self, which):
        return {"dve": self.nc.vector, "pool": self.nc.gpsimd}[which]

    def pick(self, choices=("dve", "pool")):
        self.rr += 1
        return choices[self.rr % len(choices)]

    def tt(self, eng, out, a, b, op, r, w):
        e = self.E(eng)
        self.P.op(eng, lambda: e.tensor_tensor(out, a, b, op), reads=r, writes=w)

    def ts(self, eng, out, a, s1, s2, op0, op1, r, w):
        e = self.E(eng)
        if s2 is None:
            self.P.op(eng, lambda: e.tensor_scalar(out, a, s1, None, op0), reads=r, writes=w)
        else:
            self.P.op(eng, lambda: e.tensor_scalar(out, a, s1, s2, op0, op1), reads=r, writes=w)

    def stt(self, eng, out, a, sc, b, op0, op1, r, w):
        e = self.E(eng)
        self.P.op(eng, lambda: e.scalar_tensor_tensor(out, a, sc, b, op0, op1), reads=r, writes=w)

    def cp(self, eng, out, a, r, w):
        if eng == "act":
            self.P.op("act", lambda: self.nc.scalar.copy(out, a), reads=r, writes=w)
        else:
            e = self.E(eng)
            self.P.op(eng, lambda: e.tensor_copy(out, a), reads=r, writes=w)

    def act(self, out, a, func, r, w, bias=None, scale=None):
        kw = {}
        if bias is not None:
            kw["bias"] = bias
        if scale is not None:
            kw["scale"] = scale
        self.P.op("act", lambda: self.nc.scalar.activation(out=out, in_=a, func=func, **kw), reads=r, writes=w)

    def mm(self, out, lhsT, rhs, r, w, start=True, stop=True):
        self.P.op("pe", lambda: self.nc.tensor.matmul(out, lhsT, rhs, start=start, stop=stop), reads=r, writes=w)

    def tr(self, out, a, ident, r, w):
        self.P.op("pe", lambda: self.nc.tensor.transpose(out, a, ident), reads=r, writes=w)

    def memset(self, eng, out, val, w):
        e = self.E(eng)
        self.P.op(eng, lambda: e.memset(out, val), writes=w)

    def recip(self, out, a, r, w):
        self.P.op("dve", lambda: self.nc.vector.reciprocal(out, a), reads=r, writes=w)

    def rsqrt(self, out, a, r, w):
        self.act(out, a, AF.Sqrt, r, w)
        self.recip(out, out, w, w)


def load_consts(P, io):
    nc = P.nc
    c = {}
    c["identb"] = P.sb("identb", [128, 128], BF16)
    c["identf"] = P.sb("identf", [128, 128], F32)
    c["b64"] = P.sb("b64", [128, 128], F32)
    c["b64m"] = P.sb("b64m", [128, 128], F32)
    c["mskL"] = P.sb("mskL", [64, 64], F32)
    c["mskU"] = P.sb("mskU", [64, 128], F32)
    c["i2"] = P.sb("i2", [128, 64], F32)
    c["maskR"] = P.sb("maskR", [128, 512], F32)
    c["tri"] = P.sb("tri", [128, 128], BF16)
    c["zcol"] = P.sb("zcol", [128, 1], F32)
    P.dma("pool", c["identb"][:], io["c_ident"], writes=["c_identb"])
    P.dma("sp", c["identf"][:], io["c_ident"], writes=["c_identf"])
    P.dma("sp", c["b64"][:], io["c_b64"], writes=["c_b64"])
    P.dma("sp", c["b64m"][:], io["c_b64m"], writes=["c_b64m"])
    P.dma("sp", c["mskL"][:], io["c_mskL"], writes=["c_mskL"])
    P.dma("sp", c["mskU"][:], io["c_mskU"], writes=["c_mskU"])
    P.dma("sp", c["i2"][:], io["c_i2"], writes=["c_i2"])
    P.dma("sp", c["maskR"][:], io["c_maskR"], writes=["c_maskR"])
    P.dma("pool", c["tri"][:], io["c_tri"], writes=["c_tri"])
    P.op("pool", lambda: nc.gpsimd.memset(c["zcol"][:], 0.0), writes=["c_zcol"])
    c["keys"] = ["c_identb", "c_identf", "c_b64", "c_b64m", "c_mskL", "c_mskU", "c_i2", "c_maskR", "c_tri", "c_zcol"]
    return c


B1_RATIO = 12
PRM = dict(mu_r=0, mu_k=1, mu_v=2, w0=3, a0=4, k_k=5, k_a=6, r_k=7, lnx_g=8, lnx_b=9, omk=10)


def stage_B1(P, io, cst, yT, pairs=range(8), tiles=range(9), lvl=99):
    nc = P.nc
    O = Ops(P)
    PJ = io["PJ"]
    ps = io["ps"]
    m0 = P.mark()
    CK = cst["keys"]
    prm = P.sb("prm", [128, 8, 11], F32)
    P.dma("sp", prm[:, :, 0:10], io["rw_prm"], writes=["prm"])
    O.ts("dve", prm[:, :, 10:11], prm[:, :, 6:7], -1.0, 1.0, ALU.mult, ALU.add, ["prm"], ["prm"])
    mul = P.sb("mul", [128, 3], F32)
    P.dma("sp", mul[:], io["rw_mul"], writes=["mul"])
    w2a2 = P.sb("w2a2", [128, 1024], BF16)
    g2a = P.sb("g2a", [128, 1024], BF16)
    g2b = P.sb("g2b", [32, 1024], BF16)
    P.dma("pool", w2a2[:], io["rw_w2a2"], writes=["w2a2"])
    P.dma("pool", g2a[:], io["rw_g2"][0:128, :], writes=["g2a"])
    P.dma("pool", g2b[:], io["rw_g2"][128:160, :], writes=["g2b"])
    Hst = P.sb("Hst", [128, 8, 64], F32)
    O.memset("pool", Hst[:], 0.0, ["H%d" % p for p in range(8)])

    def S(name, shape, dt, nb=2):
        return [P.sb(name + str(i), shape, dt) for i in range(nb)]
    xwc = S("xwc", [128, 512], F32); xwp = S("xwp", [128, 512], F32, 1) * 2
    xgc = S("xgc", [128, 512], F32); xgp = S("xgp", [128, 512], F32, 1) * 2
    xhc = S("xhc", [32, 512], F32, 1) * 2; xhp = S("xhp", [32, 512], F32, 1) * 2
    txa = S("txa", [128, 512], BF16); sga = S("sga", [128, 512], BF16); sgb = S("sgb", [32, 512], BF16)
    names32 = ["kc", "kp", "vc", "vp", "rc", "rp", "ks", "vs", "rs", "sw", "al", "gg", "kk", "t1", "kkn", "bb", "kh",
               "cum", "cumx", "e_r", "e_a", "e_n", "e_c", "bon"]
    U = {n: S(n, [128, 512], F32, 2 if n in ("kc", "kp", "vc", "vp", "bon", "gg") else 1) for n in names32}
    gC = S("gC", [128, 8], F32)
    QQ = S("QQ", [128, 8, 2, 64], BF16); KK = S("KK", [128, 8, 2, 64], BF16)
    BH = S("BH", [128, 8, 64], BF16); KH = S("KH", [128, 8, 64], BF16); VV = S("VV", [128, 8, 64], BF16)
    TMa = S("TMa", [64, 8, 128], BF16, 1) * 2; TMb = S("TMb", [64, 8, 128], BF16, 1) * 2
    TMk = S("TMk", [64, 8, 128], BF16, 1) * 2; TMv = S("TMv", [64, 8, 128], BF16, 1) * 2
    Lb = S("Lb", [64, 8, 64], BF16, 4); LTb = S("LTb", [64, 8, 64], BF16, 4)
    GB = S("GB", [64, 8, 128], BF16); GK = S("GK", [64, 8, 128], BF16)
    Zf = S("Zf", [64, 8, 128], F32); Zb = S("Zb", [64, 8, 128], BF16)
    MT = S("MT", [128, 8, 64], F32, 1) * 2; NN = S("NN", [128, 8, 64], F32, 1) * 2
    DG = S("DG", [128, 8, 64], F32, 1) * 2
    Hb = S("Hb", [128, 8, 64], BF16)
    QeT = S("QeT", [128, 8, 64], BF16)
    ysb = S("ysb", [128, 512], F32, 1) * 2; yc_ = S("yc", [128, 512], F32, 1) * 2; sq_ = S("sq", [128, 512], F32, 1) * 2
    rstd = S("rstd", [128, 512], F32, 1) * 2

    def v3(ap, n):
        return ap.rearrange("p (a b) -> p a b", a=n)

    def psv(bank, rows, n, w):
        return ps[bank][rows, 0:n * w].rearrange("p (a b) -> p a b", a=n)

    def psb(bank, rows, n, w):
        return ps[bank][:, :].bitcast(BF16)[rows, 0:n * w].rearrange("p (a b) -> p a b", a=n)

    def unit_gen(ct, p, u, first):
        samp = (ct == 8)
        t0 = NPR if samp else 512 * ct
        TW = 64 if samp else 512
        RW_ = NS if samp else 512
        NCH = TW // 64
        do_out = ct >= 6
        lb = ct % 2

        def load_pair(cur, prv, r0, rows, kc_, kp_, q="sp"):
            if samp:
                O.memset("pool", cur[0:rows, 0:TW], 0.0, [kc_])
                O.memset("pool", prv[0:rows, 0:TW], 0.0, [kp_])
                P.dma(q, cur[0:rows, 0:NS], PJ[r0:r0 + rows, NPR:NPR + NS], writes=[kc_])
                P.dma(q, prv[0:rows, 1:NS], PJ[r0:r0 + rows, NPR:NPR + NS - 1], writes=[kp_])
                P.dma(q, prv[0:rows, 0:1], io["shiftT"][r0:r0 + rows, :], writes=[kp_])
            else:
                P.dma(q, cur[0:rows, 0:TW], PJ[r0:r0 + rows, t0:t0 + TW], writes=[kc_])
                if t0 == 0:
                    O.memset("pool", prv[0:rows, 0:1], 0.0, [kp_])
                    P.dma(q, prv[0:rows, 1:TW], PJ[r0:r0 + rows, 0:TW - 1], writes=[kp_])
                else:
                    P.dma(q, prv[0:rows, 0:TW], PJ[r0:r0 + rows, t0 - 1:t0 + TW - 1], writes=[kp_])

        def shift(eng, out, cur, prv, mu, r, w, rows=128):
            O.tt(eng, prv[0:rows, 0:TW], prv[0:rows, 0:TW], cur[0:rows, 0:TW], ALU.subtract, r, [r[1]])
            if eng == "dve":
                O.stt(eng, out[0:rows, 0:TW], prv[0:rows, 0:TW], mu, cur[0:rows, 0:TW], ALU.mult, ALU.add, r, w)
            else:
                O.ts(eng, prv[0:rows, 0:TW], prv[0:rows, 0:TW], mu, None, ALU.mult, None, r, [r[1]])
                O.tt(eng, out[0:rows, 0:TW], prv[0:rows, 0:TW], cur[0:rows, 0:TW], ALU.add, r, w)

        kx = "L%d" % lb
        if first:
            load_pair(xwc[lb], xwp[lb], ROW["xwxa"], 128, kx + "xwc", "Lxwp")
            shift("dve", xwc[lb], xwc[lb], xwp[lb], mul[:, 0:1], [kx + "xwc", "Lxwp", "mul"], [kx + "xwc"])
            O.act(txa[lb][0:64, 0:TW], xwc[lb][0:64, 0:TW], AF.Tanh, [kx + "xwc"], [kx + "txa"])
            O.cp("pool", txa[lb][64:128, 0:TW], xwc[lb][64:128, 0:TW], [kx + "xwc"], [kx + "txa"])
            if do_out:
                load_pair(xgc[lb], xgp[lb], ROW["xg"], 128, kx + "xgc", "Lxgp")
                shift("pool", xgc[lb], xgc[lb], xgp[lb], mul[:, 1:2], [kx + "xgc", "Lxgp", "mul"], [kx + "xgc"])
                O.act(sga[lb][:, 0:TW], xgc[lb][:, 0:TW], AF.Sigmoid, [kx + "xgc"], [kx + "sga"])
                r0 = ROW["xg"] + 128
                if samp:
                    O.memset("pool", xhc[lb][:, 0:TW], 0.0, ["Lxhc"])
                    O.memset("pool", xhp[lb][:, 0:TW], 0.0, ["Lxhp"])
                    P.dma("sp", xhc[lb][:, 0:NS], PJ[r0:r0 + 32, NPR:NPR + NS], writes=["Lxhc"])
                    P.dma("sp", xhp[lb][:, 1:NS], PJ[r0:r0 + 32, NPR:NPR + NS - 1], writes=["Lxhp"])
                    P.dma("sp", xhp[lb][:, 0:1], io["shiftT"][r0:r0 + 32, :], writes=["Lxhp"])
                else:
                    P.dma("sp", xhc[lb][:, 0:TW], PJ[r0:r0 + 32, t0:t0 + TW], writes=["Lxhc"])
                    P.dma("sp", xhp[lb][:, 0:TW], PJ[r0:r0 + 32, t0 - 1:t0 + TW - 1], writes=["Lxhp"])
                shift("pool", xhc[lb], xhc[lb], xhp[lb], mul[0:32, 2:3], ["Lxhc", "Lxhp", "mul"], ["Lxhc"], rows=32)
                O.act(sgb[lb][:, 0:TW], xhc[lb][:, 0:TW], AF.Sigmoid, ["Lxhc"], [kx + "sgb"])

            yield "d"
        SINGLE = ("xhc", "xhp", "MT", "NN", "DG", "ysb", "yc", "sq", "rstd", "TMa", "TMb", "TMk", "TMv")
        K = (lambda n, u=u: "u%d_%s" % (0 if (n in SINGLE or (n in U and len(U[n]) == 1)) else u, n))
        T = (lambda n, u=u: U[n][u % len(U[n])])
        pc = lambda j: prm[:, p, j:j + 1]
        cols = slice(128 * p, 128 * p + 128)
        W_ = slice(0, TW)
        load_pair(T("kc"), T("kp"), ROW["rk"] + 128 * p, 128, K("kc"), K("kp"))
        load_pair(T("vc"), T("vp"), ROW["rv"] + 128 * p, 128, K("vc"), K("vp"))
        shift("dve", T("ks"), T("kc"), T("kp"), pc(1), [K("kc"), K("kp"), "prm"], [K("ks")])
        shift("pool", T("vs"), T("vc"), T("vp"), pc(2), [K("vc"), K("vp"), "prm"], [K("vs")])
        yield "d"
        if do_out:
            load_pair(T("rc"), T("rp"), ROW["r"] + 128 * p, 128, K("rc"), K("rp"))
            shift("pool", T("rs"), T("rc"), T("rp"), pc(0), [K("rc"), K("rp"), "prm"], [K("rs")])
        O.mm(ps[0][:, W_], w2a2[0:64, cols], txa[lb][0:64, W_], ["w2a2", kx + "txa"], ["ps0"])
        O.mm(ps[1][:, W_], w2a2[64:128, cols], txa[lb][64:128, W_], ["w2a2", kx + "txa"], ["ps1"])
        O.act(T("sw")[:, W_], ps[0][:, W_], AF.Sigmoid, ["ps0", "prm"], [K("sw")], bias=pc(3))
        O.act(T("al")[:, W_], ps[1][:, W_], AF.Sigmoid, ["ps1", "prm"], [K("al")], bias=pc(4))
        yield "d"
        if samp:
            O.memset("pool", T("sw")[:, NS:TW], 0.0, [K("sw")])
        if do_out:
            O.mm(ps[0][:, W_], g2a[:, cols], sga[lb][:, W_], ["g2a", kx + "sga"], ["ps0"], start=True, stop=False)
            O.mm(ps[0][:, W_], g2b[0:32, cols], sgb[lb][0:32, W_], ["g2b", kx + "sgb"], ["ps0"], start=False, stop=True)
            O.cp("act", T("gg")[:, W_], ps[0][:, W_], ["ps0"], [K("gg")])
        O.ts("pool", T("kk")[:, W_], T("ks")[:, W_], pc(5), None, ALU.mult, None, [K("ks"), "prm"], [K("kk")])
        O.tt("pool", T("t1")[:, W_], T("kk")[:, W_], T("kk")[:, W_], ALU.mult, [K("kk")], [K("t1")])
        O.mm(ps[1][:, W_], cst["b64"][:, :], T("t1")[:, W_], ["c_b64", K("t1")], ["ps1"])
        O.ts("dve", T("t1")[:, W_], ps[1][:, W_], 1e-24, None, ALU.max, None, ["ps1"], [K("t1")])
        O.rsqrt(T("t1")[:, W_], T("t1")[:, W_], [K("t1")], [K("t1")])
        yield "d"
        O.tt("pool", T("kkn")[:, W_], T("kk")[:, W_], T("t1")[:, W_], ALU.mult, [K("kk"), K("t1")], [K("kkn")])
        O.tt("dve", T("bb")[:, W_], T("kkn")[:, W_], T("al")[:, W_], ALU.mult, [K("kkn"), K("al")], [K("bb")])
        O.ts("dve", T("t1")[:, W_], T("al")[:, W_], pc(6), pc(10), ALU.mult, ALU.add, [K("al"), "prm", K("t1")], [K("t1")])
        O.tt("pool", T("kh")[:, W_], T("ks")[:, W_], T("t1")[:, W_], ALU.mult, [K("ks"), K("t1")], [K("kh")])
        yield "d"
        sc_o, sc_m, sc_i, sc_z = T("cum")[:, W_], cst["maskR"][:, W_], T("sw")[:, W_], cst["zcol"][:, 0:1]
        P.op("dve", (lambda sc_o=sc_o, sc_m=sc_m, sc_i=sc_i, sc_z=sc_z:
                     nc.vector.tensor_tensor_scan(sc_o, sc_m, sc_i, sc_z, ALU.mult, ALU.add)),
             reads=[K("sw")], writes=[K("cum")])
        O.tt("pool", T("cumx")[:, W_], T("cum")[:, W_], T("sw")[:, W_], ALU.subtract, [K("cum"), K("sw")], [K("cumx")])
        O.act(T("e_r")[:, W_], T("cum")[:, W_], AF.Exp, [K("cum")], [K("e_r")], scale=-C0)
        O.act(T("e_a")[:, W_], T("cumx")[:, W_], AF.Exp, [K("cumx")], [K("e_a")], scale=-C0)
        O.recip(T("e_n")[:, W_], T("e_r")[:, W_], [K("e_r")], [K("e_n")])
        yield "d"
        cv = v3(T("cum")[:, W_], NCH)
        O.tt("dve", v3(T("cumx")[:, W_], NCH), cv[:, :, 63:64].to_broadcast([128, NCH, 64]), cv, ALU.subtract,
             [K("cum"), K("e_a")], [K("cumx")])
        O.act(T("e_c")[:, W_], T("cumx")[:, W_], AF.Exp, [K("cumx")], [K("e_c")], scale=-C0)
        O.cp("pool", gC[u][:, 0:NCH], v3(T("e_r")[:, W_], NCH)[:, :, 63], [K("e_r")], [K("gC")])
        yield "d"
        qq = QQ[u]; kq = KK[u]
        O.stt("dve", qq[:, 0:NCH, 0, :], v3(T("kkn")[:, W_], NCH), -1.0, v3(T("e_a")[:, W_], NCH), ALU.mult, ALU.mult,
              [K("kkn"), K("e_a")], [K("QQ")])
        if do_out:
            O.tt("pool", qq[:, 0:NCH, 1, :], v3(T("rs")[:, W_], NCH), v3(T("e_r")[:, W_], NCH), ALU.mult,
                 [K("rs"), K("e_r")], [K("QQ")])
        O.tt("dve", kq[:, 0:NCH, 0, :], v3(T("bb")[:, W_], NCH), v3(T("e_n")[:, W_], NCH), ALU.mult, [K("bb"), K("e_n")], [K("KK")])
        O.tt("pool", kq[:, 0:NCH, 1, :], v3(T("kh")[:, W_], NCH), v3(T("e_n")[:, W_], NCH), ALU.mult, [K("kh"), K("e_n")], [K("KK")])
        O.tt("dve", BH[u][:, 0:NCH, :], v3(T("bb")[:, W_], NCH), v3(T("e_c")[:, W_], NCH), ALU.mult, [K("bb"), K("e_c")], [K("BH")])
        O.tt("pool", KH[u][:, 0:NCH, :], v3(T("kh")[:, W_], NCH), v3(T("e_c")[:, W_], NCH), ALU.mult, [K("kh"), K("e_c")], [K("KH")])
        yield "d"
        O.cp("act", VV[u][:, 0:NCH, :], v3(T("vs")[:, W_], NCH), [K("vs")], [K("VV")])
        if do_out:
            O.stt("dve", T("bon")[:, W_], T("rs")[:, W_], pc(7), T("kh")[:, W_], ALU.mult, ALU.mult, [K("rs"), K("kh"), "prm"], [K("bon")])
            O.mm(ps[0][:, W_], cst["b64"][:, :], T("bon")[:, W_], ["c_b64", K("bon")], ["ps0"])
            O.tt("dve", T("bon")[:, W_], ps[0][:, W_], T("vs")[:, W_], ALU.mult, ["ps0", K("vs"), K("bon")], [K("bon")])
        yield "DONE_D"
        if lvl < 1:
            return
        O.tt("pool", DG[u][:, 0:NCH, :], cst["i2"][:, :].unsqueeze(1).to_broadcast([128, NCH, 64]),
             gC[u][:, 0:NCH].unsqueeze(2).to_broadcast([128, NCH, 64]), ALU.mult, ["c_i2", K("gC")], [K("DG")])
        for (src, skey, dst, dkey, bank) in ((qq, K("QQ"), TMa[u], K("TMa"), 3), (BH[u], K("BH"), TMb[u], K("TMb"), 4),
                                             (KH[u], K("KH"), TMk[u], K("TMk"), 3), (VV[u], K("VV"), TMv[u], K("TMv"), 4)):
            pv = psb(bank, slice(0, 64), NCH, 128)
            for i in range(NCH):
                s_ap = src[:, i, 0, :] if src is qq else src[:, i, :]
                O.tr(pv[:, i, :], s_ap, cst["identb"][:, :], [skey, "c_identb"], ["ps%d" % bank])
            O.cp(O.pick(("act", "dve")), dst[:, 0:NCH, :], pv, ["ps%d" % bank], [dkey])
            yield "c"
        if lvl < 2:
            return
        HS = []
        NQ = 128 if do_out else 64
        for h in range(2):
            R = slice(64 * h, 64 * h + 64)
            Zf_, Zb_ = Zf[h], Zb[h]
            kz = "Z%d" % h
            GB_, GK_ = GB[h], GK[h]
            kg = "G%d" % h
            gab = psv(3, slice(0, 64), NCH, 64)
            for i in range(NCH):
                O.mm(gab[:, i, :], qq[R, i, 0, :], kq[R, i, 0, :], [K("QQ"), K("KK")], ["ps3"])
            for i in range(NCH):
                bk = 4 + i // 4
                o_ = ps[bk][0:64, (i % 4) * 128:(i % 4) * 128 + NQ]
                O.mm(o_, kq[R, i, 0, :], qq[R, i, :, :].rearrange("p a b -> p (a b)")[:, 0:NQ], [K("QQ"), K("KK")], ["ps%d" % bk])
            for i in range(NCH):
                bk = 6 + i // 4
                o_ = ps[bk][0:64, (i % 4) * 128:(i % 4) * 128 + NQ]
                O.mm(o_, kq[R, i, 1, :], qq[R, i, :, :].rearrange("p a b -> p (a b)")[:, 0:NQ], [K("QQ"), K("KK")], ["ps%d" % bk])
            li = 2 * h
            O.tt("dve", Lb[li][:, 0:NCH, :], gab, cst["mskL"][:, :].unsqueeze(1).to_broadcast([64, NCH, 64]), ALU.mult,
                 ["ps3", "c_mskL"], ["Lb%d" % li])
            yield "c"
            for (bank0, G_, kname) in ((4, GB_, kg + "B"), (6, GK_, kg + "K")):
                for hb in range((NCH + 3) // 4):
                    n4 = min(4, NCH - 4 * hb)
                    src = ps[bank0 + hb][0:64, 0:n4 * 128].rearrange("p (a b) -> p a b", a=n4)[:, :, 0:NQ]
                    O.tt("dve", G_[:, 4 * hb:4 * hb + n4, 0:NQ], src,
                         cst["mskU"][:, 0:NQ].unsqueeze(1).to_broadcast([64, n4, NQ]), ALU.mult,
                         ["ps%d" % (bank0 + hb), "c_mskU"], [kname])
            O.cp("pool", LTb[li][:, 0:NCH, :], GB_[:, 0:NCH, 0:64], [kg + "B"], ["LTb%d" % li])
            yield "c"
            if lvl < 3:
                continue
            xp = psv(3, slice(0, 64), NCH, 64)
            for i in range(NCH):
                O.mm(xp[:, i, :], GK_[:, i, 0:64], TMv[u][:, i, R], [kg + "K", K("TMv")], ["ps3"])
            O.cp("act", Zf_[:, 0:NCH, 0:64], TMa[u][:, 0:NCH, R], [K("TMa")], [kz + "f"])
            O.cp("dve", Zf_[:, 0:NCH, 64:128], xp, ["ps3"], [kz + "f"])
            yield "c"
            O.cp("pool", Zb_[:, 0:NCH, :], Zf_[:, 0:NCH, :], [kz + "f"], [kz + "b"])
            HS.append(dict(h=h, R=R, Zf=Zf_, Zb=Zb_, kz=kz, GB=GB_, GK=GK_, kg=kg, li=li, cur=li, zb0=2 + 2 * h))
        for j in range(6):
            for hd in HS:
                cur, li, Zb_, Zf_, kz, zb0 = hd["cur"], hd["li"], hd["Zb"], hd["Zf"], hd["kz"], hd["zb0"]
                for i in range(NCH):
                    bk = zb0 + i // 4
                    o_ = ps[bk][0:64, (i % 4) * 128:(i % 4) * 128 + 128]
                    O.mm(o_, LTb[cur][:, i, :], Zb_[:, i, :], ["LTb%d" % cur, kz + "b"], ["ps%d" % bk])
                yield "c"
                if j < 5:
                    nxt = li + 1 if cur == li else li
                    pn = psv(6, slice(0, 64), NCH, 64)
                    ptn = psv(7, slice(0, 64), NCH, 64)
                    for i in range(NCH):
                        O.mm(pn[:, i, :], LTb[cur][:, i, :], Lb[cur][:, i, :], ["LTb%d" % cur, "Lb%d" % cur], ["ps6"])
                    for i in range(NCH):
                        O.mm(ptn[:, i, :], Lb[cur][:, i, :], LTb[cur][:, i, :], ["LTb%d" % cur, "Lb%d" % cur], ["ps7"])
                    yield "c"
                for hb in range((NCH + 3) // 4):
                    n4 = min(4, NCH - 4 * hb)
                    src = ps[zb0 + hb][0:64, 0:n4 * 128].rearrange("p (a b) -> p a b", a=n4)
                    O.tt("dve", Zb_[:, 4 * hb:4 * hb + n4, :], Zf_[:, 4 * hb:4 * hb + n4, :], src, ALU.add,
                         ["ps%d" % (zb0 + hb), kz + "f"], [kz + "b"])
                if j < 5:
                    for hb in range((NCH + 3) // 4):
                        n4 = min(4, NCH - 4 * hb)
                        src = ps[zb0 + hb][0:64, 0:n4 * 128].rearrange("p (a b) -> p a b", a=n4)
                        O.tt("dve", Zf_[:, 4 * hb:4 * hb + n4, :], Zf_[:, 4 * hb:4 * hb + n4, :], src, ALU.add,
                             ["ps%d" % (zb0 + hb), kz + "f"], [kz + "f"])
                    O.cp("act", Lb[nxt][:, 0:NCH, :], pn, ["ps6"], ["Lb%d" % nxt])
                    O.cp("act", LTb[nxt][:, 0:NCH, :], ptn, ["ps7"], ["LTb%d" % nxt])
                    hd["cur"] = nxt
                yield "c"
        if lvl < 4:
            return
        for hd in HS:
            h, R, Zb_, kz, GB_, kg = hd["h"], hd["R"], hd["Zb"], hd["kz"], hd["GB"], hd["kg"]
            mtp = psv(3, R, NCH, 64)
            npp = psv(6, R, NCH, 64)
            for i in range(NCH):
                O.mm(mtp[:, i, :], Zb_[:, i, 0:64], TMb[u][:, i, R], [kz + "b", K("TMb")], ["ps3"])
            for i in range(NCH):
                O.mm(npp[:, i, :], TMb[u][:, i, R], Zb_[:, i, 64:128], [kz + "b", K("TMb")], ["ps6"], start=True, stop=False)
                O.mm(npp[:, i, :], TMk[u][:, i, R], TMv[u][:, i, R], [K("TMk"), K("TMv")], ["ps6"], start=False, stop=True)
            if do_out:
                qep = psv(7, R, NCH, 64)
                for i in range(NCH):
                    O.mm(qep[:, i, :], Zb_[:, i, 0:64], GB_[:, i, 64:128], [kz + "b", kg + "B"], ["ps7"])
            O.tt("dve", MT[u][R, 0:NCH, :], mtp, DG[u][R, 0:NCH, :], ALU.add, ["ps3", K("DG")], [K("MT")])
            O.cp("act", NN[u][R, 0:NCH, :], npp, ["ps6"], [K("NN")])
            yield "c"
            if do_out:
                O.tt("dve", QeT[u][R, 0:NCH, :], qep, qq[R, 0:NCH, 1, :], ALU.add, ["ps7", K("QQ")], [K("QeT")])
        if lvl < 5:
            return
        Hcur = Hst[:, p, :]
        if samp:
            P.dma("sp", Hst[:, p, :], io["s0T"][:, p, :], reads=[], writes=["H%d" % p])
        hkey = "H%d" % p
        stp = psv(3, slice(0, 128), NCH, 64)
        for i in range(NCH):
            if do_out:
                O.cp("act", Hb[u][:, i, :], Hcur, [hkey], [K("Hb")])
            for h in range(2):
                R = slice(64 * h, 64 * h + 64)
                O.mm(stp[R, i, :], MT[u][R, i, :], Hst[R, p, :], [K("MT"), hkey], ["ps3"])
            O.tt("dve", Hcur, stp[:, i, :], NN[u][:, i, :], ALU.add, ["ps3", K("NN"), hkey], [hkey])
            yield "c"
        if samp:
            P.dma("sp", io["wkv_s"][:, p, :], Hst[:, p, :], reads=[hkey], semkey="wkvs%d" % p, final=True)
        elif ct == 7:
            P.dma("sp", io["wkv_p"][:, p, :], Hst[:, p, :], reads=[hkey], semkey="wkvp%d" % p, final=True)
        if lvl < 6:
            return
        if do_out:
            ytp = psv(7, slice(0, 128), NCH, 64)
            yt1 = psv(4, slice(0, 128), NCH, 64)
            for h in range(2):
                R = slice(64 * h, 64 * h + 64)
                kz = "Z%d" % h
                kg = "G%d" % h
                for i in range(NCH):
                    O.mm(yt1[R, i, :], Hb[u][R, i, :], QeT[u][R, i, :], [K("Hb"), K("QeT")], ["ps4"])
                for i in range(NCH):
                    O.mm(ytp[R, i, :], Zb[h][:, i, 64:128], GB[h][:, i, 64:128], [kz + "b", kg + "B"], ["ps7"], start=True, stop=False)
                    O.mm(ytp[R, i, :], TMv[u][:, i, R], GK[h][:, i, 64:128], [K("TMv"), kg + "K"], ["ps7"], start=False, stop=True)
                    yield "c"
            y_ = ysb[u]
            O.cp("act", y_[:, W_], ps[4][:, W_], ["ps4"], [K("ysb")])
            O.tt("dve", y_[:, W_], y_[:, W_], ps[7][:, W_], ALU.add, ["ps7", K("ysb")], [K("ysb")])
            O.mm(ps[3][:, W_], cst["b64m"][:, :], y_[:, W_], ["c_b64m", K("ysb")], ["ps3"])
            O.tt("dve", yc_[u][:, W_], y_[:, W_], ps[3][:, W_], ALU.subtract, ["ps3", K("ysb")], [K("yc")])
            O.tt("pool", sq_[u][:, W_], yc_[u][:, W_], yc_[u][:, W_], ALU.mult, [K("yc")], [K("sq")])
            O.mm(ps[3][:, W_], cst["b64m"][:, :], sq_[u][:, W_], ["c_b64m", K("sq")], ["ps3"])
            O.ts("dve", rstd[u][:, W_], ps[3][:, W_], 64e-5, None, ALU.add, None, ["ps3"], [K("rstd")])
            O.rsqrt(rstd[u][:, W_], rstd[u][:, W_], [K("rstd")], [K("rstd")])
            yield "c"
            O.tt("pool", yc_[u][:, W_], yc_[u][:, W_], rstd[u][:, W_], ALU.mult, [K("yc"), K("rstd")], [K("yc")])
            O.ts("dve", yc_[u][:, W_], yc_[u][:, W_], pc(8), pc(9), ALU.mult, ALU.add, [K("yc"), "prm"], [K("yc")])
            O.tt("pool", yc_[u][:, W_], yc_[u][:, W_], T("bon")[:, W_], ALU.add, [K("yc"), K("bon")], [K("yc")])
            oc0 = 1024 if samp else (ct - 6) * 512
            O.tt("dve", yT[:, p, oc0:oc0 + RW_], yc_[u][:, 0:RW_], T("gg")[:, 0:RW_], ALU.mult, [K("yc"), K("gg")], ["yT%d" % p])

    units = []
    cnt_u = 0
    for ct in tiles:
        for pi_, p in enumerate(pairs):
            units.append(unit_gen(ct, p, cnt_u % 2, pi_ == 0))
            cnt_u += 1
    prev = None
    for g in units:
        d_done = False
        c_done = prev is None
        while not (d_done and c_done):
            if not d_done:
                try:
                    if next(g) == "DONE_D":
                        d_done = True
                except StopIteration:
                    d_done = True
                    g = None
            for _rep in range(B1_RATIO):
                if not c_done:
                    try:
                        next(prev)
                    except StopIteration:
                        c_done = True
        prev = g
    if prev is not None:
        for _ in prev:
            pass
    P.barrier()
    P.release(m0)


IN_SPECS = dict(
    xT=[D, NTOK], w_in=[D, 7472], shiftT=[N_FM, 1], s0T=[128, 8, 64],
    rw_prm=[128, 8, 10], rw_mul=[128, 3], rw_w2a2=[128, 1024], rw_g2=[160, 1024],
    c_ident=[128, 128], c_b64=[128, 128], c_b64m=[128, 128], c_mskL=[64, 64], c_mskU=[64, 128], c_i2=[128, 64],
    c_maskR=[128, 512], c_tri=[128, 128], c_cvec=[65, 1],
    xo=[NOWN, D], w_o=[D, D], ln1_g=[1, D], ln1_b=[1, D], w_up=[D, DFF], w_down=[DFF, D], ln2_g=[1, D], ln2_b=[1, D],
    fx_g=[64, 16], fx_bf=[16, 1], fx_padb=[128, 32], fx_clfT=[16, 1024], fx_ckT=[16, 64, 1024], fx_cv=[1024, 1024],
)


def make_consts():
    c = {}
    c["c_ident"] = np.eye(128, dtype=np.float32)
    b = np.zeros((128, 128), np.float32)
    b[:64, :64] = 1.0
    b[64:, 64:] = 1.0
    c["c_b64"] = b
    c["c_b64m"] = b / 64.0
    t = np.arange(64)
    c["c_mskL"] = (t[:, None] > t[None, :]).astype(np.float32)
    c["c_mskU"] = np.concatenate([(t[:, None] < t[None, :]), (t[:, None] <= t[None, :])], 1).astype(np.float32)
    c["c_i2"] = np.concatenate([np.eye(64), np.eye(64)], 0).astype(np.float32)
    m = np.ones((128, 512), np.float32)
    m[:, ::64] = 0.0
    c["c_maskR"] = m
    k = np.arange(128)
    c["c_tri"] = (k[:, None] <= k[None, :]).astype(np.float32)
    cv = np.full((65, 1), 1.0 / 64.0, np.float32)
    cv[64, 0] = 1e-6
    c["c_cvec"] = cv
    return c


def build(stages=("A", "B1", "B2", "C"), debug=(), pj_input=False, b1_kw=None, b2_kw=None, c_kw=None, extra_in=()):
    P = Prog()
    nc = P.nc
    io = {}
    need = set()
    if "A" in stages:
        need |= {"xT", "w_in"}
    if "B1" in stages:
        need |= {"shiftT", "s0T", "rw_prm", "rw_mul", "rw_w2a2", "rw_g2"}
    if "B2" in stages:
        need |= {"fx_g", "fx_bf", "fx_padb", "fx_clfT", "fx_ckT", "fx_cv"}
    if "C" in stages:
        need |= {"xo", "w_o", "ln1_g", "ln1_b", "w_up", "w_down", "ln2_g", "ln2_b"}
    need |= {k for k in IN_SPECS if k.startswith("c_")}
    need |= set(extra_in)
    for name, shape in IN_SPECS.items():
        if name in need:
            io[name] = nc.dram_tensor(name, list(shape), F32, kind="ExternalInput").ap()
    kind = "ExternalInput" if pj_input else ("ExternalOutput" if "PJ" in debug else "Internal")
    io["PJ"] = nc.dram_tensor("PJ", [N_FM, NTOK], F32, kind=kind).ap()
    io["VTM"] = nc.dram_tensor("VTM", [NTOK, 1024], F32, kind=kind).ap()
    io["wkv_p"] = nc.dram_tensor("wkv_p", [128, 8, 64], F32, kind="ExternalOutput").ap()
    io["wkv_s"] = nc.dram_tensor("wkv_s", [128, 8, 64], F32, kind="ExternalOutput").ap()
    io["spl_p"] = nc.dram_tensor("spl_p", [16, 3, 1024], BF16, kind="Internal").ap()
    io["spl_s"] = nc.dram_tensor("spl_s", [16, 3, NS], BF16, kind="Internal").ap()
    io["logf_o"] = nc.dram_tensor("logf_o", [16, NOWN], F32, kind="ExternalOutput").ap()
    io["ps"] = [nc.alloc_psum_tensor("psb%d" % i, [128, 512], F32) for i in range(8)]
    cst = load_consts(P, io)
    if "A" in stages:
        stage_A(P, io)
    else:
        P.barrier()
    yT = P.sb("yT", [128, 16, NOWN], BF16)
    if "A" in stages or "outs" in debug:
        io["k_out"] = nc.dram_tensor("k_out", [1024, NOWN], F32, kind="ExternalOutput").ap()
        io["v_out"] = nc.dram_tensor("v_out", [NOWN, 1024], F32, kind="ExternalOutput").ap()
        io["shA"] = nc.dram_tensor("shA", [2176, 2], F32, kind="ExternalOutput").ap()
        io["shB"] = nc.dram_tensor("shB", [1184, 2], F32, kind="ExternalOutput").ap()
        PJ = io["PJ"]
        P.dma("sp", io["k_out"], PJ[ROW["fk"]:ROW["fk"] + 1024, OWN0:NTOK], semkey="o_k", final=True)
        P.dma("sp", io["v_out"], io["VTM"][OWN0:NTOK, :], semkey="o_v", final=True)
        for ci, col in enumerate((NPR - 1, NTOK - 1)):
            P.dma("sp", io["shA"][:, ci:ci + 1], PJ[0:2176, col:col + 1], semkey="o_sa", final=True, allow_slow_non_contiguous=True)
            P.dma("sp", io["shB"][:, ci:ci + 1], PJ[ROW["r"]:ROW["r"] + 1184, col:col + 1], semkey="o_sb", final=True,
                  allow_slow_non_contiguous=True)
    if "B1" in stages:
        stage_B1(P, io, cst, yT, **(b1_kw or {}))
    if "B2" in stages:
        stage_B2(P, io, cst, yT, **(b2_kw or {}))
    if "yTin" in debug:
        io["yT_in"] = nc.dram_tensor("yT_in", [128, 16, NOWN], F32, kind="ExternalInput").ap()
        P.dma("pool", yT[:], io["yT_in"], writes=["yT%d" % i for i in range(16)], semkey="ytin")
    if "C" in stages:
        io["y_out"] = nc.dram_tensor("y_out", [NOWN, D], F32, kind="ExternalOutput").ap()
        stage_C(P, io, cst, yT, **(c_kw or {}))
    if "yT" in debug:
        io["yT_dbg"] = nc.dram_tensor("yT_dbg", [128, 16, NOWN], F32, kind="ExternalOutput").ap()
        P.dma("pool", io["yT_dbg"], yT[:], reads=["yT%d" % i for i in range(16)], semkey="ytdbg", final=True)
    P.emit()
    return P


def stage_B2(P, io, cst, yT, heads=range(16), insts=("p", "s")):
    nc = P.nc
    O = Ops(P)
    PJ = io["PJ"]
    VTM = io["VTM"]
    ps = io["ps"]
    m0 = P.mark()
    ones = P.sb("ones", [128, 512], F32)
    O.memset("pool", ones[:], 1.0, ["ones"])
    fog = P.sb("fog", [64, 16], F32)
    P.dma("sp", fog[:], io["fx_g"], writes=["fog"])
    nbf = P.sb("nbf", [16, 1], F32)
    P.dma("sp", nbf[:], io["fx_bf"], writes=["nbf"])
    O.ts("dve", nbf[:], nbf[:], -1.0, None, ALU.mult, None, ["nbf"], ["nbf"])
    cvec = P.sb("cvec", [65, 1], F32)
    P.dma("sp", cvec[:], io["c_cvec"], writes=["cvec"])
    padb = P.sb("padb", [128, 32], F32)
    P.dma("sp", padb[:], io["fx_padb"], writes=["padb"])
    Crow = P.sb("Crow", [16, NPR], F32)
    Ccum = P.sb("Ccum", [16, NPR], F32)
    Ckb = P.sb("Ckb", [128, 32, 16], F32)
    spl = P.sb("spl", [16, 3, 1024], BF16)
    q8 = P.sb("q8", [16, 1024], F32)
    lfo = P.sb("lfo", [16, NOWN], F32)
    KTa = [P.sb("KTa%d" % i, [67, NPR], BF16) for i in range(2)]
    QTa = [P.sb("QTa%d" % i, [67, 1024], BF16) for i in range(2)]
    Va = [P.sb("Va%d" % i, [128, 32, 65], BF16) for i in range(2)]
    ogT = [P.sb("ogT%d" % i, [64, 1024], F32) for i in range(2)]
    pt = [P.sb("pt%d" % i, [128, 512], BF16) for i in range(4)]
    Oa = P.sb("Oa", [65, 512], F32)
    sq = P.sb("sqf", [65, 512], F32)
    rs = P.sb("rs", [1, 512], F32)
    y1 = P.sb("y1", [64, 512], F32)
    ytmp = [P.sb("ytmp%d" % i, [64, 512], BF16) for i in range(2)]
    for i in range(2):
        O.memset("pool", KTa[i][64:67, :], 1.0, ["KTa%d" % i])
        O.memset("pool", Va[i][:, :, 64:65], 1.0, ["Va%d" % i])
    cnt = dict(h=0, s=0, p=0, y=0)

    for inst in insts:
        if inst == "p":
            TK, NQ, QPOS0 = NPR, 1024, OWN0
            qtiles = [(0, 512), (512, 512)]
            ocol0 = 0
        else:
            TK, NQ, QPOS0 = 1024 + NS, NS, 1024
            qtiles = [(0, NS)]
            ocol0 = 1024
        nblk = (TK + 127) // 128
        if inst == "p":
            P.dma("sp", Crow[:, 0:TK], PJ[ROW["f"]:ROW["f"] + 16, 0:TK], writes=["Crow"])
            fsl = slice(0, TK)
        else:
            P.dma("sp", Crow[:, 0:1024], io["fx_clfT"], writes=["Crow"])
            P.dma("sp", Crow[:, 1024:TK], PJ[ROW["f"]:ROW["f"] + 16, NPR:NPR + NS], writes=["Crow"], semkey="Crow_b")
            O.ts("dve", Crow[:, 0:1024], Crow[:, 0:1024], -1.0, None, ALU.mult, None, ["Crow"], ["Crow"])
            fsl = slice(1024, TK)
        O.act(Crow[:, fsl], Crow[:, fsl], AF.Exp, ["Crow", "nbf"], ["Crow"], bias=nbf[:, 0:1], scale=-1.0)
        O.act(Crow[:, fsl], Crow[:, fsl], AF.Ln, ["Crow"], ["Crow"], bias=1.0)
        if inst == "p":
            P.op("act", lambda: nc.scalar.mul(lfo[:, 0:1024], Crow[:, OWN0:NPR], -1.0), reads=["Crow"], writes=["lfo"])
        else:
            P.op("act", lambda: nc.scalar.mul(lfo[:, 1024:NOWN], Crow[:, 1024:TK], -1.0), reads=["Crow"], writes=["lfo"])
            P.dma("sp", io["logf_o"], lfo[:], reads=["lfo"], semkey="lfo", final=True)
        for c0 in range(0, TK, 512):
            cw = min(512, TK - c0)
            init = cst["zcol"][0:16, 0:1] if c0 == 0 else Ccum[:, c0 - 1:c0]
            o_ap, d0, d1 = Ccum[:, c0:c0 + cw], ones[0:16, 0:cw], Crow[:, c0:c0 + cw]
            P.op("dve", (lambda o_ap=o_ap, d0=d0, d1=d1, init=init: nc.vector.tensor_tensor_scan(o_ap, d0, d1, init, ALU.mult, ALU.add)),
                 reads=["Crow", "ones", "Ccum"], writes=["Ccum"])
        tp = ps[6][:, 0:nblk * 16].rearrange("p (a b) -> p a b", a=nblk)
        for kb in range(nblk):
            nk = min(128, TK - kb * 128)
            O.tr(tp[0:nk, kb, :], Ccum[:, kb * 128:kb * 128 + nk], cst["identf"][0:16, 0:16], ["Ccum", "c_identf"], ["ps6"])
        nfull = TK // 128
        if inst == "p":
            O.tt("dve", Ckb[:, 0:nblk, :], tp[:, 0:nblk, :], padb[:, 0:nblk].unsqueeze(2).to_broadcast([128, nblk, 16]), ALU.add,
                 ["ps6", "padb"], ["Ckb"])
        else:
            O.cp("dve", Ckb[:, 0:nfull, :], tp[:, 0:nfull, :], ["ps6"], ["Ckb"])
            O.cp("dve", Ckb[0:NS, nfull, :], tp[0:NS, nfull, :], ["ps6", "Ckb"], ["Ckb"])
        qs = slice(QPOS0, QPOS0 + NQ)
        O.ts("dve", q8[:, 0:NQ], Ccum[:, qs], -8.0, None, ALU.mult, None, ["Ccum"], ["q8"])
        for k3 in range(3):
            O.cp("dve", spl[:, k3, 0:NQ], q8[:, 0:NQ], ["q8"], ["spl"])
            if k3 < 2:
                O.tt("dve", q8[:, 0:NQ], q8[:, 0:NQ], spl[:, k3, 0:NQ], ALU.subtract, ["q8", "spl"], ["q8"])
        sk = "splD" + inst
        P.dma("sp", io["spl_" + inst], spl[:, :, 0:NQ], reads=["spl"], writes=[sk])
        for h in heads:
            hb = cnt["h"] % 2
            cnt["h"] += 1
            kK, kQ, kV, kG = "KTa%d" % hb, "QTa%d" % hb, "Va%d" % hb, "ogT%d" % hb
            fk0 = ROW["fk"] + 64 * h
            if inst == "p":
                P.dma("pool", KTa[hb][0:64, 0:TK], PJ[fk0:fk0 + 64, 0:TK], writes=[kK])
                P.dma("pool", Va[hb][:, 0:32, 0:64], VTM[0:NPR, 64 * h:64 * h + 64].rearrange("(b p) d -> p b d", p=128), writes=[kV])
                ocs = slice(OWN0, NPR)
            else:
                P.dma("pool", KTa[hb][0:64, 0:1024], io["fx_ckT"][h], writes=[kK])
                P.dma("pool", KTa[hb][0:64, 1024:TK], PJ[fk0:fk0 + 64, NPR:NPR + NS], writes=[kK])
                P.dma("pool", Va[hb][:, 0:8, 0:64], io["fx_cv"][:, 64 * h:64 * h + 64].rearrange("(b p) d -> p b d", p=128), writes=[kV])
                P.dma("pool", Va[hb][0:NS, 8, 0:64], VTM[NPR:NPR + NS, 64 * h:64 * h + 64], writes=[kV])
                ocs = slice(NPR, NPR + NS)
            P.dma("pool", QTa[hb][0:64, 0:NQ], PJ[ROW["q"] + 64 * h:ROW["q"] + 64 * h + 64, ocs], writes=[kQ])
            P.dma("sp", QTa[hb][64:67, 0:NQ], io["spl_" + inst][h], reads=[sk], writes=[kQ])
            P.dma("sp", ogT[hb][:, 0:NQ], PJ[ROW["og"] + 64 * h:ROW["og"] + 64 * h + 64, ocs], writes=[kG])
            for (q0, qw) in qtiles:
                ob = 2 + cnt["y"] % 2
                okey = "ps%d" % ob
                last_q = QPOS0 + q0 + qw - 1
                kbs = [kb for kb in range(nblk) if kb * 128 <= last_q]
                pend = []
                for n_, kb in enumerate(kbs):
                    nk = min(128, TK - kb * 128)
                    off = max(0, kb * 128 - (QPOS0 + q0))
                    diag = kb * 128 + nk - 1 > QPOS0 + q0
                    sb_ = (0, 1, 7)[cnt["s"] % 3]
                    cnt["s"] += 1
                    pb_ = cnt["p"] % 4
                    cnt["p"] += 1
                    skey, pkey = "ps%d" % sb_, "pt%d" % pb_
                    O.mm(ps[sb_][0:nk, off:qw], KTa[hb][0:67, kb * 128:kb * 128 + nk], QTa[hb][0:67, q0 + off:q0 + qw], [kK, kQ], [skey])
                    if len(pend) >= 2:
                        pend.pop(0)()
                    O.act(pt[pb_][0:nk, off:qw], ps[sb_][0:nk, off:qw], AF.Exp, [skey, "Ckb"], [pkey], bias=Ckb[0:nk, kb, h:h + 1], scale=0.125)
                    if diag:
                        mw = min(128, qw - off, nk)
                        O.tt(O.pick(("dve", "pool")), pt[pb_][0:nk, off:off + mw], pt[pb_][0:nk, off:off + mw], cst["tri"][0:nk, 0:mw], ALU.mult,
                             [pkey, "c_tri"], [pkey])
                    pend.append(lambda nk=nk, kb=kb, pb_=pb_, off=off, pkey=pkey, n_=n_:
                                O.mm(ps[ob][0:65, off:qw], Va[hb][0:nk, kb, 0:65], pt[pb_][0:nk, off:qw], [kV, pkey], [okey],
                                     start=(n_ == 0), stop=(n_ == len(kbs) - 1)))
                while pend:
                    pend.pop(0)()
                cnt["y"] += 1
                yb = cnt["y"] % 2
                O.cp("act", Oa[:, 0:qw], ps[ob][0:65, 0:qw], [okey], ["Oa"])
                O.act(sq[:, 0:qw], Oa[:, 0:qw], AF.Square, ["Oa"], ["sqf"])
                O.mm(ps[4][0:1, 0:qw], cvec[:, 0:1], sq[:, 0:qw], ["cvec", "sqf"], ["ps4"])
                O.act(rs[:, 0:qw], ps[4][0:1, 0:qw], AF.Sqrt, ["ps4"], ["rs"])
                O.recip(rs[:, 0:qw], rs[:, 0:qw], ["rs"], ["rs"])
                O.mm(ps[5][0:64, 0:qw], ones[0:1, 0:64], rs[:, 0:qw], ["ones", "rs"], ["ps5"])
                O.tt("dve", y1[:, 0:qw], Oa[0:64, 0:qw], ps[5][0:64, 0:qw], ALU.mult, ["Oa", "ps5"], ["y1"])
                O.act(ogT[hb][:, q0:q0 + qw], ogT[hb][:, q0:q0 + qw], AF.Sigmoid, [kG], [kG])
                O.stt("dve", ytmp[yb][:, 0:qw], y1[:, 0:qw], fog[:, h:h + 1], ogT[hb][:, q0:q0 + qw], ALU.mult, ALU.mult,
                      ["y1", "fog", kG], ["ytmp%d" % yb])
                r0 = (h % 2) * 64
                P.dma("sp", yT[r0:r0 + 64, 8 + h // 2, ocol0 + q0:ocol0 + q0 + qw], ytmp[yb][:, 0:qw],
                      reads=["ytmp%d" % yb], writes=["yT%d" % (8 + h // 2)], semkey="yT%d_%d" % (8 + h // 2, h % 2))
    P.barrier()
    P.release(m0)


TOKT = [(i * 128, 128) for i in range(8)] + [(1024, NS)]
TT3 = [(0, 512), (512, 512), (1024, NS)]


def layer_norm_tiles(P, O, Z, gvec, bvec, zkeys, out_dram=None):
    nc = P.nc
    m = P.mark()
    gb = P.sb("ln_g", [128, D], F32)
    bb = P.sb("ln_b", [128, D], F32)
    junk = P.sb("ln_junk", [128, D], F32)
    st = P.sb("ln_st", [128, 4], F32)
    P.dma("sp", gb[:], gvec.partition_broadcast(128), writes=["ln_g"])
    P.dma("sp", bb[:], bvec.partition_broadcast(128), writes=["ln_b"])
    for ti, (t0, tw) in enumerate(TOKT):
        z = Z[0:tw, ti, :]
        zk = zkeys[ti]
        s = st[0:tw, :]
        O.memset("pool", st[:, :], 0.0, ["ln_st"])
        P.op("act", (lambda z=z, tw=tw, s=s: nc.scalar.activation(out=junk[0:tw, :], in_=z, func=AF.Identity, accum_out=s[:, 0:1])),
             reads=[zk], writes=["ln_junk", "ln_st"])
        O.ts("dve", s[:, 1:2], s[:, 0:1], -1.0 / D, None, ALU.mult, None, ["ln_st"], ["ln_st"])
        O.act(z, z, AF.Identity, [zk, "ln_st"], [zk], bias=s[:, 1:2])
        P.op("act", (lambda z=z, tw=tw, s=s: nc.scalar.activation(out=junk[0:tw, :], in_=z, func=AF.Square, accum_out=s[:, 2:3])),
             reads=[zk, "ln_st"], writes=["ln_junk", "ln_st"])
        O.ts("dve", s[:, 3:4], s[:, 2:3], 1.0 / D, 1e-5, ALU.mult, ALU.add, ["ln_st"], ["ln_st"])
        O.rsqrt(s[:, 3:4], s[:, 3:4], ["ln_st"], ["ln_st"])
        O.stt("dve", z, z, s[:, 3:4], gb[0:tw, :], ALU.mult, ALU.mult, [zk, "ln_st", "ln_g"], [zk])
        O.tt("pool", z, z, bb[0:tw, :], ALU.add, [zk, "ln_b"], [zk])
        if out_dram is not None:
            P.dma("sp", out_dram[t0:t0 + tw, :], z, reads=[zk], semkey="yo%d" % ti, final=True)
    P.release(m)


def stage_C(P, io, cst, yT, dbg=None):
    nc = P.nc
    O = Ops(P)
    ps = io["ps"]
    m0 = P.mark()
    Z = P.sb("Z", [128, 9, D], F32)
    zk = ["Z%d" % i for i in range(9)]
    yk = ["yT%d" % i for i in range(16)]
    m1 = P.mark()
    wo = [P.sb("wo%d" % i, [128, 16, 512], BF16) for i in range(2)]
    wor = io["w_o"].rearrange("(kc p) c -> p kc c", p=128)
    for ti, (t0, tw) in enumerate(TOKT):
        P.dma("sp", Z[0:tw, ti, :], io["xo"][t0:t0 + tw, :], writes=[zk[ti]])
    pi = 0
    for cg in range(4):
        wb = cg % 2
        P.dma("pool", wo[wb][:], wor[:, :, cg * 512:cg * 512 + 512], writes=["wo%d" % wb])
        for ti, (t0, tw) in enumerate(TOKT):
            pb = pi % 4
            pi += 1
            for kc in range(16):
                O.mm(ps[pb][0:tw, :], yT[:, kc, t0:t0 + tw], wo[wb][:, kc, :], [yk[kc], "wo%d" % wb], ["ps%d" % pb],
                     start=(kc == 0), stop=(kc == 15))
            zs = Z[0:tw, ti, cg * 512:cg * 512 + 512]
            O.stt("dve", zs, zs, ALPHA, ps[pb][0:tw, :], ALU.mult, ALU.add, [zk[ti], "ps%d" % pb], [zk[ti]])
    def dump():
        for ti, (t0, tw) in enumerate(TOKT):
            P.dma("sp", io["y_out"][t0:t0 + tw, :], Z[0:tw, ti, :], reads=[zk[ti]], semkey="yo%d" % ti, final=True)
    if dbg == "z":
        dump()
        return
    layer_norm_tiles(P, O, Z, io["ln1_g"], io["ln1_b"], zk)
    if dbg == "h":
        dump()
        return
    hT = yT
    ei = 0
    for kc in range(16):
        for (grp, bank) in ((range(0, 4), 4), (range(4, 8), 5), (range(8, 9), 6)):
            for ti in grp:
                t0, tw = TOKT[ti]
                c0 = t0 - TOKT[grp[0]][0]
                O.tr(ps[bank][:, c0:c0 + tw], Z[0:tw, ti, kc * 128:kc * 128 + 128], cst["identf"][0:tw, 0:tw], [zk[ti], "c_identf"], ["ps%d" % bank])
            g0 = TOKT[grp[0]][0]
            gw = sum(TOKT[ti][1] for ti in grp)
            ei += 1
            O.cp("act" if ei % 2 else "dve", hT[:, kc, g0:g0 + gw], ps[bank][:, 0:gw], ["ps%d" % bank], [yk[kc]])
    for ti, (t0, tw) in enumerate(TOKT):
        O.ts("pool", Z[0:tw, ti, :], Z[0:tw, ti, :], ALPHA, None, ALU.mult, None, [zk[ti]], [zk[ti]])
    P.barrier()
    P.release(m1)
    m2 = P.mark()
    wu = [P.sb("wu%d" % i, [128, 16, 512], BF16) for i in range(2)]
    wd = [P.sb("wd%d" % i, [128, 4, D], BF16) for i in range(2)]
    aT = [P.sb("aT%d" % i, [128, 4, NOWN], BF16) for i in range(2)]
    rT = [P.sb("rT%d" % i, [128, 512], BF16) for i in range(3)]
    wur = io["w_up"].rearrange("(kc p) c -> p kc c", p=128)
    wdr = io["w_down"].rearrange("(g kc p) c -> g p kc c", p=128, kc=4)
    NG = DFF // 512

    def load_w(g):
        b = g % 2
        P.dma("pool", wu[b][:], wur[:, :, g * 512:g * 512 + 512], writes=["wu%d" % b])
        P.dma("pool", wd[b][:], wdr[g], writes=["wd%d" % b])
    load_w(0)
    st = dict(pu=0, pd=0, ri=0)

    def up(g):
        b = g % 2
        for blk in range(4):
            for (t0, tw) in TT3:
                pb = st["pu"] % 3
                st["pu"] += 1
                for kc in range(16):
                    O.mm(ps[pb][:, 0:tw], wu[b][:, kc, blk * 128:blk * 128 + 128], hT[:, kc, t0:t0 + tw], ["wu%d" % b, yk[kc]], ["ps%d" % pb],
                         start=(kc == 0), stop=(kc == 15))
                rb = st["ri"] % 3
                st["ri"] += 1
                O.act(rT[rb][:, 0:tw], ps[pb][:, 0:tw], AF.Relu, ["ps%d" % pb], ["rT%d" % rb])
                O.tt("pool", aT[b][:, blk, t0:t0 + tw], rT[rb][:, 0:tw], rT[rb][:, 0:tw], ALU.mult, ["rT%d" % rb], ["aT%d" % b])

    def down(g):
        b = g % 2
        for ti, (t0, tw) in enumerate(TOKT):
            for cg in range(4):
                pb = 3 + st["pd"] % 5
                st["pd"] += 1
                for blk in range(4):
                    O.mm(ps[pb][0:tw, :], aT[b][:, blk, t0:t0 + tw], wd[b][:, blk, cg * 512:cg * 512 + 512], ["aT%d" % b, "wd%d" % b], ["ps%d" % pb],
                         start=(blk == 0), stop=(blk == 3))
                zs = Z[0:tw, ti, cg * 512:cg * 512 + 512]
                O.tt("dve", zs, zs, ps[pb][0:tw, :], ALU.add, [zk[ti], "ps%d" % pb], [zk[ti]])
    up(0)
    for g in range(NG):
        if g + 1 < NG:
            load_w(g + 1)
            up(g + 1)
        down(g)
    P.barrier()
    P.release(m2)
    if dbg == "acc":
        dump()
        return
    layer_norm_tiles(P, O, Z, io["ln2_g"], io["ln2_b"], zk, out_dram=io["y_out"])
    P.release(m0)


_PROG = {}


def host_maps(inp):
    g = lambda k: np.asarray(inp[k], np.float32)[0]
    xp = np.asarray(inp["x_prompt"], np.float32)
    xs = np.asarray(inp["x_sample"], np.float32)
    w_in = g("w_in")
    wperm = np.ascontiguousarray(w_in[:, PERM])
    consts = make_consts()
    mu = g("rwkv_mu")
    cols = [mu[0:1024], mu[1024:2048], mu[2048:3072], g("rwkv_w0"), g("rwkv_a0"), g("rwkv_k_k"), g("rwkv_k_a"),
            g("rwkv_r_k").reshape(-1), g("rwkv_lnx_g"), g("rwkv_lnx_b")]
    rw_prm = np.ascontiguousarray(np.stack([v.reshape(8, 128) for v in cols], -1).transpose(1, 0, 2))
    rw_mul = np.zeros((128, 3), np.float32)
    rw_mul[:, 0] = mu[3072:3200]
    rw_mul[:, 1] = mu[3200:3328]
    rw_mul[:32, 2] = mu[3328:3360]
    shared = dict(consts)
    shared.update(
        w_in=wperm, rw_prm=rw_prm, rw_mul=rw_mul,
        rw_w2a2=np.ascontiguousarray(np.concatenate([g("rwkv_w2"), g("rwkv_a2")], 0)), rw_g2=g("rwkv_g2"),
        fx_g=np.ascontiguousarray(g("fox_out_g").reshape(16, 64).T), fx_bf=g("fox_b_f").reshape(16, 1),
        w_o=g("w_o"), ln1_g=g("ln1_g").reshape(1, D), ln1_b=g("ln1_b").reshape(1, D), w_up=g("w_up"), w_down=g("w_down"),
        ln2_g=g("ln2_g").reshape(1, D), ln2_b=g("ln2_b").reshape(1, D))
    maps = []
    pos = np.arange(NPR).reshape(32, 128).T
    for c in range(8):
        b, j = divmod(c, 4)
        n = 1024 * (j + 1)
        xpad = np.zeros((NTOK, D), np.float32)
        xpad[NPR - n:NPR] = xp[b, :n]
        xpad[NPR:] = xs[c]
        m = dict(shared)
        m["xT"] = np.ascontiguousarray(xpad.T)
        m["xo"] = np.ascontiguousarray(xpad[OWN0:NTOK])
        sh = np.zeros((N_FM, 1), np.float32)
        st = np.asarray(inp["state_rwkv_shift"], np.float32)[0, c, 0]
        for seg in ("rk", "rv", "xwxa", "r", "xg"):
            o, n_ = SEG[seg]
            sh[ROW[seg]:ROW[seg] + n_, 0] = st[o:o + n_]
        m["shiftT"] = sh
        s0 = np.asarray(inp["state_rwkv_wkv"], np.float32)[0, c]
        m["s0T"] = np.ascontiguousarray(s0.transpose(0, 2, 1).reshape(8, 128, 64).transpose(1, 0, 2))
        m["fx_padb"] = np.where(pos < NPR - n, -BIG, 0.0).astype(np.float32)
        m["fx_clfT"] = np.ascontiguousarray(np.asarray(inp["cache_fox_logf"], np.float32)[0, c].T)
        m["fx_ckT"] = np.ascontiguousarray(np.asarray(inp["cache_fox_k"], np.float32)[0, c].transpose(1, 2, 0))
        m["fx_cv"] = np.ascontiguousarray(np.asarray(inp["cache_fox_v"], np.float32)[0, c].reshape(1024, 1024))
        maps.append(m)
    return maps


def assemble(results):
    f32 = np.float32
    y_p = np.zeros((2, 4096, D), f32)
    y_s = np.zeros((8, NS, D), f32)
    pk = np.zeros((1, 2, 4096, 16, 64), f32)
    pv = np.zeros((1, 2, 4096, 16, 64), f32)
    pf = np.zeros((1, 2, 4096, 16), f32)
    pS = np.zeros((1, 2, 16, 64, 64), f32)
    psh = np.zeros((1, 2, 1, 3360), f32)
    sk = np.zeros((1, 8, NS, 16, 64), f32)
    sv = np.zeros((1, 8, NS, 16, 64), f32)
    sf = np.zeros((1, 8, NS, 16), f32)
    sS = np.zeros((1, 8, 16, 64, 64), f32)
    ssh = np.zeros((1, 8, 1, 3360), f32)

    def wkv(a):
        return a.reshape(2, 64, 8, 64).transpose(2, 0, 3, 1).reshape(16, 64, 64)

    def shift(r, ci):
        out = np.zeros(3360, f32)
        a, bq = r["shA"][:, ci], r["shB"][:, ci]
        out[1024:2048] = a[0:1024]
        out[2048:3072] = a[1024:2048]
        out[3072:3200] = a[2048:2176]
        out[0:1024] = bq[0:1024]
        out[3200:3360] = bq[1024:1184]
        return out
    for c in range(8):
        r = results[c]
        b, j = divmod(c, 4)
        sl = slice(1024 * j, 1024 * (j + 1))
        y_p[b, sl] = r["y_out"][0:1024]
        y_s[c] = r["y_out"][1024:NOWN]
        kT = r["k_out"]
        pk[0, b, sl] = kT[:, 0:1024].T.reshape(1024, 16, 64)
        sk[0, c] = kT[:, 1024:NOWN].T.reshape(NS, 16, 64)
        pv[0, b, sl] = r["v_out"][0:1024].reshape(1024, 16, 64)
        sv[0, c] = r["v_out"][1024:NOWN].reshape(NS, 16, 64)
        pf[0, b, sl] = r["logf_o"][:, 0:1024].T
        sf[0, c] = r["logf_o"][:, 1024:NOWN].T
        sS[0, c] = wkv(r["wkv_s"])
        ssh[0, c, 0] = shift(r, 1)
        if j == 3:
            pS[0, b] = wkv(r["wkv_p"])
            psh[0, b, 0] = shift(r, 0)
    return (y_p, y_s, pk, pv, pf, pS, psh, sk, sv, sf, sS, ssh)


def kernel(**inputs):
    if "full" not in _PROG:
        _PROG["full"] = build()
    P = _PROG["full"]
    maps = host_maps(inputs)
    res = run_bass_kernel_spmd(P.nc, maps, core_ids=list(range(8)))
    return assemble(res.results)
```

```python
import numpy as np
import concourse.bass as bass
import concourse.mybir as mybir
from concourse.bass_utils import run_bass_kernel_spmd

F32 = mybir.dt.float32
BF16 = mybir.dt.bfloat16
AF = mybir.ActivationFunctionType
ALU = mybir.AluOpType
AX = mybir.AxisListType

D = 2048
NPR = 4096
NS = 16
NTOK = NPR + NS
OWN0 = 3072
NOWN = 1024 + NS
DFF = 8192
C0 = float(np.exp(-0.5))
ALPHA = 2.0 ** 0.25
BIG = 30000.0

SEG = dict(rk=(1024, 1024), rv=(2048, 1024), xwxa=(3072, 128), fk=(3360 + 1024, 1024), f=(3360 + 3072, 16),
           r=(0, 1024), xg=(3200, 160), q=(3360, 1024), og=(3360 + 3088, 1024), fv=(3360 + 2048, 1024))
ORDER = ["rk", "rv", "xwxa", "fk", "f", "r", "xg", "q", "og", "fv"]
ROW = {}
_o = 0
for _k in ORDER:
    ROW[_k] = _o
    _o += SEG[_k][1]
N_ALL = ROW["r"]
N_OWN = ROW["fv"] - N_ALL
N_FM = ROW["fv"]
PERM = np.concatenate([np.arange(SEG[k][0], SEG[k][0] + SEG[k][1]) for k in ORDER])


EMBED_WAIT = True
SAME_ENG_NOWAIT = ()


class Prog:
    ENG = ("pe", "act", "dve", "pool", "sp")

    def __init__(self):
        self.nc = bass.Bass("TRN2", target_bir_lowering=False)
        nc = self.nc
        self.eng = {"pe": nc.tensor, "act": nc.scalar, "dve": nc.vector, "pool": nc.gpsimd, "sp": nc.sync}
        self.ops = []
        self.sb_base = 16640
        self.sb_top = 229376
        self.sb_off = self.sb_base
        self.nalloc = 0

    def sb(self, name, shape, dtype):
        per = int(np.prod(shape[1:])) * (2 if dtype == BF16 else 4)
        off = (self.sb_off + 63) // 64 * 64
        assert off + per <= self.sb_top, ("sbuf overflow", name, off, per)
        self.sb_off = off + per
        self.nalloc += 1
        return self.nc.alloc_sbuf_tensor_at("%s_%d" % (name, self.nalloc), list(shape), dtype, offset=off)

    def mark(self):
        return self.sb_off

    def release(self, m):
        self.sb_off = m

    def op(self, eng, fn, reads=(), writes=()):
        self.ops.append(dict(eng=eng, fn=fn, reads=tuple(reads), writes=tuple(writes), dma=False, bar=False))

    def dma(self, q, out, in_, reads=(), writes=(), semkey=None, final=False, **kw):
        if semkey is None:
            semkey = writes[0] if writes else reads[0]
        e = self.eng[q]
        self.ops.append(dict(eng=q, fn=(lambda e=e, out=out, in_=in_, kw=kw: e.dma_start(out=out, in_=in_, **kw)),
                             reads=tuple(reads), writes=tuple(writes), dma=True, semkey=semkey, final=final, bar=False))

    def barrier(self):
        for e in self.ENG:
            self.ops.append(dict(eng=e, fn=None, reads=(), writes=(), dma=False, bar=True))

    def emit(self):
        nc = self.nc
        ops = self.ops
        n = len(ops)
        lastw = {}
        readers = {}
        deps = [None] * n
        last_eng = {}
        dma_since = []
        bar_group = None
        for i, o in enumerate(ops):
            if o["bar"]:
                if bar_group is None:
                    bar_group = (set(last_eng.values()) | set(dma_since))
                deps[i] = set(bar_group)
                nxt = ops[i + 1] if i + 1 < n else None
                if nxt is None or not nxt["bar"]:
                    bar_group = None
                    dma_since = []
                    lastw = {}
                    readers = {}
                continue
            d = set()
            for k in o["reads"]:
                if k in lastw:
                    d.add(lastw[k])
            for k in o["writes"]:
                if k in lastw:
                    d.add(lastw[k])
                for r in readers.get(k, ()):
                    d.add(r)
            d.discard(i)
            deps[i] = d
            for k in o["reads"]:
                readers.setdefault(k, []).append(i)
            for k in o["writes"]:
                lastw[k] = i
                readers[k] = []
            if o["dma"]:
                dma_since.append(i)
            else:
                last_eng[o["eng"]] = i

        def need_wait(o, pj):
            if pj["dma"] or o["dma"] or o["bar"]:
                return not (o["bar"] and (not pj["dma"]) and pj["eng"] == o["eng"])
            if pj["eng"] != o["eng"]:
                return True
            if o["eng"] == "pe" or o["eng"] in SAME_ENG_NOWAIT:
                return False
            return any(k in pj["writes"] for k in o["reads"])

        needed = set()
        for i, o in enumerate(ops):
            for j in deps[i]:
                if need_wait(o, ops[j]):
                    needed.add(j)
            if o["dma"]:
                needed.add(i)
        esem = {e: nc.alloc_semaphore("c_" + e) for e in self.ENG}
        ecnt = {e: 0 for e in self.ENG}
        dpool = []
        dcnt = []
        keymap = {}
        token = [None] * n
        for i, o in enumerate(ops):
            if o["bar"]:
                keymap = {}
                continue
            if o["dma"]:
                sk = o["semkey"]
                if sk not in keymap:
                    idx = len(keymap)
                    if idx >= len(dpool):
                        dpool.append(nc.alloc_semaphore("d%d" % idx))
                        dcnt.append(0)
                    keymap[sk] = idx
                idx = keymap[sk]
                dcnt[idx] += 16
                token[i] = ("d:%d" % idx, dpool[idx], dcnt[idx])
            elif i in needed:
                e = o["eng"]
                ecnt[e] += 1
                token[i] = ("e:" + e, esem[e], ecnt[e])
        dsem = dpool
        seen = {e: {} for e in self.ENG}
        dlatest = {}
        nwaits = 0
        for i, o in enumerate(ops):
            e = o["eng"]
            eng = self.eng[e]
            want = {}
            for j in deps[i]:
                tk = token[j]
                if tk is None or not need_wait(o, ops[j]):
                    continue
                name, sem, val = tk
                if name.startswith("d:"):
                    val = dlatest[name]
                if want.get(name, (None, 0))[1] < val:
                    want[name] = (sem, val)
            todo = [(name, sem, val) for name, (sem, val) in want.items() if seen[e].get(name, 0) < val]
            attach = None
            if EMBED_WAIT and todo and not o["bar"] and not o["dma"]:
                attach = todo.pop()
            for name, sem, val in todo:
                eng.wait_ge(sem, val)
                seen[e][name] = val
                nwaits += 1
            if o["bar"]:
                continue
            ins = o["fn"]()
            if attach is not None:
                name, sem, val = attach
                ins._wait_ge(sem, val)
                seen[e][name] = val
            tk = token[i]
            if tk is not None:
                name, sem, val = tk
                if o["dma"]:
                    ins.then_inc(sem, 16)
                    dlatest[name] = val
                else:
                    ins.then_inc(sem, 1)
        for idx in range(len(dpool)):
            name = "d:%d" % idx
            val = dlatest.get(name, 0)
            if val and seen["sp"].get(name, 0) < val:
                self.eng["sp"].wait_ge(dpool[idx], val)
                seen["sp"][name] = val
        self.stats = dict(n_ops=n, n_waits=nwaits, n_dsem=len(dsem), cnt=dict(ecnt))
        return nc


TT_ALL = [(i * 512, 512) for i in range(8)] + [(NPR, NS)]
TT_OWN = [(OWN0 - 1, 1), (OWN0, 512), (OWN0 + 512, 512), (NPR, NS)]


def stage_A(P, io):
    nc = P.nc
    m0 = P.mark()
    xs = P.sb("xs", [128, 16, NTOK], BF16)
    wt = [P.sb("wt%d" % i, [128, 16, 512], BF16) for i in range(2)]
    NOT = 8
    ot = [P.sb("ot%d" % i, [128, 512], F32) for i in range(NOT)]
    vt = [P.sb("vt%d" % i, [128, 1024], F32) for i in range(2)]
    ps = io["ps"]
    xTr = io["xT"].rearrange("(kc p) t -> p kc t", p=128)
    wr = io["w_in"].rearrange("(kc p) c -> p kc c", p=128)
    PJ = io["PJ"]
    VTM = io["VTM"]
    for g in range(8):
        P.dma("pool", xs[:, 2 * g:2 * g + 2, :], xTr[:, 2 * g:2 * g + 2, :], writes=["xs%d" % g])
    st = dict(oi=0, pi=0, wi=0, vi=0)

    def evac(dst, src, wkey, pkey):
        st["pi"] += 1
        if st["pi"] % 2 == 0:
            P.op("act", lambda: nc.scalar.copy(dst, src), reads=[pkey], writes=[wkey])
        else:
            P.op("dve", lambda: nc.vector.tensor_copy(dst, src), reads=[pkey], writes=[wkey])

    def fm_chunk(c0, cw, tts):
        wb = st["wi"] % 2
        st["wi"] += 1
        P.dma("pool", wt[wb][:, :, 0:cw], wr[:, :, c0:c0 + cw], writes=["wt%d" % wb])
        for m in range(0, cw, 128):
            mw = min(128, cw - m)
            for (t0, tw) in tts:
                pb = st["pi"] % 6
                for kc in range(16):
                    P.op("pe", (lambda pb=pb, wb=wb, kc=kc, m=m, mw=mw, t0=t0, tw=tw:
                                nc.tensor.matmul(ps[pb][0:mw, 0:tw], wt[wb][:, kc, m:m + mw], xs[:, kc, t0:t0 + tw],
                                                 start=(kc == 0), stop=(kc == 15))),
                         reads=["wt%d" % wb, "xs%d" % (kc // 2)], writes=["ps%d" % pb])
                ob = st["oi"] % NOT
                st["oi"] += 1
                evac(ot[ob][0:mw, 0:tw], ps[pb][0:mw, 0:tw], "ot%d" % ob, "ps%d" % pb)
                P.dma("sp", PJ[c0 + m:c0 + m + mw, t0:t0 + tw], ot[ob][0:mw, 0:tw], reads=["ot%d" % ob],
                      semkey="oto%d" % ob, **({"allow_slow_non_contiguous": True} if tw == 1 else {}))

    for c0 in range(0, N_ALL, 512):
        fm_chunk(c0, min(512, N_ALL - c0), TT_ALL)
    for c0 in range(N_ALL, N_FM, 512):
        fm_chunk(c0, min(512, N_FM - c0), TT_OWN)
    wbs = []
    for h in range(2):
        wb = st["wi"] % 2
        st["wi"] += 1
        P.dma("pool", wt[wb][:, :, :], wr[:, :, N_FM + 512 * h:N_FM + 512 * h + 512], writes=["wt%d" % wb])
        wbs.append(wb)
    tb = [(i * 128, 128) for i in range(32)] + [(NPR, NS)]
    for (t0, tw) in tb:
        vb = st["vi"] % 2
        st["vi"] += 1
        for h in range(2):
            pb = st["pi"] % 6
            wb = wbs[h]
            for kc in range(16):
                P.op("pe", (lambda pb=pb, wb=wb, kc=kc, t0=t0, tw=tw:
                            nc.tensor.matmul(ps[pb][0:tw, 0:512], xs[:, kc, t0:t0 + tw], wt[wb][:, kc, :],
                                             start=(kc == 0), stop=(kc == 15))),
                     reads=["wt%d" % wb, "xs%d" % (kc // 2)], writes=["ps%d" % pb])
            evac(vt[vb][0:tw, 512 * h:512 * h + 512], ps[pb][0:tw, 0:512], "vt%d" % vb, "ps%d" % pb)
        P.dma("sp", VTM[t0:t0 + tw, :], vt[vb][0:tw, :], reads=["vt%d" % vb], semkey="vto%d" % vb)
    P.barrier()
    P.release(m0)


class Ops:
    def __init__(self, P):
        self.P = P
        self.nc = P.nc
        self.rr = 0

    def E(self, which):
        return {"dve": self.nc.vector, "pool": self.nc.gpsimd}[which]

    def pick(self, choices=("dve", "pool")):
        self.rr += 1
        return choices[self.rr % len(choices)]

    def tt(self, eng, out, a, b, op, r, w):
        e = self.E(eng)
        self.P.op(eng, lambda: e.tensor_tensor(out, a, b, op), reads=r, writes=w)

    def ts(self, eng, out, a, s1, s2, op0, op1, r, w):
        e = self.E(eng)
        if s2 is None:
            self.P.op(eng, lambda: e.tensor_scalar(out, a, s1, None, op0), reads=r, writes=w)
        else:
            self.P.op(eng, lambda: e.tensor_scalar(out, a, s1, s2, op0, op1), reads=r, writes=w)

    def stt(self, eng, out, a, sc, b, op0, op1, r, w):
        e = self.E(eng)
        self.P.op(eng, lambda: e.scalar_tensor_tensor(out, a, sc, b, op0, op1), reads=r, writes=w)

    def cp(self, eng, out, a, r, w):
        if eng == "act":
            self.P.op("act", lambda: self.nc.scalar.copy(out, a), reads=r, writes=w)
        else:
            e = self.E(eng)
            self.P.op(eng, lambda: e.tensor_copy(out, a), reads=r, writes=w)

    def act(self, out, a, func, r, w, bias=None, scale=None):
        kw = {}
        if bias is not None:
            kw["bias"] = bias
        if scale is not None:
            kw["scale"] = scale
        self.P.op("act", lambda: self.nc.scalar.activation(out=out, in_=a, func=func, **kw), reads=r, writes=w)

    def mm(self, out, lhsT, rhs, r, w, start=True, stop=True):
        self.P.op("pe", lambda: self.nc.tensor.matmul(out, lhsT, rhs, start=start, stop=stop), reads=r, writes=w)

    def tr(self, out, a, ident, r, w):
        self.P.op("pe", lambda: self.nc.tensor.transpose(out, a, ident), reads=r, writes=w)

    def memset(self, eng, out, val, w):
        e = self.E(eng)
        self.P.op(eng, lambda: e.memset(out, val), writes=w)

    def recip(self, out, a, r, w):
        self.P.op("dve", lambda: self.nc.vector.reciprocal(out, a), reads=r, writes=w)

    def rsqrt(self, out, a, r, w):
        self.act(out, a, AF.Sqrt, r, w)
        self.recip(out, out, w, w)


def load_consts(P, io):
    nc = P.nc
    c = {}
    c["identb"] = P.sb("identb", [128, 128], BF16)
    c["identf"] = P.sb("identf", [128, 128], F32)
    c["b64"] = P.sb("b64", [128, 128], F32)
    c["b64m"] = P.sb("b64m", [128, 128], F32)
    c["mskL"] = P.sb("mskL", [64, 64], F32)
    c["mskU"] = P.sb("mskU", [64, 128], F32)
    c["i2"] = P.sb("i2", [128, 64], F32)
    c["maskR"] = P.sb("maskR", [128, 512], F32)
    c["tri"] = P.sb("tri", [128, 128], BF16)
    c["zcol"] = P.sb("zcol", [128, 1], F32)
    P.dma("pool", c["identb"][:], io["c_ident"], writes=["c_identb"])
    P.dma("sp", c["identf"][:], io["c_ident"], writes=["c_identf"])
    P.dma("sp", c["b64"][:], io["c_b64"], writes=["c_b64"])
    P.dma("sp", c["b64m"][:], io["c_b64m"], writes=["c_b64m"])
    P.dma("sp", c["mskL"][:], io["c_mskL"], writes=["c_mskL"])
    P.dma("sp", c["mskU"][:], io["c_mskU"], writes=["c_mskU"])
    P.dma("sp", c["i2"][:], io["c_i2"], writes=["c_i2"])
    P.dma("sp", c["maskR"][:], io["c_maskR"], writes=["c_maskR"])
    P.dma("pool", c["tri"][:], io["c_tri"], writes=["c_tri"])
    P.op("pool", lambda: nc.gpsimd.memset(c["zcol"][:], 0.0), writes=["c_zcol"])
    c["keys"] = ["c_identb", "c_identf", "c_b64", "c_b64m", "c_mskL", "c_mskU", "c_i2", "c_maskR", "c_tri", "c_zcol"]
    return c


B1_RATIO = 12
PRM = dict(mu_r=0, mu_k=1, mu_v=2, w0=3, a0=4, k_k=5, k_a=6, r_k=7, lnx_g=8, lnx_b=9, omk=10)


def stage_B1(P, io, cst, yT, pairs=range(8), tiles=range(9), lvl=99):
    nc = P.nc
    O = Ops(P)
    PJ = io["PJ"]
    ps = io["ps"]
    m0 = P.mark()
    CK = cst["keys"]
    prm = P.sb("prm", [128, 8, 11], F32)
    P.dma("sp", prm[:, :, 0:10], io["rw_prm"], writes=["prm"])
    O.ts("dve", prm[:, :, 10:11], prm[:, :, 6:7], -1.0, 1.0, ALU.mult, ALU.add, ["prm"], ["prm"])
    mul = P.sb("mul", [128, 3], F32)
    P.dma("sp", mul[:], io["rw_mul"], writes=["mul"])
    w2a2 = P.sb("w2a2", [128, 1024], BF16)
    g2a = P.sb("g2a", [128, 1024], BF16)
    g2b = P.sb("g2b", [32, 1024], BF16)
    P.dma("pool", w2a2[:], io["rw_w2a2"], writes=["w2a2"])
    P.dma("pool", g2a[:], io["rw_g2"][0:128, :], writes=["g2a"])
    P.dma("pool", g2b[:], io["rw_g2"][128:160, :], writes=["g2b"])
    Hst = P.sb("Hst", [128, 8, 64], F32)
    O.memset("pool", Hst[:], 0.0, ["H%d" % p for p in range(8)])

    def S(name, shape, dt, nb=2):
        return [P.sb(name + str(i), shape, dt) for i in range(nb)]
    xwc = S("xwc", [128, 512], F32); xwp = S("xwp", [128, 512], F32, 1) * 2
    xgc = S("xgc", [128, 512], F32); xgp = S("xgp", [128, 512], F32, 1) * 2
    xhc = S("xhc", [32, 512], F32, 1) * 2; xhp = S("xhp", [32, 512], F32, 1) * 2
    txa = S("txa", [128, 512], BF16); sga = S("sga", [128, 512], BF16); sgb = S("sgb", [32, 512], BF16)
    names32 = ["kc", "kp", "vc", "vp", "rc", "rp", "ks", "vs", "rs", "sw", "al", "gg", "kk", "t1", "kkn", "bb", "kh",
               "cum", "cumx", "e_r", "e_a", "e_n", "e_c", "bon"]
    U = {n: S(n, [128, 512], F32, 2 if n in ("kc", "kp", "vc", "vp", "bon", "gg") else 1) for n in names32}
    gC = S("gC", [128, 8], F32)
    QQ = S("QQ", [128, 8, 2, 64], BF16); KK = S("KK", [128, 8, 2, 64], BF16)
    BH = S("BH", [128, 8, 64], BF16); KH = S("KH", [128, 8, 64], BF16); VV = S("VV", [128, 8, 64], BF16)
    TMa = S("TMa", [64, 8, 128], BF16, 1) * 2; TMb = S("TMb", [64, 8, 128], BF16, 1) * 2
    TMk = S("TMk", [64, 8, 128], BF16, 1) * 2; TMv = S("TMv", [64, 8, 128], BF16, 1) * 2
    Lb = S("Lb", [64, 8, 64], BF16, 4); LTb = S("LTb", [64, 8, 64], BF16, 4)
    GB = S("GB", [64, 8, 128], BF16); GK = S("GK", [64, 8, 128], BF16)
    Zf = S("Zf", [64, 8, 128], F32); Zb = S("Zb", [64, 8, 128], BF16)
    MT = S("MT", [128, 8, 64], F32, 1) * 2; NN = S("NN", [128, 8, 64], F32, 1) * 2
    DG = S("DG", [128, 8, 64], F32, 1) * 2
    Hb = S("Hb", [128, 8, 64], BF16)
    QeT = S("QeT", [128, 8, 64], BF16)
    ysb = S("ysb", [128, 512], F32, 1) * 2; yc_ = S("yc", [128, 512], F32, 1) * 2; sq_ = S("sq", [128, 512], F32, 1) * 2
    rstd = S("rstd", [128, 512], F32, 1) * 2

    def v3(ap, n):
        return ap.rearrange("p (a b) -> p a b", a=n)

    def psv(bank, rows, n, w):
        return ps[bank][rows, 0:n * w].rearrange("p (a b) -> p a b", a=n)

    def psb(bank, rows, n, w):
        return ps[bank][:, :].bitcast(BF16)[rows, 0:n * w].rearrange("p (a b) -> p a b", a=n)

    def unit_gen(ct, p, u, first):
        samp = (ct == 8)
        t0 = NPR if samp else 512 * ct
        TW = 64 if samp else 512
        RW_ = NS if samp else 512
        NCH = TW // 64
        do_out = ct >= 6
        lb = ct % 2

        def load_pair(cur, prv, r0, rows, kc_, kp_, q="sp"):
            if samp:
                O.memset("pool", cur[0:rows, 0:TW], 0.0, [kc_])
                O.memset("pool", prv[0:rows, 0:TW], 0.0, [kp_])
                P.dma(q, cur[0:rows, 0:NS], PJ[r0:r0 + rows, NPR:NPR + NS], writes=[kc_])
                P.dma(q, prv[0:rows, 1:NS], PJ[r0:r0 + rows, NPR:NPR + NS - 1], writes=[kp_])
                P.dma(q, prv[0:rows, 0:1], io["shiftT"][r0:r0 + rows, :], writes=[kp_])
            else:
                P.dma(q, cur[0:rows, 0:TW], PJ[r0:r0 + rows, t0:t0 + TW], writes=[kc_])
                if t0 == 0:
                    O.memset("pool", prv[0:rows, 0:1], 0.0, [kp_])
                    P.dma(q, prv[0:rows, 1:TW], PJ[r0:r0 + rows, 0:TW - 1], writes=[kp_])
                else:
                    P.dma(q, prv[0:rows, 0:TW], PJ[r0:r0 + rows, t0 - 1:t0 + TW - 1], writes=[kp_])

        def shift(eng, out, cur, prv, mu, r, w, rows=128):
            O.tt(eng, prv[0:rows, 0:TW], prv[0:rows, 0:TW], cur[0:rows, 0:TW], ALU.subtract, r, [r[1]])
            if eng == "dve":
                O.stt(eng, out[0:rows, 0:TW], prv[0:rows, 0:TW], mu, cur[0:rows, 0:TW], ALU.mult, ALU.add, r, w)
            else:
                O.ts(eng, prv[0:rows, 0:TW], prv[0:rows, 0:TW], mu, None, ALU.mult, None, r, [r[1]])
                O.tt(eng, out[0:rows, 0:TW], prv[0:rows, 0:TW], cur[0:rows, 0:TW], ALU.add, r, w)

        kx = "L%d" % lb
        if first:
            load_pair(xwc[lb], xwp[lb], ROW["xwxa"], 128, kx + "xwc", "Lxwp")
            shift("dve", xwc[lb], xwc[lb], xwp[lb], mul[:, 0:1], [kx + "xwc", "Lxwp", "mul"], [kx + "xwc"])
            O.act(txa[lb][0:64, 0:TW], xwc[lb][0:64, 0:TW], AF.Tanh, [kx + "xwc"], [kx + "txa"])
            O.cp("pool", txa[lb][64:128, 0:TW], xwc[lb][64:128, 0:TW], [kx + "xwc"], [kx + "txa"])
            if do_out:
                load_pair(xgc[lb], xgp[lb], ROW["xg"], 128, kx + "xgc", "Lxgp")
                shift("pool", xgc[lb], xgc[lb], xgp[lb], mul[:, 1:2], [kx + "xgc", "Lxgp", "mul"], [kx + "xgc"])
                O.act(sga[lb][:, 0:TW], xgc[lb][:, 0:TW], AF.Sigmoid, [kx + "xgc"], [kx + "sga"])
                r0 = ROW["xg"] + 128
                if samp:
                    O.memset("pool", xhc[lb][:, 0:TW], 0.0, ["Lxhc"])
                    O.memset("pool", xhp[lb][:, 0:TW], 0.0, ["Lxhp"])
                    P.dma("sp", xhc[lb][:, 0:NS], PJ[r0:r0 + 32, NPR:NPR + NS], writes=["Lxhc"])
                    P.dma("sp", xhp[lb][:, 1:NS], PJ[r0:r0 + 32, NPR:NPR + NS - 1], writes=["Lxhp"])
                    P.dma("sp", xhp[lb][:, 0:1], io["shiftT"][r0:r0 + 32, :], writes=["Lxhp"])
                else:
                    P.dma("sp", xhc[lb][:, 0:TW], PJ[r0:r0 + 32, t0:t0 + TW], writes=["Lxhc"])
                    P.dma("sp", xhp[lb][:, 0:TW], PJ[r0:r0 + 32, t0 - 1:t0 + TW - 1], writes=["Lxhp"])
                shift("pool", xhc[lb], xhc[lb], xhp[lb], mul[0:32, 2:3], ["Lxhc", "Lxhp", "mul"], ["Lxhc"], rows=32)
                O.act(sgb[lb][:, 0:TW], xhc[lb][:, 0:TW], AF.Sigmoid, ["Lxhc"], [kx + "sgb"])

            yield "d"
        SINGLE = ("xhc", "xhp", "MT", "NN", "DG", "ysb", "yc", "sq", "rstd", "TMa", "TMb", "TMk", "TMv")
        K = (lambda n, u=u: "u%d_%s" % (0 if (n in SINGLE or (n in U and len(U[n]) == 1)) else u, n))
        T = (lambda n, u=u: U[n][u % len(U[n])])
        pc = lambda j: prm[:, p, j:j + 1]
        cols = slice(128 * p, 128 * p + 128)
        W_ = slice(0, TW)
        load_pair(T("kc"), T("kp"), ROW["rk"] + 128 * p, 128, K("kc"), K("kp"))
        load_pair(T("vc"), T("vp"), ROW["rv"] + 128 * p, 128, K("vc"), K("vp"))
        shift("dve", T("ks"), T("kc"), T("kp"), pc(1), [K("kc"), K("kp"), "prm"], [K("ks")])
        shift("pool", T("vs"), T("vc"), T("vp"), pc(2), [K("vc"), K("vp"), "prm"], [K("vs")])
        yield "d"
        if do_out:
            load_pair(T("rc"), T("rp"), ROW["r"] + 128 * p, 128, K("rc"), K("rp"))
            shift("pool", T("rs"), T("rc"), T("rp"), pc(0), [K("rc"), K("rp"), "prm"], [K("rs")])
        O.mm(ps[0][:, W_], w2a2[0:64, cols], txa[lb][0:64, W_], ["w2a2", kx + "txa"], ["ps0"])
        O.mm(ps[1][:, W_], w2a2[64:128, cols], txa[lb][64:128, W_], ["w2a2", kx + "txa"], ["ps1"])
        O.act(T("sw")[:, W_], ps[0][:, W_], AF.Sigmoid, ["ps0", "prm"], [K("sw")], bias=pc(3))
        O.act(T("al")[:, W_], ps[1][:, W_], AF.Sigmoid, ["ps1", "prm"], [K("al")], bias=pc(4))
        yield "d"
        if samp:
            O.memset("pool", T("sw")[:, NS:TW], 0.0, [K("sw")])
        if do_out:
            O.mm(ps[0][:, W_], g2a[:, cols], sga[lb][:, W_], ["g2a", kx + "sga"], ["ps0"], start=True, stop=False)
            O.mm(ps[0][:, W_], g2b[0:32, cols], sgb[lb][0:32, W_], ["g2b", kx + "sgb"], ["ps0"], start=False, stop=True)
            O.cp("act", T("gg")[:, W_], ps[0][:, W_], ["ps0"], [K("gg")])
        O.ts("pool", T("kk")[:, W_], T("ks")[:, W_], pc(5), None, ALU.mult, None, [K("ks"), "prm"], [K("kk")])
        O.tt("pool", T("t1")[:, W_], T("kk")[:, W_], T("kk")[:, W_], ALU.mult, [K("kk")], [K("t1")])
        O.mm(ps[1][:, W_], cst["b64"][:, :], T("t1")[:, W_], ["c_b64", K("t1")], ["ps1"])
        O.ts("dve", T("t1")[:, W_], ps[1][:, W_], 1e-24, None, ALU.max, None, ["ps1"], [K("t1")])
        O.rsqrt(T("t1")[:, W_], T("t1")[:, W_], [K("t1")], [K("t1")])
        yield "d"
        O.tt("pool", T("kkn")[:, W_], T("kk")[:, W_], T("t1")[:, W_], ALU.mult, [K("kk"), K("t1")], [K("kkn")])
        O.tt("pool", T("bb")[:, W_], T("kkn")[:, W_], T("al")[:, W_], ALU.mult, [K("kkn"), K("al")], [K("bb")])
        O.ts("pool", T("t1")[:, W_], T("al")[:, W_], pc(6), pc(10), ALU.mult, ALU.add, [K("al"), "prm", K("t1")], [K("t1")])
        O.tt("pool", T("kh")[:, W_], T("ks")[:, W_], T("t1")[:, W_], ALU.mult, [K("ks"), K("t1")], [K("kh")])
        yield "d"
        sc_o, sc_m, sc_i, sc_z = T("cum")[:, W_], cst["maskR"][:, W_], T("sw")[:, W_], cst["zcol"][:, 0:1]
        P.op("dve", (lambda sc_o=sc_o, sc_m=sc_m, sc_i=sc_i, sc_z=sc_z:
                     nc.vector.tensor_tensor_scan(sc_o, sc_m, sc_i, sc_z, ALU.mult, ALU.add)),
             reads=[K("sw")], writes=[K("cum")])
        O.tt("pool", T("cumx")[:, W_], T("cum")[:, W_], T("sw")[:, W_], ALU.subtract, [K("cum"), K("sw")], [K("cumx")])
        O.act(T("e_r")[:, W_], T("cum")[:, W_], AF.Exp, [K("cum")], [K("e_r")], scale=-C0)
        O.act(T("e_a")[:, W_], T("cumx")[:, W_], AF.Exp, [K("cumx")], [K("e_a")], scale=-C0)
        O.recip(T("e_n")[:, W_], T("e_r")[:, W_], [K("e_r")], [K("e_n")])
        yield "d"
        cv = v3(T("cum")[:, W_], NCH)
        O.tt("dve", v3(T("cumx")[:, W_], NCH), cv[:, :, 63:64].to_broadcast([128, NCH, 64]), cv, ALU.subtract,
             [K("cum"), K("e_a")], [K("cumx")])
        O.act(T("e_c")[:, W_], T("cumx")[:, W_], AF.Exp, [K("cumx")], [K("e_c")], scale=-C0)
        O.cp("pool", gC[u][:, 0:NCH], v3(T("e_r")[:, W_], NCH)[:, :, 63], [K("e_r")], [K("gC")])
        yield "d"
        qq = QQ[u]; kq = KK[u]
        O.stt("dve", qq[:, 0:NCH, 0, :], v3(T("kkn")[:, W_], NCH), -1.0, v3(T("e_a")[:, W_], NCH), ALU.mult, ALU.mult,
              [K("kkn"), K("e_a")], [K("QQ")])
        if do_out:
            O.tt("pool", qq[:, 0:NCH, 1, :], v3(T("rs")[:, W_], NCH), v3(T("e_r")[:, W_], NCH), ALU.mult,
                 [K("rs"), K("e_r")], [K("QQ")])
        O.tt("pool", kq[:, 0:NCH, 0, :], v3(T("bb")[:, W_], NCH), v3(T("e_n")[:, W_], NCH), ALU.mult, [K("bb"), K("e_n")], [K("KK")])
        O.tt("pool", kq[:, 0:NCH, 1, :], v3(T("kh")[:, W_], NCH), v3(T("e_n")[:, W_], NCH), ALU.mult, [K("kh"), K("e_n")], [K("KK")])
        O.tt("pool", BH[u][:, 0:NCH, :], v3(T("bb")[:, W_], NCH), v3(T("e_c")[:, W_], NCH), ALU.mult, [K("bb"), K("e_c")], [K("BH")])
        O.tt("pool", KH[u][:, 0:NCH, :], v3(T("kh")[:, W_], NCH), v3(T("e_c")[:, W_], NCH), ALU.mult, [K("kh"), K("e_c")], [K("KH")])
        yield "d"
        O.cp("act", VV[u][:, 0:NCH, :], v3(T("vs")[:, W_], NCH), [K("vs")], [K("VV")])
        if do_out:
            O.stt("dve", T("bon")[:, W_], T("rs")[:, W_], pc(7), T("kh")[:, W_], ALU.mult, ALU.mult, [K("rs"), K("kh"), "prm"], [K("bon")])
            O.mm(ps[0][:, W_], cst["b64"][:, :], T("bon")[:, W_], ["c_b64", K("bon")], ["ps0"])
            O.tt("dve", T("bon")[:, W_], ps[0][:, W_], T("vs")[:, W_], ALU.mult, ["ps0", K("vs"), K("bon")], [K("bon")])
        yield "DONE_D"
        if lvl < 1:
            return
        O.tt("pool", DG[u][:, 0:NCH, :], cst["i2"][:, :].unsqueeze(1).to_broadcast([128, NCH, 64]),
             gC[u][:, 0:NCH].unsqueeze(2).to_broadcast([128, NCH, 64]), ALU.mult, ["c_i2", K("gC")], [K("DG")])
        for (src, skey, dst, dkey, bank) in ((qq, K("QQ"), TMa[u], K("TMa"), 3), (BH[u], K("BH"), TMb[u], K("TMb"), 4),
                                             (KH[u], K("KH"), TMk[u], K("TMk"), 3), (VV[u], K("VV"), TMv[u], K("TMv"), 4)):
            pv = psb(bank, slice(0, 64), NCH, 128)
            for i in range(NCH):
                s_ap = src[:, i, 0, :] if src is qq else src[:, i, :]
                O.tr(pv[:, i, :], s_ap, cst["identb"][:, :], [skey, "c_identb"], ["ps%d" % bank])
            O.cp(O.pick(("act", "dve")), dst[:, 0:NCH, :], pv, ["ps%d" % bank], [dkey])
            yield "c"
        if lvl < 2:
            return
        HS = []
        NQ = 128 if do_out else 64
        for h in range(2):
            R = slice(64 * h, 64 * h + 64)
            Zf_, Zb_ = Zf[h], Zb[h]
            kz = "Z%d" % h
            GB_, GK_ = GB[h], GK[h]
            kg = "G%d" % h
            gab = psv(3, slice(0, 64), NCH, 64)
            for i in range(NCH):
                O.mm(gab[:, i, :], qq[R, i, 0, :], kq[R, i, 0, :], [K("QQ"), K("KK")], ["ps3"])
            for i in range(NCH):
                bk = 4 + i // 4
                o_ = ps[bk][0:64, (i % 4) * 128:(i % 4) * 128 + NQ]
                O.mm(o_, kq[R, i, 0, :], qq[R, i, :, :].rearrange("p a b -> p (a b)")[:, 0:NQ], [K("QQ"), K("KK")], ["ps%d" % bk])
            for i in range(NCH):
                bk = 6 + i // 4
                o_ = ps[bk][0:64, (i % 4) * 128:(i % 4) * 128 + NQ]
                O.mm(o_, kq[R, i, 1, :], qq[R, i, :, :].rearrange("p a b -> p (a b)")[:, 0:NQ], [K("QQ"), K("KK")], ["ps%d" % bk])
            li = 2 * h
            O.tt("dve", Lb[li][:, 0:NCH, :], gab, cst["mskL"][:, :].unsqueeze(1).to_broadcast([64, NCH, 64]), ALU.mult,
                 ["ps3", "c_mskL"], ["Lb%d" % li])
            yield "c"
            for (bank0, G_, kname) in ((4, GB_, kg + "B"), (6, GK_, kg + "K")):
                for hb in range((NCH + 3) // 4):
                    n4 = min(4, NCH - 4 * hb)
                    src = ps[bank0 + hb][0:64, 0:n4 * 128].rearrange("p (a b) -> p a b", a=n4)[:, :, 0:NQ]
                    O.tt("dve", G_[:, 4 * hb:4 * hb + n4, 0:NQ], src,
                         cst["mskU"][:, 0:NQ].unsqueeze(1).to_broadcast([64, n4, NQ]), ALU.mult,
                         ["ps%d" % (bank0 + hb), "c_mskU"], [kname])
            O.cp("pool", LTb[li][:, 0:NCH, :], GB_[:, 0:NCH, 0:64], [kg + "B"], ["LTb%d" % li])
            yield "c"
            if lvl < 3:
                continue
            xp = psv(3, slice(0, 64), NCH, 64)
            for i in range(NCH):
                O.mm(xp[:, i, :], GK_[:, i, 0:64], TMv[u][:, i, R], [kg + "K", K("TMv")], ["ps3"])
            O.cp("act", Zf_[:, 0:NCH, 0:64], TMa[u][:, 0:NCH, R], [K("TMa")], [kz + "f"])
            O.cp("dve", Zf_[:, 0:NCH, 64:128], xp, ["ps3"], [kz + "f"])
            yield "c"
            O.cp("pool", Zb_[:, 0:NCH, :], Zf_[:, 0:NCH, :], [kz + "f"], [kz + "b"])
            HS.append(dict(h=h, R=R, Zf=Zf_, Zb=Zb_, kz=kz, GB=GB_, GK=GK_, kg=kg, li=li, cur=li, zb0=2 + 2 * h))
        for j in range(6):
            for hd in HS:
                cur, li, Zb_, Zf_, kz, zb0 = hd["cur"], hd["li"], hd["Zb"], hd["Zf"], hd["kz"], hd["zb0"]
                for i in range(NCH):
                    bk = zb0 + i // 4
                    o_ = ps[bk][0:64, (i % 4) * 128:(i % 4) * 128 + 128]
                    O.mm(o_, LTb[cur][:, i, :], Zb_[:, i, :], ["LTb%d" % cur, kz + "b"], ["ps%d" % bk])
                yield "c"
                if j < 5:
                    nxt = li + 1 if cur == li else li
                    pn = psv(6, slice(0, 64), NCH, 64)
                    ptn = psv(7, slice(0, 64), NCH, 64)
                    for i in range(NCH):
                        O.mm(pn[:, i, :], LTb[cur][:, i, :], Lb[cur][:, i, :], ["LTb%d" % cur, "Lb%d" % cur], ["ps6"])
                    for i in range(NCH):
                        O.mm(ptn[:, i, :], Lb[cur][:, i, :], LTb[cur][:, i, :], ["LTb%d" % cur, "Lb%d" % cur], ["ps7"])
                    yield "c"
                for hb in range((NCH + 3) // 4):
                    n4 = min(4, NCH - 4 * hb)
                    src = ps[zb0 + hb][0:64, 0:n4 * 128].rearrange("p (a b) -> p a b", a=n4)
                    O.tt("dve", Zb_[:, 4 * hb:4 * hb + n4, :], Zf_[:, 4 * hb:4 * hb + n4, :], src, ALU.add,
                         ["ps%d" % (zb0 + hb), kz + "f"], [kz + "b"])
                if j < 5:
                    for hb in range((NCH + 3) // 4):
                        n4 = min(4, NCH - 4 * hb)
                        src = ps[zb0 + hb][0:64, 0:n4 * 128].rearrange("p (a b) -> p a b", a=n4)
                        O.tt("dve", Zf_[:, 4 * hb:4 * hb + n4, :], Zf_[:, 4 * hb:4 * hb + n4, :], src, ALU.add,
                             ["ps%d" % (zb0 + hb), kz + "f"], [kz + "f"])
                    O.cp("act", Lb[nxt][:, 0:NCH, :], pn, ["ps6"], ["Lb%d" % nxt])
                    O.cp("act", LTb[nxt][:, 0:NCH, :], ptn, ["ps7"], ["LTb%d" % nxt])
                    hd["cur"] = nxt
                yield "c"
        if lvl < 4:
            return
        for hd in HS:
            h, R, Zb_, kz, GB_, kg = hd["h"], hd["R"], hd["Zb"], hd["kz"], hd["GB"], hd["kg"]
            mtp = psv(3, R, NCH, 64)
            npp = psv(6, R, NCH, 64)
            for i in range(NCH):
                O.mm(mtp[:, i, :], Zb_[:, i, 0:64], TMb[u][:, i, R], [kz + "b", K("TMb")], ["ps3"])
            for i in range(NCH):
                O.mm(npp[:, i, :], TMb[u][:, i, R], Zb_[:, i, 64:128], [kz + "b", K("TMb")], ["ps6"], start=True, stop=False)
                O.mm(npp[:, i, :], TMk[u][:, i, R], TMv[u][:, i, R], [K("TMk"), K("TMv")], ["ps6"], start=False, stop=True)
            if do_out:
                qep = psv(7, R, NCH, 64)
                for i in range(NCH):
                    O.mm(qep[:, i, :], Zb_[:, i, 0:64], GB_[:, i, 64:128], [kz + "b", kg + "B"], ["ps7"])
            O.tt("dve", MT[u][R, 0:NCH, :], mtp, DG[u][R, 0:NCH, :], ALU.add, ["ps3", K("DG")], [K("MT")])
            O.cp("act", NN[u][R, 0:NCH, :], npp, ["ps6"], [K("NN")])
            yield "c"
            if do_out:
                O.tt("dve", QeT[u][R, 0:NCH, :], qep, qq[R, 0:NCH, 1, :], ALU.add, ["ps7", K("QQ")], [K("QeT")])
        if lvl < 5:
            return
        Hcur = Hst[:, p, :]
        if samp:
            P.dma("sp", Hst[:, p, :], io["s0T"][:, p, :], reads=[], writes=["H%d" % p])
        hkey = "H%d" % p
        stp = psv(3, slice(0, 128), NCH, 64)
        for i in range(NCH):
            if do_out:
                O.cp("act", Hb[u][:, i, :], Hcur, [hkey], [K("Hb")])
            for h in range(2):
                R = slice(64 * h, 64 * h + 64)
                O.mm(stp[R, i, :], MT[u][R, i, :], Hst[R, p, :], [K("MT"), hkey], ["ps3"])
            O.tt("dve", Hcur, stp[:, i, :], NN[u][:, i, :], ALU.add, ["ps3", K("NN"), hkey], [hkey])
            yield "c"
        if samp:
            P.dma("sp", io["wkv_s"][:, p, :], Hst[:, p, :], reads=[hkey], semkey="wkvs%d" % p, final=True)
        elif ct == 7:
            P.dma("sp", io["wkv_p"][:, p, :], Hst[:, p, :], reads=[hkey], semkey="wkvp%d" % p, final=True)
        if lvl < 6:
            return
        if do_out:
            ytp = psv(7, slice(0, 128), NCH, 64)
            yt1 = psv(4, slice(0, 128), NCH, 64)
            for h in range(2):
                R = slice(64 * h, 64 * h + 64)
                kz = "Z%d" % h
                kg = "G%d" % h
                for i in range(NCH):
                    O.mm(yt1[R, i, :], Hb[u][R, i, :], QeT[u][R, i, :], [K("Hb"), K("QeT")], ["ps4"])
                for i in range(NCH):
                    O.mm(ytp[R, i, :], Zb[h][:, i, 64:128], GB[h][:, i, 64:128], [kz + "b", kg + "B"], ["ps7"], start=True, stop=False)
                    O.mm(ytp[R, i, :], TMv[u][:, i, R], GK[h][:, i, 64:128], [K("TMv"), kg + "K"], ["ps7"], start=False, stop=True)
                    yield "c"
            y_ = ysb[u]
            O.cp("act", y_[:, W_], ps[4][:, W_], ["ps4"], [K("ysb")])
            O.tt("dve", y_[:, W_], y_[:, W_], ps[7][:, W_], ALU.add, ["ps7", K("ysb")], [K("ysb")])
            O.mm(ps[3][:, W_], cst["b64m"][:, :], y_[:, W_], ["c_b64m", K("ysb")], ["ps3"])
            O.tt("dve", yc_[u][:, W_], y_[:, W_], ps[3][:, W_], ALU.subtract, ["ps3", K("ysb")], [K("yc")])
            O.tt("pool", sq_[u][:, W_], yc_[u][:, W_], yc_[u][:, W_], ALU.mult, [K("yc")], [K("sq")])
            O.mm(ps[3][:, W_], cst["b64m"][:, :], sq_[u][:, W_], ["c_b64m", K("sq")], ["ps3"])
            O.ts("dve", rstd[u][:, W_], ps[3][:, W_], 64e-5, None, ALU.add, None, ["ps3"], [K("rstd")])
            O.rsqrt(rstd[u][:, W_], rstd[u][:, W_], [K("rstd")], [K("rstd")])
            yield "c"
            O.tt("pool", yc_[u][:, W_], yc_[u][:, W_], rstd[u][:, W_], ALU.mult, [K("yc"), K("rstd")], [K("yc")])
            O.ts("dve", yc_[u][:, W_], yc_[u][:, W_], pc(8), pc(9), ALU.mult, ALU.add, [K("yc"), "prm"], [K("yc")])
            O.tt("pool", yc_[u][:, W_], yc_[u][:, W_], T("bon")[:, W_], ALU.add, [K("yc"), K("bon")], [K("yc")])
            oc0 = 1024 if samp else (ct - 6) * 512
            O.tt("dve", yT[:, p, oc0:oc0 + RW_], yc_[u][:, 0:RW_], T("gg")[:, 0:RW_], ALU.mult, [K("yc"), K("gg")], ["yT%d" % p])

    units = []
    cnt_u = 0
    for ct in tiles:
        for pi_, p in enumerate(pairs):
            units.append(unit_gen(ct, p, cnt_u % 2, pi_ == 0))
            cnt_u += 1
    prev = None
    for g in units:
        d_done = False
        c_done = prev is None
        while not (d_done and c_done):
            if not d_done:
                try:
                    if next(g) == "DONE_D":
                        d_done = True
                except StopIteration:
                    d_done = True
                    g = None
            for _rep in range(B1_RATIO):
                if not c_done:
                    try:
                        next(prev)
                    except StopIteration:
                        c_done = True
        prev = g
    if prev is not None:
        for _ in prev:
            pass
    P.barrier()
    P.release(m0)


IN_SPECS = dict(
    xT=[D, NTOK], w_in=[D, 7472], shiftT=[N_FM, 1], s0T=[128, 8, 64],
    rw_prm=[128, 8, 10], rw_mul=[128, 3], rw_w2a2=[128, 1024], rw_g2=[160, 1024],
    c_ident=[128, 128], c_b64=[128, 128], c_b64m=[128, 128], c_mskL=[64, 64], c_mskU=[64, 128], c_i2=[128, 64],
    c_maskR=[128, 512], c_tri=[128, 128], c_cvec=[65, 1],
    xo=[NOWN, D], w_o=[D, D], ln1_g=[1, D], ln1_b=[1, D], w_up=[D, DFF], w_down=[DFF, D], ln2_g=[1, D], ln2_b=[1, D],
    fx_g=[64, 16], fx_bf=[16, 1], fx_padb=[128, 32], fx_clfT=[16, 1024], fx_ckT=[16, 64, 1024], fx_cv=[1024, 1024],
)


def make_consts():
    c = {}
    c["c_ident"] = np.eye(128, dtype=np.float32)
    b = np.zeros((128, 128), np.float32)
    b[:64, :64] = 1.0
    b[64:, 64:] = 1.0
    c["c_b64"] = b
    c["c_b64m"] = b / 64.0
    t = np.arange(64)
    c["c_mskL"] = (t[:, None] > t[None, :]).astype(np.float32)
    c["c_mskU"] = np.concatenate([(t[:, None] < t[None, :]), (t[:, None] <= t[None, :])], 1).astype(np.float32)
    c["c_i2"] = np.concatenate([np.eye(64), np.eye(64)], 0).astype(np.float32)
    m = np.ones((128, 512), np.float32)
    m[:, ::64] = 0.0
    c["c_maskR"] = m
    k = np.arange(128)
    c["c_tri"] = (k[:, None] <= k[None, :]).astype(np.float32)
    cv = np.full((65, 1), 1.0 / 64.0, np.float32)
    cv[64, 0] = 1e-6
    c["c_cvec"] = cv
    return c


def build(stages=("A", "B1", "B2", "C"), debug=(), pj_input=False, b1_kw=None, b2_kw=None, c_kw=None, extra_in=()):
    P = Prog()
    nc = P.nc
    io = {}
    need = set()
    if "A" in stages:
        need |= {"xT", "w_in"}
    if "B1" in stages:
        need |= {"shiftT", "s0T", "rw_prm", "rw_mul", "rw_w2a2", "rw_g2"}
    if "B2" in stages:
        need |= {"fx_g", "fx_bf", "fx_padb", "fx_clfT", "fx_ckT", "fx_cv"}
    if "C" in stages:
        need |= {"xo", "w_o", "ln1_g", "ln1_b", "w_up", "w_down", "ln2_g", "ln2_b"}
    need |= {k for k in IN_SPECS if k.startswith("c_")}
    need |= set(extra_in)
    for name, shape in IN_SPECS.items():
        if name in need:
            io[name] = nc.dram_tensor(name, list(shape), F32, kind="ExternalInput").ap()
    kind = "ExternalInput" if pj_input else ("ExternalOutput" if "PJ" in debug else "Internal")
    io["PJ"] = nc.dram_tensor("PJ", [N_FM, NTOK], F32, kind=kind).ap()
    io["VTM"] = nc.dram_tensor("VTM", [NTOK, 1024], F32, kind=kind).ap()
    io["wkv_p"] = nc.dram_tensor("wkv_p", [128, 8, 64], F32, kind="ExternalOutput").ap()
    io["wkv_s"] = nc.dram_tensor("wkv_s", [128, 8, 64], F32, kind="ExternalOutput").ap()
    io["spl_p"] = nc.dram_tensor("spl_p", [16, 3, 1024], BF16, kind="Internal").ap()
    io["spl_s"] = nc.dram_tensor("spl_s", [16, 3, NS], BF16, kind="Internal").ap()
    io["logf_o"] = nc.dram_tensor("logf_o", [16, NOWN], F32, kind="ExternalOutput").ap()
    io["ps"] = [nc.alloc_psum_tensor("psb%d" % i, [128, 512], F32) for i in range(8)]
    cst = load_consts(P, io)
    if "A" in stages:
        stage_A(P, io)
    else:
        P.barrier()
    yT = P.sb("yT", [128, 16, NOWN], BF16)
    if "A" in stages or "outs" in debug:
        io["k_out"] = nc.dram_tensor("k_out", [1024, NOWN], F32, kind="ExternalOutput").ap()
        io["v_out"] = nc.dram_tensor("v_out", [NOWN, 1024], F32, kind="ExternalOutput").ap()
        io["shA"] = nc.dram_tensor("shA", [2176, 2], F32, kind="ExternalOutput").ap()
        io["shB"] = nc.dram_tensor("shB", [1184, 2], F32, kind="ExternalOutput").ap()
        PJ = io["PJ"]
        P.dma("sp", io["k_out"], PJ[ROW["fk"]:ROW["fk"] + 1024, OWN0:NTOK], semkey="o_k", final=True)
        P.dma("sp", io["v_out"], io["VTM"][OWN0:NTOK, :], semkey="o_v", final=True)
        for ci, col in enumerate((NPR - 1, NTOK - 1)):
            P.dma("sp", io["shA"][:, ci:ci + 1], PJ[0:2176, col:col + 1], semkey="o_sa", final=True, allow_slow_non_contiguous=True)
            P.dma("sp", io["shB"][:, ci:ci + 1], PJ[ROW["r"]:ROW["r"] + 1184, col:col + 1], semkey="o_sb", final=True,
                  allow_slow_non_contiguous=True)
    if "B1" in stages:
        stage_B1(P, io, cst, yT, **(b1_kw or {}))
    if "B2" in stages:
        stage_B2(P, io, cst, yT, **(b2_kw or {}))
    if "yTin" in debug:
        io["yT_in"] = nc.dram_tensor("yT_in", [128, 16, NOWN], F32, kind="ExternalInput").ap()
        P.dma("pool", yT[:], io["yT_in"], writes=["yT%d" % i for i in range(16)], semkey="ytin")
    if "C" in stages:
        io["y_out"] = nc.dram_tensor("y_out", [NOWN, D], F32, kind="ExternalOutput").ap()
        stage_C(P, io, cst, yT, **(c_kw or {}))
    if "yT" in debug:
        io["yT_dbg"] = nc.dram_tensor("yT_dbg", [128, 16, NOWN], F32, kind="ExternalOutput").ap()
        P.dma("pool", io["yT_dbg"], yT[:], reads=["yT%d" % i for i in range(16)], semkey="ytdbg", final=True)
    P.emit()
    return P


def stage_B2(P, io, cst, yT, heads=range(16), insts=("p", "s")):
    nc = P.nc
    O = Ops(P)
    PJ = io["PJ"]
    VTM = io["VTM"]
    ps = io["ps"]
    m0 = P.mark()
    ones = P.sb("ones", [128, 512], F32)
    O.memset("pool", ones[:], 1.0, ["ones"])
    fog = P.sb("fog", [64, 16], F32)
    P.dma("sp", fog[:], io["fx_g"], writes=["fog"])
    nbf = P.sb("nbf", [16, 1], F32)
    P.dma("sp", nbf[:], io["fx_bf"], writes=["nbf"])
    O.ts("dve", nbf[:], nbf[:], -1.0, None, ALU.mult, None, ["nbf"], ["nbf"])
    cvec = P.sb("cvec", [65, 1], F32)
    P.dma("sp", cvec[:], io["c_cvec"], writes=["cvec"])
    padb = P.sb("padb", [128, 32], F32)
    P.dma("sp", padb[:], io["fx_padb"], writes=["padb"])
    Crow = P.sb("Crow", [16, NPR], F32)
    Ccum = P.sb("Ccum", [16, NPR], F32)
    Ckb = P.sb("Ckb", [128, 32, 16], F32)
    spl = P.sb("spl", [16, 3, 1024], BF16)
    q8 = P.sb("q8", [16, 1024], F32)
    lfo = P.sb("lfo", [16, NOWN], F32)
    KTa = [P.sb("KTa%d" % i, [67, NPR], BF16) for i in range(2)]
    QTa = [P.sb("QTa%d" % i, [67, 1024], BF16) for i in range(2)]
    Va = [P.sb("Va%d" % i, [128, 32, 65], BF16) for i in range(2)]
    ogT = [P.sb("ogT%d" % i, [64, 1024], F32) for i in range(2)]
    pt = [P.sb("pt%d" % i, [128, 512], BF16) for i in range(4)]
    Oa = P.sb("Oa", [65, 512], F32)
    sq = P.sb("sqf", [65, 512], F32)
    rs = P.sb("rs", [1, 512], F32)
    y1 = P.sb("y1", [64, 512], F32)
    ytmp = [P.sb("ytmp%d" % i, [64, 512], BF16) for i in range(2)]
    for i in range(2):
        O.memset("pool", KTa[i][64:67, :], 1.0, ["KTa%d" % i])
        O.memset("pool", Va[i][:, :, 64:65], 1.0, ["Va%d" % i])
    cnt = dict(h=0, s=0, p=0, y=0)

    for inst in insts:
        if inst == "p":
            TK, NQ, QPOS0 = NPR, 1024, OWN0
            qtiles = [(0, 512), (512, 512)]
            ocol0 = 0
        else:
            TK, NQ, QPOS0 = 1024 + NS, NS, 1024
            qtiles = [(0, NS)]
            ocol0 = 1024
        nblk = (TK + 127) // 128
        if inst == "p":
            P.dma("sp", Crow[:, 0:TK], PJ[ROW["f"]:ROW["f"] + 16, 0:TK], writes=["Crow"])
            fsl = slice(0, TK)
        else:
            P.dma("sp", Crow[:, 0:1024], io["fx_clfT"], writes=["Crow"])
            P.dma("sp", Crow[:, 1024:TK], PJ[ROW["f"]:ROW["f"] + 16, NPR:NPR + NS], writes=["Crow"], semkey="Crow_b")
            O.ts("dve", Crow[:, 0:1024], Crow[:, 0:1024], -1.0, None, ALU.mult, None, ["Crow"], ["Crow"])
            fsl = slice(1024, TK)
        O.act(Crow[:, fsl], Crow[:, fsl], AF.Exp, ["Crow", "nbf"], ["Crow"], bias=nbf[:, 0:1], scale=-1.0)
        O.act(Crow[:, fsl], Crow[:, fsl], AF.Ln, ["Crow"], ["Crow"], bias=1.0)
        if inst == "p":
            P.op("act", lambda: nc.scalar.mul(lfo[:, 0:1024], Crow[:, OWN0:NPR], -1.0), reads=["Crow"], writes=["lfo"])
        else:
            P.op("act", lambda: nc.scalar.mul(lfo[:, 1024:NOWN], Crow[:, 1024:TK], -1.0), reads=["Crow"], writes=["lfo"])
            P.dma("sp", io["logf_o"], lfo[:], reads=["lfo"], semkey="lfo", final=True)
        for c0 in range(0, TK, 512):
            cw = min(512, TK - c0)
            init = cst["zcol"][0:16, 0:1] if c0 == 0 else Ccum[:, c0 - 1:c0]
            o_ap, d0, d1 = Ccum[:, c0:c0 + cw], ones[0:16, 0:cw], Crow[:, c0:c0 + cw]
            P.op("dve", (lambda o_ap=o_ap, d0=d0, d1=d1, init=init: nc.vector.tensor_tensor_scan(o_ap, d0, d1, init, ALU.mult, ALU.add)),
                 reads=["Crow", "ones", "Ccum"], writes=["Ccum"])
        tp = ps[6][:, 0:nblk * 16].rearrange("p (a b) -> p a b", a=nblk)
        for kb in range(nblk):
            nk = min(128, TK - kb * 128)
            O.tr(tp[0:nk, kb, :], Ccum[:, kb * 128:kb * 128 + nk], cst["identf"][0:16, 0:16], ["Ccum", "c_identf"], ["ps6"])
        nfull = TK // 128
        if inst == "p":
            O.tt("dve", Ckb[:, 0:nblk, :], tp[:, 0:nblk, :], padb[:, 0:nblk].unsqueeze(2).to_broadcast([128, nblk, 16]), ALU.add,
                 ["ps6", "padb"], ["Ckb"])
        else:
            O.cp("dve", Ckb[:, 0:nfull, :], tp[:, 0:nfull, :], ["ps6"], ["Ckb"])
            O.cp("dve", Ckb[0:NS, nfull, :], tp[0:NS, nfull, :], ["ps6", "Ckb"], ["Ckb"])
        qs = slice(QPOS0, QPOS0 + NQ)
        O.ts("dve", q8[:, 0:NQ], Ccum[:, qs], -8.0, None, ALU.mult, None, ["Ccum"], ["q8"])
        for k3 in range(3):
            O.cp("dve", spl[:, k3, 0:NQ], q8[:, 0:NQ], ["q8"], ["spl"])
            if k3 < 2:
                O.tt("dve", q8[:, 0:NQ], q8[:, 0:NQ], spl[:, k3, 0:NQ], ALU.subtract, ["q8", "spl"], ["q8"])
        sk = "splD" + inst
        P.dma("sp", io["spl_" + inst], spl[:, :, 0:NQ], reads=["spl"], writes=[sk])
        for h in heads:
            hb = cnt["h"] % 2
            cnt["h"] += 1
            kK, kQ, kV, kG = "KTa%d" % hb, "QTa%d" % hb, "Va%d" % hb, "ogT%d" % hb
            fk0 = ROW["fk"] + 64 * h
            if inst == "p":
                P.dma("pool", KTa[hb][0:64, 0:TK], PJ[fk0:fk0 + 64, 0:TK], writes=[kK])
                P.dma("pool", Va[hb][:, 0:32, 0:64], VTM[0:NPR, 64 * h:64 * h + 64].rearrange("(b p) d -> p b d", p=128), writes=[kV])
                ocs = slice(OWN0, NPR)
            else:
                P.dma("pool", KTa[hb][0:64, 0:1024], io["fx_ckT"][h], writes=[kK])
                P.dma("pool", KTa[hb][0:64, 1024:TK], PJ[fk0:fk0 + 64, NPR:NPR + NS], writes=[kK])
                P.dma("pool", Va[hb][:, 0:8, 0:64], io["fx_cv"][:, 64 * h:64 * h + 64].rearrange("(b p) d -> p b d", p=128), writes=[kV])
                P.dma("pool", Va[hb][0:NS, 8, 0:64], VTM[NPR:NPR + NS, 64 * h:64 * h + 64], writes=[kV])
                ocs = slice(NPR, NPR + NS)
            P.dma("pool", QTa[hb][0:64, 0:NQ], PJ[ROW["q"] + 64 * h:ROW["q"] + 64 * h + 64, ocs], writes=[kQ])
            P.dma("sp", QTa[hb][64:67, 0:NQ], io["spl_" + inst][h], reads=[sk], writes=[kQ])
            P.dma("sp", ogT[hb][:, 0:NQ], PJ[ROW["og"] + 64 * h:ROW["og"] + 64 * h + 64, ocs], writes=[kG])
            for (q0, qw) in qtiles:
                ob = 2 + cnt["y"] % 2
                okey = "ps%d" % ob
                last_q = QPOS0 + q0 + qw - 1
                kbs = [kb for kb in range(nblk) if kb * 128 <= last_q]
                pend = []
                for n_, kb in enumerate(kbs):
                    nk = min(128, TK - kb * 128)
                    off = max(0, kb * 128 - (QPOS0 + q0))
                    diag = kb * 128 + nk - 1 > QPOS0 + q0
                    sb_ = (0, 1, 7)[cnt["s"] % 3]
                    cnt["s"] += 1
                    pb_ = cnt["p"] % 4
                    cnt["p"] += 1
                    skey, pkey = "ps%d" % sb_, "pt%d" % pb_
                    O.mm(ps[sb_][0:nk, off:qw], KTa[hb][0:67, kb * 128:kb * 128 + nk], QTa[hb][0:67, q0 + off:q0 + qw], [kK, kQ], [skey])
                    if len(pend) >= 2:
                        pend.pop(0)()
                    O.act(pt[pb_][0:nk, off:qw], ps[sb_][0:nk, off:qw], AF.Exp, [skey, "Ckb"], [pkey], bias=Ckb[0:nk, kb, h:h + 1], scale=0.125)
                    if diag:
                        mw = min(128, qw - off, nk)
                        O.tt(O.pick(("dve", "pool")), pt[pb_][0:nk, off:off + mw], pt[pb_][0:nk, off:off + mw], cst["tri"][0:nk, 0:mw], ALU.mult,
                             [pkey, "c_tri"], [pkey])
                    pend.append(lambda nk=nk, kb=kb, pb_=pb_, off=off, pkey=pkey, n_=n_:
                                O.mm(ps[ob][0:65, off:qw], Va[hb][0:nk, kb, 0:65], pt[pb_][0:nk, off:qw], [kV, pkey], [okey],
                                     start=(n_ == 0), stop=(n_ == len(kbs) - 1)))
                while pend:
                    pend.pop(0)()
                cnt["y"] += 1
                yb = cnt["y"] % 2
                O.cp("act", Oa[:, 0:qw], ps[ob][0:65, 0:qw], [okey], ["Oa"])
                O.act(sq[:, 0:qw], Oa[:, 0:qw], AF.Square, ["Oa"], ["sqf"])
                O.mm(ps[4][0:1, 0:qw], cvec[:, 0:1], sq[:, 0:qw], ["cvec", "sqf"], ["ps4"])
                O.act(rs[:, 0:qw], ps[4][0:1, 0:qw], AF.Sqrt, ["ps4"], ["rs"])
                O.recip(rs[:, 0:qw], rs[:, 0:qw], ["rs"], ["rs"])
                O.mm(ps[5][0:64, 0:qw], ones[0:1, 0:64], rs[:, 0:qw], ["ones", "rs"], ["ps5"])
                O.tt("dve", y1[:, 0:qw], Oa[0:64, 0:qw], ps[5][0:64, 0:qw], ALU.mult, ["Oa", "ps5"], ["y1"])
                O.act(ogT[hb][:, q0:q0 + qw], ogT[hb][:, q0:q0 + qw], AF.Sigmoid, [kG], [kG])
                O.stt("dve", ytmp[yb][:, 0:qw], y1[:, 0:qw], fog[:, h:h + 1], ogT[hb][:, q0:q0 + qw], ALU.mult, ALU.mult,
                      ["y1", "fog", kG], ["ytmp%d" % yb])
                r0 = (h % 2) * 64
                P.dma("sp", yT[r0:r0 + 64, 8 + h // 2, ocol0 + q0:ocol0 + q0 + qw], ytmp[yb][:, 0:qw],
                      reads=["ytmp%d" % yb], writes=["yT%d" % (8 + h // 2)], semkey="yT%d_%d" % (8 + h // 2, h % 2))
    P.barrier()
    P.release(m0)


TOKT = [(i * 128, 128) for i in range(8)] + [(1024, NS)]
TT3 = [(0, 512), (512, 512), (1024, NS)]


def layer_norm_tiles(P, O, Z, gvec, bvec, zkeys, out_dram=None):
    nc = P.nc
    m = P.mark()
    gb = P.sb("ln_g", [128, D], F32)
    bb = P.sb("ln_b", [128, D], F32)
    junk = P.sb("ln_junk", [128, D], F32)
    st = P.sb("ln_st", [128, 4], F32)
    P.dma("sp", gb[:], gvec.partition_broadcast(128), writes=["ln_g"])
    P.dma("sp", bb[:], bvec.partition_broadcast(128), writes=["ln_b"])
    for ti, (t0, tw) in enumerate(TOKT):
        z = Z[0:tw, ti, :]
        zk = zkeys[ti]
        s = st[0:tw, :]
        O.memset("pool", st[:, :], 0.0, ["ln_st"])
        P.op("act", (lambda z=z, tw=tw, s=s: nc.scalar.activation(out=junk[0:tw, :], in_=z, func=AF.Identity, accum_out=s[:, 0:1])),
             reads=[zk], writes=["ln_junk", "ln_st"])
        O.ts("dve", s[:, 1:2], s[:, 0:1], -1.0 / D, None, ALU.mult, None, ["ln_st"], ["ln_st"])
        O.act(z, z, AF.Identity, [zk, "ln_st"], [zk], bias=s[:, 1:2])
        P.op("act", (lambda z=z, tw=tw, s=s: nc.scalar.activation(out=junk[0:tw, :], in_=z, func=AF.Square, accum_out=s[:, 2:3])),
             reads=[zk, "ln_st"], writes=["ln_junk", "ln_st"])
        O.ts("dve", s[:, 3:4], s[:, 2:3], 1.0 / D, 1e-5, ALU.mult, ALU.add, ["ln_st"], ["ln_st"])
        O.rsqrt(s[:, 3:4], s[:, 3:4], ["ln_st"], ["ln_st"])
        O.stt("dve", z, z, s[:, 3:4], gb[0:tw, :], ALU.mult, ALU.mult, [zk, "ln_st", "ln_g"], [zk])
        O.tt("pool", z, z, bb[0:tw, :], ALU.add, [zk, "ln_b"], [zk])
        if out_dram is not None:
            P.dma("sp", out_dram[t0:t0 + tw, :], z, reads=[zk], semkey="yo%d" % ti, final=True)
    P.release(m)


def stage_C(P, io, cst, yT, dbg=None):
    nc = P.nc
    O = Ops(P)
    ps = io["ps"]
    m0 = P.mark()
    Z = P.sb("Z", [128, 9, D], F32)
    zk = ["Z%d" % i for i in range(9)]
    yk = ["yT%d" % i for i in range(16)]
    m1 = P.mark()
    wo = [P.sb("wo%d" % i, [128, 16, 512], BF16) for i in range(2)]
    wor = io["w_o"].rearrange("(kc p) c -> p kc c", p=128)
    for ti, (t0, tw) in enumerate(TOKT):
        P.dma("sp", Z[0:tw, ti, :], io["xo"][t0:t0 + tw, :], writes=[zk[ti]])
    pi = 0
    for cg in range(4):
        wb = cg % 2
        P.dma("pool", wo[wb][:], wor[:, :, cg * 512:cg * 512 + 512], writes=["wo%d" % wb])
        for ti, (t0, tw) in enumerate(TOKT):
            pb = pi % 4
            pi += 1
            for kc in range(16):
                O.mm(ps[pb][0:tw, :], yT[:, kc, t0:t0 + tw], wo[wb][:, kc, :], [yk[kc], "wo%d" % wb], ["ps%d" % pb],
                     start=(kc == 0), stop=(kc == 15))
            zs = Z[0:tw, ti, cg * 512:cg * 512 + 512]
            O.stt("dve", zs, zs, ALPHA, ps[pb][0:tw, :], ALU.mult, ALU.add, [zk[ti], "ps%d" % pb], [zk[ti]])
    def dump():
        for ti, (t0, tw) in enumerate(TOKT):
            P.dma("sp", io["y_out"][t0:t0 + tw, :], Z[0:tw, ti, :], reads=[zk[ti]], semkey="yo%d" % ti, final=True)
    if dbg == "z":
        dump()
        return
    layer_norm_tiles(P, O, Z, io["ln1_g"], io["ln1_b"], zk)
    if dbg == "h":
        dump()
        return
    hT = yT
    ei = 0
    for kc in range(16):
        for (grp, bank) in ((range(0, 4), 4), (range(4, 8), 5), (range(8, 9), 6)):
            for ti in grp:
                t0, tw = TOKT[ti]
                c0 = t0 - TOKT[grp[0]][0]
                O.tr(ps[bank][:, c0:c0 + tw], Z[0:tw, ti, kc * 128:kc * 128 + 128], cst["identf"][0:tw, 0:tw], [zk[ti], "c_identf"], ["ps%d" % bank])
            g0 = TOKT[grp[0]][0]
            gw = sum(TOKT[ti][1] for ti in grp)
            ei += 1
            O.cp("act" if ei % 2 else "dve", hT[:, kc, g0:g0 + gw], ps[bank][:, 0:gw], ["ps%d" % bank], [yk[kc]])
    for ti, (t0, tw) in enumerate(TOKT):
        O.ts("pool", Z[0:tw, ti, :], Z[0:tw, ti, :], ALPHA, None, ALU.mult, None, [zk[ti]], [zk[ti]])
    P.barrier()
    P.release(m1)
    m2 = P.mark()
    wu = [P.sb("wu%d" % i, [128, 16, 512], BF16) for i in range(2)]
    wd = [P.sb("wd%d" % i, [128, 4, D], BF16) for i in range(2)]
    aT = [P.sb("aT%d" % i, [128, 4, NOWN], BF16) for i in range(2)]
    rT = [P.sb("rT%d" % i, [128, 512], BF16) for i in range(3)]
    wur = io["w_up"].rearrange("(kc p) c -> p kc c", p=128)
    wdr = io["w_down"].rearrange("(g kc p) c -> g p kc c", p=128, kc=4)
    NG = DFF // 512

    def load_w(g):
        b = g % 2
        P.dma("pool", wu[b][:], wur[:, :, g * 512:g * 512 + 512], writes=["wu%d" % b])
        P.dma("pool", wd[b][:], wdr[g], writes=["wd%d" % b])
    load_w(0)
    st = dict(pu=0, pd=0, ri=0)

    def up(g):
        b = g % 2
        for blk in range(4):
            for (t0, tw) in TT3:
                pb = st["pu"] % 3
                st["pu"] += 1
                for kc in range(16):
                    O.mm(ps[pb][:, 0:tw], wu[b][:, kc, blk * 128:blk * 128 + 128], hT[:, kc, t0:t0 + tw], ["wu%d" % b, yk[kc]], ["ps%d" % pb],
                         start=(kc == 0), stop=(kc == 15))
                rb = st["ri"] % 3
                st["ri"] += 1
                O.act(rT[rb][:, 0:tw], ps[pb][:, 0:tw], AF.Relu, ["ps%d" % pb], ["rT%d" % rb])
                O.tt("pool", aT[b][:, blk, t0:t0 + tw], rT[rb][:, 0:tw], rT[rb][:, 0:tw], ALU.mult, ["rT%d" % rb], ["aT%d" % b])

    def down(g):
        b = g % 2
        for ti, (t0, tw) in enumerate(TOKT):
            for cg in range(4):
                pb = 3 + st["pd"] % 5
                st["pd"] += 1
                for blk in range(4):
                    O.mm(ps[pb][0:tw, :], aT[b][:, blk, t0:t0 + tw], wd[b][:, blk, cg * 512:cg * 512 + 512], ["aT%d" % b, "wd%d" % b], ["ps%d" % pb],
                         start=(blk == 0), stop=(blk == 3))
                zs = Z[0:tw, ti, cg * 512:cg * 512 + 512]
                O.tt("dve", zs, zs, ps[pb][0:tw, :], ALU.add, [zk[ti], "ps%d" % pb], [zk[ti]])
    up(0)
    for g in range(NG):
        if g + 1 < NG:
            load_w(g + 1)
            up(g + 1)
        down(g)
    P.barrier()
    P.release(m2)
    if dbg == "acc":
        dump()
        return
    layer_norm_tiles(P, O, Z, io["ln2_g"], io["ln2_b"], zk, out_dram=io["y_out"])
    P.release(m0)


_PROG = {}


def host_maps(inp):
    g = lambda k: np.asarray(inp[k], np.float32)[0]
    xp = np.asarray(inp["x_prompt"], np.float32)
    xs = np.asarray(inp["x_sample"], np.float32)
    w_in = g("w_in")
    wperm = np.ascontiguousarray(w_in[:, PERM])
    consts = make_consts()
    mu = g("rwkv_mu")
    cols = [mu[0:1024], mu[1024:2048], mu[2048:3072], g("rwkv_w0"), g("rwkv_a0"), g("rwkv_k_k"), g("rwkv_k_a"),
            g("rwkv_r_k").reshape(-1), g("rwkv_lnx_g"), g("rwkv_lnx_b")]
    rw_prm = np.ascontiguousarray(np.stack([v.reshape(8, 128) for v in cols], -1).transpose(1, 0, 2))
    rw_mul = np.zeros((128, 3), np.float32)
    rw_mul[:, 0] = mu[3072:3200]
    rw_mul[:, 1] = mu[3200:3328]
    rw_mul[:32, 2] = mu[3328:3360]
    shared = dict(consts)
    shared.update(
        w_in=wperm, rw_prm=rw_prm, rw_mul=rw_mul,
        rw_w2a2=np.ascontiguousarray(np.concatenate([g("rwkv_w2"), g("rwkv_a2")], 0)), rw_g2=g("rwkv_g2"),
        fx_g=np.ascontiguousarray(g("fox_out_g").reshape(16, 64).T), fx_bf=g("fox_b_f").reshape(16, 1),
        w_o=g("w_o"), ln1_g=g("ln1_g").reshape(1, D), ln1_b=g("ln1_b").reshape(1, D), w_up=g("w_up"), w_down=g("w_down"),
        ln2_g=g("ln2_g").reshape(1, D), ln2_b=g("ln2_b").reshape(1, D))
    maps = []
    pos = np.arange(NPR).reshape(32, 128).T
    for c in range(8):
        b, j = divmod(c, 4)
        n = 1024 * (j + 1)
        xpad = np.zeros((NTOK, D), np.float32)
        xpad[NPR - n:NPR] = xp[b, :n]
        xpad[NPR:] = xs[c]
        m = dict(shared)
        m["xT"] = np.ascontiguousarray(xpad.T)
        m["xo"] = np.ascontiguousarray(xpad[OWN0:NTOK])
        sh = np.zeros((N_FM, 1), np.float32)
        st = np.asarray(inp["state_rwkv_shift"], np.float32)[0, c, 0]
        for seg in ("rk", "rv", "xwxa", "r", "xg"):
            o, n_ = SEG[seg]
            sh[ROW[seg]:ROW[seg] + n_, 0] = st[o:o + n_]
        m["shiftT"] = sh
        s0 = np.asarray(inp["state_rwkv_wkv"], np.float32)[0, c]
        m["s0T"] = np.ascontiguousarray(s0.transpose(0, 2, 1).reshape(8, 128, 64).transpose(1, 0, 2))
        m["fx_padb"] = np.where(pos < NPR - n, -BIG, 0.0).astype(np.float32)
        m["fx_clfT"] = np.ascontiguousarray(np.asarray(inp["cache_fox_logf"], np.float32)[0, c].T)
        m["fx_ckT"] = np.ascontiguousarray(np.asarray(inp["cache_fox_k"], np.float32)[0, c].transpose(1, 2, 0))
        m["fx_cv"] = np.ascontiguousarray(np.asarray(inp["cache_fox_v"], np.float32)[0, c].reshape(1024, 1024))
        maps.append(m)
    return maps


def assemble(results):
    f32 = np.float32
    y_p = np.zeros((2, 4096, D), f32)
    y_s = np.zeros((8, NS, D), f32)
    pk = np.zeros((1, 2, 4096, 16, 64), f32)
    pv = np.zeros((1, 2, 4096, 16, 64), f32)
    pf = np.zeros((1, 2, 4096, 16), f32)
    pS = np.zeros((1, 2, 16, 64, 64), f32)
    psh = np.zeros((1, 2, 1, 3360), f32)
    sk = np.zeros((1, 8, NS, 16, 64), f32)
    sv = np.zeros((1, 8, NS, 16, 64), f32)
    sf = np.zeros((1, 8, NS, 16), f32)
    sS = np.zeros((1, 8, 16, 64, 64), f32)
    ssh = np.zeros((1, 8, 1, 3360), f32)

    def wkv(a):
        return a.reshape(2, 64, 8, 64).transpose(2, 0, 3, 1).reshape(16, 64, 64)

    def shift(r, ci):
        out = np.zeros(3360, f32)
        a, bq = r["shA"][:, ci], r["shB"][:, ci]
        out[1024:2048] = a[0:1024]
        out[2048:3072] = a[1024:2048]
        out[3072:3200] = a[2048:2176]
        out[0:1024] = bq[0:1024]
        out[3200:3360] = bq[1024:1184]
        return out
    for c in range(8):
        r = results[c]
        b, j = divmod(c, 4)
        sl = slice(1024 * j, 1024 * (j + 1))
        y_p[b, sl] = r["y_out"][0:1024]
        y_s[c] = r["y_out"][1024:NOWN]
        kT = r["k_out"]
        pk[0, b, sl] = kT[:, 0:1024].T.reshape(1024, 16, 64)
        sk[0, c] = kT[:, 1024:NOWN].T.reshape(NS, 16, 64)
        pv[0, b, sl] = r["v_out"][0:1024].reshape(1024, 16, 64)
        sv[0, c] = r["v_out"][1024:NOWN].reshape(NS, 16, 64)
        pf[0, b, sl] = r["logf_o"][:, 0:1024].T
        sf[0, c] = r["logf_o"][:, 1024:NOWN].T
        sS[0, c] = wkv(r["wkv_s"])
        ssh[0, c, 0] = shift(r, 1)
        if j == 3:
            pS[0, b] = wkv(r["wkv_p"])
            psh[0, b, 0] = shift(r, 0)
    return (y_p, y_s, pk, pv, pf, pS, psh, sk, sv, sf, sS, ssh)


def kernel(**inputs):
    if "full" not in _PROG:
        _PROG["full"] = build()
    P = _PROG["full"]
    maps = host_maps(inputs)
    res = run_bass_kernel_spmd(P.nc, maps, core_ids=list(range(8)))
    return assemble(res.results)
```
